# Optimizing a Trainium2 kernel written in Bass

```python
import jax
import jax.numpy as jnp
from jax import lax
import numpy as np

D_MODEL = 1024
BATCH = 8
SEQ = 4096
DEPTH = 1

NSA_HEADS = 8
NSA_KV_GROUPS = 2
NSA_HEAD_DIM = 64
NSA_CMP_STRIDE = 16
NSA_CMP_LEN = 2 * NSA_CMP_STRIDE
NSA_SLC_LEN = 64
NSA_TOPN = 16
NSA_WINDOW = 512
NSA_CMP_HIDDEN = 2 * NSA_HEAD_DIM
NSA_Q_BLOCK = 64
NSA_FORCE_BONUS = 1e4

MLA_HEADS = 8
MLA_Q_RANK = 384
MLA_KV_RANK = 256
MLA_NOPE_DIM = 64
MLA_ROPE_DIM = 32
MLA_V_DIM = 64
MLA_Q_BLOCK = 128

MIX_WIDTH = NSA_HEADS * NSA_HEAD_DIM + MLA_HEADS * MLA_V_DIM

MEM_TOKENS = 256
MEM_HEADS = 4
MEM_HEAD_DIM = D_MODEL // MEM_HEADS

D_FF = 2816
CONV_WIDTH = 3

ROPE_THETA = 10000.0
LN_EPS = 1e-5
RMS_EPS = 1e-6
NEG_INF = -1e30
DEEPNORM_ALPHA = (2.0 * DEPTH) ** 0.25
DEEPNORM_BETA = (8.0 * DEPTH) ** -0.25

NSA_Q_COLS = NSA_HEADS * NSA_HEAD_DIM
NSA_KV_COLS = 3 * 2 * NSA_KV_GROUPS * NSA_HEAD_DIM
NSA_GATE_COLS = 3 * NSA_HEADS
_C1 = NSA_Q_COLS
_C2 = _C1 + NSA_KV_COLS
_C3 = _C2 + NSA_GATE_COLS
_C4 = _C3 + MLA_Q_RANK
_C5 = _C4 + MLA_KV_RANK
IN_COLS = _C5 + MLA_ROPE_DIM
IN_SPLITS = (_C1, _C2, _C3, _C4, _C5)

kernel_name = "hymba_nsa_mla_deepnorm_convffn"


def layer_norm(x, g, b):
    xf = x.astype(jnp.float32)
    mu = jnp.mean(xf, -1, keepdims=True)
    var = jnp.mean(jnp.square(xf - mu), -1, keepdims=True)
    return ((xf - mu) * lax.rsqrt(var + LN_EPS) * g.astype(jnp.float32) + b.astype(jnp.float32)).astype(x.dtype)


def rms_norm(x, g):
    xf = x.astype(jnp.float32)
    return (xf * lax.rsqrt(jnp.mean(xf * xf, -1, keepdims=True) + RMS_EPS) * g.astype(jnp.float32)).astype(x.dtype)


def rope(x, pos):
    d = x.shape[-1]
    inv = ROPE_THETA ** (-jnp.arange(0, d, 2, dtype=jnp.float32) / d)
    ang = pos.astype(jnp.float32)[:, :, None, None] * inv
    cos, sin = jnp.cos(ang), jnp.sin(ang)
    x1, x2 = jnp.split(x.astype(jnp.float32), 2, axis=-1)
    return jnp.concatenate([x1 * cos - x2 * sin, x2 * cos + x1 * sin], -1).astype(x.dtype)


def masked_softmax(s, mask, axis=-1):
    s = jnp.where(mask, s.astype(jnp.float32), NEG_INF)
    return jax.nn.softmax(s, axis=axis) * mask


def compress(kv, pos_emb, w1, b1, w2, b2):
    B, S, G, dh = kv.shape
    ch = kv.reshape(B, S // NSA_CMP_STRIDE, NSA_CMP_STRIDE, G, dh)
    blocks = jnp.concatenate([ch[:, :-1], ch[:, 1:]], axis=2)
    blocks = blocks + pos_emb[None, None, :, None, :]
    nc = blocks.shape[1]
    flat = blocks.transpose(0, 1, 3, 2, 4).reshape(B, nc, G, NSA_CMP_LEN * dh)
    return jax.nn.gelu(flat @ w1 + b1) @ w2 + b2


def nsa_attention(q, k_cmp, v_cmp, k_slc, v_slc, k_win, v_win, gates):
    B, S, H, dh = q.shape
    G = NSA_KV_GROUPS
    R = H // G
    NC = k_cmp.shape[1]
    NS = S // NSA_SLC_LEN
    TOPN = min(NSA_TOPN, NS)
    QB = NSA_Q_BLOCK
    scale = dh ** -0.5
    qg = q.reshape(B, S, G, R, dh)
    cmp_start = jnp.arange(NC)[:, None] * NSA_CMP_STRIDE
    cmp_end = jnp.arange(NC) * NSA_CMP_STRIDE + NSA_CMP_LEN - 1
    slc_start = jnp.arange(NS)[None, :] * NSA_SLC_LEN
    cover = jnp.clip(jnp.minimum(cmp_start + NSA_CMP_LEN, slc_start + NSA_SLC_LEN)
                     - jnp.maximum(cmp_start, slc_start), 0, None).astype(jnp.float32) / NSA_CMP_LEN
    kb = k_slc.reshape(B, NS, NSA_SLC_LEN, G, dh).transpose(0, 3, 1, 2, 4)
    vb = v_slc.reshape(B, NS, NSA_SLC_LEN, G, dh).transpose(0, 3, 1, 2, 4)
    pad = ((0, 0), (NSA_WINDOW, 0), (0, 0), (0, 0))
    kw = jnp.pad(k_win, pad)
    vw = jnp.pad(v_win, pad)
    bi = jnp.arange(B)[:, None, None, None]
    gi = jnp.arange(G)[None, :, None, None]
    blk = jnp.arange(NS)
    in_blk = jnp.arange(NSA_SLC_LEN)
    span = jnp.arange(NSA_WINDOW + QB)

    def query_block(c):
        t0 = c * QB
        t = t0 + jnp.arange(QB)
        qc = lax.dynamic_slice_in_dim(qg, t0, QB, axis=1)
        s_c = jnp.einsum('bqgrd,bngd->bgrqn', qc, k_cmp) * scale
        p_c = masked_softmax(s_c, cmp_end[None, :] <= t[:, None])
        o_c = jnp.einsum('bgrqn,bngd->bqgrd', p_c.astype(v_cmp.dtype), v_cmp)
        imp = jnp.einsum('bgrqn,nj->bgqj', p_c, cover)
        cur = (t // NSA_SLC_LEN)[:, None]
        forced = (blk == 0) | (blk == cur) | (blk == cur - 1)
        score = jnp.where(blk <= cur, jnp.where(forced, NSA_FORCE_BONUS, imp), NEG_INF)
        _, idx = lax.top_k(score, TOPN)
        ks = kb[bi, gi, idx]
        vs = vb[bi, gi, idx]
        kpos = idx[..., None] * NSA_SLC_LEN + in_blk
        m_s = (kpos <= t[None, None, :, None, None])[:, :, None]
        s_s = jnp.einsum('bqgrd,bgqnld->bgrqnl', qc, ks) * scale
        p_s = masked_softmax(s_s, m_s, axis=(-2, -1))
        o_s = jnp.einsum('bgrqnl,bgqnld->bqgrd', p_s.astype(vs.dtype), vs)
        kwc = lax.dynamic_slice_in_dim(kw, t0, NSA_WINDOW + QB, axis=1)
        vwc = lax.dynamic_slice_in_dim(vw, t0, NSA_WINDOW + QB, axis=1)
        spos = t0 - NSA_WINDOW + span
        diff = t[:, None] - spos[None, :]
        m_w = (spos[None, :] >= 0) & (diff >= 0) & (diff < NSA_WINDOW)
        s_w = jnp.einsum('bqgrd,bkgd->bgrqk', qc, kwc) * scale
        p_w = masked_softmax(s_w, m_w)
        o_w = jnp.einsum('bgrqk,bkgd->bqgrd', p_w.astype(vwc.dtype), vwc)
        gc = lax.dynamic_slice_in_dim(gates, t0, QB, axis=1)
        return gc[..., 0:1] * o_c + gc[..., 1:2] * o_s + gc[..., 2:3] * o_w

    out = lax.map(query_block, jnp.arange(S // QB))
    return out.transpose(1, 0, 2, 3, 4, 5).reshape(B, S, H * dh)


def causal_attention_blocks(q, k, v):
    B, S, H, _ = q.shape
    dv = v.shape[-1]
    scale = q.shape[-1] ** -0.5
    QB = MLA_Q_BLOCK
    kpos = jnp.arange(S)

    def query_block(c):
        t0 = c * QB
        qc = lax.dynamic_slice_in_dim(q, t0, QB, axis=1)
        s = jnp.einsum('bqhd,bkhd->bhqk', qc, k) * scale
        p = masked_softmax(s, kpos[None, :] <= (t0 + jnp.arange(QB))[:, None])
        return jnp.einsum('bhqk,bkhd->bqhd', p.astype(v.dtype), v)

    out = lax.map(query_block, jnp.arange(S // QB))
    return out.transpose(1, 0, 2, 3, 4).reshape(B, S, H * dv)


def memory_cross_attention(x, mem, wq, wk, wv, wo):
    B, S, _ = x.shape
    M = mem.shape[1]
    q = (x @ wq).reshape(B, S, MEM_HEADS, MEM_HEAD_DIM)
    k = (mem @ wk).reshape(B, M, MEM_HEADS, MEM_HEAD_DIM)
    v = (mem @ wv).reshape(B, M, MEM_HEADS, MEM_HEAD_DIM)
    s = jnp.einsum('bqhd,bkhd->bhqk', q, k).astype(jnp.float32) * (MEM_HEAD_DIM ** -0.5)
    p = jax.nn.softmax(s, axis=-1)
    o = jnp.einsum('bhqk,bkhd->bqhd', p.astype(v.dtype), v).reshape(B, S, D_MODEL)
    return o @ wo


def conv_ffn(x, w_up, conv_w, conv_b, w_down):
    gate, up = jnp.split(x @ w_up, 2, axis=-1)
    gate = lax.conv_general_dilated(
        gate, conv_w[:, None, :], window_strides=(1,), padding=[(CONV_WIDTH - 1, 0)],
        dimension_numbers=('NWC', 'WIO', 'NWC'), feature_group_count=D_FF) + conv_b
    return (jax.nn.silu(gate) * up) @ w_down


def _w(k, shape, fan_in, scale=1.0):
    return jax.random.normal(k, shape, jnp.float32) * (scale * fan_in ** -0.5)


def _gain(k, shape):
    return 1.0 + 0.01 * jax.random.normal(k, shape, jnp.float32)


def _small(k, shape):
    return 0.01 * jax.random.normal(k, shape, jnp.float32)


def setup_inputs(seed: int = 0) -> dict:
    key = jax.random.key(seed)
    ks = jax.random.split(key, 40)
    L = DEPTH
    dh = NSA_HEAD_DIM
    flat = NSA_CMP_LEN * dh
    offset = jax.random.randint(ks[2], (BATCH, 1), 0, 1024, dtype=jnp.int32)
    positions = (offset + jnp.arange(SEQ, dtype=jnp.int32)[None, :]).astype(jnp.int32)
    return {
        "x": jax.random.normal(ks[0], (BATCH, SEQ, D_MODEL), jnp.float32),
        "mem": jax.random.normal(ks[1], (BATCH, MEM_TOKENS, D_MODEL), jnp.float32),
        "positions": positions,
        "w_in": _w(ks[3], (L, D_MODEL, IN_COLS), D_MODEL),
        "nsa_k_pos": 0.02 * jax.random.normal(ks[4], (L, NSA_CMP_LEN, dh), jnp.float32),
        "nsa_ck_w1": _w(ks[5], (L, flat, NSA_CMP_HIDDEN), flat),
        "nsa_ck_b1": _small(ks[6], (L, NSA_CMP_HIDDEN)),
        "nsa_ck_w2": _w(ks[7], (L, NSA_CMP_HIDDEN, dh), NSA_CMP_HIDDEN),
        "nsa_ck_b2": _small(ks[8], (L, dh)),
        "nsa_v_pos": 0.02 * jax.random.normal(ks[9], (L, NSA_CMP_LEN, dh), jnp.float32),
        "nsa_cv_w1": _w(ks[10], (L, flat, NSA_CMP_HIDDEN), flat),
        "nsa_cv_b1": _small(ks[11], (L, NSA_CMP_HIDDEN)),
        "nsa_cv_w2": _w(ks[12], (L, NSA_CMP_HIDDEN, dh), NSA_CMP_HIDDEN),
        "nsa_cv_b2": _small(ks[13], (L, dh)),
        "mla_q_norm": _gain(ks[14], (L, MLA_Q_RANK)),
        "mla_w_uq": _w(ks[15], (L, MLA_Q_RANK, MLA_HEADS * (MLA_NOPE_DIM + MLA_ROPE_DIM)), MLA_Q_RANK),
        "mla_kv_norm": _gain(ks[16], (L, MLA_KV_RANK)),
        "mla_w_ukv": _w(ks[17], (L, MLA_KV_RANK, MLA_HEADS * (MLA_NOPE_DIM + MLA_V_DIM)), MLA_KV_RANK),
        "w_o": _w(ks[18], (L, MIX_WIDTH, D_MODEL), MIX_WIDTH, DEEPNORM_BETA),
        "ln1_g": _gain(ks[19], (L, D_MODEL)),
        "ln1_b": _small(ks[20], (L, D_MODEL)),
        "mem_wq": _w(ks[21], (L, D_MODEL, D_MODEL), D_MODEL),
        "mem_wk": _w(ks[22], (L, D_MODEL, D_MODEL), D_MODEL),
        "mem_wv": _w(ks[23], (L, D_MODEL, D_MODEL), D_MODEL),
        "mem_wo": _w(ks[24], (L, D_MODEL, D_MODEL), D_MODEL, DEEPNORM_BETA),
        "ln2_g": _gain(ks[25], (L, D_MODEL)),
        "ln2_b": _small(ks[26], (L, D_MODEL)),
        "ffn_w_up": _w(ks[27], (L, D_MODEL, 2 * D_FF), D_MODEL),
        "ffn_conv_w": _w(ks[28], (L, CONV_WIDTH, D_FF), CONV_WIDTH),
        "ffn_conv_b": _small(ks[29], (L, D_FF)),
        "ffn_w_down": _w(ks[30], (L, D_FF, D_MODEL), D_FF, DEEPNORM_BETA),
        "ln3_g": _gain(ks[31], (L, D_MODEL)),
        "ln3_b": _small(ks[32], (L, D_MODEL)),
    }


def reference(x, mem, positions, w_in, nsa_k_pos, nsa_ck_w1, nsa_ck_b1, nsa_ck_w2, nsa_ck_b2,
              nsa_v_pos, nsa_cv_w1, nsa_cv_b1, nsa_cv_w2, nsa_cv_b2,
              mla_q_norm, mla_w_uq, mla_kv_norm, mla_w_ukv, w_o, ln1_g, ln1_b,
              mem_wq, mem_wk, mem_wv, mem_wo, ln2_g, ln2_b,
              ffn_w_up, ffn_conv_w, ffn_conv_b, ffn_w_down, ln3_g, ln3_b):
    B, S, _ = x.shape
    G = NSA_KV_GROUPS
    R = NSA_HEADS // NSA_KV_GROUPS
    dh = NSA_HEAD_DIM
    pos_cmp = positions[:, NSA_CMP_LEN - 1::NSA_CMP_STRIDE]
    for l in range(DEPTH):
        h = x @ w_in[l]
        nq, nkv, ng, mq, mkv, mkr = jnp.split(h, IN_SPLITS, axis=-1)
        q_n = rope(nq.reshape(B, S, NSA_HEADS, dh), positions)
        kv = nkv.reshape(B, S, 3, 2, G, dh)
        k_cmp = rope(compress(kv[:, :, 0, 0], nsa_k_pos[l], nsa_ck_w1[l], nsa_ck_b1[l], nsa_ck_w2[l], nsa_ck_b2[l]), pos_cmp)
        v_cmp = compress(kv[:, :, 0, 1], nsa_v_pos[l], nsa_cv_w1[l], nsa_cv_b1[l], nsa_cv_w2[l], nsa_cv_b2[l])
        k_slc = rope(kv[:, :, 1, 0], positions)
        k_win = rope(kv[:, :, 2, 0], positions)
        gates = jax.nn.sigmoid(ng.astype(jnp.float32)).astype(x.dtype).reshape(B, S, G, R, 3)
        o_nsa = nsa_attention(q_n, k_cmp, v_cmp, k_slc, kv[:, :, 1, 1], k_win, kv[:, :, 2, 1], gates)
        q_m = (rms_norm(mq, mla_q_norm[l]) @ mla_w_uq[l]).reshape(B, S, MLA_HEADS, MLA_NOPE_DIM + MLA_ROPE_DIM)
        q_nope, q_pe = jnp.split(q_m, [MLA_NOPE_DIM], axis=-1)
        q_m = jnp.concatenate([q_nope, rope(q_pe, positions)], axis=-1)
        kv_m = (rms_norm(mkv, mla_kv_norm[l]) @ mla_w_ukv[l]).reshape(B, S, MLA_HEADS, MLA_NOPE_DIM + MLA_V_DIM)
        k_nope, v_m = jnp.split(kv_m, [MLA_NOPE_DIM], axis=-1)
        k_pe = jnp.broadcast_to(rope(mkr[:, :, None, :], positions), (B, S, MLA_HEADS, MLA_ROPE_DIM))
        k_m = jnp.concatenate([k_nope, k_pe], axis=-1)
        o_mla = causal_attention_blocks(q_m, k_m, v_m)
        mix = jnp.concatenate([o_nsa, o_mla], axis=-1) @ w_o[l]
        x = layer_norm(DEEPNORM_ALPHA * x + mix, ln1_g[l], ln1_b[l])
        x = layer_norm(DEEPNORM_ALPHA * x + memory_cross_attention(x, mem, mem_wq[l], mem_wk[l], mem_wv[l], mem_wo[l]),
                       ln2_g[l], ln2_b[l])
        x = layer_norm(DEEPNORM_ALPHA * x + conv_ffn(x, ffn_w_up[l], ffn_conv_w[l], ffn_conv_b[l], ffn_w_down[l]),
                       ln3_g[l], ln3_b[l])
    return x
```

```python
import numpy as np
from contextlib import ExitStack
import concourse.bass as bass
import concourse.mybir as mybir
from concourse.bass_utils import run_bass_kernel_spmd

F32 = mybir.dt.float32
BF16 = mybir.dt.bfloat16
I32 = mybir.dt.int32
AF = mybir.ActivationFunctionType
ALU = mybir.AluOpType

S_LEN = 4096
D = 1024
NT = 32
NB = 8
DFF = 2816
NCH = 22
ALPHA = 2.0 ** 0.25
PI = float(np.pi)
NEG = -30000.0

SEM_LIMIT = 20000
DMA_RING = 8
import os as _os
SAME_ENGINE_SYNC = set(_os.environ.get("KSES", "act,dve,pool").split(","))


class Sched:
    STREAMS = ("pe", "act", "dve", "pool", "sp")

    def __init__(self, nc):
        self.nc = nc
        self.ops = []
        self.last_writer = {}
        self.readers = {}
        self.out_dmas = []

    def op(self, stream, fn, reads=(), writes=(), dma=False, out=False):
        idx = len(self.ops)
        deps = set()
        reads = list(reads) + ["PHASE"]
        for r in reads:
            w = self.last_writer.get(r)
            if w is not None:
                deps.add(w)
        for w_ in writes:
            w = self.last_writer.get(w_)
            if w is not None:
                deps.add(w)
            for rd in self.readers.get(w_, ()):
                deps.add(rd)
        for r in reads:
            self.readers.setdefault(r, []).append(idx)
        for w_ in writes:
            self.last_writer[w_] = idx
            self.readers[w_] = []
        self.ops.append(dict(stream=stream, fn=fn, deps=deps, dma=dma))
        if out:
            self.out_dmas.append(idx)
        return idx

    def pe(self, fn, reads=(), writes=()):
        return self.op("pe", fn, reads, writes)

    def act(self, fn, reads=(), writes=()):
        return self.op("act", fn, reads, writes)

    def dve(self, fn, reads=(), writes=()):
        return self.op("dve", fn, reads, writes)

    def pool(self, fn, reads=(), writes=()):
        return self.op("pool", fn, reads, writes)

    def dma(self, out_ap, in_ap, reads=(), writes=(), q="sp", out=False):
        return self.op(q, lambda e: e.dma_start(out=out_ap, in_=in_ap), reads, writes, dma=True, out=out)

    def emit(self, es):
        nc = self.nc
        ops = self.ops
        ops.append(dict(stream="sp", fn=None, deps=set(self.out_dmas), dma=False))
        n = len(ops)
        dma_count = {s: 0 for s in self.STREAMS}
        dma_hist = {s: [] for s in self.STREAMS}
        for i, o in enumerate(ops):
            if o["dma"]:
                s = o["stream"]
                k = dma_count[s]
                o["dma_k"] = k
                if k >= DMA_RING:
                    o["deps"].add(dma_hist[s][k - DMA_RING])
                dma_hist[s].append(i)
                dma_count[s] += 1
        has_dep = [False] * n
        for i, o in enumerate(ops):
            for d in o["deps"]:
                od = ops[d]
                if od["dma"] or od["stream"] != o["stream"]:
                    has_dep[d] = True
                elif od["stream"] in SAME_ENGINE_SYNC:
                    has_dep[d] = True
        sem_cnt = [0]

        def new_sem(tag):
            sem_cnt[0] += 1
            return es.enter_context(nc.semaphore(f"s_{tag}_{sem_cnt[0]}"))

        cur_sem, cur_cnt, rings = {}, {}, {}
        for i, o in enumerate(ops):
            s = o["stream"]
            if o["dma"]:
                if s not in rings:
                    rings[s] = [new_sem(f"dq{s}") for _ in range(DMA_RING)]
                k = o["dma_k"]
                o["sig"] = (rings[s][k % DMA_RING], 16 * (k // DMA_RING + 1), 16)
            elif has_dep[i]:
                if s not in cur_sem or cur_cnt[s] >= SEM_LIMIT:
                    cur_sem[s] = new_sem(s)
                    cur_cnt[s] = 0
                cur_cnt[s] += 1
                o["sig"] = (cur_sem[s], cur_cnt[s], 1)
            else:
                o["sig"] = None
        waited = {s: {} for s in self.STREAMS}
        per_stream = {s: [] for s in self.STREAMS}
        for i, o in enumerate(ops):
            s = o["stream"]
            need = {}
            for d in o["deps"]:
                od = ops[d]
                if (not od["dma"]) and od["stream"] == s and (s not in SAME_ENGINE_SYNC):
                    continue
                sem, val, _ = od["sig"]
                key = id(sem)
                if waited[s].get(key, 0) >= val:
                    continue
                if key not in need or need[key][1] < val:
                    need[key] = (sem, val)
            for key, (sem, val) in need.items():
                waited[s][key] = val
            o["waits"] = list(need.values())
            per_stream[s].append(o)
        self.n_sems = sem_cnt[0]
        self.stream_sizes = {s: len(v) for s, v in per_stream.items()}
        block = es.enter_context(nc.Block())

        def run(stream_ops):
            def body(eng):
                for o in stream_ops:
                    for sem, val in o["waits"]:
                        eng.wait_ge(sem, val)
                    if o["fn"] is None:
                        continue
                    ins = o["fn"](eng)
                    if o["sig"] is not None:
                        ins.then_inc(o["sig"][0], o["sig"][2])
            return body

        block.tensor(run(per_stream["pe"]))
        block.scalar(run(per_stream["act"]))
        block.vector(run(per_stream["dve"]))
        block.gpsimd(run(per_stream["pool"]))
        block.sync(run(per_stream["sp"]))


def MM(out, lhsT, rhs, start, stop):
    return lambda e: e.matmul(out, lhsT=lhsT, rhs=rhs, start=start, stop=stop, skip_group_check=True)


def TR(out, in_, ident):
    return lambda e: e.transpose(out=out, in_=in_, identity=ident)


def ACT(out, in_, func, **kw):
    return lambda e: e.activation(out=out, in_=in_, func=func, **kw)


def TT(out, in0, in1, op):
    return lambda e: e.tensor_tensor(out=out, in0=in0, in1=in1, op=op)


def TS(out, in0, s1, s2, op0, op1=None):
    if op1 is None:
        return lambda e: e.tensor_scalar(out=out, in0=in0, scalar1=s1, scalar2=None, op0=op0)
    return lambda e: e.tensor_scalar(out=out, in0=in0, scalar1=s1, scalar2=s2, op0=op0, op1=op1)


def STT(out, in0, scalar, in1, op0, op1):
    return lambda e: e.scalar_tensor_tensor(out=out, in0=in0, scalar=scalar, in1=in1, op0=op0, op1=op1)


def CP(out, in_):
    return lambda e: e.tensor_copy(out=out, in_=in_)


def MEMSET(ap, v):
    return lambda e: e.memset(ap, v)


def RECIP(out, in_):
    return lambda e: e.reciprocal(out=out, in_=in_)


def ASEL(out, in_, pattern, op, fill, base, cm):
    return lambda e: e.affine_select(out=out, in_=in_, pattern=pattern, compare_op=op, fill=fill,
                                     base=base, channel_multiplier=cm)


DT_SIZE = {F32: 4, BF16: 2, I32: 4}
ARENA_BYTES = 212480


class Builder:
    def __init__(self, nc, es, dbg=()):
        self.nc = nc
        self.es = es
        self.S = Sched(nc)
        self.dbg = set(dbg)
        self.dbg_outs = {}
        self.arena = nc.alloc_sbuf_tensor("arena", [128, ARENA_BYTES // 4], F32).ap()
        self.top = 0
        self.banks = [es.enter_context(nc.psum_tensor(f"pb{i}", [128, 512], F32)).ap() for i in range(8)]
        self.bank_i = 0
        self.pinned = set()
        self.din = {}

    def inp(self, name, shape, dt=F32):
        ap = self.nc.dram_tensor(name, list(shape), dt, kind="ExternalInput").ap()
        self.din[name] = ap
        return ap

    def alloc(self, name, shape, dt=F32):
        n = int(np.prod(shape[1:]))
        nbytes = (n * DT_SIZE[dt] + 31) // 32 * 32
        off = self.top
        self.top += nbytes
        self.max_top = max(getattr(self, "max_top", 0), self.top)
        assert self.top <= ARENA_BYTES, f"arena overflow at {name}: {self.top}"
        ap = self.arena[0:shape[0], off // 4:(off + nbytes) // 4]
        if dt != F32:
            ap = ap.bitcast(dt)
        ap = ap[:, 0:n]
        if len(shape) == 3:
            ap = ap.rearrange("p (a b) -> p a b", b=shape[2])
        elif len(shape) == 4:
            ap = ap.rearrange("p (a b c) -> p a b c", b=shape[2], c=shape[3])
        return ap

    def ring(self, name, shape, dt, n):
        return [(self.alloc(f"{name}{i}", shape, dt), f"{name}{i}") for i in range(n)]

    def mark(self):
        return self.top

    def release(self, mark):
        scr = self.barrier_scr
        self.S.op("pool", MEMSET(scr, 0.0), reads=[], writes=["PHASE", "barrier_scr"])
        self.top = mark

    def bank(self, pin=False):
        while self.bank_i in self.pinned:
            self.bank_i = (self.bank_i + 1) % 8
        i = self.bank_i
        self.bank_i = (i + 1) % 8
        if pin:
            self.pinned.add(i)
        return self.banks[i], ("PB", i)

    def unpin(self, key):
        self.pinned.discard(key[1])

    def dump(self, name, ap, reads, dt=F32):
        if name not in self.dbg:
            return
        shape = list(ap.shape)
        d = self.nc.dram_tensor("dbg_" + name, shape, dt, kind="ExternalOutput").ap()
        self.dbg_outs[name] = shape
        self.S.dma(d, ap, reads=reads, out=True)


def build_program(dbg=(), stop_after=None):
    nc = bass.Bass("TRN2", target_bir_lowering=False)
    es = ExitStack()
    with es:
        b = Builder(nc, es, dbg)
        S = b.S
        xT = b.inp("xT", [D, S_LEN])
        x_in = b.inp("x", [S_LEN, D])
        memT = b.inp("memT", [D, 256])
        pos = b.inp("pos", [1, S_LEN], I32)
        posc = b.inp("posc", [1, 256], I32)
        wfm_nsa = b.inp("wfm_nsa", [D, 2304])
        wtm_nsa = b.inp("wtm_nsa", [D, 280])
        wfm_mla = b.inp("wfm_mla", [D, 896])
        w1k_d = b.inp("w1k", [128, 32 * 128])
        w1v_d = b.inp("w1v", [128, 32 * 128])
        poskT_d = b.inp("poskT", [128, 64])
        posvT_d = b.inp("posvT", [128, 64])
        w2k_d = b.inp("w2k", [128, 256])
        w2v_d = b.inp("w2v", [128, 64])
        b2v_d = b.inp("b2v", [1, 64])
        cover_d = b.inp("cover", [256, 64])
        cand_d = b.inp("cand", [S_LEN, 64])
        forced_d = b.inp("forced", [S_LEN, 64])
        psc_d = b.inp("psc", [128, 16])
        wuq_a_d = b.inp("wuq_a", [384, 768])
        wuq_b_d = b.inp("wuq_b", [384, 768])
        wuk_d = b.inp("wuk", [256, 512])
        wuv_d = b.inp("wuv", [256, 512])
        wo_d = b.inp("w_o", [D, D])
        mwq_d = b.inp("mem_wq", [D, D])
        mwk_d = b.inp("mem_wk", [D, D])
        mwv_d = b.inp("mem_wv", [D, D])
        mwo_d = b.inp("mem_wo", [D, D])
        wup_d = b.inp("w_up", [D, 2 * DFF])
        wdn_d = b.inp("w_dn", [DFF, D])
        lngb_d = b.inp("lngb", [6, D])
        convw_d = b.inp("convw", [128, NCH * 3])
        convb_d = b.inp("convb", [128, NCH])
        out_d = nc.dram_tensor("out", [S_LEN, D], F32, kind="ExternalOutput").ap()
        wo_s = nc.dram_tensor("wo_s", [D, D], BF16).ap()
        mwq_s = nc.dram_tensor("mwq_s", [D, D], BF16).ap()
        mwo_s = nc.dram_tensor("mwo_s", [D, D], BF16).ap()
        wup_s = nc.dram_tensor("wup_s", [D, 2 * DFF], BF16).ap()
        wdn_s = nc.dram_tensor("wdn_s", [DFF, D], BF16).ap()
        oT_s = nc.dram_tensor("oT_s", [D, S_LEN], BF16).ap()

        ident = b.alloc("ident", [128, 128], BF16)
        caus = b.alloc("caus", [128, 128], BF16)
        wedge = b.alloc("wedge", [128, 128], BF16)
        ones = b.alloc("ones", [128, 128], BF16)
        psc = b.alloc("psc", [128, 16], F32)
        b.barrier_scr = b.alloc("barrier_scr", [128, 8], F32)
        zeros = b.alloc("zeros", [128, 512], BF16)
        S.pool(MEMSET(zeros, 0.0), writes=["zeros"])
        S.pool(MEMSET(ones, 1.0), writes=["ones"])
        S.pool(MEMSET(ident, 1.0), writes=["ident"])
        S.pool(ASEL(ident, ident, [[1, 128]], ALU.is_equal, 0.0, 0, -1), reads=["ident"], writes=["ident"])
        S.pool(ASEL(caus, zeros[:, 0:128], [[1, 128]], ALU.is_ge, NEG, 0, -1), reads=["zeros"], writes=["caus"])
        S.pool(ASEL(wedge, zeros[:, 0:128], [[-1, 128]], ALU.is_gt, NEG, 0, 1), reads=["zeros"], writes=["wedge"])
        S.dma(psc, psc_d, writes=["psc"])
        for (dst, src, rows, key) in ((wo_s, wo_d, D, "wo_s"), (mwq_s, mwq_d, D, "mwq_s"), (mwo_s, mwo_d, D, "mwo_s"),
                                      (wup_s, wup_d, D, "wup_s"), (wdn_s, wdn_d, DFF, "wdn_s")):
            for r0 in range(0, rows, 128):
                S.dma(dst[r0:r0 + 128, :], src[r0:r0 + 128, :], writes=[(key, r0)], q="pool")
        base_mark = b.mark()

        PSC = lambda c: psc[:, c:c + 1]

        def load_xT_block(tb, ring_slot):
            xt, xk = ring_slot
            src = xT.rearrange("(kc p) t -> p kc t", p=128)
            for k0 in range(0, 8, 2):
                S.dma(xt[:, k0:k0 + 2, :], src[:, k0:k0 + 2, tb * 512:(tb + 1) * 512], writes=[(xk, k0)], q="pool")
            return xt, [(xk, k0) for k0 in range(0, 8, 2)]

        def rope_tables(pos_src, n, inv_c, sgn_c, nsg_c, tiles, prow=slice(0, 128)):
            (posi, kpi), (posf, kpf), (m1, km1), (m2, km2), (cos, kc), (sin, ks) = tiles
            S.dma(posi[:, 0:n], pos_src.partition_broadcast(128), writes=[kpi])
            S.dve(CP(posf[:, 0:n], posi[:, 0:n]), reads=[kpi], writes=[kpf])
            C1_, C2_ = 6.28125, 2 * PI - 6.28125
            S.dve(TS(m1[:, 0:n], posf[:, 0:n], PSC(inv_c), None, ALU.mult), reads=[kpf, "psc"], writes=[km1])
            S.dve(TS(m2[:, 0:n], posf[:, 0:n], PSC(inv_c), 0.5 * PI, ALU.mult, ALU.add), reads=[kpf, "psc"], writes=[km2])
            S.dve(TS(sin[:, 0:n], m1[:, 0:n], 1.0 / (2 * PI), None, ALU.mult), reads=[km1], writes=[ks])
            S.dve(CP(posi[:, 0:n], sin[:, 0:n]), reads=[ks], writes=[kpi])
            S.dve(CP(cos[:, 0:n], posi[:, 0:n]), reads=[kpi], writes=[kc])
            S.dve(STT(m1[:, 0:n], cos[:, 0:n], -C1_, m1[:, 0:n], ALU.mult, ALU.add), reads=[kc, km1], writes=[km1])
            S.dve(STT(m1[:, 0:n], cos[:, 0:n], -C2_, m1[:, 0:n], ALU.mult, ALU.add), reads=[kc, km1], writes=[km1])
            S.dve(TS(m1[:, 0:n], m1[:, 0:n], PI, -PI, ALU.min, ALU.max), reads=[km1], writes=[km1])
            S.act(ACT(sin[:, 0:n], m1[:, 0:n], AF.Sin, scale=PSC(sgn_c)), reads=[km1, "psc"], writes=[ks])
            S.dve(TS(cos[:, 0:n], m2[:, 0:n], 1.0 / (2 * PI), None, ALU.mult), reads=[km2], writes=[kc])
            S.dve(CP(posi[:, 0:n], cos[:, 0:n]), reads=[kc], writes=[kpi])
            S.dve(CP(posf[:, 0:n], posi[:, 0:n]), reads=[kpi], writes=[kpf])
            S.dve(STT(m2[:, 0:n], posf[:, 0:n], -C1_, m2[:, 0:n], ALU.mult, ALU.add), reads=[kpf, km2], writes=[km2])
            S.dve(STT(m2[:, 0:n], posf[:, 0:n], -C2_, m2[:, 0:n], ALU.mult, ALU.add), reads=[kpf, km2], writes=[km2])
            S.dve(TS(m2[:, 0:n], m2[:, 0:n], PI, -PI, ALU.min, ALU.max), reads=[km2], writes=[km2])
            S.act(ACT(cos[:, 0:n], m2[:, 0:n], AF.Sin), reads=[km2], writes=[kc])

        def proj_fm(xt, xkeys, w, wkey, col0, ncols=128):
            pb, pk = b.bank()
            for kc in range(8):
                S.pe(MM(pb[0:ncols, :], w[:, kc, col0:col0 + ncols], xt[:, kc, :], kc == 0, kc == 7),
                     reads=[wkey] + xkeys, writes=[pk])
            return pb, pk

        QT = b.alloc("QT", [128, 4, S_LEN], BF16)
        KS = b.alloc("KS", [128, 2, S_LEN], BF16)
        KW = b.alloc("KW", [128, 2, S_LEN], BF16)
        VS = b.alloc("VS", [128, NT, 2, 65], BF16)
        VW = b.alloc("VW", [128, NT, 2, 65], BF16)
        GATES = b.alloc("GATES", [128, NT, 24], F32)
        KCMP = b.alloc("KCMP", [128, 2, 256], BF16)
        CV = b.alloc("CV", [128, 2, 2, 129], BF16)
        nsa_state_mark = b.mark()
        KC = b.alloc("KC", [128, S_LEN], BF16)
        VC = b.alloc("VC", [128, S_LEN], BF16)
        cmp_mark = b.mark()
        WFM = b.alloc("WFM", [128, 8, 2304], BF16)
        WTM = b.alloc("WTM", [128, 8, 280], BF16)
        XTR = b.ring("XT", [128, 8, 512], BF16, 2)
        tabs = [b.ring(nm, [128, 512], dt_, 2 if nm in ("cos", "sin") else 1) for nm, dt_ in
                (("posi", I32), ("posf", F32), ("m1", F32), ("m2", F32), ("cos", F32), ("sin", F32))]
        T1 = b.ring("T1", [128, 512], F32, 2)
        T2 = b.ring("T2", [128, 512], F32, 2)

        S.pool(MEMSET(VS, 1.0), writes=["VS"])
        S.pool(MEMSET(VW, 1.0), writes=["VW"])
        for kc in range(8):
            S.dma(WFM[:, kc, :], wfm_nsa[kc * 128:(kc + 1) * 128, :], writes=["WFM"], q="pool")
        S.dma(WTM, wtm_nsa.rearrange("(kc p) n -> p kc n", p=128), writes=["WTM"], q="pool")

        rope_groups = [(QT[:, hp, :], hp, 4 + hp) for hp in range(4)] + \
                      [(KS[:, g, :], 8 + g, 10 + g) for g in range(2)] + \
                      [(KW[:, g, :], 12 + g, 14 + g) for g in range(2)]
        for tb in range(NB):
            xt, xkeys = load_xT_block(tb, XTR[tb % 2])
            tl = [t[tb % len(t)] for t in tabs]
            rope_tables(pos[:, tb * 512:(tb + 1) * 512], 512, 0, 1, 2, tl)
            cos, kcos = tl[4]
            sin, ksin = tl[5]
            tsl = slice(tb * 512, (tb + 1) * 512)
            for gi, (dst, ga, gb) in enumerate(rope_groups):
                pa, pka = proj_fm(xt, xkeys, WFM, "WFM", ga * 128)
                pb_, pkb = proj_fm(xt, xkeys, WFM, "WFM", gb * 128)
                t1, k1 = T1[gi % 2]
                t2, k2 = T2[gi % 2]
                S.dve(TT(t1, pa, cos, ALU.mult), reads=[pka, kcos], writes=[k1])
                S.dve(TT(t2, pb_, sin, ALU.mult), reads=[pkb, ksin], writes=[k2])
                S.pool(TT(dst[:, tsl], t1, t2, ALU.add), reads=[k1, k2], writes=[("ropeout", gi, tb)])
            for dst, gidx, nm in ((KC, 16, "KC"), (VC, 17, "VC")):
                pa, pka = proj_fm(xt, xkeys, WFM, "WFM", gidx * 128)
                S.act(ACT(dst[:, tsl], pa, AF.Copy), reads=[pka], writes=[(nm, tb)])
            for tt in range(4):
                kt = tb * 4 + tt
                pb_, pk = b.bank()
                for kc in range(8):
                    S.pe(MM(pb_[:, 0:280], xt[:, kc, tt * 128:(tt + 1) * 128], WTM[:, kc, :], kc == 0, kc == 7),
                         reads=["WTM"] + xkeys, writes=[pk])
                S.act(ACT(VS[:, kt, :, 0:64], pb_[:, 0:128].rearrange("p (g d) -> p g d", d=64), AF.Copy),
                      reads=[pk, "VS"], writes=[("VS", kt)])
                S.act(ACT(VW[:, kt, :, 0:64], pb_[:, 128:256].rearrange("p (g d) -> p g d", d=64), AF.Copy),
                      reads=[pk, "VW"], writes=[("VW", kt)])
                S.act(ACT(GATES[:, kt, :], pb_[:, 256:280], AF.Sigmoid), reads=[pk], writes=[("GATES", kt)])
        QTK = [("ropeout", gi, tb) for gi in range(8) for tb in range(NB)]
        b.dump("QT", QT[:, 0, :], QTK, BF16)
        b.dump("KS", KS[:, 1, :], QTK, BF16)
        b.dump("GATES", GATES, [("GATES", kt) for kt in range(NT)])
        b.dump("VS", VS, [("VS", kt) for kt in range(NT)], BF16)
        if stop_after == "1a":
            S.emit(es)
            return nc, b
        b.release(cmp_mark)

        W1 = b.alloc("W1", [128, 32, 128], BF16)
        POST = b.alloc("POST", [128, 32, 2], BF16)
        W2K = b.alloc("W2K", [128, 256], BF16)
        W2V = b.alloc("W2V", [128, 64], BF16)
        B2V = b.alloc("B2V", [128, 64], F32)
        C1 = b.alloc("C1", [128, 2], F32)
        U = b.alloc("U", [128, 256], F32)
        U2 = b.alloc("U2", [128, 256], F32)
        U3 = b.alloc("U3", [128, 256], F32)
        GT = b.alloc("GT", [128, 256], BF16)
        ctab = [b.ring(nm, [128, 512], dt_, 1) for nm, dt_ in
                (("cposi", I32), ("cposf", F32), ("cm1", F32), ("cm2", F32), ("ccos", F32), ("csin", F32))]
        ctl = [t[0] for t in ctab]
        rope_tables(posc, 256, 0, 1, 2, ctl)
        ccos, kccos = ctl[4]
        csin, kcsin = ctl[5]
        S.dma(W2K, w2k_d, writes=["W2K"], q="pool")
        S.dma(W2V, w2v_d, writes=["W2V"], q="pool")
        S.dma(B2V, b2v_d.partition_broadcast(128), writes=["B2V"])
        S.dma(CV[:, 0, 0, 0:64], cover_d[0:128, :], reads=[], writes=[("CVc", 0)], q="pool")
        S.dma(CV[:, 1, 0, 0:64], cover_d[128:256, :], reads=[], writes=[("CVc", 1)], q="pool")
        S.pool(MEMSET(GT, 0.0), writes=["GT"])
        for which in range(2):
            src = KC if which == 0 else VC
            S.dma(W1, (w1k_d if which == 0 else w1v_d).rearrange("p (l h) -> p l h", h=128), writes=["W1"], q="pool")
            S.dma(POST, (poskT_d if which == 0 else posvT_d).rearrange("p (l t) -> p l t", t=2), writes=["POST"], q="pool")
            pc, pkc = b.bank()
            for l in range(32):
                S.pe(MM(pc[:, 0:2], W1[0:64, l, :], POST[0:64, l, :], l == 0, l == 31), reads=["W1", "POST"], writes=[pkc])
            S.dve(TS(C1, pc[:, 0:2], PSC(6 + which), None, ALU.add), reads=[pkc, "psc"], writes=["C1"])
            for g in range(2):
                rows = slice(g * 64, (g + 1) * 64)
                ph, pkh = b.bank()
                for l in range(32):
                    S.pe(MM(ph[:, 0:255], W1[rows, l, :], src[rows, l:l + 16 * 254 + 1:16], l == 0, l == 31),
                         reads=["W1"] + [("KC" if which == 0 else "VC", tb) for tb in range(NB)], writes=[pkh])
                S.act(ACT(U[:, 0:255], ph[:, 0:255], AF.Identity, bias=C1[:, 0:1], scale=1.0), reads=[pkh, "C1"], writes=["U"])
                S.dve(TT(U2[:, 0:255], U[:, 0:255], U[:, 0:255], ALU.mult), reads=["U"], writes=["U2"])
                S.dve(TS(U2[:, 0:255], U2[:, 0:255], 0.044715, 1.0, ALU.mult, ALU.add), reads=["U2"], writes=["U2"])
                S.dve(TT(U2[:, 0:255], U2[:, 0:255], U[:, 0:255], ALU.mult), reads=["U2", "U"], writes=["U2"])
                S.act(ACT(U3[:, 0:255], U2[:, 0:255], AF.Tanh, scale=0.7978845608028654), reads=["U2"], writes=["U3"])
                S.dve(TS(U3[:, 0:255], U3[:, 0:255], 0.5, 0.5, ALU.mult, ALU.add), reads=["U3"], writes=["U3"])
                S.dve(TT(GT[:, 0:255], U3[:, 0:255], U[:, 0:255], ALU.mult), reads=["U3", "U", "GT"], writes=["GT"])
                if which == 0:
                    pa, pka = b.bank()
                    pb_, pkb = b.bank()
                    S.pe(MM(pa[:, 0:256], W2K[:, 0:128], GT, True, True), reads=["W2K", "GT"], writes=[pka])
                    S.pe(MM(pb_[:, 0:256], W2K[:, 128:256], GT, True, True), reads=["W2K", "GT"], writes=[pkb])
                    S.dve(STT(U[:, 0:256], pa[:, 0:256], PSC(8), ccos[:, 0:256], ALU.add, ALU.mult),
                          reads=[pka, kccos, "psc", "U"], writes=["U"])
                    S.dve(STT(U2[:, 0:256], pb_[:, 0:256], PSC(9), csin[:, 0:256], ALU.add, ALU.mult),
                          reads=[pkb, kcsin, "psc", "U2"], writes=["U2"])
                    S.pool(TT(KCMP[:, g, :], U[:, 0:256], U2[:, 0:256], ALU.add), reads=["U", "U2"], writes=[("KCMP", g)])
                else:
                    for nt in range(2):
                        pv, pkv = b.bank()
                        S.pe(MM(pv[:, 0:64], GT[:, nt * 128:(nt + 1) * 128], W2V, True, True), reads=["GT", "W2V"], writes=[pkv])
                        S.dve(TT(CV[:, nt, g, 65:129], pv[:, 0:64], B2V, ALU.add), reads=[pkv, "B2V"], writes=[("CVv", nt, g)])
        for nt in range(2):
            S.pool(CP(CV[:, nt, 1, 0:64], CV[:, nt, 0, 0:64]), reads=[("CVc", nt)], writes=[("CVc2", nt)])
            S.pool(MEMSET(CV[:, nt, :, 64:65], 1.0), writes=[("CVo", nt)])
        CVK = [("CVc", nt) for nt in range(2)] + [("CVc2", nt) for nt in range(2)] + [("CVo", nt) for nt in range(2)] + \
              [("CVv", nt, g) for nt in range(2) for g in range(2)]
        b.dump("KCMP", KCMP, [("KCMP", 0), ("KCMP", 1)], BF16)
        b.dump("CV", CV, CVK, BF16)
        if stop_after == "1ap":
            S.emit(es)
            return nc, b
        b.release(nsa_state_mark)

        ETAB = b.alloc("ETAB", [128, 32, 128], BF16)
        S.pool(MEMSET(ETAB, 1.0), writes=["ETAB"])
        S.pool(ASEL(ETAB, ETAB, [[-2, 32], [-1, 2], [0, 64]], ALU.is_equal, 0.0, 0, 1), reads=["ETAB"], writes=["ETAB"])
        MASKC = b.ring("MASKC", [128, 2, 512], BF16, 2)
        PT = b.ring("PT", [128, 512], BF16, 6)
        ONSA = b.alloc("ONSA", [128, 4, 512], F32)
        ONSAB = b.alloc("ONSAB", [128, 4, 512], BF16)
        OTB = b.ring("OTB", [128, 4, 512], BF16, 2)
        IMP = b.alloc("IMP", [128, 2, 4, 64], F32)
        CANDT = b.ring("CANDT", [128, 4, 64], F32, 2)
        FORCT = b.ring("FORCT", [128, 4, 64], F32, 2)
        SM = b.ring("SM", [128, 16], F32, 4)
        IMPM = b.alloc("IMPM", [128, 64], F32)
        SCR = b.alloc("SCR", [128, 64], F32)
        M8 = b.alloc("M8", [128, 16], F32)
        SEL = b.alloc("SEL", [128, 64], F32)
        NEGB8 = b.alloc("NEGB8", [128, 8, 64], BF16)
        NEGT = b.alloc("NEGT", [128, 2, 512], BF16)
        S.pool(MEMSET(NEGT, 0.0), writes=[("NEGT", 0), ("NEGT", 1)])
        pt_i = [0]
        sm_i = [0]
        KSX = [b.alloc("KSL", [128, 2, S_LEN], BF16), b.alloc("KSH", [128, 2, S_LEN], BF16)]
        KWX = [b.alloc("KWL", [128, 2, S_LEN], BF16), b.alloc("KWH", [128, 2, S_LEN], BF16)]
        KCX = [b.alloc("KCL", [128, 2, 256], BF16), b.alloc("KCH", [128, 2, 256], BF16)]
        for half in range(2):
            rws = slice(half * 64, half * 64 + 64)
            for dst, src, nm in ((KSX[half], KS, "KSX"), (KWX[half], KW, "KWX"), (KCX[half], KCMP, "KCX")):
                S.pool(MEMSET(dst, 0.0), writes=[(nm, half)])
                if nm == "KSX":
                    S.act(ACT(dst[rws], src[rws], AF.Copy), reads=[(nm, half)], writes=[(nm, half)])
                else:
                    S.dve(CP(dst[rws], src[rws]), reads=[(nm, half)], writes=[(nm, half)])

        ATT = {"staged": []}
        DEPTH = 3
        NPT = 6

        def _emit_pv(item):
            (kt, jl, jh, pt, pkt, accs, nper, Vfn, v_keys, last_kt, is_last, post_fn) = item
            for j in range(jl, jh + 1):
                acc, pka, j0 = accs[j // nper]
                S.pe(MM(acc[:, j - j0, :], pt[:, j * 128:(j + 1) * 128], Vfn(kt), False, last_kt[j] == kt),
                     reads=[pkt] + v_keys, writes=[pka])
            if is_last:
                for (_a, pka, _j) in accs:
                    b.unpin(pka)
                post_fn(accs)

        def att_flush():
            while ATT["staged"]:
                _emit_pv(ATT["staged"].pop(0))

        def attention(QTa, KTa, Vfn, tiles, qb, scale, ncols, q_keys, k_keys, v_keys, post_fn, blockmask=None):
            nper = min(4, 512 // ncols)
            accs = []
            for j0 in range(0, 4, nper):
                pa, pka = b.bank(pin=True)
                S.pe(MM(pa[:, 0:nper * ncols], zeros[:, 0:128], zeros[:, 0:nper * ncols], True, True), reads=["zeros"], writes=[pka])
                accs.append((pa[:, 0:nper * ncols].rearrange("p (j c) -> p j c", c=ncols), pka, j0))
            last_kt = {}
            for (kt, jl, jh, masks) in tiles:
                for j in range(jl, jh + 1):
                    last_kt[j] = kt
            for ti, (kt, jl, jh, masks) in enumerate(tiles):
                c0, c1 = jl * 128, (jh + 1) * 128
                ps, pks = b.bank()
                nmm = 1 + (1 if blockmask is not None else 0) + len(masks)
                done = 1
                S.pe(MM(ps[:, c0:c1], KTa[:, kt * 128:(kt + 1) * 128], QTa[:, qb * 512 + c0:qb * 512 + c1], True, done == nmm),
                     reads=q_keys + k_keys, writes=[pks])
                if blockmask is not None:
                    done += 1
                    negt, nkey = blockmask
                    S.pe(MM(ps[:, c0:c1], ETAB[:, kt, :], negt[:, c0:c1], False, done == nmm),
                         reads=["ETAB", nkey], writes=[pks])
                for (kind, j) in masks:
                    done += 1
                    S.pe(MM(ps[:, j * 128:(j + 1) * 128], ident, caus if kind == "c" else wedge, False, done == nmm),
                         reads=["ident", "caus", "wedge"], writes=[pks])
                pt, pkt = PT[pt_i[0] % NPT]
                pt_i[0] += 1
                S.act(ACT(pt[:, c0:c1], ps[:, c0:c1], AF.Exp, scale=scale), reads=[pks], writes=[pkt])
                ATT["staged"].append((kt, jl, jh, pt, pkt, accs, nper, Vfn, v_keys, last_kt, ti == len(tiles) - 1, post_fn))
                while len(ATT["staged"]) > DEPTH:
                    _emit_pv(ATT["staged"].pop(0))

        def recip_sums(accs, ncols, sumcol):
            sm, smk = SM[sm_i[0] % 4]
            sm_i[0] += 1
            for (acc, pka, j0) in accs:
                nper = acc.shape[1]
                S.dve(TS(sm[:, j0:j0 + nper], acc[:, :, sumcol], 1e-30, None, ALU.max), reads=[pka], writes=[smk])
            S.dve(RECIP(sm[:, 4:8], sm[:, 0:4]), reads=[smk], writes=[smk])
            return sm, smk

        ALLQ = []
        for qb in range(NB):
            mk, mkk = MASKC[qb % 2]
            for nt in range(2):
                S.pool(ASEL(mk[:, nt, :], zeros, [[1, 512]], ALU.is_ge, NEG, qb * 512 - 2048 * nt - 31, -16),
                       reads=["zeros"], writes=[(mkk, nt)])
            cand, candk = CANDT[qb % 2]
            forc, forck = FORCT[qb % 2]
            S.dma(cand, cand_d[qb * 512:(qb + 1) * 512, :].rearrange("(j p) c -> p j c", p=128), writes=[candk])
            S.dma(forc, forced_d[qb * 512:(qb + 1) * 512, :].rearrange("(j p) c -> p j c", p=128), writes=[forck])
            if stop_after == "2a_tab":
                b.dump("ETAB", ETAB, ["ETAB"], BF16)
                b.dump("MASKC", mk, [(mkk, 0), (mkk, 1)], BF16)
                b.dump("CAND", cand, [candk])
                S.emit(es)
                return nc, b
            gview = lambda br, h: GATES[:, qb * 4:(qb + 1) * 4, h * 3 + br]
            gkeys = [("GATES", kt) for kt in range(qb * 4, qb * 4 + 4)]
            def cmp_scores(h):
                g, hp, half = h // 4, h // 2, h % 2
                ets = []
                for nt in range(2):
                    ps, pks = b.bank()
                    S.pe(MM(ps, KCX[half][:, g, nt * 128:(nt + 1) * 128], QT[:, hp, qb * 512:(qb + 1) * 512], True, False),
                         reads=[("KCX", half)], writes=[pks])
                    S.pe(MM(ps, ident, mk[:, nt, :], False, True), reads=["ident", (mkk, nt)], writes=[pks])
                    pt, pkt = PT[pt_i[0] % 6]
                    pt_i[0] += 1
                    S.act(ACT(pt, ps, AF.Exp, scale=0.125), reads=[pks], writes=[pkt])
                    ets.append((pt, pkt))
                return ets

            def cmp_pv(h, ets):
                g = h // 4
                accs = []
                for j0 in (0, 2):
                    pa, pka = b.bank()
                    acc = pa[:, 0:258].rearrange("p (j c) -> p j c", c=129)
                    for j in (j0, j0 + 1):
                        for nt in range(2):
                            S.pe(MM(acc[:, j - j0, :], ets[nt][0][:, j * 128:(j + 1) * 128], CV[:, nt, g, :], nt == 0, nt == 1),
                                 reads=[ets[nt][1]], writes=[pka])
                    accs.append((acc, pka, j0))
                sm, smk = recip_sums(accs, 129, 64)
                S.dve(TT(sm[:, 8:12], sm[:, 4:8], gview(0, h), ALU.mult), reads=[smk] + gkeys, writes=[smk])
                for (acc, pka, j0) in accs:
                    for j in (j0, j0 + 1):
                        S.dve(TS(ONSA[:, j, h * 64:(h + 1) * 64], acc[:, j - j0, 65:129], sm[:, 8 + j:9 + j], None, ALU.mult),
                              reads=[pka, smk], writes=[("ONSA", h)])
                        if h % 4 == 0:
                            S.dve(TS(IMP[:, g, j, :], acc[:, j - j0, 0:64], sm[:, 4 + j:5 + j], None, ALU.mult),
                                  reads=[pka, smk], writes=[("IMP", g)])
                        else:
                            S.dve(STT(IMP[:, g, j, :], acc[:, j - j0, 0:64], sm[:, 4 + j:5 + j], IMP[:, g, j, :], ALU.mult, ALU.add),
                                  reads=[pka, smk, ("IMP", g)], writes=[("IMP", g)])

            ets_cur = cmp_scores(0)
            for h in range(8):
                ets_next = cmp_scores(h + 1) if h + 1 < 8 else None
                cmp_pv(h, ets_cur)
                ets_cur = ets_next
            if stop_after == "2a_cmp":
                b.dump("IMP", IMP, [("IMP", 0), ("IMP", 1)])
                b.dump("ONSA_c", ONSA, [("ONSA", h) for h in range(8)])
                S.emit(es)
                return nc, b
            if qb == 1:
                b.dump("IMP", IMP, [("IMP", 0), ("IMP", 1)])
                b.dump("ONSA_c", ONSA, [("ONSA", h) for h in range(8)])
            def sel_dve():
                for g in range(2):
                    for j in range(4):
                        negb = NEGB8[:, g * 4 + j, :]
                        S.dve(TT(IMPM, IMP[:, g, j, :], cand[:, j, :], ALU.mult), reads=[("IMP", g), candk], writes=["IMPM"])
                        S.dve(lambda e: e.max(out=M8[:, 0:8], in_=IMPM), reads=["IMPM"], writes=["M8a"])
                        S.dve(lambda e: e.match_replace(out=SCR, in_to_replace=M8[:, 0:8], in_values=IMPM, imm_value=-1.0),
                              reads=["IMPM", "M8a"], writes=["SCR"])
                        S.dve(lambda e: e.max(out=M8[:, 8:16], in_=SCR), reads=["SCR"], writes=["M8b"])
                        S.dve(TS(SEL, IMPM, M8[:, 12:13], None, ALU.is_ge), reads=["IMPM", "M8b"], writes=["SEL"])
                        S.dve(TT(SEL, SEL, cand[:, j, :], ALU.mult), reads=["SEL", candk], writes=["SEL"])
                        S.dve(TT(SEL, SEL, forc[:, j, :], ALU.add), reads=["SEL", forck], writes=["SEL"])
                        S.dve(TS(negb, SEL, -1.0, -NEG, ALU.add, ALU.mult), reads=["SEL"], writes=[("NEGB", g, j)])

            def sel_pe():
                for g in range(2):
                    for j in range(4):
                        negb = NEGB8[:, g * 4 + j, :]
                        pb_, pk = b.bank()
                        ptr = pb_.bitcast(BF16)
                        S.pe(TR(ptr[0:64, 0:128], negb, ident), reads=[("NEGB", g, j), "ident"], writes=[pk])
                        S.act(ACT(NEGT[0:64, g, j * 128:(j + 1) * 128], ptr[0:64, 0:128], AF.Copy), reads=[pk], writes=[("NEGT", g)])

            sel_dve()
            if stop_after == "2a_sel":
                b.dump("NEGT", NEGT, [("NEGT", 0), ("NEGT", 1)], BF16)
                S.emit(es)
                return nc, b
            if qb == 3:
                b.dump("NEGT", NEGT, [("NEGT", 0), ("NEGT", 1)], BF16)
            for br in (2, 1):
                if br == 1:
                    sel_pe()
                for h in range(8):
                    g, hp, half = h // 4, h // 2, h % 2
                    QTa = QT[:, hp, :]
                    if br == 1:
                        KTa = KSX[half][:, g, :]
                        kkeys = [("KSX", half)]
                        Vt = VS
                        tiles = []
                        for kt in range(0, 4 * qb + 4):
                            jl = max(kt - 4 * qb, 0)
                            masks = [("c", kt - 4 * qb)] if kt >= 4 * qb else []
                            tiles.append((kt, jl, 3, masks))
                        bm = (NEGT[:, g, :], ("NEGT", g))
                    else:
                        KTa = KWX[half][:, g, :]
                        kkeys = [("KWX", half)]
                        Vt = VW
                        tiles = []
                        for kt in range(max(0, 4 * qb - 4), 4 * qb + 4):
                            jl = max(kt - 4 * qb, 0)
                            jh = min(kt + 4 - 4 * qb, 3)
                            masks = []
                            if kt >= 4 * qb:
                                masks.append(("c", kt - 4 * qb))
                            if 0 <= kt + 4 - 4 * qb <= 3:
                                masks.append(("w", kt + 4 - 4 * qb))
                            tiles.append((kt, jl, jh, masks))
                        bm = None

                    def post_nsa(accs, h=h, br=br, qb=qb):
                        sm, smk = recip_sums(accs, 65, 64)
                        S.dve(TT(sm[:, 8:12], sm[:, 4:8], GATES[:, qb * 4:(qb + 1) * 4, h * 3 + br], ALU.mult),
                              reads=[smk], writes=[smk])
                        acc, pka, _ = accs[0]
                        for j in range(4):
                            dst = ONSA[:, j, h * 64:(h + 1) * 64]
                            S.dve(STT(dst, acc[:, j, 0:64], sm[:, 8 + j:9 + j], dst, ALU.mult, ALU.add),
                                  reads=[pka, smk, ("ONSA", h)], writes=[("ONSA", h)])

                    attention(QTa, KTa, lambda kt, Vt=Vt, g=g: Vt[:, kt, g, :], tiles, qb, 0.125, 65,
                              [], kkeys, [], post_nsa, blockmask=bm)
            att_flush()
            if qb == 3:
                b.dump("ONSA", ONSA, [("ONSA", h) for h in range(8)])
            S.act(ACT(ONSAB, ONSA, AF.Copy), reads=[("ONSA", h) for h in range(8)], writes=["ONSAB"])
            otb, otk = OTB[qb % 2]
            for fc in range(4):
                pb_, pk = b.bank()
                ptr = pb_.bitcast(BF16)
                for j in range(4):
                    S.pe(TR(ptr[:, j * 128:(j + 1) * 128], ONSAB[:, j, fc * 128:(fc + 1) * 128], ident),
                         reads=["ONSAB", "ident"], writes=[pk])
                S.dve(CP(otb[:, fc, :], ptr[:, 0:512]), reads=[pk], writes=[(otk, fc)])
            S.dma(oT_s[0:512, qb * 512:(qb + 1) * 512].rearrange("(fc p) t -> p fc t", p=128), otb,
                  reads=[(otk, fc) for fc in range(4)], writes=[("oT_s", 0, qb)])
            if stop_after == "2a_qb0":
                b.dump("ONSA0", ONSA, [("ONSA", h) for h in range(8)])
                S.emit(es)
                return nc, b
        if stop_after == "2a":
            S.emit(es)
            return nc, b
        b.release(base_mark)

        QM = b.alloc("QM", [128, 8, S_LEN], BF16)
        NMKV = b.alloc("NMKV", [128, 2, S_LEN], BF16)
        KPE = b.alloc("KPE", [128, S_LEN], BF16)
        S.pool(MEMSET(QM[96:128], 0.0), writes=["QMpad"])
        mla_mark = b.mark()
        WFM2 = b.alloc("WFM2", [128, 8, 896], BF16)
        WUQA = b.alloc("WUQA", [128, 3, 768], BF16)
        WUQB = b.alloc("WUQB", [128, 3, 768], BF16)
        XTR = b.ring("XTb", [128, 8, 512], BF16, 2)
        tabs = [b.ring(nm, [128, 512], dt_, 2) for nm, dt_ in
                (("bposi", I32), ("bposf", F32), ("bm1", F32), ("bm2", F32), ("bcos", F32), ("bsin", F32))]
        T1 = b.ring("bT1", [128, 512], F32, 2)
        T2 = b.ring("bT2", [128, 512], F32, 2)
        SQ = b.ring("SQ", [128, 512], BF16, 3)
        RR = b.ring("RR", [128, 512], F32, 2)
        NMQ = b.ring("NMQ", [128, 3, 512], BF16, 2)
        for kc in range(8):
            S.dma(WFM2[:, kc, :], wfm_mla[kc * 128:(kc + 1) * 128, :], writes=["WFM2"], q="pool")
        S.dma(WUQA, wuq_a_d.rearrange("(kc p) n -> p kc n", p=128), writes=["WUQA"], q="pool")
        S.dma(WUQB, wuq_b_d.rearrange("(kc p) n -> p kc n", p=128), writes=["WUQB"], q="pool")
        pe_rows = slice(64, 96)
        for tb in range(NB):
            xt, xkeys = load_xT_block(tb, XTR[tb % 2])
            tl = [t[tb % len(t)] for t in tabs]
            rope_tables(pos[:, tb * 512:(tb + 1) * 512], 512, 3, 4, 5, tl)
            cos, kcos = tl[4]
            sin, ksin = tl[5]
            tsl = slice(tb * 512, (tb + 1) * 512)

            def rmsnorm_group(g0, nchunks, width, gcol, dst_fn, dkey):
                pbs = [proj_fm(xt, xkeys, WFM2, "WFM2", (g0 + c) * 128) for c in range(nchunks)]
                pss, pkss = b.bank()
                for c in range(nchunks):
                    sq, sqk = SQ[c]
                    S.act(ACT(sq, pbs[c][0], AF.Square), reads=[pbs[c][1]], writes=[sqk])
                    S.pe(MM(pss, ones, sq, c == 0, c == nchunks - 1), reads=["ones", sqk], writes=[pkss])
                rr, rrk = RR[0]
                r2, r2k = RR[1]
                S.act(ACT(rr, pss, AF.Sqrt, scale=1.0 / width, bias=1e-6), reads=[pkss], writes=[rrk])
                S.dve(RECIP(r2, rr), reads=[rrk], writes=[r2k])
                for c in range(nchunks):
                    S.dve(STT(dst_fn(c), pbs[c][0], PSC(gcol + c), r2, ALU.mult, ALU.mult),
                          reads=[pbs[c][1], r2k, "psc"], writes=[(dkey, c)])

            nmq, nmqk = NMQ[tb % 2]
            rmsnorm_group(0, 3, 384.0, 10, lambda c: nmq[:, c, :], nmqk)
            rmsnorm_group(3, 2, 256.0, 13, lambda c: NMKV[:, c, tsl], ("NMKV", tb))
            pa, pka = proj_fm(xt, xkeys, WFM2, "WFM2", 5 * 128)
            pb_, pkb = proj_fm(xt, xkeys, WFM2, "WFM2", 6 * 128)
            t1, k1 = T1[0]
            t2, k2 = T2[0]
            S.dve(TT(t1[pe_rows], pa[pe_rows], cos[pe_rows], ALU.mult), reads=[pka, kcos], writes=[k1])
            S.dve(TT(t2[pe_rows], pb_[pe_rows], sin[pe_rows], ALU.mult), reads=[pkb, ksin], writes=[k2])
            S.pool(TT(KPE[pe_rows, tsl], t1[pe_rows], t2[pe_rows], ALU.add), reads=[k1, k2], writes=[("KPE", tb)])
            for h in range(8):
                pa, pka = b.bank()
                pb_, pkb = b.bank()
                for c in range(3):
                    S.pe(MM(pa[0:96, :], WUQA[:, c, h * 96:(h + 1) * 96], nmq[:, c, :], c == 0, c == 2),
                         reads=["WUQA"] + [(nmqk, cc) for cc in range(3)], writes=[pka])
                for c in range(3):
                    S.pe(MM(pb_[0:96, :], WUQB[:, c, h * 96:(h + 1) * 96], nmq[:, c, :], c == 0, c == 2),
                         reads=["WUQB"] + [(nmqk, cc) for cc in range(3)], writes=[pkb])
                S.act(ACT(QM[0:64, h, tsl], pa[0:64, :], AF.Copy), reads=[pka], writes=[("QMn", h, tb)])
                t1, k1 = T1[(h + 1) % 2]
                t2, k2 = T2[(h + 1) % 2]
                S.dve(TT(t1[pe_rows], pa[pe_rows], cos[pe_rows], ALU.mult), reads=[pka, kcos], writes=[k1])
                S.dve(TT(t2[pe_rows], pb_[pe_rows], sin[pe_rows], ALU.mult), reads=[pkb, ksin], writes=[k2])
                S.pool(TT(QM[pe_rows, h, tsl], t1[pe_rows], t2[pe_rows], ALU.add), reads=[k1, k2], writes=[("QMp", h, tb)])
        QMK = [("QMn", h, tb) for h in range(8) for tb in range(NB)] + [("QMp", h, tb) for h in range(8) for tb in range(NB)]
        NMKVK = [(("NMKV", tb), c) for tb in range(NB) for c in range(2)]
        KPEK = [("KPE", tb) for tb in range(NB)]
        b.dump("QM", QM[0:96, 0, :], QMK, BF16)
        b.dump("NMKV", NMKV[:, 0, :], NMKVK, BF16)
        b.dump("KPE", KPE[64:96, :], KPEK, BF16)
        if stop_after == "1b":
            S.emit(es)
            return nc, b
        b.release(mla_mark)

        WUK = b.alloc("WUK", [128, 2, 512], BF16)
        WUV = b.alloc("WUV", [128, 2, 512], BF16)
        KM = b.ring("KM", [128, S_LEN], BF16, 2)
        VM = b.alloc("VM", [128, NT, 2, 65], BF16)
        PT = b.ring("PTb", [128, 512], BF16, 6)
        SM = b.ring("SMb", [128, 16], F32, 4)
        OM = b.ring("OM", [128, 4, 128], BF16, 2)
        OTB2 = b.ring("OTB2", [128, 512], BF16, 2)
        S.dma(WUK, wuk_d.rearrange("(kc p) n -> p kc n", p=128), writes=["WUK"], q="pool")
        S.dma(WUV, wuv_d.rearrange("(kc p) n -> p kc n", p=128), writes=["WUV"], q="pool")
        S.pool(MEMSET(VM, 1.0), writes=["VM"])
        for (km_, kmk_) in KM:
            S.pool(MEMSET(km_[96:128, :], 0.0), writes=[(kmk_, "pad")])
        mla_scale = 96.0 ** -0.5
        for hp in range(4):
            for hh in range(2):
                h = hp * 2 + hh
                km, kmk = KM[hh]
                for kb in range(NB):
                    pa, pka = b.bank()
                    for c in range(2):
                        S.pe(MM(pa[0:64, :], WUK[:, c, h * 64:(h + 1) * 64], NMKV[:, c, kb * 512:(kb + 1) * 512], c == 0, c == 1),
                             reads=["WUK"], writes=[pka])
                    S.dve(CP(km[0:64, kb * 512:(kb + 1) * 512], pa[0:64, :]), reads=[pka], writes=[(kmk, kb)])
                S.pool(CP(km[pe_rows, :], KPE[pe_rows, :]), reads=[], writes=[(kmk, "pe")])
            for kt in range(NT):
                pa, pka = b.bank()
                for c in range(2):
                    S.pe(MM(pa[:, 0:128], NMKV[:, c, kt * 128:(kt + 1) * 128], WUV[:, c, hp * 128:(hp + 1) * 128], c == 0, c == 1),
                         reads=["WUV"], writes=[pka])
                S.dve(CP(VM[:, kt, :, 0:64], pa[:, 0:128].rearrange("p (g d) -> p g d", d=64)),
                      reads=[pka, "VM"], writes=[("VM", kt)])
            if hp == 0:
                b.dump("KM0", KM[0][0][0:96, :], [("KM0", kb) for kb in range(NB)] + [("KM0", "pe")], BF16)
                b.dump("VM", VM, [("VM", kt) for kt in range(NT)], BF16)
            for qb in range(NB):
                om, omk = OM[qb % 2]
                for hh in range(2):
                    h = hp * 2 + hh
                    km, kmk = KM[hh]
                    tiles = []
                    for kt in range(0, 4 * qb + 4):
                        jl = max(kt - 4 * qb, 0)
                        masks = [("c", kt - 4 * qb)] if kt >= 4 * qb else []
                        tiles.append((kt, jl, 3, masks))

                    def post_mla(accs, hh=hh, qb=qb, om=om, omk=omk, hp=hp):
                        sm, smk = recip_sums(accs, 65, 64)
                        acc, pka, _ = accs[0]
                        for j in range(4):
                            S.dve(TS(om[:, j, hh * 64:(hh + 1) * 64], acc[:, j, 0:64], sm[:, 4 + j:5 + j], None, ALU.mult),
                                  reads=[pka, smk], writes=[(omk, hh)])
                        if hh == 1:
                            pb_, pk = b.bank()
                            ptr = pb_.bitcast(BF16)
                            for j in range(4):
                                S.pe(TR(ptr[:, j * 128:(j + 1) * 128], om[:, j, :], ident), reads=[(omk, 0), (omk, 1), "ident"], writes=[pk])
                            otb, otk = OTB2[qb % 2]
                            S.dve(CP(otb, ptr[:, 0:512]), reads=[pk], writes=[otk])
                            S.dma(oT_s[512 + hp * 128:512 + (hp + 1) * 128, qb * 512:(qb + 1) * 512], otb, reads=[otk],
                                  writes=[("oT_s", 1 + hp, qb)])

                    attention(QM[:, h, :], km, lambda kt, hh=hh: VM[:, kt, hh, :], tiles, qb, mla_scale, 65,
                              [], [(kmk, kb) for kb in range(NB)] + [(kmk, "pe"), (kmk, "pad")],
                              [("VM", kt) for kt in range(NT)] + ["VM"], post_mla)
            att_flush()
        if stop_after == "2b":
            S.emit(es)
            return nc, b
        b.release(base_mark)

        LNT = b.alloc("LNT", [128, 6, D], F32)
        CW = b.alloc("CW", [128, NCH, 3], F32)
        CB = b.alloc("CB", [128, NCH], F32)
        HIST = b.alloc("HIST", [128, NCH, 2], F32)
        KMEM = b.alloc("KMEM", [128, 8, 256], BF16)
        VMEM = b.alloc("VMEM", [128, 2, 4, 257], BF16)
        NWS = 6
        WS = b.ring("WS", [128, 8, 512], BF16, NWS)
        OTL = b.ring("OTL", [128, 8, 512], BF16, 1)
        XIN = b.ring("XIN", [128, D], F32, 2)
        Y = b.alloc("Y", [128, D], F32)
        XR = b.alloc("XR", [128, 4, D], F32)
        XB = b.alloc("XB", [128, D], BF16)
        XRT = b.alloc("XRT", [128, 8, 512], BF16)
        QME = b.alloc("QME", [128, 8, 512], BF16)
        OME = b.alloc("OME", [128, 4, D], BF16)
        OMET = b.alloc("OMET", [128, 8, 512], BF16)
        HT = b.alloc("HT", [128, NCH, 512], BF16)
        GC = b.ring("GC", [128, 514], F32, 2)
        GA = b.ring("GA", [128, 512], F32, 2)
        GS = b.ring("GS", [128, 512], F32, 2)
        GT0 = b.ring("GT0", [128, 512], F32, 2)
        GT1 = b.ring("GT1", [128, 512], F32, 2)
        PT = b.ring("PTc", [128, 512], BF16, 4)
        SM = b.ring("SMc", [128, 16], F32, 4)
        STAT = b.alloc("STAT", [128, 32], F32)
        MEMTB = b.alloc("MEMTB", [128, 8, 256], BF16)

        S.dma(LNT.rearrange("p k d -> p (k d)"), lngb_d.rearrange("(o k) d -> o (k d)", o=1).partition_broadcast(128),
              writes=["LNT"])
        S.dma(CW, convw_d.rearrange("p (c k) -> p c k", k=3), writes=["CW"])
        S.dma(CB, convb_d, writes=["CB"])
        S.pool(MEMSET(HIST, 0.0), writes=["HIST"])
        S.pool(MEMSET(VMEM, 1.0), writes=["VMEM"])
        S.dma(MEMTB, memT.rearrange("(kc p) t -> p kc t", p=128), writes=["MEMTB"], q="pool")
        ws_i = [0]

        def stream_w(src_ap, keys_src):
            ws, wsk = WS[ws_i[0] % NWS]
            ws_i[0] += 1
            kk, nn = src_ap.shape[1], src_ap.shape[2]
            S.dma(ws[:, 0:kk, 0:nn], src_ap, reads=keys_src, writes=[wsk])
            return ws, wsk

        wo_v = wo_s.rearrange("(kc p) n -> p kc n", p=128)
        mwq_v = mwq_s.rearrange("(kc p) n -> p kc n", p=128)
        mwo_v = mwo_s.rearrange("(kc p) n -> p kc n", p=128)
        wup_v = wup_s.rearrange("(kc p) n -> p kc n", p=128)
        wdn_v = wdn_s.rearrange("(c p) n -> p c n", p=128)
        allk = lambda key, rows: [(key, r0) for r0 in range(0, rows, 128)]

        for nb in range(2):
            ws, wsk = WS[ws_i[0] % NWS]
            ws_i[0] += 1
            S.dma(ws, mwk_d.rearrange("(kc p) n -> p kc n", p=128)[:, :, nb * 512:(nb + 1) * 512], writes=[wsk], q="pool")
            for oc in range(4):
                pa, pka = b.bank()
                for kc in range(8):
                    S.pe(MM(pa[:, 0:256], ws[:, kc, oc * 128:(oc + 1) * 128], MEMTB[:, kc, :], kc == 0, kc == 7),
                         reads=[wsk, "MEMTB"], writes=[pka])
                S.act(ACT(KMEM[:, nb * 4 + oc, :], pa[:, 0:256], AF.Copy), reads=[pka], writes=[("KMEM", nb * 4 + oc)])
        for nb in range(2):
            ws, wsk = WS[ws_i[0] % NWS]
            ws_i[0] += 1
            S.dma(ws, mwv_d.rearrange("(kc p) n -> p kc n", p=128)[:, :, nb * 512:(nb + 1) * 512], writes=[wsk], q="pool")
            for kt in range(2):
                pa, pka = b.bank()
                for kc in range(8):
                    S.pe(MM(pa, MEMTB[:, kc, kt * 128:(kt + 1) * 128], ws[:, kc, :], kc == 0, kc == 7),
                         reads=[wsk, "MEMTB"], writes=[pka])
                S.act(ACT(VMEM[:, kt, nb * 2:nb * 2 + 2, 0:256], pa.rearrange("p (h d) -> p h d", d=256), AF.Copy),
                      reads=[pka, "VMEM"], writes=[("VMEM", kt, nb)])
        KMEMK = [("KMEM", i) for i in range(8)]
        VMEMK = [("VMEM", kt, nb) for kt in range(2) for nb in range(2)]

        def layer_norm(src, srcks, ln_idx, dst, dstk):
            S.dve(lambda e: e.bn_stats(out=STAT[:, 0:6], in_=src[:, 0:512]), reads=srcks, writes=["STATa"])
            S.dve(lambda e: e.bn_stats(out=STAT[:, 6:12], in_=src[:, 512:1024]), reads=srcks, writes=["STATb"])
            S.dve(lambda e: e.bn_aggr(out=STAT[:, 12:14], in_=STAT[:, 0:12].rearrange("p (a b) -> p a b", b=6)),
                  reads=["STATa", "STATb"], writes=["STATc"])
            S.act(ACT(STAT[:, 16:17], STAT[:, 13:14], AF.Sqrt, scale=1.0, bias=1e-5), reads=["STATc"], writes=["STATd"])
            S.dve(RECIP(STAT[:, 17:18], STAT[:, 16:17]), reads=["STATd"], writes=["STATe"])
            S.dve(TS(dst, src, STAT[:, 12:13], STAT[:, 17:18], ALU.subtract, ALU.mult), reads=srcks + ["STATc", "STATe"], writes=[dstk])
            S.pool(TT(dst, dst, LNT[:, 2 * ln_idx, :], ALU.mult), reads=[dstk, "LNT"], writes=[dstk])
            S.pool(TT(dst, dst, LNT[:, 2 * ln_idx + 1, :], ALU.add), reads=[dstk, "LNT"], writes=[dstk])

        def to_feature_major(src, srck, tt, tag):
            S.act(ACT(XB, src, AF.Copy), reads=[srck], writes=["XB"])
            for half in range(2):
                pb_, pk = b.bank()
                ptr = pb_.bitcast(BF16)
                for c in range(4):
                    S.pe(TR(ptr[:, c * 128:(c + 1) * 128], XB[:, (half * 4 + c) * 128:(half * 4 + c + 1) * 128], ident),
                         reads=["XB", "ident"], writes=[pk])
                S.dve(CP(XRT[:, half * 4:half * 4 + 4, tt * 128:(tt + 1) * 128],
                         ptr[:, 0:512].rearrange("p (c t) -> p c t", t=128)), reads=[pk], writes=[("XRT", tag, tt, half)])

        def res_mm_ln(tiles, lhsT_fn, lhs_keys, wblk, res_fn, ln_idx):
            for tt in tiles:
                res, resk = res_fn(tt)
                for nbk in range(2):
                    pa, pka = b.bank()
                    ws, wsk = wblk[nbk]
                    for kc in range(8):
                        S.pe(MM(pa, lhsT_fn(kc, tt), ws[:, kc, :], kc == 0, kc == 7), reads=lhs_keys + [wsk], writes=[pka])
                    S.dve(STT(Y[:, nbk * 512:(nbk + 1) * 512], res[:, nbk * 512:(nbk + 1) * 512], ALPHA, pa, ALU.mult, ALU.add),
                          reads=[resk, pka], writes=[("Y", nbk)])
                layer_norm(Y, [("Y", 0), ("Y", 1)], ln_idx, XR[:, tt, :], ("XR", tt))

        XRTK = lambda tag, tiles=range(4): [("XRT", tag, tt, half) for tt in tiles for half in range(2)]

        for tb in range(NB):
            tsl = slice(tb * 512, (tb + 1) * 512)
            GRP = ((0, 1), (2, 3))
            otl, otlk = OTL[0]
            S.dma(otl, oT_s.rearrange("(kc p) t -> p kc t", p=128)[:, :, tsl],
                  reads=[("oT_s", i, tb) for i in range(5)], writes=[otlk])

            def res_x(tt, tb=tb):
                xi, xik = XIN[tt % 2]
                S.dma(xi, x_in[tb * 512 + tt * 128:tb * 512 + (tt + 1) * 128, :], writes=[xik])
                return xi, xik

            wo_blk = [stream_w(wo_v[:, :, nbk * 512:(nbk + 1) * 512], allk("wo_s", D)) for nbk in range(2)]
            wq_blk = [stream_w(mwq_v[:, :, nbk * 512:(nbk + 1) * 512], allk("mwq_s", D)) for nbk in range(2)]
            wmo_blk = [stream_w(mwo_v[:, :, nbk * 512:(nbk + 1) * 512], allk("mwo_s", D)) for nbk in range(2)]

            def stage_q(tiles):
                c0, c1 = tiles[0] * 128, (tiles[-1] + 1) * 128
                for nbk in range(2):
                    ws, wsk = wq_blk[nbk]
                    for oc in range(4):
                        pa, pka = b.bank()
                        for kc in range(8):
                            S.pe(MM(pa[:, 0:c1 - c0], ws[:, kc, oc * 128:(oc + 1) * 128], XRT[:, kc, c0:c1], kc == 0, kc == 7),
                                 reads=[wsk] + XRTK("a", tiles), writes=[pka])
                        S.act(ACT(QME[:, nbk * 4 + oc, c0:c1], pa[:, 0:c1 - c0], AF.Copy), reads=[pka],
                              writes=[("QME", nbk * 4 + oc, tiles[0])])

            def stage_att(tiles):
                c0, c1 = tiles[0] * 128, (tiles[-1] + 1) * 128
                for h in range(4):
                    pts = []
                    for kt in range(2):
                        ps, pks = b.bank()
                        for dc in range(2):
                            S.pe(MM(ps[:, 0:c1 - c0], KMEM[:, 2 * h + dc, kt * 128:(kt + 1) * 128], QME[:, 2 * h + dc, c0:c1], dc == 0, dc == 1),
                                 reads=KMEMK + [("QME", 2 * h + dc, tiles[0])], writes=[pks])
                        pt, pkt = PT[pt_i[0] % 4]
                        pt_i[0] += 1
                        S.act(ACT(pt[:, 0:c1 - c0], ps[:, 0:c1 - c0], AF.Exp, scale=1.0 / 16.0), reads=[pks], writes=[pkt])
                        pts.append((pt, pkt))
                    for ji, j in enumerate(tiles):
                        pa, pka = b.bank()
                        for kt in range(2):
                            S.pe(MM(pa[:, 0:257], pts[kt][0][:, ji * 128:(ji + 1) * 128], VMEM[:, kt, h, :], kt == 0, kt == 1),
                                 reads=[pts[kt][1]] + VMEMK + ["VMEM"], writes=[pka])
                        sm, smk = SM[sm_i[0] % 4]
                        sm_i[0] += 1
                        S.dve(RECIP(sm[:, 0:1], pa[:, 256:257]), reads=[pka], writes=[smk])
                        S.dve(TS(OME[:, j, h * 256:(h + 1) * 256], pa[:, 0:256], sm[:, 0:1], None, ALU.mult),
                              reads=[pka, smk], writes=[("OME", j, h)])

            def stage_omet(tiles):
                for j in tiles:
                    for half in range(2):
                        pb_, pk = b.bank()
                        ptr = pb_.bitcast(BF16)
                        for c in range(4):
                            S.pe(TR(ptr[:, c * 128:(c + 1) * 128], OME[:, j, (half * 4 + c) * 128:(half * 4 + c + 1) * 128], ident),
                                 reads=[("OME", j, hh_) for hh_ in range(4)] + ["ident"], writes=[pk])
                        S.dve(CP(OMET[:, half * 4:half * 4 + 4, j * 128:(j + 1) * 128],
                                 ptr[:, 0:512].rearrange("p (c t) -> p c t", t=128)), reads=[pk], writes=[("OMET", j, half)])

            for gi, tiles in enumerate(GRP):
                res_mm_ln(tiles, lambda kc, tt: otl[:, kc, tt * 128:(tt + 1) * 128], [otlk], wo_blk, res_x, 0)
            if tb == 0:
                b.dump("X1", XR, [("XR", tt) for tt in range(4)])
            for gi, tiles in enumerate(GRP):
                for tt in tiles:
                    to_feature_major(XR[:, tt, :], ("XR", tt), tt, "a")
                stage_q(tiles)
            for gi, tiles in enumerate(GRP):
                stage_att(tiles)
            OMETK = lambda tiles: [("OMET", j, half) for j in tiles for half in range(2)]
            for gi, tiles in enumerate(GRP):
                stage_omet(tiles)
                res_mm_ln(tiles, lambda kc, tt: OMET[:, kc, tt * 128:(tt + 1) * 128], OMETK(tiles), wmo_blk,
                          lambda tt: (XR[:, tt, :], ("XR", tt)), 1)
            for gi, tiles in enumerate(GRP):
                for tt in tiles:
                    to_feature_major(XR[:, tt, :], ("XR", tt), tt, "b")
            if tb == 0:
                b.dump("X2", XR, [("XR", tt) for tt in range(4)])
            for cb in range(6):
                ncol = 512 if cb < 5 else 256
                wg, wgk = stream_w(wup_v[:, :, cb * 512:cb * 512 + ncol], allk("wup_s", D))
                wu, wuk_ = stream_w(wup_v[:, :, DFF + cb * 512:DFF + cb * 512 + ncol], allk("wup_s", D))
                for cc in range(ncol // 128):
                    c = cb * 4 + cc
                    pg, pkg = b.bank()
                    pu, pku = b.bank()
                    for kc in range(8):
                        S.pe(MM(pg, wg[:, kc, cc * 128:(cc + 1) * 128], XRT[:, kc, :], kc == 0, kc == 7),
                             reads=[wgk] + XRTK("b"), writes=[pkg])
                    for kc in range(8):
                        S.pe(MM(pu, wu[:, kc, cc * 128:(cc + 1) * 128], XRT[:, kc, :], kc == 0, kc == 7),
                             reads=[wuk_] + XRTK("b"), writes=[pku])
                    gc, gck = GC[c % 2]
                    ga, gak = GA[c % 2]
                    gs, gsk = GS[c % 2]
                    S.pool(CP(gc[:, 0:2], HIST[:, c, :]), reads=[("HIST", c)], writes=[(gck, "h")])
                    S.act(ACT(gc[:, 2:514], pg, AF.Copy), reads=[pkg], writes=[(gck, "m")])
                    S.pool(CP(HIST[:, c, :], gc[:, 512:514]), reads=[(gck, "m"), (gck, "h")], writes=[("HIST", c)])
                    S.act(ACT(ga, pg, AF.Identity, scale=CW[:, c, 2:3], bias=CB[:, c:c + 1]), reads=[pkg, "CW", "CB"], writes=[gak])
                    g1, g1k = GT1[c % 2]
                    g0, g0k = GT0[c % 2]
                    S.act(ACT(g1, gc[:, 1:513], AF.Copy, scale=CW[:, c, 1:2]), reads=[(gck, "m"), (gck, "h"), "CW"], writes=[g1k])
                    S.act(ACT(g0, gc[:, 0:512], AF.Copy, scale=CW[:, c, 0:1]), reads=[(gck, "m"), (gck, "h"), "CW"], writes=[g0k])
                    S.pool(TT(g0, g0, g1, ALU.add), reads=[g0k, g1k], writes=[g0k])
                    S.pool(TT(ga, ga, g0, ALU.add), reads=[g0k, gak], writes=[gak])
                    S.act(ACT(gs, ga, AF.Silu), reads=[gak], writes=[gsk])
                    S.dve(TT(HT[:, c, :], gs, pu, ALU.mult), reads=[gsk, pku], writes=[("HT", c)])
            HTK = [("HT", c) for c in range(NCH)]
            if tb == 0:
                b.dump("HT", HT, HTK, BF16)
            accb = [b.bank() for _ in range(8)]
            for c0 in range(0, NCH, 8):
                nc_ = min(8, NCH - c0)
                wblk = [stream_w(wdn_v[:, c0:c0 + nc_, nbk * 512:(nbk + 1) * 512], allk("wdn_s", DFF)) for nbk in range(2)]
                for tt in range(4):
                    for nbk in range(2):
                        pa, pka = accb[tt * 2 + nbk]
                        ws, wsk = wblk[nbk]
                        for ci in range(nc_):
                            c = c0 + ci
                            S.pe(MM(pa, HT[:, c, tt * 128:(tt + 1) * 128], ws[:, ci, :], c == 0, c == NCH - 1),
                                 reads=HTK + [wsk], writes=[pka])
            for tt in range(4):
                for nbk in range(2):
                    pa, pka = accb[tt * 2 + nbk]
                    S.dve(STT(Y[:, nbk * 512:(nbk + 1) * 512], XR[:, tt, nbk * 512:(nbk + 1) * 512], ALPHA, pa, ALU.mult, ALU.add),
                          reads=[("XR", tt), pka], writes=[("Y", nbk)])
                layer_norm(Y, [("Y", 0), ("Y", 1)], 2, XR[:, tt, :], ("XR", tt))
                S.dma(out_d[tb * 512 + tt * 128:tb * 512 + (tt + 1) * 128, :], XR[:, tt, :], reads=[("XR", tt)], out=True, q="pool")
        S.emit(es)
    return nc, b


def _rot(cols, half):
    return np.concatenate([cols[half:], cols[:half]])


def prep_shared(inp):
    f = np.float32
    w_in = np.asarray(inp["w_in"])[0]
    C1, C2, C3, C4, C5 = 512, 1280, 1304, 1688, 1944
    groups = []
    for hp in range(4):
        groups.append(np.arange(hp * 128, hp * 128 + 128))
    for hp in range(4):
        groups.append(np.concatenate([_rot(np.arange(h * 64, h * 64 + 64), 32) for h in (2 * hp, 2 * hp + 1)]))
    for base in (C1 + 256, C1 + 512):
        for g in range(2):
            c = np.arange(base + g * 64, base + g * 64 + 64)
            groups.append(np.concatenate([c, c]))
        for g in range(2):
            c = _rot(np.arange(base + g * 64, base + g * 64 + 64), 32)
            groups.append(np.concatenate([c, c]))
    groups.append(np.arange(C1, C1 + 128))
    groups.append(np.arange(C1 + 128, C1 + 256))
    wfm_nsa = np.ascontiguousarray(w_in[:, np.concatenate(groups)])
    tm_cols = np.concatenate([np.arange(C1 + 256 + 128, C1 + 512), np.arange(C1 + 512 + 128, C1 + 768), np.arange(C2, C2 + 24)])
    wtm_nsa = np.ascontiguousarray(w_in[:, tm_cols])
    wfm_mla = np.zeros((D, 896), f)
    wfm_mla[:, 0:384] = w_in[:, C3:C4]
    wfm_mla[:, 384:640] = w_in[:, C4:C5]
    wfm_mla[:, 640 + 64:640 + 96] = w_in[:, C5:C5 + 32]
    wfm_mla[:, 768 + 64:768 + 96] = w_in[:, _rot(np.arange(C5, C5 + 32), 16)]

    def w1_layout(w1):
        a = np.asarray(w1)[0].reshape(32, 64, 128).transpose(1, 0, 2)
        return np.ascontiguousarray(np.concatenate([a, a], 0).reshape(128, 32 * 128))

    def posT_layout(p):
        a = np.asarray(p)[0].T
        a = np.repeat(a[:, :, None], 2, axis=2)
        return np.ascontiguousarray(np.concatenate([a, a], 0).reshape(128, 64))

    w2k = np.asarray(inp["nsa_ck_w2"])[0]
    rc = _rot(np.arange(64), 32)
    w2k_l = np.ascontiguousarray(np.concatenate([w2k, w2k, w2k[:, rc], w2k[:, rc]], 1))
    b2k = np.asarray(inp["nsa_ck_b2"])[0]
    cover = np.zeros((256, 64), f)
    n = np.arange(256)[:, None] * 16
    j = np.arange(64)[None, :] * 64
    cover[:, :] = np.clip(np.minimum(n + 32, j + 64) - np.maximum(n, j), 0, None).astype(f) / 32.0
    t = np.arange(S_LEN)
    cur = (t // 64)[:, None]
    blk = np.arange(64)[None, :]
    forced = ((blk == 0) | (blk == cur) | (blk == cur - 1)) & (blk <= cur)
    cand = (blk >= 1) & (blk <= cur - 2)
    psc = np.zeros((128, 16), f)
    p = np.arange(128)
    d64 = p % 64
    psc[:, 0] = 10000.0 ** (-(2.0 * (d64 % 32)) / 64.0)
    sg = np.where(d64 < 32, -1.0, 1.0)
    psc[:, 1] = sg
    psc[:, 2] = -sg * np.pi
    d32 = p % 32
    psc[:, 3] = 10000.0 ** (-(2.0 * (d32 % 16)) / 32.0)
    sg2 = np.where(d32 < 16, -1.0, 1.0)
    psc[:, 4] = sg2
    psc[:, 5] = -sg2 * np.pi
    psc[:, 6] = np.asarray(inp["nsa_ck_b1"])[0]
    psc[:, 7] = np.asarray(inp["nsa_cv_b1"])[0]
    psc[:, 8] = np.concatenate([b2k, b2k])
    psc[:, 9] = np.concatenate([b2k[rc], b2k[rc]])
    psc[:, 10:13] = np.asarray(inp["mla_q_norm"])[0].reshape(3, 128).T
    psc[:, 13:15] = np.asarray(inp["mla_kv_norm"])[0].reshape(2, 128).T
    psc[:, 15] = -np.pi
    wuq = np.asarray(inp["mla_w_uq"])[0]
    wuq_b = np.zeros_like(wuq)
    for h in range(8):
        pe = np.arange(h * 96 + 64, h * 96 + 96)
        wuq_b[:, pe] = wuq[:, _rot(pe, 16)]
    wukv = np.asarray(inp["mla_w_ukv"])[0]
    wuk = np.ascontiguousarray(np.concatenate([wukv[:, h * 128:h * 128 + 64] for h in range(8)], 1))
    wuv = np.ascontiguousarray(np.concatenate([wukv[:, h * 128 + 64:h * 128 + 128] for h in range(8)], 1))
    lngb = np.stack([np.asarray(inp[k])[0] for k in ("ln1_g", "ln1_b", "ln2_g", "ln2_b", "ln3_g", "ln3_b")]).astype(f)
    convw = np.ascontiguousarray(np.asarray(inp["ffn_conv_w"])[0].T.reshape(NCH, 128, 3).transpose(1, 0, 2).reshape(128, NCH * 3))
    convb = np.ascontiguousarray(np.asarray(inp["ffn_conv_b"])[0].reshape(NCH, 128).T)
    return {
        "wfm_nsa": wfm_nsa, "wtm_nsa": wtm_nsa, "wfm_mla": wfm_mla,
        "w1k": w1_layout(inp["nsa_ck_w1"]), "w1v": w1_layout(inp["nsa_cv_w1"]),
        "poskT": posT_layout(inp["nsa_k_pos"]), "posvT": posT_layout(inp["nsa_v_pos"]),
        "w2k": w2k_l, "w2v": np.ascontiguousarray(np.asarray(inp["nsa_cv_w2"])[0]),
        "b2v": np.ascontiguousarray(np.asarray(inp["nsa_cv_b2"])[0][None, :]),
        "cover": cover, "cand": cand.astype(f), "forced": forced.astype(f), "psc": psc,
        "wuq_a": np.ascontiguousarray(wuq), "wuq_b": wuq_b, "wuk": wuk, "wuv": wuv,
        "w_o": np.ascontiguousarray(np.asarray(inp["w_o"])[0]),
        "mem_wq": np.ascontiguousarray(np.asarray(inp["mem_wq"])[0]),
        "mem_wk": np.ascontiguousarray(np.asarray(inp["mem_wk"])[0]),
        "mem_wv": np.ascontiguousarray(np.asarray(inp["mem_wv"])[0]),
        "mem_wo": np.ascontiguousarray(np.asarray(inp["mem_wo"])[0]),
        "w_up": np.ascontiguousarray(np.asarray(inp["ffn_w_up"])[0]),
        "w_dn": np.ascontiguousarray(np.asarray(inp["ffn_w_down"])[0]),
        "lngb": lngb, "convw": convw, "convb": convb,
    }


def prep_core(inp, bi):
    x = np.asarray(inp["x"])[bi]
    pos = np.asarray(inp["positions"])[bi].astype(np.int32)
    posc = np.zeros((1, 256), np.int32)
    posc[0, :255] = pos[31::16][:255]
    return {
        "xT": np.ascontiguousarray(x.T), "x": np.ascontiguousarray(x),
        "memT": np.ascontiguousarray(np.asarray(inp["mem"])[bi].T),
        "pos": np.ascontiguousarray(pos[None, :]), "posc": posc,
    }


_CACHE = {}


def kernel(**inputs):
    if "nc" not in _CACHE:
        _CACHE["nc"] = build_program()[0]
    nc = _CACHE["nc"]
    shared = prep_shared(inputs)
    in_maps = []
    for bi in range(8):
        m = dict(shared)
        m.update(prep_core(inputs, bi))
        in_maps.append(m)
    res = run_bass_kernel_spmd(nc, in_maps, core_ids=list(range(8)))
    return np.stack([np.asarray(r["out"], dtype=np.float32) for r in res.results], 0)
```

```python
import numpy as np
from contextlib import ExitStack
import concourse.bass as bass
import concourse.mybir as mybir
from concourse.bass_utils import run_bass_kernel_spmd

F32 = mybir.dt.float32
BF16 = mybir.dt.bfloat16
I32 = mybir.dt.int32
AF = mybir.ActivationFunctionType
ALU = mybir.AluOpType

S_LEN = 4096
D = 1024
NT = 32
NB = 8
DFF = 2816
NCH = 22
ALPHA = 2.0 ** 0.25
PI = float(np.pi)
NEG = -30000.0

SEM_LIMIT = 20000
DMA_RING = 8
import os as _os
SAME_ENGINE_SYNC = set(_os.environ.get("KSES", "act,dve,pool").split(","))


class Sched:
    STREAMS = ("pe", "act", "dve", "pool", "sp")

    def __init__(self, nc):
        self.nc = nc
        self.ops = []
        self.last_writer = {}
        self.readers = {}
        self.out_dmas = []

    def op(self, stream, fn, reads=(), writes=(), dma=False, out=False):
        idx = len(self.ops)
        deps = set()
        reads = list(reads) + ["PHASE"]
        for r in reads:
            w = self.last_writer.get(r)
            if w is not None:
                deps.add(w)
        for w_ in writes:
            w = self.last_writer.get(w_)
            if w is not None:
                deps.add(w)
            for rd in self.readers.get(w_, ()):
                deps.add(rd)
        for r in reads:
            self.readers.setdefault(r, []).append(idx)
        for w_ in writes:
            self.last_writer[w_] = idx
            self.readers[w_] = []
        self.ops.append(dict(stream=stream, fn=fn, deps=deps, dma=dma))
        if out:
            self.out_dmas.append(idx)
        return idx

    def pe(self, fn, reads=(), writes=()):
        return self.op("pe", fn, reads, writes)

    def act(self, fn, reads=(), writes=()):
        return self.op("act", fn, reads, writes)

    def dve(self, fn, reads=(), writes=()):
        return self.op("dve", fn, reads, writes)

    def pool(self, fn, reads=(), writes=()):
        return self.op("pool", fn, reads, writes)

    def dma(self, out_ap, in_ap, reads=(), writes=(), q="sp", out=False):
        return self.op(q, lambda e: e.dma_start(out=out_ap, in_=in_ap), reads, writes, dma=True, out=out)

    def emit(self, es):
        nc = self.nc
        ops = self.ops
        ops.append(dict(stream="sp", fn=None, deps=set(self.out_dmas), dma=False))
        n = len(ops)
        dma_count = {s: 0 for s in self.STREAMS}
        dma_hist = {s: [] for s in self.STREAMS}
        for i, o in enumerate(ops):
            if o["dma"]:
                s = o["stream"]
                k = dma_count[s]
                o["dma_k"] = k
                if k >= DMA_RING:
                    o["deps"].add(dma_hist[s][k - DMA_RING])
                dma_hist[s].append(i)
                dma_count[s] += 1
        has_dep = [False] * n
        for i, o in enumerate(ops):
            for d in o["deps"]:
                od = ops[d]
                if od["dma"] or od["stream"] != o["stream"]:
                    has_dep[d] = True
                elif od["stream"] in SAME_ENGINE_SYNC:
                    has_dep[d] = True
        sem_cnt = [0]

        def new_sem(tag):
            sem_cnt[0] += 1
            return es.enter_context(nc.semaphore(f"s_{tag}_{sem_cnt[0]}"))

        cur_sem, cur_cnt, rings = {}, {}, {}
        for i, o in enumerate(ops):
            s = o["stream"]
            if o["dma"]:
                if s not in rings:
                    rings[s] = [new_sem(f"dq{s}") for _ in range(DMA_RING)]
                k = o["dma_k"]
                o["sig"] = (rings[s][k % DMA_RING], 16 * (k // DMA_RING + 1), 16)
            elif has_dep[i]:
                if s not in cur_sem or cur_cnt[s] >= SEM_LIMIT:
                    cur_sem[s] = new_sem(s)
                    cur_cnt[s] = 0
                cur_cnt[s] += 1
                o["sig"] = (cur_sem[s], cur_cnt[s], 1)
            else:
                o["sig"] = None
        waited = {s: {} for s in self.STREAMS}
        per_stream = {s: [] for s in self.STREAMS}
        for i, o in enumerate(ops):
            s = o["stream"]
            need = {}
            for d in o["deps"]:
                od = ops[d]
                if (not od["dma"]) and od["stream"] == s and (s not in SAME_ENGINE_SYNC):
                    continue
                sem, val, _ = od["sig"]
                key = id(sem)
                if waited[s].get(key, 0) >= val:
                    continue
                if key not in need or need[key][1] < val:
                    need[key] = (sem, val)
            for key, (sem, val) in need.items():
                waited[s][key] = val
            o["waits"] = list(need.values())
            per_stream[s].append(o)
        self.n_sems = sem_cnt[0]
        self.stream_sizes = {s: len(v) for s, v in per_stream.items()}
        block = es.enter_context(nc.Block())

        def run(stream_ops):
            def body(eng):
                for o in stream_ops:
                    for sem, val in o["waits"]:
                        eng.wait_ge(sem, val)
                    if o["fn"] is None:
                        continue
                    ins = o["fn"](eng)
                    if o["sig"] is not None:
                        ins.then_inc(o["sig"][0], o["sig"][2])
            return body

        block.tensor(run(per_stream["pe"]))
        block.scalar(run(per_stream["act"]))
        block.vector(run(per_stream["dve"]))
        block.gpsimd(run(per_stream["pool"]))
        block.sync(run(per_stream["sp"]))


def MM(out, lhsT, rhs, start, stop):
    return lambda e: e.matmul(out, lhsT=lhsT, rhs=rhs, start=start, stop=stop, skip_group_check=True)


def TR(out, in_, ident):
    return lambda e: e.transpose(out=out, in_=in_, identity=ident)


def ACT(out, in_, func, **kw):
    return lambda e: e.activation(out=out, in_=in_, func=func, **kw)


def TT(out, in0, in1, op):
    return lambda e: e.tensor_tensor(out=out, in0=in0, in1=in1, op=op)


def TS(out, in0, s1, s2, op0, op1=None):
    if op1 is None:
        return lambda e: e.tensor_scalar(out=out, in0=in0, scalar1=s1, scalar2=None, op0=op0)
    return lambda e: e.tensor_scalar(out=out, in0=in0, scalar1=s1, scalar2=s2, op0=op0, op1=op1)


def STT(out, in0, scalar, in1, op0, op1):
    return lambda e: e.scalar_tensor_tensor(out=out, in0=in0, scalar=scalar, in1=in1, op0=op0, op1=op1)


def CP(out, in_):
    return lambda e: e.tensor_copy(out=out, in_=in_)


def MEMSET(ap, v):
    return lambda e: e.memset(ap, v)


def RECIP(out, in_):
    return lambda e: e.reciprocal(out=out, in_=in_)


def ASEL(out, in_, pattern, op, fill, base, cm):
    return lambda e: e.affine_select(out=out, in_=in_, pattern=pattern, compare_op=op, fill=fill,
                                     base=base, channel_multiplier=cm)


DT_SIZE = {F32: 4, BF16: 2, I32: 4}
ARENA_BYTES = 204 * 1024


class Builder:
    def __init__(self, nc, es, dbg=()):
        self.nc = nc
        self.es = es
        self.S = Sched(nc)
        self.dbg = set(dbg)
        self.dbg_outs = {}
        self.arena = nc.alloc_sbuf_tensor("arena", [128, ARENA_BYTES // 4], F32).ap()
        self.top = 0
        self.banks = [es.enter_context(nc.psum_tensor(f"pb{i}", [128, 512], F32)).ap() for i in range(8)]
        self.bank_i = 0
        self.pinned = set()
        self.din = {}

    def inp(self, name, shape, dt=F32):
        ap = self.nc.dram_tensor(name, list(shape), dt, kind="ExternalInput").ap()
        self.din[name] = ap
        return ap

    def alloc(self, name, shape, dt=F32):
        n = int(np.prod(shape[1:]))
        nbytes = (n * DT_SIZE[dt] + 31) // 32 * 32
        off = self.top
        self.top += nbytes
        self.max_top = max(getattr(self, "max_top", 0), self.top)
        assert self.top <= ARENA_BYTES, f"arena overflow at {name}: {self.top}"
        ap = self.arena[0:shape[0], off // 4:(off + nbytes) // 4]
        if dt != F32:
            ap = ap.bitcast(dt)
        ap = ap[:, 0:n]
        if len(shape) == 3:
            ap = ap.rearrange("p (a b) -> p a b", b=shape[2])
        elif len(shape) == 4:
            ap = ap.rearrange("p (a b c) -> p a b c", b=shape[2], c=shape[3])
        return ap

    def ring(self, name, shape, dt, n):
        return [(self.alloc(f"{name}{i}", shape, dt), f"{name}{i}") for i in range(n)]

    def mark(self):
        return self.top

    def release(self, mark):
        scr = self.barrier_scr
        self.S.op("pool", MEMSET(scr, 0.0), reads=[], writes=["PHASE", "barrier_scr"])
        self.top = mark

    def bank(self, pin=False):
        while self.bank_i in self.pinned:
            self.bank_i = (self.bank_i + 1) % 8
        i = self.bank_i
        self.bank_i = (i + 1) % 8
        if pin:
            self.pinned.add(i)
        return self.banks[i], ("PB", i)

    def unpin(self, key):
        self.pinned.discard(key[1])

    def dump(self, name, ap, reads, dt=F32):
        if name not in self.dbg:
            return
        shape = list(ap.shape)
        d = self.nc.dram_tensor("dbg_" + name, shape, dt, kind="ExternalOutput").ap()
        self.dbg_outs[name] = shape
        self.S.dma(d, ap, reads=reads, out=True)


def build_program(dbg=(), stop_after=None):
    nc = bass.Bass("TRN2", target_bir_lowering=False)
    es = ExitStack()
    with es:
        b = Builder(nc, es, dbg)
        S = b.S
        xT = b.inp("xT", [D, S_LEN])
        x_in = b.inp("x", [S_LEN, D])
        memT = b.inp("memT", [D, 256])
        pos = b.inp("pos", [1, S_LEN], I32)
        posc = b.inp("posc", [1, 256], I32)
        wfm_nsa = b.inp("wfm_nsa", [D, 2304])
        wtm_nsa = b.inp("wtm_nsa", [D, 280])
        wfm_mla = b.inp("wfm_mla", [D, 896])
        w1k_d = b.inp("w1k", [128, 32 * 128])
        w1v_d = b.inp("w1v", [128, 32 * 128])
        poskT_d = b.inp("poskT", [128, 64])
        posvT_d = b.inp("posvT", [128, 64])
        w2k_d = b.inp("w2k", [128, 256])
        w2v_d = b.inp("w2v", [128, 64])
        b2v_d = b.inp("b2v", [1, 64])
        cover_d = b.inp("cover", [256, 64])
        cand_d = b.inp("cand", [S_LEN, 64])
        forced_d = b.inp("forced", [S_LEN, 64])
        psc_d = b.inp("psc", [128, 16])
        wuq_a_d = b.inp("wuq_a", [384, 768])
        wuq_b_d = b.inp("wuq_b", [384, 768])
        wuk_d = b.inp("wuk", [256, 512])
        wuv_d = b.inp("wuv", [256, 512])
        wo_d = b.inp("w_o", [D, D])
        mwq_d = b.inp("mem_wq", [D, D])
        mwk_d = b.inp("mem_wk", [D, D])
        mwv_d = b.inp("mem_wv", [D, D])
        mwo_d = b.inp("mem_wo", [D, D])
        wup_d = b.inp("w_up", [D, 2 * DFF])
        wdn_d = b.inp("w_dn", [DFF, D])
        lngb_d = b.inp("lngb", [6, D])
        convw_d = b.inp("convw", [128, NCH * 3])
        convb_d = b.inp("convb", [128, NCH])
        out_d = nc.dram_tensor("out", [S_LEN, D], F32, kind="ExternalOutput").ap()
        wo_s = nc.dram_tensor("wo_s", [D, D], BF16).ap()
        mwq_s = nc.dram_tensor("mwq_s", [D, D], BF16).ap()
        mwo_s = nc.dram_tensor("mwo_s", [D, D], BF16).ap()
        wup_s = nc.dram_tensor("wup_s", [D, 2 * DFF], BF16).ap()
        wdn_s = nc.dram_tensor("wdn_s", [DFF, D], BF16).ap()
        oT_s = nc.dram_tensor("oT_s", [D, S_LEN], BF16).ap()

        ident = b.alloc("ident", [128, 128], BF16)
        caus = b.alloc("caus", [128, 128], BF16)
        wedge = b.alloc("wedge", [128, 128], BF16)
        ones = b.alloc("ones", [128, 128], BF16)
        psc = b.alloc("psc", [128, 16], F32)
        b.barrier_scr = b.alloc("barrier_scr", [128, 8], F32)
        zeros = b.alloc("zeros", [128, 512], BF16)
        S.pool(MEMSET(zeros, 0.0), writes=["zeros"])
        S.pool(MEMSET(ones, 1.0), writes=["ones"])
        S.pool(MEMSET(ident, 1.0), writes=["ident"])
        S.pool(ASEL(ident, ident, [[1, 128]], ALU.is_equal, 0.0, 0, -1), reads=["ident"], writes=["ident"])
        S.pool(ASEL(caus, zeros[:, 0:128], [[1, 128]], ALU.is_ge, NEG, 0, -1), reads=["zeros"], writes=["caus"])
        S.pool(ASEL(wedge, zeros[:, 0:128], [[-1, 128]], ALU.is_gt, NEG, 0, 1), reads=["zeros"], writes=["wedge"])
        S.dma(psc, psc_d, writes=["psc"])
        for (dst, src, rows, key) in ((wo_s, wo_d, D, "wo_s"), (mwq_s, mwq_d, D, "mwq_s"), (mwo_s, mwo_d, D, "mwo_s"),
                                      (wup_s, wup_d, D, "wup_s"), (wdn_s, wdn_d, DFF, "wdn_s")):
            for r0 in range(0, rows, 128):
                S.dma(dst[r0:r0 + 128, :], src[r0:r0 + 128, :], writes=[(key, r0)], q="pool")
        base_mark = b.mark()

        PSC = lambda c: psc[:, c:c + 1]

        def load_xT_block(tb, ring_slot):
            xt, xk = ring_slot
            src = xT.rearrange("(kc p) t -> p kc t", p=128)
            for k0 in range(0, 8, 2):
                S.dma(xt[:, k0:k0 + 2, :], src[:, k0:k0 + 2, tb * 512:(tb + 1) * 512], writes=[(xk, k0)], q="pool")
            return xt, [(xk, k0) for k0 in range(0, 8, 2)]

        def rope_tables(pos_src, n, inv_c, sgn_c, nsg_c, tiles, prow=slice(0, 128)):
            (posi, kpi), (posf, kpf), (m1, km1), (m2, km2), (cos, kc), (sin, ks) = tiles
            S.dma(posi[:, 0:n], pos_src.partition_broadcast(128), writes=[kpi])
            S.dve(CP(posf[:, 0:n], posi[:, 0:n]), reads=[kpi], writes=[kpf])
            C1_, C2_ = 6.28125, 2 * PI - 6.28125
            S.dve(TS(m1[:, 0:n], posf[:, 0:n], PSC(inv_c), None, ALU.mult), reads=[kpf, "psc"], writes=[km1])
            S.dve(TS(m2[:, 0:n], posf[:, 0:n], PSC(inv_c), 0.5 * PI, ALU.mult, ALU.add), reads=[kpf, "psc"], writes=[km2])
            S.dve(TS(sin[:, 0:n], m1[:, 0:n], 1.0 / (2 * PI), None, ALU.mult), reads=[km1], writes=[ks])
            S.dve(CP(posi[:, 0:n], sin[:, 0:n]), reads=[ks], writes=[kpi])
            S.dve(CP(cos[:, 0:n], posi[:, 0:n]), reads=[kpi], writes=[kc])
            S.dve(STT(m1[:, 0:n], cos[:, 0:n], -C1_, m1[:, 0:n], ALU.mult, ALU.add), reads=[kc, km1], writes=[km1])
            S.dve(STT(m1[:, 0:n], cos[:, 0:n], -C2_, m1[:, 0:n], ALU.mult, ALU.add), reads=[kc, km1], writes=[km1])
            S.dve(TS(m1[:, 0:n], m1[:, 0:n], PI, -PI, ALU.min, ALU.max), reads=[km1], writes=[km1])
            S.act(ACT(sin[:, 0:n], m1[:, 0:n], AF.Sin, scale=PSC(sgn_c)), reads=[km1, "psc"], writes=[ks])
            S.dve(TS(cos[:, 0:n], m2[:, 0:n], 1.0 / (2 * PI), None, ALU.mult), reads=[km2], writes=[kc])
            S.dve(CP(posi[:, 0:n], cos[:, 0:n]), reads=[kc], writes=[kpi])
            S.dve(CP(posf[:, 0:n], posi[:, 0:n]), reads=[kpi], writes=[kpf])
            S.dve(STT(m2[:, 0:n], posf[:, 0:n], -C1_, m2[:, 0:n], ALU.mult, ALU.add), reads=[kpf, km2], writes=[km2])
            S.dve(STT(m2[:, 0:n], posf[:, 0:n], -C2_, m2[:, 0:n], ALU.mult, ALU.add), reads=[kpf, km2], writes=[km2])
            S.dve(TS(m2[:, 0:n], m2[:, 0:n], PI, -PI, ALU.min, ALU.max), reads=[km2], writes=[km2])
            S.act(ACT(cos[:, 0:n], m2[:, 0:n], AF.Sin), reads=[km2], writes=[kc])

        def proj_fm(xt, xkeys, w, wkey, col0, ncols=128):
            pb, pk = b.bank()
            for kc in range(8):
                S.pe(MM(pb[0:ncols, :], w[:, kc, col0:col0 + ncols], xt[:, kc, :], kc == 0, kc == 7),
                     reads=[wkey] + xkeys, writes=[pk])
            return pb, pk

        QT = b.alloc("QT", [128, 4, S_LEN], BF16)
        KS = b.alloc("KS", [128, 2, S_LEN], BF16)
        KW = b.alloc("KW", [128, 2, S_LEN], BF16)
        VS = b.alloc("VS", [128, NT, 2, 65], BF16)
        VW = b.alloc("VW", [128, NT, 2, 65], BF16)
        GATES = b.alloc("GATES", [128, NT, 24], F32)
        KCMP = b.alloc("KCMP", [128, 2, 256], BF16)
        CV = b.alloc("CV", [128, 2, 2, 129], BF16)
        nsa_state_mark = b.mark()
        KC = b.alloc("KC", [128, S_LEN], BF16)
        VC = b.alloc("VC", [128, S_LEN], BF16)
        cmp_mark = b.mark()
        WFM = b.alloc("WFM", [128, 8, 2304], BF16)
        WTM = b.alloc("WTM", [128, 8, 280], BF16)
        XTR = b.ring("XT", [128, 8, 512], BF16, 2)
        tabs = [b.ring(nm, [128, 512], dt_, 2 if nm in ("cos", "sin") else 1) for nm, dt_ in
                (("posi", I32), ("posf", F32), ("m1", F32), ("m2", F32), ("cos", F32), ("sin", F32))]
        T1 = b.ring("T1", [128, 512], F32, 2)
        T2 = b.ring("T2", [128, 512], F32, 2)

        S.pool(MEMSET(VS, 1.0), writes=["VS"])
        S.pool(MEMSET(VW, 1.0), writes=["VW"])
        for kc in range(8):
            S.dma(WFM[:, kc, :], wfm_nsa[kc * 128:(kc + 1) * 128, :], writes=["WFM"], q="pool")
        S.dma(WTM, wtm_nsa.rearrange("(kc p) n -> p kc n", p=128), writes=["WTM"], q="pool")

        rope_groups = [(QT[:, hp, :], hp, 4 + hp) for hp in range(4)] + \
                      [(KS[:, g, :], 8 + g, 10 + g) for g in range(2)] + \
                      [(KW[:, g, :], 12 + g, 14 + g) for g in range(2)]
        for tb in range(NB):
            xt, xkeys = load_xT_block(tb, XTR[tb % 2])
            tl = [t[tb % len(t)] for t in tabs]
            rope_tables(pos[:, tb * 512:(tb + 1) * 512], 512, 0, 1, 2, tl)
            cos, kcos = tl[4]
            sin, ksin = tl[5]
            tsl = slice(tb * 512, (tb + 1) * 512)
            for gi, (dst, ga, gb) in enumerate(rope_groups):
                pa, pka = proj_fm(xt, xkeys, WFM, "WFM", ga * 128)
                pb_, pkb = proj_fm(xt, xkeys, WFM, "WFM", gb * 128)
                t1, k1 = T1[gi % 2]
                t2, k2 = T2[gi % 2]
                S.dve(TT(t1, pa, cos, ALU.mult), reads=[pka, kcos], writes=[k1])
                S.dve(TT(t2, pb_, sin, ALU.mult), reads=[pkb, ksin], writes=[k2])
                S.pool(TT(dst[:, tsl], t1, t2, ALU.add), reads=[k1, k2], writes=[("ropeout", gi, tb)])
            for dst, gidx, nm in ((KC, 16, "KC"), (VC, 17, "VC")):
                pa, pka = proj_fm(xt, xkeys, WFM, "WFM", gidx * 128)
                S.act(ACT(dst[:, tsl], pa, AF.Copy), reads=[pka], writes=[(nm, tb)])
            for tt in range(4):
                kt = tb * 4 + tt
                pb_, pk = b.bank()
                for kc in range(8):
                    S.pe(MM(pb_[:, 0:280], xt[:, kc, tt * 128:(tt + 1) * 128], WTM[:, kc, :], kc == 0, kc == 7),
                         reads=["WTM"] + xkeys, writes=[pk])
                S.act(ACT(VS[:, kt, :, 0:64], pb_[:, 0:128].rearrange("p (g d) -> p g d", d=64), AF.Copy),
                      reads=[pk, "VS"], writes=[("VS", kt)])
                S.act(ACT(VW[:, kt, :, 0:64], pb_[:, 128:256].rearrange("p (g d) -> p g d", d=64), AF.Copy),
                      reads=[pk, "VW"], writes=[("VW", kt)])
                S.act(ACT(GATES[:, kt, :], pb_[:, 256:280], AF.Sigmoid), reads=[pk], writes=[("GATES", kt)])
        QTK = [("ropeout", gi, tb) for gi in range(8) for tb in range(NB)]
        b.dump("QT", QT[:, 0, :], QTK, BF16)
        b.dump("KS", KS[:, 1, :], QTK, BF16)
        b.dump("GATES", GATES, [("GATES", kt) for kt in range(NT)])
        b.dump("VS", VS, [("VS", kt) for kt in range(NT)], BF16)
        if stop_after == "1a":
            S.emit(es)
            return nc, b
        b.release(cmp_mark)

        W1 = b.alloc("W1", [128, 32, 128], BF16)
        POST = b.alloc("POST", [128, 32, 2], BF16)
        W2K = b.alloc("W2K", [128, 256], BF16)
        W2V = b.alloc("W2V", [128, 64], BF16)
        B2V = b.alloc("B2V", [128, 64], F32)
        C1 = b.alloc("C1", [128, 2], F32)
        U = b.alloc("U", [128, 256], F32)
        U2 = b.alloc("U2", [128, 256], F32)
        U3 = b.alloc("U3", [128, 256], F32)
        GT = b.alloc("GT", [128, 256], BF16)
        ctab = [b.ring(nm, [128, 512], dt_, 1) for nm, dt_ in
                (("cposi", I32), ("cposf", F32), ("cm1", F32), ("cm2", F32), ("ccos", F32), ("csin", F32))]
        ctl = [t[0] for t in ctab]
        rope_tables(posc, 256, 0, 1, 2, ctl)
        ccos, kccos = ctl[4]
        csin, kcsin = ctl[5]
        S.dma(W2K, w2k_d, writes=["W2K"], q="pool")
        S.dma(W2V, w2v_d, writes=["W2V"], q="pool")
        S.dma(B2V, b2v_d.partition_broadcast(128), writes=["B2V"])
        S.dma(CV[:, 0, 0, 0:64], cover_d[0:128, :], reads=[], writes=[("CVc", 0)], q="pool")
        S.dma(CV[:, 1, 0, 0:64], cover_d[128:256, :], reads=[], writes=[("CVc", 1)], q="pool")
        S.pool(MEMSET(GT, 0.0), writes=["GT"])
        for which in range(2):
            src = KC if which == 0 else VC
            S.dma(W1, (w1k_d if which == 0 else w1v_d).rearrange("p (l h) -> p l h", h=128), writes=["W1"], q="pool")
            S.dma(POST, (poskT_d if which == 0 else posvT_d).rearrange("p (l t) -> p l t", t=2), writes=["POST"], q="pool")
            pc, pkc = b.bank()
            for l in range(32):
                S.pe(MM(pc[:, 0:2], W1[0:64, l, :], POST[0:64, l, :], l == 0, l == 31), reads=["W1", "POST"], writes=[pkc])
            S.dve(TS(C1, pc[:, 0:2], PSC(6 + which), None, ALU.add), reads=[pkc, "psc"], writes=["C1"])
            for g in range(2):
                rows = slice(g * 64, (g + 1) * 64)
                ph, pkh = b.bank()
                for l in range(32):
                    S.pe(MM(ph[:, 0:255], W1[rows, l, :], src[rows, l:l + 16 * 254 + 1:16], l == 0, l == 31),
                         reads=["W1"] + [("KC" if which == 0 else "VC", tb) for tb in range(NB)], writes=[pkh])
                S.act(ACT(U[:, 0:255], ph[:, 0:255], AF.Identity, bias=C1[:, 0:1], scale=1.0), reads=[pkh, "C1"], writes=["U"])
                S.dve(TT(U2[:, 0:255], U[:, 0:255], U[:, 0:255], ALU.mult), reads=["U"], writes=["U2"])
                S.dve(TS(U2[:, 0:255], U2[:, 0:255], 0.044715, 1.0, ALU.mult, ALU.add), reads=["U2"], writes=["U2"])
                S.dve(TT(U2[:, 0:255], U2[:, 0:255], U[:, 0:255], ALU.mult), reads=["U2", "U"], writes=["U2"])
                S.act(ACT(U3[:, 0:255], U2[:, 0:255], AF.Tanh, scale=0.7978845608028654), reads=["U2"], writes=["U3"])
                S.dve(TS(U3[:, 0:255], U3[:, 0:255], 0.5, 0.5, ALU.mult, ALU.add), reads=["U3"], writes=["U3"])
                S.dve(TT(GT[:, 0:255], U3[:, 0:255], U[:, 0:255], ALU.mult), reads=["U3", "U", "GT"], writes=["GT"])
                if which == 0:
                    pa, pka = b.bank()
                    pb_, pkb = b.bank()
                    S.pe(MM(pa[:, 0:256], W2K[:, 0:128], GT, True, True), reads=["W2K", "GT"], writes=[pka])
                    S.pe(MM(pb_[:, 0:256], W2K[:, 128:256], GT, True, True), reads=["W2K", "GT"], writes=[pkb])
                    S.dve(STT(U[:, 0:256], pa[:, 0:256], PSC(8), ccos[:, 0:256], ALU.add, ALU.mult),
                          reads=[pka, kccos, "psc", "U"], writes=["U"])
                    S.dve(STT(U2[:, 0:256], pb_[:, 0:256], PSC(9), csin[:, 0:256], ALU.add, ALU.mult),
                          reads=[pkb, kcsin, "psc", "U2"], writes=["U2"])
                    S.pool(TT(KCMP[:, g, :], U[:, 0:256], U2[:, 0:256], ALU.add), reads=["U", "U2"], writes=[("KCMP", g)])
                else:
                    for nt in range(2):
                        pv, pkv = b.bank()
                        S.pe(MM(pv[:, 0:64], GT[:, nt * 128:(nt + 1) * 128], W2V, True, True), reads=["GT", "W2V"], writes=[pkv])
                        S.dve(TT(CV[:, nt, g, 65:129], pv[:, 0:64], B2V, ALU.add), reads=[pkv, "B2V"], writes=[("CVv", nt, g)])
        for nt in range(2):
            S.pool(CP(CV[:, nt, 1, 0:64], CV[:, nt, 0, 0:64]), reads=[("CVc", nt)], writes=[("CVc2", nt)])
            S.pool(MEMSET(CV[:, nt, :, 64:65], 1.0), writes=[("CVo", nt)])
        CVK = [("CVc", nt) for nt in range(2)] + [("CVc2", nt) for nt in range(2)] + [("CVo", nt) for nt in range(2)] + \
              [("CVv", nt, g) for nt in range(2) for g in range(2)]
        b.dump("KCMP", KCMP, [("KCMP", 0), ("KCMP", 1)], BF16)
        b.dump("CV", CV, CVK, BF16)
        if stop_after == "1ap":
            S.emit(es)
            return nc, b
        b.release(nsa_state_mark)

        ETAB = b.alloc("ETAB", [128, 32, 128], BF16)
        S.pool(MEMSET(ETAB, 1.0), writes=["ETAB"])
        S.pool(ASEL(ETAB, ETAB, [[-2, 32], [-1, 2], [0, 64]], ALU.is_equal, 0.0, 0, 1), reads=["ETAB"], writes=["ETAB"])
        MASKC = b.ring("MASKC", [128, 2, 512], BF16, 2)
        PT = b.ring("PT", [128, 512], BF16, 6)
        ONSA = b.alloc("ONSA", [128, 4, 512], F32)
        ONSAB = b.alloc("ONSAB", [128, 4, 512], BF16)
        OTB = b.ring("OTB", [128, 4, 512], BF16, 2)
        IMP = b.alloc("IMP", [128, 2, 4, 64], F32)
        CANDT = b.ring("CANDT", [128, 4, 64], F32, 2)
        FORCT = b.ring("FORCT", [128, 4, 64], F32, 2)
        SM = b.ring("SM", [128, 16], F32, 4)
        IMPM = b.alloc("IMPM", [128, 64], F32)
        SCR = b.alloc("SCR", [128, 64], F32)
        M8 = b.alloc("M8", [128, 16], F32)
        SEL = b.alloc("SEL", [128, 64], F32)
        NEGB8 = b.alloc("NEGB8", [128, 8, 64], BF16)
        NEGT = b.alloc("NEGT", [128, 2, 512], BF16)
        S.pool(MEMSET(NEGT, 0.0), writes=[("NEGT", 0), ("NEGT", 1)])
        pt_i = [0]
        sm_i = [0]
        KSX = [b.alloc("KSL", [128, 2, S_LEN], BF16), b.alloc("KSH", [128, 2, S_LEN], BF16)]
        KWX = [b.alloc("KWL", [128, 2, S_LEN], BF16), b.alloc("KWH", [128, 2, S_LEN], BF16)]
        KCX = [b.alloc("KCL", [128, 2, 256], BF16), b.alloc("KCH", [128, 2, 256], BF16)]
        for half in range(2):
            rws = slice(half * 64, half * 64 + 64)
            for dst, src, nm in ((KSX[half], KS, "KSX"), (KWX[half], KW, "KWX"), (KCX[half], KCMP, "KCX")):
                S.pool(MEMSET(dst, 0.0), writes=[(nm, half)])
                if nm == "KSX":
                    S.act(ACT(dst[rws], src[rws], AF.Copy), reads=[(nm, half)], writes=[(nm, half)])
                else:
                    S.dve(CP(dst[rws], src[rws]), reads=[(nm, half)], writes=[(nm, half)])

        ATT = {"staged": []}
        DEPTH = 3
        NPT = 6

        def _emit_pv(item):
            (kt, jl, jh, pt, pkt, accs, nper, Vfn, v_keys, last_kt, is_last, post_fn) = item
            for j in range(jl, jh + 1):
                acc, pka, j0 = accs[j // nper]
                S.pe(MM(acc[:, j - j0, :], pt[:, j * 128:(j + 1) * 128], Vfn(kt), False, last_kt[j] == kt),
                     reads=[pkt] + v_keys, writes=[pka])
            if is_last:
                for (_a, pka, _j) in accs:
                    b.unpin(pka)
                post_fn(accs)

        def att_flush():
            while ATT["staged"]:
                _emit_pv(ATT["staged"].pop(0))

        def attention(QTa, KTa, Vfn, tiles, qb, scale, ncols, q_keys, k_keys, v_keys, post_fn, blockmask=None):
            nper = min(4, 512 // ncols)
            accs = []
            for j0 in range(0, 4, nper):
                pa, pka = b.bank(pin=True)
                S.pe(MM(pa[:, 0:nper * ncols], zeros[:, 0:128], zeros[:, 0:nper * ncols], True, True), reads=["zeros"], writes=[pka])
                accs.append((pa[:, 0:nper * ncols].rearrange("p (j c) -> p j c", c=ncols), pka, j0))
            last_kt = {}
            for (kt, jl, jh, masks) in tiles:
                for j in range(jl, jh + 1):
                    last_kt[j] = kt
            for ti, (kt, jl, jh, masks) in enumerate(tiles):
                c0, c1 = jl * 128, (jh + 1) * 128
                ps, pks = b.bank()
                nmm = 1 + (1 if blockmask is not None else 0) + len(masks)
                done = 1
                S.pe(MM(ps[:, c0:c1], KTa[:, kt * 128:(kt + 1) * 128], QTa[:, qb * 512 + c0:qb * 512 + c1], True, done == nmm),
                     reads=q_keys + k_keys, writes=[pks])
                if blockmask is not None:
                    done += 1
                    negt, nkey = blockmask
                    S.pe(MM(ps[:, c0:c1], ETAB[:, kt, :], negt[:, c0:c1], False, done == nmm),
                         reads=["ETAB", nkey], writes=[pks])
                for (kind, j) in masks:
                    done += 1
                    S.pe(MM(ps[:, j * 128:(j + 1) * 128], ident, caus if kind == "c" else wedge, False, done == nmm),
                         reads=["ident", "caus", "wedge"], writes=[pks])
                pt, pkt = PT[pt_i[0] % NPT]
                pt_i[0] += 1
                S.act(ACT(pt[:, c0:c1], ps[:, c0:c1], AF.Exp, scale=scale), reads=[pks], writes=[pkt])
                ATT["staged"].append((kt, jl, jh, pt, pkt, accs, nper, Vfn, v_keys, last_kt, ti == len(tiles) - 1, post_fn))
                while len(ATT["staged"]) > DEPTH:
                    _emit_pv(ATT["staged"].pop(0))

        def recip_sums(accs, ncols, sumcol):
            sm, smk = SM[sm_i[0] % 4]
            sm_i[0] += 1
            for (acc, pka, j0) in accs:
                nper = acc.shape[1]
                S.dve(TS(sm[:, j0:j0 + nper], acc[:, :, sumcol], 1e-30, None, ALU.max), reads=[pka], writes=[smk])
            S.dve(RECIP(sm[:, 4:8], sm[:, 0:4]), reads=[smk], writes=[smk])
            return sm, smk

        ALLQ = []
        for qb in range(NB):
            mk, mkk = MASKC[qb % 2]
            for nt in range(2):
                S.pool(ASEL(mk[:, nt, :], zeros, [[1, 512]], ALU.is_ge, NEG, qb * 512 - 2048 * nt - 31, -16),
                       reads=["zeros"], writes=[(mkk, nt)])
            cand, candk = CANDT[qb % 2]
            forc, forck = FORCT[qb % 2]
            S.dma(cand, cand_d[qb * 512:(qb + 1) * 512, :].rearrange("(j p) c -> p j c", p=128), writes=[candk])
            S.dma(forc, forced_d[qb * 512:(qb + 1) * 512, :].rearrange("(j p) c -> p j c", p=128), writes=[forck])
            if stop_after == "2a_tab":
                b.dump("ETAB", ETAB, ["ETAB"], BF16)
                b.dump("MASKC", mk, [(mkk, 0), (mkk, 1)], BF16)
                b.dump("CAND", cand, [candk])
                S.emit(es)
                return nc, b
            gview = lambda br, h: GATES[:, qb * 4:(qb + 1) * 4, h * 3 + br]
            gkeys = [("GATES", kt) for kt in range(qb * 4, qb * 4 + 4)]
            def cmp_scores(h):
                g, hp, half = h // 4, h // 2, h % 2
                ets = []
                for nt in range(2):
                    ps, pks = b.bank()
                    S.pe(MM(ps, KCX[half][:, g, nt * 128:(nt + 1) * 128], QT[:, hp, qb * 512:(qb + 1) * 512], True, False),
                         reads=[("KCX", half)], writes=[pks])
                    S.pe(MM(ps, ident, mk[:, nt, :], False, True), reads=["ident", (mkk, nt)], writes=[pks])
                    pt, pkt = PT[pt_i[0] % 6]
                    pt_i[0] += 1
                    S.act(ACT(pt, ps, AF.Exp, scale=0.125), reads=[pks], writes=[pkt])
                    ets.append((pt, pkt))
                return ets

            def cmp_pv(h, ets):
                g = h // 4
                accs = []
                for j0 in (0, 2):
                    pa, pka = b.bank()
                    acc = pa[:, 0:258].rearrange("p (j c) -> p j c", c=129)
                    for j in (j0, j0 + 1):
                        for nt in range(2):
                            S.pe(MM(acc[:, j - j0, :], ets[nt][0][:, j * 128:(j + 1) * 128], CV[:, nt, g, :], nt == 0, nt == 1),
                                 reads=[ets[nt][1]], writes=[pka])
                    accs.append((acc, pka, j0))
                sm, smk = recip_sums(accs, 129, 64)
                S.dve(TT(sm[:, 8:12], sm[:, 4:8], gview(0, h), ALU.mult), reads=[smk] + gkeys, writes=[smk])
                for (acc, pka, j0) in accs:
                    for j in (j0, j0 + 1):
                        S.dve(TS(ONSA[:, j, h * 64:(h + 1) * 64], acc[:, j - j0, 65:129], sm[:, 8 + j:9 + j], None, ALU.mult),
                              reads=[pka, smk], writes=[("ONSA", h)])
                        if qb < 2:
                            pass
                        elif h % 4 == 0:
                            S.dve(TS(IMP[:, g, j, :], acc[:, j - j0, 0:64], sm[:, 4 + j:5 + j], None, ALU.mult),
                                  reads=[pka, smk], writes=[("IMP", g)])
                        else:
                            S.dve(STT(IMP[:, g, j, :], acc[:, j - j0, 0:64], sm[:, 4 + j:5 + j], IMP[:, g, j, :], ALU.mult, ALU.add),
                                  reads=[pka, smk, ("IMP", g)], writes=[("IMP", g)])

            ets_cur = cmp_scores(0)
            for h in range(8):
                ets_next = cmp_scores(h + 1) if h + 1 < 8 else None
                cmp_pv(h, ets_cur)
                ets_cur = ets_next
            if stop_after == "2a_cmp":
                b.dump("IMP", IMP, [("IMP", 0), ("IMP", 1)])
                b.dump("ONSA_c", ONSA, [("ONSA", h) for h in range(8)])
                S.emit(es)
                return nc, b
            if qb == 1:
                b.dump("IMP", IMP, [("IMP", 0), ("IMP", 1)])
                b.dump("ONSA_c", ONSA, [("ONSA", h) for h in range(8)])
            def sel_dve():
                for g in range(2):
                    for j in range(4):
                        negb = NEGB8[:, g * 4 + j, :]
                        S.dve(TT(IMPM, IMP[:, g, j, :], cand[:, j, :], ALU.mult), reads=[("IMP", g), candk], writes=["IMPM"])
                        S.dve(lambda e: e.max(out=M8[:, 0:8], in_=IMPM), reads=["IMPM"], writes=["M8a"])
                        S.dve(lambda e: e.match_replace(out=SCR, in_to_replace=M8[:, 0:8], in_values=IMPM, imm_value=-1.0),
                              reads=["IMPM", "M8a"], writes=["SCR"])
                        S.dve(lambda e: e.max(out=M8[:, 8:16], in_=SCR), reads=["SCR"], writes=["M8b"])
                        S.dve(TS(SEL, IMPM, M8[:, 12:13], None, ALU.is_ge), reads=["IMPM", "M8b"], writes=["SEL"])
                        S.dve(TT(SEL, SEL, cand[:, j, :], ALU.mult), reads=["SEL", candk], writes=["SEL"])
                        S.dve(TT(SEL, SEL, forc[:, j, :], ALU.add), reads=["SEL", forck], writes=["SEL"])
                        S.dve(TS(negb, SEL, -1.0, -NEG, ALU.add, ALU.mult), reads=["SEL"], writes=[("NEGB", g, j)])

            def sel_pe():
                for g in range(2):
                    for j in range(4):
                        negb = NEGB8[:, g * 4 + j, :]
                        pb_, pk = b.bank()
                        ptr = pb_.bitcast(BF16)
                        S.pe(TR(ptr[0:64, 0:128], negb, ident), reads=[("NEGB", g, j), "ident"], writes=[pk])
                        S.act(ACT(NEGT[0:64, g, j * 128:(j + 1) * 128], ptr[0:64, 0:128], AF.Copy), reads=[pk], writes=[("NEGT", g)])

            if qb >= 2:
                sel_dve()
            if stop_after == "2a_sel":
                b.dump("NEGT", NEGT, [("NEGT", 0), ("NEGT", 1)], BF16)
                S.emit(es)
                return nc, b
            if qb == 3:
                b.dump("NEGT", NEGT, [("NEGT", 0), ("NEGT", 1)], BF16)
            for br in (2, 1):
                if br == 1 and qb >= 2:
                    sel_pe()
                for h in range(8):
                    g, hp, half = h // 4, h // 2, h % 2
                    QTa = QT[:, hp, :]
                    if br == 1:
                        KTa = KSX[half][:, g, :]
                        kkeys = [("KSX", half)]
                        Vt = VS
                        tiles = []
                        for kt in range(0, 4 * qb + 4):
                            jl = max(kt - 4 * qb, 0)
                            masks = [("c", kt - 4 * qb)] if kt >= 4 * qb else []
                            tiles.append((kt, jl, 3, masks))
                        bm = (NEGT[:, g, :], ("NEGT", g)) if qb >= 2 else None
                    else:
                        KTa = KWX[half][:, g, :]
                        kkeys = [("KWX", half)]
                        Vt = VW
                        tiles = []
                        for kt in range(max(0, 4 * qb - 4), 4 * qb + 4):
                            jl = max(kt - 4 * qb, 0)
                            jh = min(kt + 4 - 4 * qb, 3)
                            masks = []
                            if kt >= 4 * qb:
                                masks.append(("c", kt - 4 * qb))
                            if 0 <= kt + 4 - 4 * qb <= 3:
                                masks.append(("w", kt + 4 - 4 * qb))
                            tiles.append((kt, jl, jh, masks))
                        bm = None

                    def post_nsa(accs, h=h, br=br, qb=qb):
                        sm, smk = recip_sums(accs, 65, 64)
                        S.dve(TT(sm[:, 8:12], sm[:, 4:8], GATES[:, qb * 4:(qb + 1) * 4, h * 3 + br], ALU.mult),
                              reads=[smk], writes=[smk])
                        acc, pka, _ = accs[0]
                        for j in range(4):
                            dst = ONSA[:, j, h * 64:(h + 1) * 64]
                            S.dve(STT(dst, acc[:, j, 0:64], sm[:, 8 + j:9 + j], dst, ALU.mult, ALU.add),
                                  reads=[pka, smk, ("ONSA", h)], writes=[("ONSA", h)])

                    attention(QTa, KTa, lambda kt, Vt=Vt, g=g: Vt[:, kt, g, :], tiles, qb, 0.125, 65,
                              [], kkeys, [], post_nsa, blockmask=bm)
            att_flush()
            if qb == 3:
                b.dump("ONSA", ONSA, [("ONSA", h) for h in range(8)])
            S.act(ACT(ONSAB, ONSA, AF.Copy), reads=[("ONSA", h) for h in range(8)], writes=["ONSAB"])
            otb, otk = OTB[qb % 2]
            for fc in range(4):
                pb_, pk = b.bank()
                ptr = pb_.bitcast(BF16)
                for j in range(4):
                    S.pe(TR(ptr[:, j * 128:(j + 1) * 128], ONSAB[:, j, fc * 128:(fc + 1) * 128], ident),
                         reads=["ONSAB", "ident"], writes=[pk])
                S.dve(CP(otb[:, fc, :], ptr[:, 0:512]), reads=[pk], writes=[(otk, fc)])
            S.dma(oT_s[0:512, qb * 512:(qb + 1) * 512].rearrange("(fc p) t -> p fc t", p=128), otb,
                  reads=[(otk, fc) for fc in range(4)], writes=[("oT_s", 0, qb)])
            if stop_after == "2a_qb0":
                b.dump("ONSA0", ONSA, [("ONSA", h) for h in range(8)])
                S.emit(es)
                return nc, b
        if stop_after == "2a":
            S.emit(es)
            return nc, b
        b.release(base_mark)

        QM = b.alloc("QM", [128, 8, S_LEN], BF16)
        NMKV = b.alloc("NMKV", [128, 2, S_LEN], BF16)
        KPE = b.alloc("KPE", [128, S_LEN], BF16)
        S.pool(MEMSET(QM[96:128], 0.0), writes=["QMpad"])
        mla_mark = b.mark()
        WFM2 = b.alloc("WFM2", [128, 8, 896], BF16)
        WUQA = b.alloc("WUQA", [128, 3, 768], BF16)
        WUQB = b.alloc("WUQB", [128, 3, 768], BF16)
        XTR = b.ring("XTb", [128, 8, 512], BF16, 2)
        tabs = [b.ring(nm, [128, 512], dt_, 2) for nm, dt_ in
                (("bposi", I32), ("bposf", F32), ("bm1", F32), ("bm2", F32), ("bcos", F32), ("bsin", F32))]
        T1 = b.ring("bT1", [128, 512], F32, 2)
        T2 = b.ring("bT2", [128, 512], F32, 2)
        SQ = b.ring("SQ", [128, 512], BF16, 3)
        RR = b.ring("RR", [128, 512], F32, 2)
        NMQ = b.ring("NMQ", [128, 3, 512], BF16, 2)
        for kc in range(8):
            S.dma(WFM2[:, kc, :], wfm_mla[kc * 128:(kc + 1) * 128, :], writes=["WFM2"], q="pool")
        S.dma(WUQA, wuq_a_d.rearrange("(kc p) n -> p kc n", p=128), writes=["WUQA"], q="pool")
        S.dma(WUQB, wuq_b_d.rearrange("(kc p) n -> p kc n", p=128), writes=["WUQB"], q="pool")
        pe_rows = slice(64, 96)
        for tb in range(NB):
            xt, xkeys = load_xT_block(tb, XTR[tb % 2])
            tl = [t[tb % len(t)] for t in tabs]
            rope_tables(pos[:, tb * 512:(tb + 1) * 512], 512, 3, 4, 5, tl)
            cos, kcos = tl[4]
            sin, ksin = tl[5]
            tsl = slice(tb * 512, (tb + 1) * 512)

            def rmsnorm_group(g0, nchunks, width, gcol, dst_fn, dkey):
                pbs = [proj_fm(xt, xkeys, WFM2, "WFM2", (g0 + c) * 128) for c in range(nchunks)]
                pss, pkss = b.bank()
                for c in range(nchunks):
                    sq, sqk = SQ[c]
                    S.act(ACT(sq, pbs[c][0], AF.Square), reads=[pbs[c][1]], writes=[sqk])
                    S.pe(MM(pss, ones, sq, c == 0, c == nchunks - 1), reads=["ones", sqk], writes=[pkss])
                rr, rrk = RR[0]
                r2, r2k = RR[1]
                S.act(ACT(rr, pss, AF.Sqrt, scale=1.0 / width, bias=1e-6), reads=[pkss], writes=[rrk])
                S.dve(RECIP(r2, rr), reads=[rrk], writes=[r2k])
                for c in range(nchunks):
                    S.dve(STT(dst_fn(c), pbs[c][0], PSC(gcol + c), r2, ALU.mult, ALU.mult),
                          reads=[pbs[c][1], r2k, "psc"], writes=[(dkey, c)])

            nmq, nmqk = NMQ[tb % 2]
            rmsnorm_group(0, 3, 384.0, 10, lambda c: nmq[:, c, :], nmqk)
            rmsnorm_group(3, 2, 256.0, 13, lambda c: NMKV[:, c, tsl], ("NMKV", tb))
            pa, pka = proj_fm(xt, xkeys, WFM2, "WFM2", 5 * 128)
            pb_, pkb = proj_fm(xt, xkeys, WFM2, "WFM2", 6 * 128)
            t1, k1 = T1[0]
            t2, k2 = T2[0]
            S.dve(TT(t1[pe_rows], pa[pe_rows], cos[pe_rows], ALU.mult), reads=[pka, kcos], writes=[k1])
            S.dve(TT(t2[pe_rows], pb_[pe_rows], sin[pe_rows], ALU.mult), reads=[pkb, ksin], writes=[k2])
            S.pool(TT(KPE[pe_rows, tsl], t1[pe_rows], t2[pe_rows], ALU.add), reads=[k1, k2], writes=[("KPE", tb)])
            for h in range(8):
                pa, pka = b.bank()
                pb_, pkb = b.bank()
                for c in range(3):
                    S.pe(MM(pa[0:96, :], WUQA[:, c, h * 96:(h + 1) * 96], nmq[:, c, :], c == 0, c == 2),
                         reads=["WUQA"] + [(nmqk, cc) for cc in range(3)], writes=[pka])
                for c in range(3):
                    S.pe(MM(pb_[0:96, :], WUQB[:, c, h * 96:(h + 1) * 96], nmq[:, c, :], c == 0, c == 2),
                         reads=["WUQB"] + [(nmqk, cc) for cc in range(3)], writes=[pkb])
                S.act(ACT(QM[0:64, h, tsl], pa[0:64, :], AF.Copy), reads=[pka], writes=[("QMn", h, tb)])
                t1, k1 = T1[(h + 1) % 2]
                t2, k2 = T2[(h + 1) % 2]
                S.dve(TT(t1[pe_rows], pa[pe_rows], cos[pe_rows], ALU.mult), reads=[pka, kcos], writes=[k1])
                S.dve(TT(t2[pe_rows], pb_[pe_rows], sin[pe_rows], ALU.mult), reads=[pkb, ksin], writes=[k2])
                S.pool(TT(QM[pe_rows, h, tsl], t1[pe_rows], t2[pe_rows], ALU.add), reads=[k1, k2], writes=[("QMp", h, tb)])
        QMK = [("QMn", h, tb) for h in range(8) for tb in range(NB)] + [("QMp", h, tb) for h in range(8) for tb in range(NB)]
        NMKVK = [(("NMKV", tb), c) for tb in range(NB) for c in range(2)]
        KPEK = [("KPE", tb) for tb in range(NB)]
        b.dump("QM", QM[0:96, 0, :], QMK, BF16)
        b.dump("NMKV", NMKV[:, 0, :], NMKVK, BF16)
        b.dump("KPE", KPE[64:96, :], KPEK, BF16)
        if stop_after == "1b":
            S.emit(es)
            return nc, b
        b.release(mla_mark)

        WUK = b.alloc("WUK", [128, 2, 512], BF16)
        WUV = b.alloc("WUV", [128, 2, 512], BF16)
        KM = b.ring("KM", [128, S_LEN], BF16, 2)
        VM = b.alloc("VM", [128, NT, 2, 65], BF16)
        PT = b.ring("PTb", [128, 512], BF16, 6)
        SM = b.ring("SMb", [128, 16], F32, 4)
        OM = b.ring("OM", [128, 4, 128], BF16, 2)
        OTB2 = b.ring("OTB2", [128, 512], BF16, 2)
        S.dma(WUK, wuk_d.rearrange("(kc p) n -> p kc n", p=128), writes=["WUK"], q="pool")
        S.dma(WUV, wuv_d.rearrange("(kc p) n -> p kc n", p=128), writes=["WUV"], q="pool")
        S.pool(MEMSET(VM, 1.0), writes=["VM"])
        for (km_, kmk_) in KM:
            S.pool(MEMSET(km_[96:128, :], 0.0), writes=[(kmk_, "pad")])
        mla_scale = 96.0 ** -0.5
        for hp in range(4):
            for hh in range(2):
                h = hp * 2 + hh
                km, kmk = KM[hh]
                for kb in range(NB):
                    pa, pka = b.bank()
                    for c in range(2):
                        S.pe(MM(pa[0:64, :], WUK[:, c, h * 64:(h + 1) * 64], NMKV[:, c, kb * 512:(kb + 1) * 512], c == 0, c == 1),
                             reads=["WUK"], writes=[pka])
                    S.dve(CP(km[0:64, kb * 512:(kb + 1) * 512], pa[0:64, :]), reads=[pka], writes=[(kmk, kb)])
                S.pool(CP(km[pe_rows, :], KPE[pe_rows, :]), reads=[], writes=[(kmk, "pe")])
            for kt in range(NT):
                pa, pka = b.bank()
                for c in range(2):
                    S.pe(MM(pa[:, 0:128], NMKV[:, c, kt * 128:(kt + 1) * 128], WUV[:, c, hp * 128:(hp + 1) * 128], c == 0, c == 1),
                         reads=["WUV"], writes=[pka])
                S.dve(CP(VM[:, kt, :, 0:64], pa[:, 0:128].rearrange("p (g d) -> p g d", d=64)),
                      reads=[pka, "VM"], writes=[("VM", kt)])
            if hp == 0:
                b.dump("KM0", KM[0][0][0:96, :], [("KM0", kb) for kb in range(NB)] + [("KM0", "pe")], BF16)
                b.dump("VM", VM, [("VM", kt) for kt in range(NT)], BF16)
            for qb in range(NB):
                om, omk = OM[qb % 2]
                for hh in range(2):
                    h = hp * 2 + hh
                    km, kmk = KM[hh]
                    tiles = []
                    for kt in range(0, 4 * qb + 4):
                        jl = max(kt - 4 * qb, 0)
                        masks = [("c", kt - 4 * qb)] if kt >= 4 * qb else []
                        tiles.append((kt, jl, 3, masks))

                    def post_mla(accs, hh=hh, qb=qb, om=om, omk=omk, hp=hp):
                        sm, smk = recip_sums(accs, 65, 64)
                        acc, pka, _ = accs[0]
                        for j in range(4):
                            S.dve(TS(om[:, j, hh * 64:(hh + 1) * 64], acc[:, j, 0:64], sm[:, 4 + j:5 + j], None, ALU.mult),
                                  reads=[pka, smk], writes=[(omk, hh)])
                        if hh == 1:
                            pb_, pk = b.bank()
                            ptr = pb_.bitcast(BF16)
                            for j in range(4):
                                S.pe(TR(ptr[:, j * 128:(j + 1) * 128], om[:, j, :], ident), reads=[(omk, 0), (omk, 1), "ident"], writes=[pk])
                            otb, otk = OTB2[qb % 2]
                            S.dve(CP(otb, ptr[:, 0:512]), reads=[pk], writes=[otk])
                            S.dma(oT_s[512 + hp * 128:512 + (hp + 1) * 128, qb * 512:(qb + 1) * 512], otb, reads=[otk],
                                  writes=[("oT_s", 1 + hp, qb)])

                    attention(QM[:, h, :], km, lambda kt, hh=hh: VM[:, kt, hh, :], tiles, qb, mla_scale, 65,
                              [], [(kmk, kb) for kb in range(NB)] + [(kmk, "pe"), (kmk, "pad")],
                              [("VM", kt) for kt in range(NT)] + ["VM"], post_mla)
            att_flush()
        if stop_after == "2b":
            S.emit(es)
            return nc, b
        b.release(base_mark)

        LNT = b.alloc("LNT", [128, 6, D], F32)
        CW = b.alloc("CW", [128, NCH, 3], F32)
        CB = b.alloc("CB", [128, NCH], F32)
        HIST = b.alloc("HIST", [128, NCH, 2], F32)
        KMEM = b.alloc("KMEM", [128, 8, 256], BF16)
        VMEM = b.alloc("VMEM", [128, 2, 4, 257], BF16)
        NWS = 6
        WS = b.ring("WS", [128, 8, 512], BF16, NWS)
        OTL = b.ring("OTL", [128, 8, 512], BF16, 1)
        XIN = b.ring("XIN", [128, D], F32, 2)
        Y = b.alloc("Y", [128, D], F32)
        XR = b.alloc("XR", [128, 4, D], F32)
        XB = b.alloc("XB", [128, D], BF16)
        XRT = b.alloc("XRT", [128, 8, 512], BF16)
        QME = b.alloc("QME", [128, 8, 512], BF16)
        OME = b.alloc("OME", [128, 4, D], BF16)
        OMET = b.alloc("OMET", [128, 8, 512], BF16)
        HT = b.alloc("HT", [128, NCH, 512], BF16)
        GC = b.ring("GC", [128, 514], F32, 2)
        GA = b.ring("GA", [128, 512], F32, 2)
        GS = b.ring("GS", [128, 512], F32, 2)
        PT = b.ring("PTc", [128, 512], BF16, 4)
        SM = b.ring("SMc", [128, 16], F32, 4)
        STAT = b.alloc("STAT", [128, 32], F32)
        MEMTB = b.alloc("MEMTB", [128, 8, 256], BF16)

        S.dma(LNT.rearrange("p k d -> p (k d)"), lngb_d.rearrange("(o k) d -> o (k d)", o=1).partition_broadcast(128),
              writes=["LNT"])
        S.dma(CW, convw_d.rearrange("p (c k) -> p c k", k=3), writes=["CW"])
        S.dma(CB, convb_d, writes=["CB"])
        S.pool(MEMSET(HIST, 0.0), writes=["HIST"])
        S.pool(MEMSET(VMEM, 1.0), writes=["VMEM"])
        S.dma(MEMTB, memT.rearrange("(kc p) t -> p kc t", p=128), writes=["MEMTB"], q="pool")
        ws_i = [0]

        def stream_w(src_ap, keys_src):
            ws, wsk = WS[ws_i[0] % NWS]
            ws_i[0] += 1
            kk, nn = src_ap.shape[1], src_ap.shape[2]
            S.dma(ws[:, 0:kk, 0:nn], src_ap, reads=keys_src, writes=[wsk])
            return ws, wsk

        wo_v = wo_s.rearrange("(kc p) n -> p kc n", p=128)
        mwq_v = mwq_s.rearrange("(kc p) n -> p kc n", p=128)
        mwo_v = mwo_s.rearrange("(kc p) n -> p kc n", p=128)
        wup_v = wup_s.rearrange("(kc p) n -> p kc n", p=128)
        wdn_v = wdn_s.rearrange("(c p) n -> p c n", p=128)
        allk = lambda key, rows: [(key, r0) for r0 in range(0, rows, 128)]

        for nb in range(2):
            ws, wsk = WS[ws_i[0] % NWS]
            ws_i[0] += 1
            S.dma(ws, mwk_d.rearrange("(kc p) n -> p kc n", p=128)[:, :, nb * 512:(nb + 1) * 512], writes=[wsk], q="pool")
            for oc in range(4):
                pa, pka = b.bank()
                for kc in range(8):
                    S.pe(MM(pa[:, 0:256], ws[:, kc, oc * 128:(oc + 1) * 128], MEMTB[:, kc, :], kc == 0, kc == 7),
                         reads=[wsk, "MEMTB"], writes=[pka])
                S.act(ACT(KMEM[:, nb * 4 + oc, :], pa[:, 0:256], AF.Copy), reads=[pka], writes=[("KMEM", nb * 4 + oc)])
        for nb in range(2):
            ws, wsk = WS[ws_i[0] % NWS]
            ws_i[0] += 1
            S.dma(ws, mwv_d.rearrange("(kc p) n -> p kc n", p=128)[:, :, nb * 512:(nb + 1) * 512], writes=[wsk], q="pool")
            for kt in range(2):
                pa, pka = b.bank()
                for kc in range(8):
                    S.pe(MM(pa, MEMTB[:, kc, kt * 128:(kt + 1) * 128], ws[:, kc, :], kc == 0, kc == 7),
                         reads=[wsk, "MEMTB"], writes=[pka])
                S.act(ACT(VMEM[:, kt, nb * 2:nb * 2 + 2, 0:256], pa.rearrange("p (h d) -> p h d", d=256), AF.Copy),
                      reads=[pka, "VMEM"], writes=[("VMEM", kt, nb)])
        KMEMK = [("KMEM", i) for i in range(8)]
        VMEMK = [("VMEM", kt, nb) for kt in range(2) for nb in range(2)]

        def layer_norm(src, srcks, ln_idx, dst, dstk):
            S.dve(lambda e: e.bn_stats(out=STAT[:, 0:6], in_=src[:, 0:512]), reads=srcks, writes=["STATa"])
            S.dve(lambda e: e.bn_stats(out=STAT[:, 6:12], in_=src[:, 512:1024]), reads=srcks, writes=["STATb"])
            S.dve(lambda e: e.bn_aggr(out=STAT[:, 12:14], in_=STAT[:, 0:12].rearrange("p (a b) -> p a b", b=6)),
                  reads=["STATa", "STATb"], writes=["STATc"])
            S.act(ACT(STAT[:, 16:17], STAT[:, 13:14], AF.Sqrt, scale=1.0, bias=1e-5), reads=["STATc"], writes=["STATd"])
            S.dve(RECIP(STAT[:, 17:18], STAT[:, 16:17]), reads=["STATd"], writes=["STATe"])
            S.dve(TS(dst, src, STAT[:, 12:13], STAT[:, 17:18], ALU.subtract, ALU.mult), reads=srcks + ["STATc", "STATe"], writes=[dstk])
            S.pool(TT(dst, dst, LNT[:, 2 * ln_idx, :], ALU.mult), reads=[dstk, "LNT"], writes=[dstk])
            S.pool(TT(dst, dst, LNT[:, 2 * ln_idx + 1, :], ALU.add), reads=[dstk, "LNT"], writes=[dstk])

        def to_feature_major(src, srck, tt, tag):
            S.act(ACT(XB, src, AF.Copy), reads=[srck], writes=["XB"])
            for half in range(2):
                pb_, pk = b.bank()
                ptr = pb_.bitcast(BF16)
                for c in range(4):
                    S.pe(TR(ptr[:, c * 128:(c + 1) * 128], XB[:, (half * 4 + c) * 128:(half * 4 + c + 1) * 128], ident),
                         reads=["XB", "ident"], writes=[pk])
                S.dve(CP(XRT[:, half * 4:half * 4 + 4, tt * 128:(tt + 1) * 128],
                         ptr[:, 0:512].rearrange("p (c t) -> p c t", t=128)), reads=[pk], writes=[("XRT", tag, tt, half)])

        def res_mm_ln(tiles, lhsT_fn, lhs_keys, wblk, res_fn, ln_idx):
            for tt in tiles:
                res, resk = res_fn(tt)
                for nbk in range(2):
                    pa, pka = b.bank()
                    ws, wsk = wblk[nbk]
                    for kc in range(8):
                        S.pe(MM(pa, lhsT_fn(kc, tt), ws[:, kc, :], kc == 0, kc == 7), reads=lhs_keys + [wsk], writes=[pka])
                    S.dve(STT(Y[:, nbk * 512:(nbk + 1) * 512], res[:, nbk * 512:(nbk + 1) * 512], ALPHA, pa, ALU.mult, ALU.add),
                          reads=[resk, pka], writes=[("Y", nbk)])
                layer_norm(Y, [("Y", 0), ("Y", 1)], ln_idx, XR[:, tt, :], ("XR", tt))

        XRTK = lambda tag, tiles=range(4): [("XRT", tag, tt, half) for tt in tiles for half in range(2)]

        for tb in range(NB):
            tsl = slice(tb * 512, (tb + 1) * 512)
            GRP = ((0, 1), (2, 3))
            otl, otlk = OTL[0]
            S.dma(otl, oT_s.rearrange("(kc p) t -> p kc t", p=128)[:, :, tsl],
                  reads=[("oT_s", i, tb) for i in range(5)], writes=[otlk])

            def res_x(tt, tb=tb):
                xi, xik = XIN[tt % 2]
                S.dma(xi, x_in[tb * 512 + tt * 128:tb * 512 + (tt + 1) * 128, :], writes=[xik])
                return xi, xik

            wo_blk = [stream_w(wo_v[:, :, nbk * 512:(nbk + 1) * 512], allk("wo_s", D)) for nbk in range(2)]
            wq_blk = [stream_w(mwq_v[:, :, nbk * 512:(nbk + 1) * 512], allk("mwq_s", D)) for nbk in range(2)]
            wmo_blk = [stream_w(mwo_v[:, :, nbk * 512:(nbk + 1) * 512], allk("mwo_s", D)) for nbk in range(2)]

            def stage_q(tiles):
                c0, c1 = tiles[0] * 128, (tiles[-1] + 1) * 128
                for nbk in range(2):
                    ws, wsk = wq_blk[nbk]
                    for oc in range(4):
                        pa, pka = b.bank()
                        for kc in range(8):
                            S.pe(MM(pa[:, 0:c1 - c0], ws[:, kc, oc * 128:(oc + 1) * 128], XRT[:, kc, c0:c1], kc == 0, kc == 7),
                                 reads=[wsk] + XRTK("a", tiles), writes=[pka])
                        S.act(ACT(QME[:, nbk * 4 + oc, c0:c1], pa[:, 0:c1 - c0], AF.Copy), reads=[pka],
                              writes=[("QME", nbk * 4 + oc, tiles[0])])

            def stage_att(tiles):
                c0, c1 = tiles[0] * 128, (tiles[-1] + 1) * 128
                for h in range(4):
                    pts = []
                    for kt in range(2):
                        ps, pks = b.bank()
                        for dc in range(2):
                            S.pe(MM(ps[:, 0:c1 - c0], KMEM[:, 2 * h + dc, kt * 128:(kt + 1) * 128], QME[:, 2 * h + dc, c0:c1], dc == 0, dc == 1),
                                 reads=KMEMK + [("QME", 2 * h + dc, tiles[0])], writes=[pks])
                        pt, pkt = PT[pt_i[0] % 4]
                        pt_i[0] += 1
                        S.act(ACT(pt[:, 0:c1 - c0], ps[:, 0:c1 - c0], AF.Exp, scale=1.0 / 16.0), reads=[pks], writes=[pkt])
                        pts.append((pt, pkt))
                    for ji, j in enumerate(tiles):
                        pa, pka = b.bank()
                        for kt in range(2):
                            S.pe(MM(pa[:, 0:257], pts[kt][0][:, ji * 128:(ji + 1) * 128], VMEM[:, kt, h, :], kt == 0, kt == 1),
                                 reads=[pts[kt][1]] + VMEMK + ["VMEM"], writes=[pka])
                        sm, smk = SM[sm_i[0] % 4]
                        sm_i[0] += 1
                        S.dve(RECIP(sm[:, 0:1], pa[:, 256:257]), reads=[pka], writes=[smk])
                        S.dve(TS(OME[:, j, h * 256:(h + 1) * 256], pa[:, 0:256], sm[:, 0:1], None, ALU.mult),
                              reads=[pka, smk], writes=[("OME", j, h)])

            def stage_omet(tiles):
                for j in tiles:
                    for half in range(2):
                        pb_, pk = b.bank()
                        ptr = pb_.bitcast(BF16)
                        for c in range(4):
                            S.pe(TR(ptr[:, c * 128:(c + 1) * 128], OME[:, j, (half * 4 + c) * 128:(half * 4 + c + 1) * 128], ident),
                                 reads=[("OME", j, hh_) for hh_ in range(4)] + ["ident"], writes=[pk])
                        S.dve(CP(OMET[:, half * 4:half * 4 + 4, j * 128:(j + 1) * 128],
                                 ptr[:, 0:512].rearrange("p (c t) -> p c t", t=128)), reads=[pk], writes=[("OMET", j, half)])

            for gi, tiles in enumerate(GRP):
                res_mm_ln(tiles, lambda kc, tt: otl[:, kc, tt * 128:(tt + 1) * 128], [otlk], wo_blk, res_x, 0)
            if tb == 0:
                b.dump("X1", XR, [("XR", tt) for tt in range(4)])
            for gi, tiles in enumerate(GRP):
                for tt in tiles:
                    to_feature_major(XR[:, tt, :], ("XR", tt), tt, "a")
                stage_q(tiles)
            for gi, tiles in enumerate(GRP):
                stage_att(tiles)
            OMETK = lambda tiles: [("OMET", j, half) for j in tiles for half in range(2)]
            for gi, tiles in enumerate(GRP):
                stage_omet(tiles)
                res_mm_ln(tiles, lambda kc, tt: OMET[:, kc, tt * 128:(tt + 1) * 128], OMETK(tiles), wmo_blk,
                          lambda tt: (XR[:, tt, :], ("XR", tt)), 1)
            for gi, tiles in enumerate(GRP):
                for tt in tiles:
                    to_feature_major(XR[:, tt, :], ("XR", tt), tt, "b")
            if tb == 0:
                b.dump("X2", XR, [("XR", tt) for tt in range(4)])
            for cb in range(6):
                ncol = 512 if cb < 5 else 256
                wg, wgk = stream_w(wup_v[:, :, cb * 512:cb * 512 + ncol], allk("wup_s", D))
                wu, wuk_ = stream_w(wup_v[:, :, DFF + cb * 512:DFF + cb * 512 + ncol], allk("wup_s", D))
                for cc in range(ncol // 128):
                    c = cb * 4 + cc
                    pg, pkg = b.bank()
                    pu, pku = b.bank()
                    for kc in range(8):
                        S.pe(MM(pg, wg[:, kc, cc * 128:(cc + 1) * 128], XRT[:, kc, :], kc == 0, kc == 7),
                             reads=[wgk] + XRTK("b"), writes=[pkg])
                    for kc in range(8):
                        S.pe(MM(pu, wu[:, kc, cc * 128:(cc + 1) * 128], XRT[:, kc, :], kc == 0, kc == 7),
                             reads=[wuk_] + XRTK("b"), writes=[pku])
                    gc, gck = GC[c % 2]
                    ga, gak = GA[c % 2]
                    gs, gsk = GS[c % 2]
                    S.pool(CP(gc[:, 0:2], HIST[:, c, :]), reads=[("HIST", c)], writes=[(gck, "h")])
                    S.act(ACT(gc[:, 2:514], pg, AF.Copy), reads=[pkg], writes=[(gck, "m")])
                    S.pool(CP(HIST[:, c, :], gc[:, 512:514]), reads=[(gck, "m"), (gck, "h")], writes=[("HIST", c)])
                    S.act(ACT(ga, pg, AF.Identity, scale=CW[:, c, 2:3], bias=CB[:, c:c + 1]), reads=[pkg, "CW", "CB"], writes=[gak])
                    S.dve(STT(ga, gc[:, 1:513], CW[:, c, 1:2], ga, ALU.mult, ALU.add), reads=[(gck, "m"), (gck, "h"), gak, "CW"], writes=[gak])
                    S.dve(STT(ga, gc[:, 0:512], CW[:, c, 0:1], ga, ALU.mult, ALU.add), reads=[(gck, "m"), (gck, "h"), gak, "CW"], writes=[gak])
                    S.act(ACT(gs, ga, AF.Silu), reads=[gak], writes=[gsk])
                    S.dve(TT(HT[:, c, :], gs, pu, ALU.mult), reads=[gsk, pku], writes=[("HT", c)])
            HTK = [("HT", c) for c in range(NCH)]
            if tb == 0:
                b.dump("HT", HT, HTK, BF16)
            accb = [b.bank() for _ in range(8)]
            for c0 in range(0, NCH, 8):
                nc_ = min(8, NCH - c0)
                wblk = [stream_w(wdn_v[:, c0:c0 + nc_, nbk * 512:(nbk + 1) * 512], allk("wdn_s", DFF)) for nbk in range(2)]
                for tt in range(4):
                    for nbk in range(2):
                        pa, pka = accb[tt * 2 + nbk]
                        ws, wsk = wblk[nbk]
                        for ci in range(nc_):
                            c = c0 + ci
                            S.pe(MM(pa, HT[:, c, tt * 128:(tt + 1) * 128], ws[:, ci, :], c == 0, c == NCH - 1),
                                 reads=HTK + [wsk], writes=[pka])
            for tt in range(4):
                for nbk in range(2):
                    pa, pka = accb[tt * 2 + nbk]
                    S.dve(STT(Y[:, nbk * 512:(nbk + 1) * 512], XR[:, tt, nbk * 512:(nbk + 1) * 512], ALPHA, pa, ALU.mult, ALU.add),
                          reads=[("XR", tt), pka], writes=[("Y", nbk)])
                layer_norm(Y, [("Y", 0), ("Y", 1)], 2, XR[:, tt, :], ("XR", tt))
                S.dma(out_d[tb * 512 + tt * 128:tb * 512 + (tt + 1) * 128, :], XR[:, tt, :], reads=[("XR", tt)], out=True, q="pool")
        S.emit(es)
    return nc, b


def _rot(cols, half):
    return np.concatenate([cols[half:], cols[:half]])


def prep_shared(inp):
    f = np.float32
    w_in = np.asarray(inp["w_in"])[0]
    C1, C2, C3, C4, C5 = 512, 1280, 1304, 1688, 1944
    groups = []
    for hp in range(4):
        groups.append(np.arange(hp * 128, hp * 128 + 128))
    for hp in range(4):
        groups.append(np.concatenate([_rot(np.arange(h * 64, h * 64 + 64), 32) for h in (2 * hp, 2 * hp + 1)]))
    for base in (C1 + 256, C1 + 512):
        for g in range(2):
            c = np.arange(base + g * 64, base + g * 64 + 64)
            groups.append(np.concatenate([c, c]))
        for g in range(2):
            c = _rot(np.arange(base + g * 64, base + g * 64 + 64), 32)
            groups.append(np.concatenate([c, c]))
    groups.append(np.arange(C1, C1 + 128))
    groups.append(np.arange(C1 + 128, C1 + 256))
    wfm_nsa = np.ascontiguousarray(w_in[:, np.concatenate(groups)])
    tm_cols = np.concatenate([np.arange(C1 + 256 + 128, C1 + 512), np.arange(C1 + 512 + 128, C1 + 768), np.arange(C2, C2 + 24)])
    wtm_nsa = np.ascontiguousarray(w_in[:, tm_cols])
    wfm_mla = np.zeros((D, 896), f)
    wfm_mla[:, 0:384] = w_in[:, C3:C4]
    wfm_mla[:, 384:640] = w_in[:, C4:C5]
    wfm_mla[:, 640 + 64:640 + 96] = w_in[:, C5:C5 + 32]
    wfm_mla[:, 768 + 64:768 + 96] = w_in[:, _rot(np.arange(C5, C5 + 32), 16)]

    def w1_layout(w1):
        a = np.asarray(w1)[0].reshape(32, 64, 128).transpose(1, 0, 2)
        return np.ascontiguousarray(np.concatenate([a, a], 0).reshape(128, 32 * 128))

    def posT_layout(p):
        a = np.asarray(p)[0].T
        a = np.repeat(a[:, :, None], 2, axis=2)
        return np.ascontiguousarray(np.concatenate([a, a], 0).reshape(128, 64))

    w2k = np.asarray(inp["nsa_ck_w2"])[0]
    rc = _rot(np.arange(64), 32)
    w2k_l = np.ascontiguousarray(np.concatenate([w2k, w2k, w2k[:, rc], w2k[:, rc]], 1))
    b2k = np.asarray(inp["nsa_ck_b2"])[0]
    cover = np.zeros((256, 64), f)
    n = np.arange(256)[:, None] * 16
    j = np.arange(64)[None, :] * 64
    cover[:, :] = np.clip(np.minimum(n + 32, j + 64) - np.maximum(n, j), 0, None).astype(f) / 32.0
    t = np.arange(S_LEN)
    cur = (t // 64)[:, None]
    blk = np.arange(64)[None, :]
    forced = ((blk == 0) | (blk == cur) | (blk == cur - 1)) & (blk <= cur)
    cand = (blk >= 1) & (blk <= cur - 2)
    psc = np.zeros((128, 16), f)
    p = np.arange(128)
    d64 = p % 64
    psc[:, 0] = 10000.0 ** (-(2.0 * (d64 % 32)) / 64.0)
    sg = np.where(d64 < 32, -1.0, 1.0)
    psc[:, 1] = sg
    psc[:, 2] = -sg * np.pi
    d32 = p % 32
    psc[:, 3] = 10000.0 ** (-(2.0 * (d32 % 16)) / 32.0)
    sg2 = np.where(d32 < 16, -1.0, 1.0)
    psc[:, 4] = sg2
    psc[:, 5] = -sg2 * np.pi
    psc[:, 6] = np.asarray(inp["nsa_ck_b1"])[0]
    psc[:, 7] = np.asarray(inp["nsa_cv_b1"])[0]
    psc[:, 8] = np.concatenate([b2k, b2k])
    psc[:, 9] = np.concatenate([b2k[rc], b2k[rc]])
    psc[:, 10:13] = np.asarray(inp["mla_q_norm"])[0].reshape(3, 128).T
    psc[:, 13:15] = np.asarray(inp["mla_kv_norm"])[0].reshape(2, 128).T
    psc[:, 15] = -np.pi
    wuq = np.asarray(inp["mla_w_uq"])[0]
    wuq_b = np.zeros_like(wuq)
    for h in range(8):
        pe = np.arange(h * 96 + 64, h * 96 + 96)
        wuq_b[:, pe] = wuq[:, _rot(pe, 16)]
    wukv = np.asarray(inp["mla_w_ukv"])[0]
    wuk = np.ascontiguousarray(np.concatenate([wukv[:, h * 128:h * 128 + 64] for h in range(8)], 1))
    wuv = np.ascontiguousarray(np.concatenate([wukv[:, h * 128 + 64:h * 128 + 128] for h in range(8)], 1))
    lngb = np.stack([np.asarray(inp[k])[0] for k in ("ln1_g", "ln1_b", "ln2_g", "ln2_b", "ln3_g", "ln3_b")]).astype(f)
    convw = np.ascontiguousarray(np.asarray(inp["ffn_conv_w"])[0].T.reshape(NCH, 128, 3).transpose(1, 0, 2).reshape(128, NCH * 3))
    convb = np.ascontiguousarray(np.asarray(inp["ffn_conv_b"])[0].reshape(NCH, 128).T)
    return {
        "wfm_nsa": wfm_nsa, "wtm_nsa": wtm_nsa, "wfm_mla": wfm_mla,
        "w1k": w1_layout(inp["nsa_ck_w1"]), "w1v": w1_layout(inp["nsa_cv_w1"]),
        "poskT": posT_layout(inp["nsa_k_pos"]), "posvT": posT_layout(inp["nsa_v_pos"]),
        "w2k": w2k_l, "w2v": np.ascontiguousarray(np.asarray(inp["nsa_cv_w2"])[0]),
        "b2v": np.ascontiguousarray(np.asarray(inp["nsa_cv_b2"])[0][None, :]),
        "cover": cover, "cand": cand.astype(f), "forced": forced.astype(f), "psc": psc,
        "wuq_a": np.ascontiguousarray(wuq), "wuq_b": wuq_b, "wuk": wuk, "wuv": wuv,
        "w_o": np.ascontiguousarray(np.asarray(inp["w_o"])[0]),
        "mem_wq": np.ascontiguousarray(np.asarray(inp["mem_wq"])[0]),
        "mem_wk": np.ascontiguousarray(np.asarray(inp["mem_wk"])[0]),
        "mem_wv": np.ascontiguousarray(np.asarray(inp["mem_wv"])[0]),
        "mem_wo": np.ascontiguousarray(np.asarray(inp["mem_wo"])[0]),
        "w_up": np.ascontiguousarray(np.asarray(inp["ffn_w_up"])[0]),
        "w_dn": np.ascontiguousarray(np.asarray(inp["ffn_w_down"])[0]),
        "lngb": lngb, "convw": convw, "convb": convb,
    }


def prep_core(inp, bi):
    x = np.asarray(inp["x"])[bi]
    pos = np.asarray(inp["positions"])[bi].astype(np.int32)
    posc = np.zeros((1, 256), np.int32)
    posc[0, :255] = pos[31::16][:255]
    return {
        "xT": np.ascontiguousarray(x.T), "x": np.ascontiguousarray(x),
        "memT": np.ascontiguousarray(np.asarray(inp["mem"])[bi].T),
        "pos": np.ascontiguousarray(pos[None, :]), "posc": posc,
    }


_CACHE = {}


def kernel(**inputs):
    if "nc" not in _CACHE:
        _CACHE["nc"] = build_program()[0]
    nc = _CACHE["nc"]
    shared = prep_shared(inputs)
    in_maps = []
    for bi in range(8):
        m = dict(shared)
        m.update(prep_core(inputs, bi))
        in_maps.append(m)
    res = run_bass_kernel_spmd(nc, in_maps, core_ids=list(range(8)))
    return np.stack([np.asarray(r["out"], dtype=np.float32) for r in res.results], 0)
```

```python
import numpy as np
from contextlib import ExitStack
import concourse.bass as bass
import concourse.mybir as mybir
from concourse.bass_utils import run_bass_kernel_spmd

F32 = mybir.dt.float32
BF16 = mybir.dt.bfloat16
I32 = mybir.dt.int32
AF = mybir.ActivationFunctionType
ALU = mybir.AluOpType

S_LEN = 4096
D = 1024
NT = 32
NB = 8
DFF = 2816
NCH = 22
ALPHA = 2.0 ** 0.25
PI = float(np.pi)
NEG = -30000.0

SEM_LIMIT = 20000
DMA_RING = 8
import os as _os
SAME_ENGINE_SYNC = set(_os.environ.get("KSES", "act,dve,pool").split(","))


class Sched:
    STREAMS = ("pe", "act", "dve", "pool", "sp")

    def __init__(self, nc):
        self.nc = nc
        self.ops = []
        self.last_writer = {}
        self.readers = {}
        self.out_dmas = []

    def op(self, stream, fn, reads=(), writes=(), dma=False, out=False):
        idx = len(self.ops)
        deps = set()
        reads = list(reads) + ["PHASE"]
        for r in reads:
            w = self.last_writer.get(r)
            if w is not None:
                deps.add(w)
        for w_ in writes:
            w = self.last_writer.get(w_)
            if w is not None:
                deps.add(w)
            for rd in self.readers.get(w_, ()):
                deps.add(rd)
        for r in reads:
            self.readers.setdefault(r, []).append(idx)
        for w_ in writes:
            self.last_writer[w_] = idx
            self.readers[w_] = []
        self.ops.append(dict(stream=stream, fn=fn, deps=deps, dma=dma))
        if out:
            self.out_dmas.append(idx)
        return idx

    def pe(self, fn, reads=(), writes=()):
        return self.op("pe", fn, reads, writes)

    def act(self, fn, reads=(), writes=()):
        return self.op("act", fn, reads, writes)

    def dve(self, fn, reads=(), writes=()):
        return self.op("dve", fn, reads, writes)

    def pool(self, fn, reads=(), writes=()):
        return self.op("pool", fn, reads, writes)

    def dma(self, out_ap, in_ap, reads=(), writes=(), q="sp", out=False):
        return self.op(q, lambda e: e.dma_start(out=out_ap, in_=in_ap), reads, writes, dma=True, out=out)

    def emit(self, es):
        nc = self.nc
        ops = self.ops
        ops.append(dict(stream="sp", fn=None, deps=set(self.out_dmas), dma=False))
        n = len(ops)
        dma_count = {s: 0 for s in self.STREAMS}
        dma_hist = {s: [] for s in self.STREAMS}
        for i, o in enumerate(ops):
            if o["dma"]:
                s = o["stream"]
                k = dma_count[s]
                o["dma_k"] = k
                if k >= DMA_RING:
                    o["deps"].add(dma_hist[s][k - DMA_RING])
                dma_hist[s].append(i)
                dma_count[s] += 1
        has_dep = [False] * n
        for i, o in enumerate(ops):
            for d in o["deps"]:
                od = ops[d]
                if od["dma"] or od["stream"] != o["stream"]:
                    has_dep[d] = True
                elif od["stream"] in SAME_ENGINE_SYNC:
                    has_dep[d] = True
        sem_cnt = [0]

        def new_sem(tag):
            sem_cnt[0] += 1
            return es.enter_context(nc.semaphore(f"s_{tag}_{sem_cnt[0]}"))

        cur_sem, cur_cnt, rings = {}, {}, {}
        for i, o in enumerate(ops):
            s = o["stream"]
            if o["dma"]:
                if s not in rings:
                    rings[s] = [new_sem(f"dq{s}") for _ in range(DMA_RING)]
                k = o["dma_k"]
                o["sig"] = (rings[s][k % DMA_RING], 16 * (k // DMA_RING + 1), 16)
            elif has_dep[i]:
                if s not in cur_sem or cur_cnt[s] >= SEM_LIMIT:
                    cur_sem[s] = new_sem(s)
                    cur_cnt[s] = 0
                cur_cnt[s] += 1
                o["sig"] = (cur_sem[s], cur_cnt[s], 1)
            else:
                o["sig"] = None
        waited = {s: {} for s in self.STREAMS}
        per_stream = {s: [] for s in self.STREAMS}
        for i, o in enumerate(ops):
            s = o["stream"]
            need = {}
            for d in o["deps"]:
                od = ops[d]
                if (not od["dma"]) and od["stream"] == s and (s not in SAME_ENGINE_SYNC):
                    continue
                sem, val, _ = od["sig"]
                key = id(sem)
                if waited[s].get(key, 0) >= val:
                    continue
                if key not in need or need[key][1] < val:
                    need[key] = (sem, val)
            for key, (sem, val) in need.items():
                waited[s][key] = val
            o["waits"] = list(need.values())
            per_stream[s].append(o)
        self.n_sems = sem_cnt[0]
        self.stream_sizes = {s: len(v) for s, v in per_stream.items()}
        block = es.enter_context(nc.Block())

        def run(stream_ops):
            def body(eng):
                for o in stream_ops:
                    for sem, val in o["waits"]:
                        eng.wait_ge(sem, val)
                    if o["fn"] is None:
                        continue
                    ins = o["fn"](eng)
                    if o["sig"] is not None:
                        ins.then_inc(o["sig"][0], o["sig"][2])
            return body

        block.tensor(run(per_stream["pe"]))
        block.scalar(run(per_stream["act"]))
        block.vector(run(per_stream["dve"]))
        block.gpsimd(run(per_stream["pool"]))
        block.sync(run(per_stream["sp"]))


def MM(out, lhsT, rhs, start, stop):
    return lambda e: e.matmul(out, lhsT=lhsT, rhs=rhs, start=start, stop=stop, skip_group_check=True)


def TR(out, in_, ident):
    return lambda e: e.transpose(out=out, in_=in_, identity=ident)


def ACT(out, in_, func, **kw):
    return lambda e: e.activation(out=out, in_=in_, func=func, **kw)


def TT(out, in0, in1, op):
    return lambda e: e.tensor_tensor(out=out, in0=in0, in1=in1, op=op)


def TS(out, in0, s1, s2, op0, op1=None):
    if op1 is None:
        return lambda e: e.tensor_scalar(out=out, in0=in0, scalar1=s1, scalar2=None, op0=op0)
    return lambda e: e.tensor_scalar(out=out, in0=in0, scalar1=s1, scalar2=s2, op0=op0, op1=op1)


def STT(out, in0, scalar, in1, op0, op1):
    return lambda e: e.scalar_tensor_tensor(out=out, in0=in0, scalar=scalar, in1=in1, op0=op0, op1=op1)


def CP(out, in_):
    return lambda e: e.tensor_copy(out=out, in_=in_)


def MEMSET(ap, v):
    return lambda e: e.memset(ap, v)


def RECIP(out, in_):
    return lambda e: e.reciprocal(out=out, in_=in_)


def ASEL(out, in_, pattern, op, fill, base, cm):
    return lambda e: e.affine_select(out=out, in_=in_, pattern=pattern, compare_op=op, fill=fill,
                                     base=base, channel_multiplier=cm)


DT_SIZE = {F32: 4, BF16: 2, I32: 4}
ARENA_BYTES = 204 * 1024


class Builder:
    def __init__(self, nc, es, dbg=()):
        self.nc = nc
        self.es = es
        self.S = Sched(nc)
        self.dbg = set(dbg)
        self.dbg_outs = {}
        self.arena = nc.alloc_sbuf_tensor("arena", [128, ARENA_BYTES // 4], F32).ap()
        self.top = 0
        self.banks = [es.enter_context(nc.psum_tensor(f"pb{i}", [128, 512], F32)).ap() for i in range(8)]
        self.bank_i = 0
        self.pinned = set()
        self.din = {}

    def inp(self, name, shape, dt=F32):
        ap = self.nc.dram_tensor(name, list(shape), dt, kind="ExternalInput").ap()
        self.din[name] = ap
        return ap

    def alloc(self, name, shape, dt=F32):
        n = int(np.prod(shape[1:]))
        nbytes = (n * DT_SIZE[dt] + 31) // 32 * 32
        off = self.top
        self.top += nbytes
        self.max_top = max(getattr(self, "max_top", 0), self.top)
        assert self.top <= ARENA_BYTES, f"arena overflow at {name}: {self.top}"
        ap = self.arena[0:shape[0], off // 4:(off + nbytes) // 4]
        if dt != F32:
            ap = ap.bitcast(dt)
        ap = ap[:, 0:n]
        if len(shape) == 3:
            ap = ap.rearrange("p (a b) -> p a b", b=shape[2])
        elif len(shape) == 4:
            ap = ap.rearrange("p (a b c) -> p a b c", b=shape[2], c=shape[3])
        return ap

    def ring(self, name, shape, dt, n):
        return [(self.alloc(f"{name}{i}", shape, dt), f"{name}{i}") for i in range(n)]

    def mark(self):
        return self.top

    def release(self, mark):
        scr = self.barrier_scr
        self.S.op("pool", MEMSET(scr, 0.0), reads=[], writes=["PHASE", "barrier_scr"])
        self.top = mark

    def bank(self, pin=False):
        while self.bank_i in self.pinned:
            self.bank_i = (self.bank_i + 1) % 8
        i = self.bank_i
        self.bank_i = (i + 1) % 8
        if pin:
            self.pinned.add(i)
        return self.banks[i], ("PB", i)

    def unpin(self, key):
        self.pinned.discard(key[1])

    def dump(self, name, ap, reads, dt=F32):
        if name not in self.dbg:
            return
        shape = list(ap.shape)
        d = self.nc.dram_tensor("dbg_" + name, shape, dt, kind="ExternalOutput").ap()
        self.dbg_outs[name] = shape
        self.S.dma(d, ap, reads=reads, out=True)


def build_program(dbg=(), stop_after=None):
    nc = bass.Bass("TRN2", target_bir_lowering=False)
    es = ExitStack()
    with es:
        b = Builder(nc, es, dbg)
        S = b.S
        xT = b.inp("xT", [D, S_LEN])
        x_in = b.inp("x", [S_LEN, D])
        memT = b.inp("memT", [D, 256])
        pos = b.inp("pos", [1, S_LEN], I32)
        posc = b.inp("posc", [1, 256], I32)
        wfm_nsa = b.inp("wfm_nsa", [D, 2304])
        wtm_nsa = b.inp("wtm_nsa", [D, 280])
        wfm_mla = b.inp("wfm_mla", [D, 896])
        w1k_d = b.inp("w1k", [128, 32 * 128])
        w1v_d = b.inp("w1v", [128, 32 * 128])
        poskT_d = b.inp("poskT", [128, 64])
        posvT_d = b.inp("posvT", [128, 64])
        w2k_d = b.inp("w2k", [128, 256])
        w2v_d = b.inp("w2v", [128, 64])
        b2v_d = b.inp("b2v", [1, 64])
        cover_d = b.inp("cover", [256, 64])
        cand_d = b.inp("cand", [S_LEN, 64])
        forced_d = b.inp("forced", [S_LEN, 64])
        psc_d = b.inp("psc", [128, 16])
        wuq_a_d = b.inp("wuq_a", [384, 768])
        wuq_b_d = b.inp("wuq_b", [384, 768])
        wuk_d = b.inp("wuk", [256, 512])
        wuv_d = b.inp("wuv", [256, 512])
        wo_d = b.inp("w_o", [D, D])
        mwq_d = b.inp("mem_wq", [D, D])
        mwk_d = b.inp("mem_wk", [D, D])
        mwv_d = b.inp("mem_wv", [D, D])
        mwo_d = b.inp("mem_wo", [D, D])
        wup_d = b.inp("w_up", [D, 2 * DFF])
        wdn_d = b.inp("w_dn", [DFF, D])
        lngb_d = b.inp("lngb", [6, D])
        convw_d = b.inp("convw", [128, NCH * 3])
        convb_d = b.inp("convb", [128, NCH])
        out_d = nc.dram_tensor("out", [S_LEN, D], F32, kind="ExternalOutput").ap()
        wo_s = nc.dram_tensor("wo_s", [D, D], BF16).ap()
        mwq_s = nc.dram_tensor("mwq_s", [D, D], BF16).ap()
        mwo_s = nc.dram_tensor("mwo_s", [D, D], BF16).ap()
        wup_s = nc.dram_tensor("wup_s", [D, 2 * DFF], BF16).ap()
        wdn_s = nc.dram_tensor("wdn_s", [DFF, D], BF16).ap()
        oT_s = nc.dram_tensor("oT_s", [D, S_LEN], BF16).ap()

        ident = b.alloc("ident", [128, 128], BF16)
        caus = b.alloc("caus", [128, 128], BF16)
        wedge = b.alloc("wedge", [128, 128], BF16)
        ones = b.alloc("ones", [128, 128], BF16)
        psc = b.alloc("psc", [128, 16], F32)
        b.barrier_scr = b.alloc("barrier_scr", [128, 8], F32)
        zeros = b.alloc("zeros", [128, 512], BF16)
        S.pool(MEMSET(zeros, 0.0), writes=["zeros"])
        S.pool(MEMSET(ones, 1.0), writes=["ones"])
        S.pool(MEMSET(ident, 1.0), writes=["ident"])
        S.pool(ASEL(ident, ident, [[1, 128]], ALU.is_equal, 0.0, 0, -1), reads=["ident"], writes=["ident"])
        S.pool(ASEL(caus, zeros[:, 0:128], [[1, 128]], ALU.is_ge, NEG, 0, -1), reads=["zeros"], writes=["caus"])
        S.pool(ASEL(wedge, zeros[:, 0:128], [[-1, 128]], ALU.is_gt, NEG, 0, 1), reads=["zeros"], writes=["wedge"])
        S.dma(psc, psc_d, writes=["psc"])
        for (dst, src, rows, key) in ((wo_s, wo_d, D, "wo_s"), (mwq_s, mwq_d, D, "mwq_s"), (mwo_s, mwo_d, D, "mwo_s"),
                                      (wup_s, wup_d, D, "wup_s"), (wdn_s, wdn_d, DFF, "wdn_s")):
            for r0 in range(0, rows, 128):
                S.dma(dst[r0:r0 + 128, :], src[r0:r0 + 128, :], writes=[(key, r0)], q="pool")
        base_mark = b.mark()

        PSC = lambda c: psc[:, c:c + 1]

        def load_xT_block(tb, ring_slot):
            xt, xk = ring_slot
            src = xT.rearrange("(kc p) t -> p kc t", p=128)
            for k0 in range(0, 8, 2):
                S.dma(xt[:, k0:k0 + 2, :], src[:, k0:k0 + 2, tb * 512:(tb + 1) * 512], writes=[(xk, k0)], q="pool")
            return xt, [(xk, k0) for k0 in range(0, 8, 2)]

        def rope_tables(pos_src, n, inv_c, sgn_c, nsg_c, tiles, prow=slice(0, 128)):
            (posi, kpi), (posf, kpf), (m1, km1), (m2, km2), (cos, kc), (sin, ks) = tiles
            S.dma(posi[:, 0:n], pos_src.partition_broadcast(128), writes=[kpi])
            S.dve(CP(posf[:, 0:n], posi[:, 0:n]), reads=[kpi], writes=[kpf])
            C1_, C2_ = 6.28125, 2 * PI - 6.28125
            S.dve(TS(m1[:, 0:n], posf[:, 0:n], PSC(inv_c), None, ALU.mult), reads=[kpf, "psc"], writes=[km1])
            S.dve(TS(m2[:, 0:n], posf[:, 0:n], PSC(inv_c), 0.5 * PI, ALU.mult, ALU.add), reads=[kpf, "psc"], writes=[km2])
            S.dve(TS(sin[:, 0:n], m1[:, 0:n], 1.0 / (2 * PI), None, ALU.mult), reads=[km1], writes=[ks])
            S.dve(CP(posi[:, 0:n], sin[:, 0:n]), reads=[ks], writes=[kpi])
            S.dve(CP(cos[:, 0:n], posi[:, 0:n]), reads=[kpi], writes=[kc])
            S.dve(STT(m1[:, 0:n], cos[:, 0:n], -C1_, m1[:, 0:n], ALU.mult, ALU.add), reads=[kc, km1], writes=[km1])
            S.dve(STT(m1[:, 0:n], cos[:, 0:n], -C2_, m1[:, 0:n], ALU.mult, ALU.add), reads=[kc, km1], writes=[km1])
            S.dve(TS(m1[:, 0:n], m1[:, 0:n], PI, -PI, ALU.min, ALU.max), reads=[km1], writes=[km1])
            S.act(ACT(sin[:, 0:n], m1[:, 0:n], AF.Sin, scale=PSC(sgn_c)), reads=[km1, "psc"], writes=[ks])
            S.dve(TS(cos[:, 0:n], m2[:, 0:n], 1.0 / (2 * PI), None, ALU.mult), reads=[km2], writes=[kc])
            S.dve(CP(posi[:, 0:n], cos[:, 0:n]), reads=[kc], writes=[kpi])
            S.dve(CP(posf[:, 0:n], posi[:, 0:n]), reads=[kpi], writes=[kpf])
            S.dve(STT(m2[:, 0:n], posf[:, 0:n], -C1_, m2[:, 0:n], ALU.mult, ALU.add), reads=[kpf, km2], writes=[km2])
            S.dve(STT(m2[:, 0:n], posf[:, 0:n], -C2_, m2[:, 0:n], ALU.mult, ALU.add), reads=[kpf, km2], writes=[km2])
            S.dve(TS(m2[:, 0:n], m2[:, 0:n], PI, -PI, ALU.min, ALU.max), reads=[km2], writes=[km2])
            S.act(ACT(cos[:, 0:n], m2[:, 0:n], AF.Sin), reads=[km2], writes=[kc])

        def proj_fm(xt, xkeys, w, wkey, col0, ncols=128):
            pb, pk = b.bank()
            for kc in range(8):
                S.pe(MM(pb[0:ncols, :], w[:, kc, col0:col0 + ncols], xt[:, kc, :], kc == 0, kc == 7),
                     reads=[wkey] + xkeys, writes=[pk])
            return pb, pk

        QT = b.alloc("QT", [128, 4, S_LEN], BF16)
        KS = b.alloc("KS", [128, 2, S_LEN], BF16)
        KW = b.alloc("KW", [128, 2, S_LEN], BF16)
        VS = b.alloc("VS", [128, NT, 2, 65], BF16)
        VW = b.alloc("VW", [128, NT, 2, 65], BF16)
        GATES = b.alloc("GATES", [128, NT, 24], F32)
        KCMP = b.alloc("KCMP", [128, 2, 256], BF16)
        CV = b.alloc("CV", [128, 2, 2, 129], BF16)
        nsa_state_mark = b.mark()
        KC = b.alloc("KC", [128, S_LEN], BF16)
        VC = b.alloc("VC", [128, S_LEN], BF16)
        cmp_mark = b.mark()
        WFM = b.alloc("WFM", [128, 8, 2304], BF16)
        WTM = b.alloc("WTM", [128, 8, 280], BF16)
        XTR = b.ring("XT", [128, 8, 512], BF16, 2)
        tabs = [b.ring(nm, [128, 512], dt_, 2 if nm in ("cos", "sin") else 1) for nm, dt_ in
                (("posi", I32), ("posf", F32), ("m1", F32), ("m2", F32), ("cos", F32), ("sin", F32))]
        T1 = b.ring("T1", [128, 512], F32, 2)
        T2 = b.ring("T2", [128, 512], F32, 2)

        S.pool(MEMSET(VS, 1.0), writes=["VS"])
        S.pool(MEMSET(VW, 1.0), writes=["VW"])
        for kc in range(8):
            S.dma(WFM[:, kc, :], wfm_nsa[kc * 128:(kc + 1) * 128, :], writes=["WFM"], q="pool")
        S.dma(WTM, wtm_nsa.rearrange("(kc p) n -> p kc n", p=128), writes=["WTM"], q="pool")

        rope_groups = [(QT[:, hp, :], hp, 4 + hp) for hp in range(4)] + \
                      [(KS[:, g, :], 8 + g, 10 + g) for g in range(2)] + \
                      [(KW[:, g, :], 12 + g, 14 + g) for g in range(2)]
        for tb in range(NB):
            xt, xkeys = load_xT_block(tb, XTR[tb % 2])
            tl = [t[tb % len(t)] for t in tabs]
            rope_tables(pos[:, tb * 512:(tb + 1) * 512], 512, 0, 1, 2, tl)
            cos, kcos = tl[4]
            sin, ksin = tl[5]
            tsl = slice(tb * 512, (tb + 1) * 512)
            for gi, (dst, ga, gb) in enumerate(rope_groups):
                pa, pka = proj_fm(xt, xkeys, WFM, "WFM", ga * 128)
                pb_, pkb = proj_fm(xt, xkeys, WFM, "WFM", gb * 128)
                t1, k1 = T1[gi % 2]
                t2, k2 = T2[gi % 2]
                S.dve(TT(t1, pa, cos, ALU.mult), reads=[pka, kcos], writes=[k1])
                S.dve(TT(t2, pb_, sin, ALU.mult), reads=[pkb, ksin], writes=[k2])
                S.pool(TT(dst[:, tsl], t1, t2, ALU.add), reads=[k1, k2], writes=[("ropeout", gi, tb)])
            for dst, gidx, nm in ((KC, 16, "KC"), (VC, 17, "VC")):
                pa, pka = proj_fm(xt, xkeys, WFM, "WFM", gidx * 128)
                S.act(ACT(dst[:, tsl], pa, AF.Copy), reads=[pka], writes=[(nm, tb)])
            for tt in range(4):
                kt = tb * 4 + tt
                pb_, pk = b.bank()
                for kc in range(8):
                    S.pe(MM(pb_[:, 0:280], xt[:, kc, tt * 128:(tt + 1) * 128], WTM[:, kc, :], kc == 0, kc == 7),
                         reads=["WTM"] + xkeys, writes=[pk])
                S.act(ACT(VS[:, kt, :, 0:64], pb_[:, 0:128].rearrange("p (g d) -> p g d", d=64), AF.Copy),
                      reads=[pk, "VS"], writes=[("VS", kt)])
                S.act(ACT(VW[:, kt, :, 0:64], pb_[:, 128:256].rearrange("p (g d) -> p g d", d=64), AF.Copy),
                      reads=[pk, "VW"], writes=[("VW", kt)])
                S.act(ACT(GATES[:, kt, :], pb_[:, 256:280], AF.Sigmoid), reads=[pk], writes=[("GATES", kt)])
        QTK = [("ropeout", gi, tb) for gi in range(8) for tb in range(NB)]
        b.dump("QT", QT[:, 0, :], QTK, BF16)
        b.dump("KS", KS[:, 1, :], QTK, BF16)
        b.dump("GATES", GATES, [("GATES", kt) for kt in range(NT)])
        b.dump("VS", VS, [("VS", kt) for kt in range(NT)], BF16)
        if stop_after == "1a":
            S.emit(es)
            return nc, b
        b.release(cmp_mark)

        W1 = b.alloc("W1", [128, 32, 128], BF16)
        POST = b.alloc("POST", [128, 32, 2], BF16)
        W2K = b.alloc("W2K", [128, 256], BF16)
        W2V = b.alloc("W2V", [128, 64], BF16)
        B2V = b.alloc("B2V", [128, 64], F32)
        C1 = b.alloc("C1", [128, 2], F32)
        U = b.alloc("U", [128, 256], F32)
        U2 = b.alloc("U2", [128, 256], F32)
        U3 = b.alloc("U3", [128, 256], F32)
        GT = b.alloc("GT", [128, 256], BF16)
        ctab = [b.ring(nm, [128, 512], dt_, 1) for nm, dt_ in
                (("cposi", I32), ("cposf", F32), ("cm1", F32), ("cm2", F32), ("ccos", F32), ("csin", F32))]
        ctl = [t[0] for t in ctab]
        rope_tables(posc, 256, 0, 1, 2, ctl)
        ccos, kccos = ctl[4]
        csin, kcsin = ctl[5]
        S.dma(W2K, w2k_d, writes=["W2K"], q="pool")
        S.dma(W2V, w2v_d, writes=["W2V"], q="pool")
        S.dma(B2V, b2v_d.partition_broadcast(128), writes=["B2V"])
        S.dma(CV[:, 0, 0, 0:64], cover_d[0:128, :], reads=[], writes=[("CVc", 0)], q="pool")
        S.dma(CV[:, 1, 0, 0:64], cover_d[128:256, :], reads=[], writes=[("CVc", 1)], q="pool")
        S.pool(MEMSET(GT, 0.0), writes=["GT"])
        for which in range(2):
            src = KC if which == 0 else VC
            S.dma(W1, (w1k_d if which == 0 else w1v_d).rearrange("p (l h) -> p l h", h=128), writes=["W1"], q="pool")
            S.dma(POST, (poskT_d if which == 0 else posvT_d).rearrange("p (l t) -> p l t", t=2), writes=["POST"], q="pool")
            pc, pkc = b.bank()
            for l in range(32):
                S.pe(MM(pc[:, 0:2], W1[0:64, l, :], POST[0:64, l, :], l == 0, l == 31), reads=["W1", "POST"], writes=[pkc])
            S.dve(TS(C1, pc[:, 0:2], PSC(6 + which), None, ALU.add), reads=[pkc, "psc"], writes=["C1"])
            for g in range(2):
                rows = slice(g * 64, (g + 1) * 64)
                ph, pkh = b.bank()
                for l in range(32):
                    S.pe(MM(ph[:, 0:255], W1[rows, l, :], src[rows, l:l + 16 * 254 + 1:16], l == 0, l == 31),
                         reads=["W1"] + [("KC" if which == 0 else "VC", tb) for tb in range(NB)], writes=[pkh])
                S.act(ACT(U[:, 0:255], ph[:, 0:255], AF.Identity, bias=C1[:, 0:1], scale=1.0), reads=[pkh, "C1"], writes=["U"])
                S.dve(TT(U2[:, 0:255], U[:, 0:255], U[:, 0:255], ALU.mult), reads=["U"], writes=["U2"])
                S.dve(TS(U2[:, 0:255], U2[:, 0:255], 0.044715, 1.0, ALU.mult, ALU.add), reads=["U2"], writes=["U2"])
                S.dve(TT(U2[:, 0:255], U2[:, 0:255], U[:, 0:255], ALU.mult), reads=["U2", "U"], writes=["U2"])
                S.act(ACT(U3[:, 0:255], U2[:, 0:255], AF.Tanh, scale=0.7978845608028654), reads=["U2"], writes=["U3"])
                S.dve(TS(U3[:, 0:255], U3[:, 0:255], 0.5, 0.5, ALU.mult, ALU.add), reads=["U3"], writes=["U3"])
                S.dve(TT(GT[:, 0:255], U3[:, 0:255], U[:, 0:255], ALU.mult), reads=["U3", "U", "GT"], writes=["GT"])
                if which == 0:
                    pa, pka = b.bank()
                    pb_, pkb = b.bank()
                    S.pe(MM(pa[:, 0:256], W2K[:, 0:128], GT, True, True), reads=["W2K", "GT"], writes=[pka])
                    S.pe(MM(pb_[:, 0:256], W2K[:, 128:256], GT, True, True), reads=["W2K", "GT"], writes=[pkb])
                    S.dve(STT(U[:, 0:256], pa[:, 0:256], PSC(8), ccos[:, 0:256], ALU.add, ALU.mult),
                          reads=[pka, kccos, "psc", "U"], writes=["U"])
                    S.dve(STT(U2[:, 0:256], pb_[:, 0:256], PSC(9), csin[:, 0:256], ALU.add, ALU.mult),
                          reads=[pkb, kcsin, "psc", "U2"], writes=["U2"])
                    S.pool(TT(KCMP[:, g, :], U[:, 0:256], U2[:, 0:256], ALU.add), reads=["U", "U2"], writes=[("KCMP", g)])
                else:
                    for nt in range(2):
                        pv, pkv = b.bank()
                        S.pe(MM(pv[:, 0:64], GT[:, nt * 128:(nt + 1) * 128], W2V, True, True), reads=["GT", "W2V"], writes=[pkv])
                        S.dve(TT(CV[:, nt, g, 65:129], pv[:, 0:64], B2V, ALU.add), reads=[pkv, "B2V"], writes=[("CVv", nt, g)])
        for nt in range(2):
            S.pool(CP(CV[:, nt, 1, 0:64], CV[:, nt, 0, 0:64]), reads=[("CVc", nt)], writes=[("CVc2", nt)])
            S.pool(MEMSET(CV[:, nt, :, 64:65], 1.0), writes=[("CVo", nt)])
        CVK = [("CVc", nt) for nt in range(2)] + [("CVc2", nt) for nt in range(2)] + [("CVo", nt) for nt in range(2)] + \
              [("CVv", nt, g) for nt in range(2) for g in range(2)]
        b.dump("KCMP", KCMP, [("KCMP", 0), ("KCMP", 1)], BF16)
        b.dump("CV", CV, CVK, BF16)
        if stop_after == "1ap":
            S.emit(es)
            return nc, b
        b.release(nsa_state_mark)

        ETAB = b.alloc("ETAB", [128, 32, 128], BF16)
        S.pool(MEMSET(ETAB, 1.0), writes=["ETAB"])
        S.pool(ASEL(ETAB, ETAB, [[-2, 32], [-1, 2], [0, 64]], ALU.is_equal, 0.0, 0, 1), reads=["ETAB"], writes=["ETAB"])
        MASKC = b.ring("MASKC", [128, 2, 512], BF16, 2)
        PT = b.ring("PT", [128, 512], BF16, 6)
        ONSA = b.alloc("ONSA", [128, 4, 512], F32)
        ONSAB = b.alloc("ONSAB", [128, 4, 512], BF16)
        OTB = b.ring("OTB", [128, 4, 512], BF16, 2)
        IMP = b.alloc("IMP", [128, 2, 4, 64], F32)
        CANDT = b.ring("CANDT", [128, 4, 64], F32, 2)
        FORCT = b.ring("FORCT", [128, 4, 64], F32, 2)
        SM = b.ring("SM", [128, 16], F32, 4)
        IMPM = b.alloc("IMPM", [128, 64], F32)
        SCR = b.alloc("SCR", [128, 64], F32)
        M8 = b.alloc("M8", [128, 16], F32)
        SEL = b.alloc("SEL", [128, 64], F32)
        NEGB8 = b.alloc("NEGB8", [128, 8, 64], BF16)
        NEGT = b.alloc("NEGT", [128, 2, 512], BF16)
        S.pool(MEMSET(NEGT, 0.0), writes=[("NEGT", 0), ("NEGT", 1)])
        pt_i = [0]
        sm_i = [0]
        KSX = [b.alloc("KSL", [128, 2, S_LEN], BF16), b.alloc("KSH", [128, 2, S_LEN], BF16)]
        KWX = [b.alloc("KWL", [128, 2, S_LEN], BF16), b.alloc("KWH", [128, 2, S_LEN], BF16)]
        KCX = [b.alloc("KCL", [128, 2, 256], BF16), b.alloc("KCH", [128, 2, 256], BF16)]
        for half in range(2):
            rws = slice(half * 64, half * 64 + 64)
            for dst, src, nm in ((KSX[half], KS, "KSX"), (KWX[half], KW, "KWX"), (KCX[half], KCMP, "KCX")):
                S.pool(MEMSET(dst, 0.0), writes=[(nm, half)])
                if nm == "KSX":
                    S.act(ACT(dst[rws], src[rws], AF.Copy), reads=[(nm, half)], writes=[(nm, half)])
                else:
                    S.dve(CP(dst[rws], src[rws]), reads=[(nm, half)], writes=[(nm, half)])

        ATT = {"staged": []}
        DEPTH = 3
        NPT = 6

        def _emit_pv(item):
            (kt, jl, jh, pt, pkt, accs, nper, Vfn, v_keys, last_kt, is_last, post_fn) = item
            for j in range(jl, jh + 1):
                acc, pka, j0 = accs[j // nper]
                S.pe(MM(acc[:, j - j0, :], pt[:, j * 128:(j + 1) * 128], Vfn(kt), False, last_kt[j] == kt),
                     reads=[pkt] + v_keys, writes=[pka])
            if is_last:
                for (_a, pka, _j) in accs:
                    b.unpin(pka)
                post_fn(accs)

        def att_flush():
            while ATT["staged"]:
                _emit_pv(ATT["staged"].pop(0))

        def attention(QTa, KTa, Vfn, tiles, qb, scale, ncols, q_keys, k_keys, v_keys, post_fn, blockmask=None):
            nper = min(4, 512 // ncols)
            accs = []
            for j0 in range(0, 4, nper):
                pa, pka = b.bank(pin=True)
                S.pe(MM(pa[:, 0:nper * ncols], zeros[:, 0:128], zeros[:, 0:nper * ncols], True, True), reads=["zeros"], writes=[pka])
                accs.append((pa[:, 0:nper * ncols].rearrange("p (j c) -> p j c", c=ncols), pka, j0))
            last_kt = {}
            for (kt, jl, jh, masks) in tiles:
                for j in range(jl, jh + 1):
                    last_kt[j] = kt
            for ti, (kt, jl, jh, masks) in enumerate(tiles):
                c0, c1 = jl * 128, (jh + 1) * 128
                ps, pks = b.bank()
                nmm = 1 + (1 if blockmask is not None else 0) + len(masks)
                done = 1
                S.pe(MM(ps[:, c0:c1], KTa[:, kt * 128:(kt + 1) * 128], QTa[:, qb * 512 + c0:qb * 512 + c1], True, done == nmm),
                     reads=q_keys + k_keys, writes=[pks])
                if blockmask is not None:
                    done += 1
                    negt, nkey = blockmask
                    S.pe(MM(ps[:, c0:c1], ETAB[:, kt, :], negt[:, c0:c1], False, done == nmm),
                         reads=["ETAB", nkey], writes=[pks])
                for (kind, j) in masks:
                    done += 1
                    S.pe(MM(ps[:, j * 128:(j + 1) * 128], ident, caus if kind == "c" else wedge, False, done == nmm),
                         reads=["ident", "caus", "wedge"], writes=[pks])
                pt, pkt = PT[pt_i[0] % NPT]
                pt_i[0] += 1
                S.act(ACT(pt[:, c0:c1], ps[:, c0:c1], AF.Exp, scale=scale), reads=[pks], writes=[pkt])
                ATT["staged"].append((kt, jl, jh, pt, pkt, accs, nper, Vfn, v_keys, last_kt, ti == len(tiles) - 1, post_fn))
                while len(ATT["staged"]) > DEPTH:
                    _emit_pv(ATT["staged"].pop(0))

        def recip_sums(accs, ncols, sumcol):
            sm, smk = SM[sm_i[0] % 4]
            sm_i[0] += 1
            for (acc, pka, j0) in accs:
                nper = acc.shape[1]
                S.dve(TS(sm[:, j0:j0 + nper], acc[:, :, sumcol], 1e-30, None, ALU.max), reads=[pka], writes=[smk])
            S.dve(RECIP(sm[:, 4:8], sm[:, 0:4]), reads=[smk], writes=[smk])
            return sm, smk

        ALLQ = []
        for qb in range(NB):
            mk, mkk = MASKC[qb % 2]
            for nt in range(2):
                S.pool(ASEL(mk[:, nt, :], zeros, [[1, 512]], ALU.is_ge, NEG, qb * 512 - 2048 * nt - 31, -16),
                       reads=["zeros"], writes=[(mkk, nt)])
            cand, candk = CANDT[qb % 2]
            forc, forck = FORCT[qb % 2]
            S.dma(cand, cand_d[qb * 512:(qb + 1) * 512, :].rearrange("(j p) c -> p j c", p=128), writes=[candk])
            S.dma(forc, forced_d[qb * 512:(qb + 1) * 512, :].rearrange("(j p) c -> p j c", p=128), writes=[forck])
            if stop_after == "2a_tab":
                b.dump("ETAB", ETAB, ["ETAB"], BF16)
                b.dump("MASKC", mk, [(mkk, 0), (mkk, 1)], BF16)
                b.dump("CAND", cand, [candk])
                S.emit(es)
                return nc, b
            gview = lambda br, h: GATES[:, qb * 4:(qb + 1) * 4, h * 3 + br]
            gkeys = [("GATES", kt) for kt in range(qb * 4, qb * 4 + 4)]
            def cmp_scores(h):
                g, hp, half = h // 4, h // 2, h % 2
                ets = []
                for nt in (range(2) if qb >= 4 else range(1)):
                    ps, pks = b.bank()
                    S.pe(MM(ps, KCX[half][:, g, nt * 128:(nt + 1) * 128], QT[:, hp, qb * 512:(qb + 1) * 512], True, False),
                         reads=[("KCX", half)], writes=[pks])
                    S.pe(MM(ps, ident, mk[:, nt, :], False, True), reads=["ident", (mkk, nt)], writes=[pks])
                    pt, pkt = PT[pt_i[0] % 6]
                    pt_i[0] += 1
                    S.act(ACT(pt, ps, AF.Exp, scale=0.125), reads=[pks], writes=[pkt])
                    ets.append((pt, pkt, nt))
                return ets

            def cmp_pv(h, ets):
                g = h // 4
                accs = []
                for j0 in (0, 2):
                    pa, pka = b.bank()
                    acc = pa[:, 0:258].rearrange("p (j c) -> p j c", c=129)
                    for j in (j0, j0 + 1):
                        for ei, (pt_, pkt_, nt) in enumerate(ets):
                            S.pe(MM(acc[:, j - j0, :], pt_[:, j * 128:(j + 1) * 128], CV[:, nt, g, :], ei == 0, ei == len(ets) - 1),
                                 reads=[pkt_], writes=[pka])
                    accs.append((acc, pka, j0))
                sm, smk = recip_sums(accs, 129, 64)
                S.dve(TT(sm[:, 8:12], sm[:, 4:8], gview(0, h), ALU.mult), reads=[smk] + gkeys, writes=[smk])
                for (acc, pka, j0) in accs:
                    for j in (j0, j0 + 1):
                        S.dve(TS(ONSA[:, j, h * 64:(h + 1) * 64], acc[:, j - j0, 65:129], sm[:, 8 + j:9 + j], None, ALU.mult),
                              reads=[pka, smk], writes=[("ONSA", h)])
                        if qb < 2:
                            pass
                        elif h % 4 == 0:
                            S.dve(TS(IMP[:, g, j, :], acc[:, j - j0, 0:64], sm[:, 4 + j:5 + j], None, ALU.mult),
                                  reads=[pka, smk], writes=[("IMP", g)])
                        else:
                            S.dve(STT(IMP[:, g, j, :], acc[:, j - j0, 0:64], sm[:, 4 + j:5 + j], IMP[:, g, j, :], ALU.mult, ALU.add),
                                  reads=[pka, smk, ("IMP", g)], writes=[("IMP", g)])

            ets_cur = cmp_scores(0)
            for h in range(8):
                ets_next = cmp_scores(h + 1) if h + 1 < 8 else None
                cmp_pv(h, ets_cur)
                ets_cur = ets_next
            if stop_after == "2a_cmp":
                b.dump("IMP", IMP, [("IMP", 0), ("IMP", 1)])
                b.dump("ONSA_c", ONSA, [("ONSA", h) for h in range(8)])
                S.emit(es)
                return nc, b
            if qb == 1:
                b.dump("IMP", IMP, [("IMP", 0), ("IMP", 1)])
                b.dump("ONSA_c", ONSA, [("ONSA", h) for h in range(8)])
            def sel_dve():
                for g in range(2):
                    for j in range(4):
                        negb = NEGB8[:, g * 4 + j, :]
                        S.dve(TT(IMPM, IMP[:, g, j, :], cand[:, j, :], ALU.mult), reads=[("IMP", g), candk], writes=["IMPM"])
                        S.dve(lambda e: e.max(out=M8[:, 0:8], in_=IMPM), reads=["IMPM"], writes=["M8a"])
                        S.dve(lambda e: e.match_replace(out=SCR, in_to_replace=M8[:, 0:8], in_values=IMPM, imm_value=-1.0),
                              reads=["IMPM", "M8a"], writes=["SCR"])
                        S.dve(lambda e: e.max(out=M8[:, 8:16], in_=SCR), reads=["SCR"], writes=["M8b"])
                        S.dve(TS(SEL, IMPM, M8[:, 12:13], None, ALU.is_ge), reads=["IMPM", "M8b"], writes=["SEL"])
                        S.dve(TT(SEL, SEL, cand[:, j, :], ALU.mult), reads=["SEL", candk], writes=["SEL"])
                        S.dve(TT(SEL, SEL, forc[:, j, :], ALU.add), reads=["SEL", forck], writes=["SEL"])
                        S.dve(TS(negb, SEL, -1.0, -NEG, ALU.add, ALU.mult), reads=["SEL"], writes=[("NEGB", g, j)])

            def sel_pe():
                for g in range(2):
                    for j in range(4):
                        negb = NEGB8[:, g * 4 + j, :]
                        pb_, pk = b.bank()
                        ptr = pb_.bitcast(BF16)
                        S.pe(TR(ptr[0:64, 0:128], negb, ident), reads=[("NEGB", g, j), "ident"], writes=[pk])
                        S.act(ACT(NEGT[0:64, g, j * 128:(j + 1) * 128], ptr[0:64, 0:128], AF.Copy), reads=[pk], writes=[("NEGT", g)])

            if qb >= 2:
                sel_dve()
            if stop_after == "2a_sel":
                b.dump("NEGT", NEGT, [("NEGT", 0), ("NEGT", 1)], BF16)
                S.emit(es)
                return nc, b
            if qb == 3:
                b.dump("NEGT", NEGT, [("NEGT", 0), ("NEGT", 1)], BF16)
            for br in (2, 1):
                if br == 1 and qb >= 2:
                    sel_pe()
                for h in range(8):
                    g, hp, half = h // 4, h // 2, h % 2
                    QTa = QT[:, hp, :]
                    if br == 1:
                        KTa = KSX[half][:, g, :]
                        kkeys = [("KSX", half)]
                        Vt = VS
                        tiles = []
                        for kt in range(0, 4 * qb + 4):
                            jl = max(kt - 4 * qb, 0)
                            masks = [("c", kt - 4 * qb)] if kt >= 4 * qb else []
                            tiles.append((kt, jl, 3, masks))
                        bm = (NEGT[:, g, :], ("NEGT", g)) if qb >= 2 else None
                    else:
                        KTa = KWX[half][:, g, :]
                        kkeys = [("KWX", half)]
                        Vt = VW
                        tiles = []
                        for kt in range(max(0, 4 * qb - 4), 4 * qb + 4):
                            jl = max(kt - 4 * qb, 0)
                            jh = min(kt + 4 - 4 * qb, 3)
                            masks = []
                            if kt >= 4 * qb:
                                masks.append(("c", kt - 4 * qb))
                            if 0 <= kt + 4 - 4 * qb <= 3:
                                masks.append(("w", kt + 4 - 4 * qb))
                            tiles.append((kt, jl, jh, masks))
                        bm = None

                    def post_nsa(accs, h=h, br=br, qb=qb):
                        sm, smk = recip_sums(accs, 65, 64)
                        S.dve(TT(sm[:, 8:12], sm[:, 4:8], GATES[:, qb * 4:(qb + 1) * 4, h * 3 + br], ALU.mult),
                              reads=[smk], writes=[smk])
                        acc, pka, _ = accs[0]
                        for j in range(4):
                            dst = ONSA[:, j, h * 64:(h + 1) * 64]
                            S.dve(STT(dst, acc[:, j, 0:64], sm[:, 8 + j:9 + j], dst, ALU.mult, ALU.add),
                                  reads=[pka, smk, ("ONSA", h)], writes=[("ONSA", h)])

                    attention(QTa, KTa, lambda kt, Vt=Vt, g=g: Vt[:, kt, g, :], tiles, qb, 0.125, 65,
                              [], kkeys, [], post_nsa, blockmask=bm)
            att_flush()
            if qb == 3:
                b.dump("ONSA", ONSA, [("ONSA", h) for h in range(8)])
            S.act(ACT(ONSAB, ONSA, AF.Copy), reads=[("ONSA", h) for h in range(8)], writes=["ONSAB"])
            otb, otk = OTB[qb % 2]
            for fc in range(4):
                pb_, pk = b.bank()
                ptr = pb_.bitcast(BF16)
                for j in range(4):
                    S.pe(TR(ptr[:, j * 128:(j + 1) * 128], ONSAB[:, j, fc * 128:(fc + 1) * 128], ident),
                         reads=["ONSAB", "ident"], writes=[pk])
                S.dve(CP(otb[:, fc, :], ptr[:, 0:512]), reads=[pk], writes=[(otk, fc)])
            S.dma(oT_s[0:512, qb * 512:(qb + 1) * 512].rearrange("(fc p) t -> p fc t", p=128), otb,
                  reads=[(otk, fc) for fc in range(4)], writes=[("oT_s", 0, qb)])
            if stop_after == "2a_qb0":
                b.dump("ONSA0", ONSA, [("ONSA", h) for h in range(8)])
                S.emit(es)
                return nc, b
        if stop_after == "2a":
            S.emit(es)
            return nc, b
        b.release(base_mark)

        QM = b.alloc("QM", [128, 8, S_LEN], BF16)
        NMKV = b.alloc("NMKV", [128, 2, S_LEN], BF16)
        KPE = b.alloc("KPE", [128, S_LEN], BF16)
        S.pool(MEMSET(QM[96:128], 0.0), writes=["QMpad"])
        mla_mark = b.mark()
        WFM2 = b.alloc("WFM2", [128, 8, 896], BF16)
        WUQA = b.alloc("WUQA", [128, 3, 768], BF16)
        WUQB = b.alloc("WUQB", [128, 3, 768], BF16)
        XTR = b.ring("XTb", [128, 8, 512], BF16, 2)
        tabs = [b.ring(nm, [128, 512], dt_, 2) for nm, dt_ in
                (("bposi", I32), ("bposf", F32), ("bm1", F32), ("bm2", F32), ("bcos", F32), ("bsin", F32))]
        T1 = b.ring("bT1", [128, 512], F32, 2)
        T2 = b.ring("bT2", [128, 512], F32, 2)
        SQ = b.ring("SQ", [128, 512], BF16, 3)
        RR = b.ring("RR", [128, 512], F32, 2)
        NMQ = b.ring("NMQ", [128, 3, 512], BF16, 2)
        for kc in range(8):
            S.dma(WFM2[:, kc, :], wfm_mla[kc * 128:(kc + 1) * 128, :], writes=["WFM2"], q="pool")
        S.dma(WUQA, wuq_a_d.rearrange("(kc p) n -> p kc n", p=128), writes=["WUQA"], q="pool")
        S.dma(WUQB, wuq_b_d.rearrange("(kc p) n -> p kc n", p=128), writes=["WUQB"], q="pool")
        pe_rows = slice(64, 96)
        for tb in range(NB):
            xt, xkeys = load_xT_block(tb, XTR[tb % 2])
            tl = [t[tb % len(t)] for t in tabs]
            rope_tables(pos[:, tb * 512:(tb + 1) * 512], 512, 3, 4, 5, tl)
            cos, kcos = tl[4]
            sin, ksin = tl[5]
            tsl = slice(tb * 512, (tb + 1) * 512)

            def rmsnorm_group(g0, nchunks, width, gcol, dst_fn, dkey):
                pbs = [proj_fm(xt, xkeys, WFM2, "WFM2", (g0 + c) * 128) for c in range(nchunks)]
                pss, pkss = b.bank()
                for c in range(nchunks):
                    sq, sqk = SQ[c]
                    S.act(ACT(sq, pbs[c][0], AF.Square), reads=[pbs[c][1]], writes=[sqk])
                    S.pe(MM(pss, ones, sq, c == 0, c == nchunks - 1), reads=["ones", sqk], writes=[pkss])
                rr, rrk = RR[0]
                r2, r2k = RR[1]
                S.act(ACT(rr, pss, AF.Sqrt, scale=1.0 / width, bias=1e-6), reads=[pkss], writes=[rrk])
                S.dve(RECIP(r2, rr), reads=[rrk], writes=[r2k])
                for c in range(nchunks):
                    S.dve(STT(dst_fn(c), pbs[c][0], PSC(gcol + c), r2, ALU.mult, ALU.mult),
                          reads=[pbs[c][1], r2k, "psc"], writes=[(dkey, c)])

            nmq, nmqk = NMQ[tb % 2]
            rmsnorm_group(0, 3, 384.0, 10, lambda c: nmq[:, c, :], nmqk)
            rmsnorm_group(3, 2, 256.0, 13, lambda c: NMKV[:, c, tsl], ("NMKV", tb))
            pa, pka = proj_fm(xt, xkeys, WFM2, "WFM2", 5 * 128)
            pb_, pkb = proj_fm(xt, xkeys, WFM2, "WFM2", 6 * 128)
            t1, k1 = T1[0]
            t2, k2 = T2[0]
            S.dve(TT(t1[pe_rows], pa[pe_rows], cos[pe_rows], ALU.mult), reads=[pka, kcos], writes=[k1])
            S.dve(TT(t2[pe_rows], pb_[pe_rows], sin[pe_rows], ALU.mult), reads=[pkb, ksin], writes=[k2])
            S.pool(TT(KPE[pe_rows, tsl], t1[pe_rows], t2[pe_rows], ALU.add), reads=[k1, k2], writes=[("KPE", tb)])
            for h in range(8):
                pa, pka = b.bank()
                pb_, pkb = b.bank()
                for c in range(3):
                    S.pe(MM(pa[0:96, :], WUQA[:, c, h * 96:(h + 1) * 96], nmq[:, c, :], c == 0, c == 2),
                         reads=["WUQA"] + [(nmqk, cc) for cc in range(3)], writes=[pka])
                for c in range(3):
                    S.pe(MM(pb_[0:96, :], WUQB[:, c, h * 96:(h + 1) * 96], nmq[:, c, :], c == 0, c == 2),
                         reads=["WUQB"] + [(nmqk, cc) for cc in range(3)], writes=[pkb])
                S.act(ACT(QM[0:64, h, tsl], pa[0:64, :], AF.Copy), reads=[pka], writes=[("QMn", h, tb)])
                t1, k1 = T1[(h + 1) % 2]
                t2, k2 = T2[(h + 1) % 2]
                S.dve(TT(t1[pe_rows], pa[pe_rows], cos[pe_rows], ALU.mult), reads=[pka, kcos], writes=[k1])
                S.dve(TT(t2[pe_rows], pb_[pe_rows], sin[pe_rows], ALU.mult), reads=[pkb, ksin], writes=[k2])
                S.pool(TT(QM[pe_rows, h, tsl], t1[pe_rows], t2[pe_rows], ALU.add), reads=[k1, k2], writes=[("QMp", h, tb)])
        QMK = [("QMn", h, tb) for h in range(8) for tb in range(NB)] + [("QMp", h, tb) for h in range(8) for tb in range(NB)]
        NMKVK = [(("NMKV", tb), c) for tb in range(NB) for c in range(2)]
        KPEK = [("KPE", tb) for tb in range(NB)]
        b.dump("QM", QM[0:96, 0, :], QMK, BF16)
        b.dump("NMKV", NMKV[:, 0, :], NMKVK, BF16)
        b.dump("KPE", KPE[64:96, :], KPEK, BF16)
        if stop_after == "1b":
            S.emit(es)
            return nc, b
        b.release(mla_mark)

        WUK = b.alloc("WUK", [128, 2, 512], BF16)
        WUV = b.alloc("WUV", [128, 2, 512], BF16)
        KM = b.ring("KM", [128, S_LEN], BF16, 2)
        VM = b.alloc("VM", [128, NT, 2, 65], BF16)
        PT = b.ring("PTb", [128, 512], BF16, 6)
        SM = b.ring("SMb", [128, 16], F32, 4)
        OM = b.ring("OM", [128, 4, 128], BF16, 2)
        OTB2 = b.ring("OTB2", [128, 512], BF16, 2)
        S.dma(WUK, wuk_d.rearrange("(kc p) n -> p kc n", p=128), writes=["WUK"], q="pool")
        S.dma(WUV, wuv_d.rearrange("(kc p) n -> p kc n", p=128), writes=["WUV"], q="pool")
        S.pool(MEMSET(VM, 1.0), writes=["VM"])
        for (km_, kmk_) in KM:
            S.pool(MEMSET(km_[96:128, :], 0.0), writes=[(kmk_, "pad")])
        mla_scale = 96.0 ** -0.5
        for hp in range(4):
            for hh in range(2):
                h = hp * 2 + hh
                km, kmk = KM[hh]
                for kb in range(NB):
                    pa, pka = b.bank()
                    for c in range(2):
                        S.pe(MM(pa[0:64, :], WUK[:, c, h * 64:(h + 1) * 64], NMKV[:, c, kb * 512:(kb + 1) * 512], c == 0, c == 1),
                             reads=["WUK"], writes=[pka])
                    S.dve(CP(km[0:64, kb * 512:(kb + 1) * 512], pa[0:64, :]), reads=[pka], writes=[(kmk, kb)])
                S.pool(CP(km[pe_rows, :], KPE[pe_rows, :]), reads=[], writes=[(kmk, "pe")])
            for kt in range(NT):
                pa, pka = b.bank()
                for c in range(2):
                    S.pe(MM(pa[:, 0:128], NMKV[:, c, kt * 128:(kt + 1) * 128], WUV[:, c, hp * 128:(hp + 1) * 128], c == 0, c == 1),
                         reads=["WUV"], writes=[pka])
                S.dve(CP(VM[:, kt, :, 0:64], pa[:, 0:128].rearrange("p (g d) -> p g d", d=64)),
                      reads=[pka, "VM"], writes=[("VM", kt)])
            if hp == 0:
                b.dump("KM0", KM[0][0][0:96, :], [("KM0", kb) for kb in range(NB)] + [("KM0", "pe")], BF16)
                b.dump("VM", VM, [("VM", kt) for kt in range(NT)], BF16)
            for qb in range(NB):
                om, omk = OM[qb % 2]
                for hh in range(2):
                    h = hp * 2 + hh
                    km, kmk = KM[hh]
                    tiles = []
                    for kt in range(0, 4 * qb + 4):
                        jl = max(kt - 4 * qb, 0)
                        masks = [("c", kt - 4 * qb)] if kt >= 4 * qb else []
                        tiles.append((kt, jl, 3, masks))

                    def post_mla(accs, hh=hh, qb=qb, om=om, omk=omk, hp=hp):
                        sm, smk = recip_sums(accs, 65, 64)
                        acc, pka, _ = accs[0]
                        for j in range(4):
                            S.dve(TS(om[:, j, hh * 64:(hh + 1) * 64], acc[:, j, 0:64], sm[:, 4 + j:5 + j], None, ALU.mult),
                                  reads=[pka, smk], writes=[(omk, hh)])
                        if hh == 1:
                            pb_, pk = b.bank()
                            ptr = pb_.bitcast(BF16)
                            for j in range(4):
                                S.pe(TR(ptr[:, j * 128:(j + 1) * 128], om[:, j, :], ident), reads=[(omk, 0), (omk, 1), "ident"], writes=[pk])
                            otb, otk = OTB2[qb % 2]
                            S.dve(CP(otb, ptr[:, 0:512]), reads=[pk], writes=[otk])
                            S.dma(oT_s[512 + hp * 128:512 + (hp + 1) * 128, qb * 512:(qb + 1) * 512], otb, reads=[otk],
                                  writes=[("oT_s", 1 + hp, qb)])

                    attention(QM[:, h, :], km, lambda kt, hh=hh: VM[:, kt, hh, :], tiles, qb, mla_scale, 65,
                              [], [(kmk, kb) for kb in range(NB)] + [(kmk, "pe"), (kmk, "pad")],
                              [("VM", kt) for kt in range(NT)] + ["VM"], post_mla)
            att_flush()
        if stop_after == "2b":
            S.emit(es)
            return nc, b
        b.release(base_mark)

        LNT = b.alloc("LNT", [128, 6, D], F32)
        CW = b.alloc("CW", [128, NCH, 3], F32)
        CB = b.alloc("CB", [128, NCH], F32)
        HIST = b.alloc("HIST", [128, NCH, 2], F32)
        KMEM = b.alloc("KMEM", [128, 8, 256], BF16)
        VMEM = b.alloc("VMEM", [128, 2, 4, 257], BF16)
        NWS = 6
        WS = b.ring("WS", [128, 8, 512], BF16, NWS)
        OTL = b.ring("OTL", [128, 8, 512], BF16, 1)
        XIN = b.ring("XIN", [128, D], F32, 2)
        Y = b.alloc("Y", [128, D], F32)
        XR = b.alloc("XR", [128, 4, D], F32)
        XB = b.alloc("XB", [128, D], BF16)
        XRT = b.alloc("XRT", [128, 8, 512], BF16)
        QME = b.alloc("QME", [128, 8, 512], BF16)
        OME = b.alloc("OME", [128, 4, D], BF16)
        OMET = b.alloc("OMET", [128, 8, 512], BF16)
        HT = b.alloc("HT", [128, NCH, 512], BF16)
        GC = b.ring("GC", [128, 514], F32, 2)
        GA = b.ring("GA", [128, 512], F32, 2)
        GS = b.ring("GS", [128, 512], F32, 2)
        PT = b.ring("PTc", [128, 512], BF16, 4)
        SM = b.ring("SMc", [128, 16], F32, 4)
        STAT = b.alloc("STAT", [128, 32], F32)
        MEMTB = b.alloc("MEMTB", [128, 8, 256], BF16)

        S.dma(LNT.rearrange("p k d -> p (k d)"), lngb_d.rearrange("(o k) d -> o (k d)", o=1).partition_broadcast(128),
              writes=["LNT"])
        S.dma(CW, convw_d.rearrange("p (c k) -> p c k", k=3), writes=["CW"])
        S.dma(CB, convb_d, writes=["CB"])
        S.pool(MEMSET(HIST, 0.0), writes=["HIST"])
        S.pool(MEMSET(VMEM, 1.0), writes=["VMEM"])
        S.dma(MEMTB, memT.rearrange("(kc p) t -> p kc t", p=128), writes=["MEMTB"], q="pool")
        ws_i = [0]

        def stream_w(src_ap, keys_src):
            ws, wsk = WS[ws_i[0] % NWS]
            ws_i[0] += 1
            kk, nn = src_ap.shape[1], src_ap.shape[2]
            S.dma(ws[:, 0:kk, 0:nn], src_ap, reads=keys_src, writes=[wsk])
            return ws, wsk

        wo_v = wo_s.rearrange("(kc p) n -> p kc n", p=128)
        mwq_v = mwq_s.rearrange("(kc p) n -> p kc n", p=128)
        mwo_v = mwo_s.rearrange("(kc p) n -> p kc n", p=128)
        wup_v = wup_s.rearrange("(kc p) n -> p kc n", p=128)
        wdn_v = wdn_s.rearrange("(c p) n -> p c n", p=128)
        allk = lambda key, rows: [(key, r0) for r0 in range(0, rows, 128)]

        for nb in range(2):
            ws, wsk = WS[ws_i[0] % NWS]
            ws_i[0] += 1
            S.dma(ws, mwk_d.rearrange("(kc p) n -> p kc n", p=128)[:, :, nb * 512:(nb + 1) * 512], writes=[wsk], q="pool")
            for oc in range(4):
                pa, pka = b.bank()
                for kc in range(8):
                    S.pe(MM(pa[:, 0:256], ws[:, kc, oc * 128:(oc + 1) * 128], MEMTB[:, kc, :], kc == 0, kc == 7),
                         reads=[wsk, "MEMTB"], writes=[pka])
                S.act(ACT(KMEM[:, nb * 4 + oc, :], pa[:, 0:256], AF.Copy), reads=[pka], writes=[("KMEM", nb * 4 + oc)])
        for nb in range(2):
            ws, wsk = WS[ws_i[0] % NWS]
            ws_i[0] += 1
            S.dma(ws, mwv_d.rearrange("(kc p) n -> p kc n", p=128)[:, :, nb * 512:(nb + 1) * 512], writes=[wsk], q="pool")
            for kt in range(2):
                pa, pka = b.bank()
                for kc in range(8):
                    S.pe(MM(pa, MEMTB[:, kc, kt * 128:(kt + 1) * 128], ws[:, kc, :], kc == 0, kc == 7),
                         reads=[wsk, "MEMTB"], writes=[pka])
                S.act(ACT(VMEM[:, kt, nb * 2:nb * 2 + 2, 0:256], pa.rearrange("p (h d) -> p h d", d=256), AF.Copy),
                      reads=[pka, "VMEM"], writes=[("VMEM", kt, nb)])
        KMEMK = [("KMEM", i) for i in range(8)]
        VMEMK = [("VMEM", kt, nb) for kt in range(2) for nb in range(2)]

        def layer_norm(src, srcks, ln_idx, dst, dstk):
            S.dve(lambda e: e.bn_stats(out=STAT[:, 0:6], in_=src[:, 0:512]), reads=srcks, writes=["STATa"])
            S.dve(lambda e: e.bn_stats(out=STAT[:, 6:12], in_=src[:, 512:1024]), reads=srcks, writes=["STATb"])
            S.dve(lambda e: e.bn_aggr(out=STAT[:, 12:14], in_=STAT[:, 0:12].rearrange("p (a b) -> p a b", b=6)),
                  reads=["STATa", "STATb"], writes=["STATc"])
            S.act(ACT(STAT[:, 16:17], STAT[:, 13:14], AF.Sqrt, scale=1.0, bias=1e-5), reads=["STATc"], writes=["STATd"])
            S.dve(RECIP(STAT[:, 17:18], STAT[:, 16:17]), reads=["STATd"], writes=["STATe"])
            S.dve(TS(dst, src, STAT[:, 12:13], STAT[:, 17:18], ALU.subtract, ALU.mult), reads=srcks + ["STATc", "STATe"], writes=[dstk])
            S.pool(TT(dst, dst, LNT[:, 2 * ln_idx, :], ALU.mult), reads=[dstk, "LNT"], writes=[dstk])
            S.pool(TT(dst, dst, LNT[:, 2 * ln_idx + 1, :], ALU.add), reads=[dstk, "LNT"], writes=[dstk])

        def to_feature_major(src, srck, tt, tag):
            S.act(ACT(XB, src, AF.Copy), reads=[srck], writes=["XB"])
            for half in range(2):
                pb_, pk = b.bank()
                ptr = pb_.bitcast(BF16)
                for c in range(4):
                    S.pe(TR(ptr[:, c * 128:(c + 1) * 128], XB[:, (half * 4 + c) * 128:(half * 4 + c + 1) * 128], ident),
                         reads=["XB", "ident"], writes=[pk])
                S.dve(CP(XRT[:, half * 4:half * 4 + 4, tt * 128:(tt + 1) * 128],
                         ptr[:, 0:512].rearrange("p (c t) -> p c t", t=128)), reads=[pk], writes=[("XRT", tag, tt, half)])

        def res_mm_ln(tiles, lhsT_fn, lhs_keys, wblk, res_fn, ln_idx):
            for tt in tiles:
                res, resk = res_fn(tt)
                for nbk in range(2):
                    pa, pka = b.bank()
                    ws, wsk = wblk[nbk]
                    for kc in range(8):
                        S.pe(MM(pa, lhsT_fn(kc, tt), ws[:, kc, :], kc == 0, kc == 7), reads=lhs_keys + [wsk], writes=[pka])
                    S.dve(STT(Y[:, nbk * 512:(nbk + 1) * 512], res[:, nbk * 512:(nbk + 1) * 512], ALPHA, pa, ALU.mult, ALU.add),
                          reads=[resk, pka], writes=[("Y", nbk)])
                layer_norm(Y, [("Y", 0), ("Y", 1)], ln_idx, XR[:, tt, :], ("XR", tt))

        XRTK = lambda tag, tiles=range(4): [("XRT", tag, tt, half) for tt in tiles for half in range(2)]

        for tb in range(NB):
            tsl = slice(tb * 512, (tb + 1) * 512)
            GRP = ((0, 1), (2, 3))
            otl, otlk = OTL[0]
            S.dma(otl, oT_s.rearrange("(kc p) t -> p kc t", p=128)[:, :, tsl],
                  reads=[("oT_s", i, tb) for i in range(5)], writes=[otlk])

            def res_x(tt, tb=tb):
                xi, xik = XIN[tt % 2]
                S.dma(xi, x_in[tb * 512 + tt * 128:tb * 512 + (tt + 1) * 128, :], writes=[xik])
                return xi, xik

            wo_blk = [stream_w(wo_v[:, :, nbk * 512:(nbk + 1) * 512], allk("wo_s", D)) for nbk in range(2)]
            wq_blk = [stream_w(mwq_v[:, :, nbk * 512:(nbk + 1) * 512], allk("mwq_s", D)) for nbk in range(2)]
            wmo_blk = [stream_w(mwo_v[:, :, nbk * 512:(nbk + 1) * 512], allk("mwo_s", D)) for nbk in range(2)]

            def stage_q(tiles):
                c0, c1 = tiles[0] * 128, (tiles[-1] + 1) * 128
                for nbk in range(2):
                    ws, wsk = wq_blk[nbk]
                    for oc in range(4):
                        pa, pka = b.bank()
                        for kc in range(8):
                            S.pe(MM(pa[:, 0:c1 - c0], ws[:, kc, oc * 128:(oc + 1) * 128], XRT[:, kc, c0:c1], kc == 0, kc == 7),
                                 reads=[wsk] + XRTK("a", tiles), writes=[pka])
                        S.act(ACT(QME[:, nbk * 4 + oc, c0:c1], pa[:, 0:c1 - c0], AF.Copy), reads=[pka],
                              writes=[("QME", nbk * 4 + oc, tiles[0])])

            def stage_att(tiles):
                c0, c1 = tiles[0] * 128, (tiles[-1] + 1) * 128
                for h in range(4):
                    pts = []
                    for kt in range(2):
                        ps, pks = b.bank()
                        for dc in range(2):
                            S.pe(MM(ps[:, 0:c1 - c0], KMEM[:, 2 * h + dc, kt * 128:(kt + 1) * 128], QME[:, 2 * h + dc, c0:c1], dc == 0, dc == 1),
                                 reads=KMEMK + [("QME", 2 * h + dc, tiles[0])], writes=[pks])
                        pt, pkt = PT[pt_i[0] % 4]
                        pt_i[0] += 1
                        S.act(ACT(pt[:, 0:c1 - c0], ps[:, 0:c1 - c0], AF.Exp, scale=1.0 / 16.0), reads=[pks], writes=[pkt])
                        pts.append((pt, pkt))
                    for ji, j in enumerate(tiles):
                        pa, pka = b.bank()
                        for kt in range(2):
                            S.pe(MM(pa[:, 0:257], pts[kt][0][:, ji * 128:(ji + 1) * 128], VMEM[:, kt, h, :], kt == 0, kt == 1),
                                 reads=[pts[kt][1]] + VMEMK + ["VMEM"], writes=[pka])
                        sm, smk = SM[sm_i[0] % 4]
                        sm_i[0] += 1
                        S.dve(RECIP(sm[:, 0:1], pa[:, 256:257]), reads=[pka], writes=[smk])
                        S.dve(TS(OME[:, j, h * 256:(h + 1) * 256], pa[:, 0:256], sm[:, 0:1], None, ALU.mult),
                              reads=[pka, smk], writes=[("OME", j, h)])

            def stage_omet(tiles):
                for j in tiles:
                    for half in range(2):
                        pb_, pk = b.bank()
                        ptr = pb_.bitcast(BF16)
                        for c in range(4):
                            S.pe(TR(ptr[:, c * 128:(c + 1) * 128], OME[:, j, (half * 4 + c) * 128:(half * 4 + c + 1) * 128], ident),
                                 reads=[("OME", j, hh_) for hh_ in range(4)] + ["ident"], writes=[pk])
                        S.dve(CP(OMET[:, half * 4:half * 4 + 4, j * 128:(j + 1) * 128],
                                 ptr[:, 0:512].rearrange("p (c t) -> p c t", t=128)), reads=[pk], writes=[("OMET", j, half)])

            for gi, tiles in enumerate(GRP):
                res_mm_ln(tiles, lambda kc, tt: otl[:, kc, tt * 128:(tt + 1) * 128], [otlk], wo_blk, res_x, 0)
            if tb == 0:
                b.dump("X1", XR, [("XR", tt) for tt in range(4)])
            for gi, tiles in enumerate(GRP):
                for tt in tiles:
                    to_feature_major(XR[:, tt, :], ("XR", tt), tt, "a")
                stage_q(tiles)
            for gi, tiles in enumerate(GRP):
                stage_att(tiles)
            OMETK = lambda tiles: [("OMET", j, half) for j in tiles for half in range(2)]
            for gi, tiles in enumerate(GRP):
                stage_omet(tiles)
                res_mm_ln(tiles, lambda kc, tt: OMET[:, kc, tt * 128:(tt + 1) * 128], OMETK(tiles), wmo_blk,
                          lambda tt: (XR[:, tt, :], ("XR", tt)), 1)
            for gi, tiles in enumerate(GRP):
                for tt in tiles:
                    to_feature_major(XR[:, tt, :], ("XR", tt), tt, "b")
            if tb == 0:
                b.dump("X2", XR, [("XR", tt) for tt in range(4)])
            for cb in range(6):
                ncol = 512 if cb < 5 else 256
                wg, wgk = stream_w(wup_v[:, :, cb * 512:cb * 512 + ncol], allk("wup_s", D))
                wu, wuk_ = stream_w(wup_v[:, :, DFF + cb * 512:DFF + cb * 512 + ncol], allk("wup_s", D))
                for cc in range(ncol // 128):
                    c = cb * 4 + cc
                    pg, pkg = b.bank()
                    pu, pku = b.bank()
                    for kc in range(8):
                        S.pe(MM(pg, wg[:, kc, cc * 128:(cc + 1) * 128], XRT[:, kc, :], kc == 0, kc == 7),
                             reads=[wgk] + XRTK("b"), writes=[pkg])
                    for kc in range(8):
                        S.pe(MM(pu, wu[:, kc, cc * 128:(cc + 1) * 128], XRT[:, kc, :], kc == 0, kc == 7),
                             reads=[wuk_] + XRTK("b"), writes=[pku])
                    gc, gck = GC[c % 2]
                    ga, gak = GA[c % 2]
                    gs, gsk = GS[c % 2]
                    S.pool(CP(gc[:, 0:2], HIST[:, c, :]), reads=[("HIST", c)], writes=[(gck, "h")])
                    S.act(ACT(gc[:, 2:514], pg, AF.Copy), reads=[pkg], writes=[(gck, "m")])
                    S.pool(CP(HIST[:, c, :], gc[:, 512:514]), reads=[(gck, "m"), (gck, "h")], writes=[("HIST", c)])
                    S.act(ACT(ga, pg, AF.Identity, scale=CW[:, c, 2:3], bias=CB[:, c:c + 1]), reads=[pkg, "CW", "CB"], writes=[gak])
                    S.dve(STT(ga, gc[:, 1:513], CW[:, c, 1:2], ga, ALU.mult, ALU.add), reads=[(gck, "m"), (gck, "h"), gak, "CW"], writes=[gak])
                    S.dve(STT(ga, gc[:, 0:512], CW[:, c, 0:1], ga, ALU.mult, ALU.add), reads=[(gck, "m"), (gck, "h"), gak, "CW"], writes=[gak])
                    S.act(ACT(gs, ga, AF.Silu), reads=[gak], writes=[gsk])
                    S.dve(TT(HT[:, c, :], gs, pu, ALU.mult), reads=[gsk, pku], writes=[("HT", c)])
            HTK = [("HT", c) for c in range(NCH)]
            if tb == 0:
                b.dump("HT", HT, HTK, BF16)
            accb = [b.bank() for _ in range(8)]
            for c0 in range(0, NCH, 8):
                nc_ = min(8, NCH - c0)
                wblk = [stream_w(wdn_v[:, c0:c0 + nc_, nbk * 512:(nbk + 1) * 512], allk("wdn_s", DFF)) for nbk in range(2)]
                for tt in range(4):
                    for nbk in range(2):
                        pa, pka = accb[tt * 2 + nbk]
                        ws, wsk = wblk[nbk]
                        for ci in range(nc_):
                            c = c0 + ci
                            S.pe(MM(pa, HT[:, c, tt * 128:(tt + 1) * 128], ws[:, ci, :], c == 0, c == NCH - 1),
                                 reads=HTK + [wsk], writes=[pka])
            for tt in range(4):
                for nbk in range(2):
                    pa, pka = accb[tt * 2 + nbk]
                    S.dve(STT(Y[:, nbk * 512:(nbk + 1) * 512], XR[:, tt, nbk * 512:(nbk + 1) * 512], ALPHA, pa, ALU.mult, ALU.add),
                          reads=[("XR", tt), pka], writes=[("Y", nbk)])
                layer_norm(Y, [("Y", 0), ("Y", 1)], 2, XR[:, tt, :], ("XR", tt))
                S.dma(out_d[tb * 512 + tt * 128:tb * 512 + (tt + 1) * 128, :], XR[:, tt, :], reads=[("XR", tt)], out=True, q="pool")
        S.emit(es)
    return nc, b


def _rot(cols, half):
    return np.concatenate([cols[half:], cols[:half]])


def prep_shared(inp):
    f = np.float32
    w_in = np.asarray(inp["w_in"])[0]
    C1, C2, C3, C4, C5 = 512, 1280, 1304, 1688, 1944
    groups = []
    for hp in range(4):
        groups.append(np.arange(hp * 128, hp * 128 + 128))
    for hp in range(4):
        groups.append(np.concatenate([_rot(np.arange(h * 64, h * 64 + 64), 32) for h in (2 * hp, 2 * hp + 1)]))
    for base in (C1 + 256, C1 + 512):
        for g in range(2):
            c = np.arange(base + g * 64, base + g * 64 + 64)
            groups.append(np.concatenate([c, c]))
        for g in range(2):
            c = _rot(np.arange(base + g * 64, base + g * 64 + 64), 32)
            groups.append(np.concatenate([c, c]))
    groups.append(np.arange(C1, C1 + 128))
    groups.append(np.arange(C1 + 128, C1 + 256))
    wfm_nsa = np.ascontiguousarray(w_in[:, np.concatenate(groups)])
    tm_cols = np.concatenate([np.arange(C1 + 256 + 128, C1 + 512), np.arange(C1 + 512 + 128, C1 + 768), np.arange(C2, C2 + 24)])
    wtm_nsa = np.ascontiguousarray(w_in[:, tm_cols])
    wfm_mla = np.zeros((D, 896), f)
    wfm_mla[:, 0:384] = w_in[:, C3:C4]
    wfm_mla[:, 384:640] = w_in[:, C4:C5]
    wfm_mla[:, 640 + 64:640 + 96] = w_in[:, C5:C5 + 32]
    wfm_mla[:, 768 + 64:768 + 96] = w_in[:, _rot(np.arange(C5, C5 + 32), 16)]

    def w1_layout(w1):
        a = np.asarray(w1)[0].reshape(32, 64, 128).transpose(1, 0, 2)
        return np.ascontiguousarray(np.concatenate([a, a], 0).reshape(128, 32 * 128))

    def posT_layout(p):
        a = np.asarray(p)[0].T
        a = np.repeat(a[:, :, None], 2, axis=2)
        return np.ascontiguousarray(np.concatenate([a, a], 0).reshape(128, 64))

    w2k = np.asarray(inp["nsa_ck_w2"])[0]
    rc = _rot(np.arange(64), 32)
    w2k_l = np.ascontiguousarray(np.concatenate([w2k, w2k, w2k[:, rc], w2k[:, rc]], 1))
    b2k = np.asarray(inp["nsa_ck_b2"])[0]
    cover = np.zeros((256, 64), f)
    n = np.arange(256)[:, None] * 16
    j = np.arange(64)[None, :] * 64
    cover[:, :] = np.clip(np.minimum(n + 32, j + 64) - np.maximum(n, j), 0, None).astype(f) / 32.0
    t = np.arange(S_LEN)
    cur = (t // 64)[:, None]
    blk = np.arange(64)[None, :]
    forced = ((blk == 0) | (blk == cur) | (blk == cur - 1)) & (blk <= cur)
    cand = (blk >= 1) & (blk <= cur - 2)
    psc = np.zeros((128, 16), f)
    p = np.arange(128)
    d64 = p % 64
    psc[:, 0] = 10000.0 ** (-(2.0 * (d64 % 32)) / 64.0)
    sg = np.where(d64 < 32, -1.0, 1.0)
    psc[:, 1] = sg
    psc[:, 2] = -sg * np.pi
    d32 = p % 32
    psc[:, 3] = 10000.0 ** (-(2.0 * (d32 % 16)) / 32.0)
    sg2 = np.where(d32 < 16, -1.0, 1.0)
    psc[:, 4] = sg2
    psc[:, 5] = -sg2 * np.pi
    psc[:, 6] = np.asarray(inp["nsa_ck_b1"])[0]
    psc[:, 7] = np.asarray(inp["nsa_cv_b1"])[0]
    psc[:, 8] = np.concatenate([b2k, b2k])
    psc[:, 9] = np.concatenate([b2k[rc], b2k[rc]])
    psc[:, 10:13] = np.asarray(inp["mla_q_norm"])[0].reshape(3, 128).T
    psc[:, 13:15] = np.asarray(inp["mla_kv_norm"])[0].reshape(2, 128).T
    psc[:, 15] = -np.pi
    wuq = np.asarray(inp["mla_w_uq"])[0]
    wuq_b = np.zeros_like(wuq)
    for h in range(8):
        pe = np.arange(h * 96 + 64, h * 96 + 96)
        wuq_b[:, pe] = wuq[:, _rot(pe, 16)]
    wukv = np.asarray(inp["mla_w_ukv"])[0]
    wuk = np.ascontiguousarray(np.concatenate([wukv[:, h * 128:h * 128 + 64] for h in range(8)], 1))
    wuv = np.ascontiguousarray(np.concatenate([wukv[:, h * 128 + 64:h * 128 + 128] for h in range(8)], 1))
    lngb = np.stack([np.asarray(inp[k])[0] for k in ("ln1_g", "ln1_b", "ln2_g", "ln2_b", "ln3_g", "ln3_b")]).astype(f)
    convw = np.ascontiguousarray(np.asarray(inp["ffn_conv_w"])[0].T.reshape(NCH, 128, 3).transpose(1, 0, 2).reshape(128, NCH * 3))
    convb = np.ascontiguousarray(np.asarray(inp["ffn_conv_b"])[0].reshape(NCH, 128).T)
    return {
        "wfm_nsa": wfm_nsa, "wtm_nsa": wtm_nsa, "wfm_mla": wfm_mla,
        "w1k": w1_layout(inp["nsa_ck_w1"]), "w1v": w1_layout(inp["nsa_cv_w1"]),
        "poskT": posT_layout(inp["nsa_k_pos"]), "posvT": posT_layout(inp["nsa_v_pos"]),
        "w2k": w2k_l, "w2v": np.ascontiguousarray(np.asarray(inp["nsa_cv_w2"])[0]),
        "b2v": np.ascontiguousarray(np.asarray(inp["nsa_cv_b2"])[0][None, :]),
        "cover": cover, "cand": cand.astype(f), "forced": forced.astype(f), "psc": psc,
        "wuq_a": np.ascontiguousarray(wuq), "wuq_b": wuq_b, "wuk": wuk, "wuv": wuv,
        "w_o": np.ascontiguousarray(np.asarray(inp["w_o"])[0]),
        "mem_wq": np.ascontiguousarray(np.asarray(inp["mem_wq"])[0]),
        "mem_wk": np.ascontiguousarray(np.asarray(inp["mem_wk"])[0]),
        "mem_wv": np.ascontiguousarray(np.asarray(inp["mem_wv"])[0]),
        "mem_wo": np.ascontiguousarray(np.asarray(inp["mem_wo"])[0]),
        "w_up": np.ascontiguousarray(np.asarray(inp["ffn_w_up"])[0]),
        "w_dn": np.ascontiguousarray(np.asarray(inp["ffn_w_down"])[0]),
        "lngb": lngb, "convw": convw, "convb": convb,
    }


def prep_core(inp, bi):
    x = np.asarray(inp["x"])[bi]
    pos = np.asarray(inp["positions"])[bi].astype(np.int32)
    posc = np.zeros((1, 256), np.int32)
    posc[0, :255] = pos[31::16][:255]
    return {
        "xT": np.ascontiguousarray(x.T), "x": np.ascontiguousarray(x),
        "memT": np.ascontiguousarray(np.asarray(inp["mem"])[bi].T),
        "pos": np.ascontiguousarray(pos[None, :]), "posc": posc,
    }


_CACHE = {}


def kernel(**inputs):
    if "nc" not in _CACHE:
        _CACHE["nc"] = build_program()[0]
    nc = _CACHE["nc"]
    shared = prep_shared(inputs)
    in_maps = []
    for bi in range(8):
        m = dict(shared)
        m.update(prep_core(inputs, bi))
        in_maps.append(m)
    res = run_bass_kernel_spmd(nc, in_maps, core_ids=list(range(8)))
    return np.stack([np.asarray(r["out"], dtype=np.float32) for r in res.results], 0)
```

```python
import numpy as np
from contextlib import ExitStack
import concourse.bass as bass
import concourse.mybir as mybir
from concourse.bass_utils import run_bass_kernel_spmd

F32 = mybir.dt.float32
BF16 = mybir.dt.bfloat16
I32 = mybir.dt.int32
AF = mybir.ActivationFunctionType
ALU = mybir.AluOpType

S_LEN = 4096
D = 1024
NT = 32
NB = 8
DFF = 2816
NCH = 22
ALPHA = 2.0 ** 0.25
PI = float(np.pi)
NEG = -30000.0

SEM_LIMIT = 20000
DMA_RING = 8
import os as _os
SAME_ENGINE_SYNC = set(_os.environ.get("KSES", "act,dve,pool").split(","))


class Sched:
    STREAMS = ("pe", "act", "dve", "pool", "sp")

    def __init__(self, nc):
        self.nc = nc
        self.ops = []
        self.last_writer = {}
        self.readers = {}
        self.out_dmas = []

    def op(self, stream, fn, reads=(), writes=(), dma=False, out=False):
        idx = len(self.ops)
        deps = set()
        reads = list(reads) + ["PHASE"]
        for r in reads:
            w = self.last_writer.get(r)
            if w is not None:
                deps.add(w)
        for w_ in writes:
            w = self.last_writer.get(w_)
            if w is not None:
                deps.add(w)
            for rd in self.readers.get(w_, ()):
                deps.add(rd)
        for r in reads:
            self.readers.setdefault(r, []).append(idx)
        for w_ in writes:
            self.last_writer[w_] = idx
            self.readers[w_] = []
        self.ops.append(dict(stream=stream, fn=fn, deps=deps, dma=dma))
        if out:
            self.out_dmas.append(idx)
        return idx

    def pe(self, fn, reads=(), writes=()):
        return self.op("pe", fn, reads, writes)

    def act(self, fn, reads=(), writes=()):
        return self.op("act", fn, reads, writes)

    def dve(self, fn, reads=(), writes=()):
        return self.op("dve", fn, reads, writes)

    def pool(self, fn, reads=(), writes=()):
        return self.op("pool", fn, reads, writes)

    def dma(self, out_ap, in_ap, reads=(), writes=(), q="sp", out=False):
        return self.op(q, lambda e: e.dma_start(out=out_ap, in_=in_ap), reads, writes, dma=True, out=out)

    def emit(self, es):
        nc = self.nc
        ops = self.ops
        ops.append(dict(stream="sp", fn=None, deps=set(self.out_dmas), dma=False))
        n = len(ops)
        dma_count = {s: 0 for s in self.STREAMS}
        dma_hist = {s: [] for s in self.STREAMS}
        for i, o in enumerate(ops):
            if o["dma"]:
                s = o["stream"]
                k = dma_count[s]
                o["dma_k"] = k
                if k >= DMA_RING:
                    o["deps"].add(dma_hist[s][k - DMA_RING])
                dma_hist[s].append(i)
                dma_count[s] += 1
        has_dep = [False] * n
        for i, o in enumerate(ops):
            for d in o["deps"]:
                od = ops[d]
                if od["dma"] or od["stream"] != o["stream"]:
                    has_dep[d] = True
                elif od["stream"] in SAME_ENGINE_SYNC:
                    has_dep[d] = True
        sem_cnt = [0]

        def new_sem(tag):
            sem_cnt[0] += 1
            return es.enter_context(nc.semaphore(f"s_{tag}_{sem_cnt[0]}"))

        cur_sem, cur_cnt, rings = {}, {}, {}
        for i, o in enumerate(ops):
            s = o["stream"]
            if o["dma"]:
                if s not in rings:
                    rings[s] = [new_sem(f"dq{s}") for _ in range(DMA_RING)]
                k = o["dma_k"]
                o["sig"] = (rings[s][k % DMA_RING], 16 * (k // DMA_RING + 1), 16)
            elif has_dep[i]:
                if s not in cur_sem or cur_cnt[s] >= SEM_LIMIT:
                    cur_sem[s] = new_sem(s)
                    cur_cnt[s] = 0
                cur_cnt[s] += 1
                o["sig"] = (cur_sem[s], cur_cnt[s], 1)
            else:
                o["sig"] = None
        waited = {s: {} for s in self.STREAMS}
        per_stream = {s: [] for s in self.STREAMS}
        for i, o in enumerate(ops):
            s = o["stream"]
            need = {}
            for d in o["deps"]:
                od = ops[d]
                if (not od["dma"]) and od["stream"] == s and (s not in SAME_ENGINE_SYNC):
                    continue
                sem, val, _ = od["sig"]
                key = id(sem)
                if waited[s].get(key, 0) >= val:
                    continue
                if key not in need or need[key][1] < val:
                    need[key] = (sem, val)
            for key, (sem, val) in need.items():
                waited[s][key] = val
            o["waits"] = list(need.values())
            per_stream[s].append(o)
        self.n_sems = sem_cnt[0]
        self.stream_sizes = {s: len(v) for s, v in per_stream.items()}
        block = es.enter_context(nc.Block())

        def run(stream_ops):
            def body(eng):
                for o in stream_ops:
                    for sem, val in o["waits"]:
                        eng.wait_ge(sem, val)
                    if o["fn"] is None:
                        continue
                    ins = o["fn"](eng)
                    if o["sig"] is not None:
                        ins.then_inc(o["sig"][0], o["sig"][2])
            return body

        block.tensor(run(per_stream["pe"]))
        block.scalar(run(per_stream["act"]))
        block.vector(run(per_stream["dve"]))
        block.gpsimd(run(per_stream["pool"]))
        block.sync(run(per_stream["sp"]))


def MM(out, lhsT, rhs, start, stop):
    return lambda e: e.matmul(out, lhsT=lhsT, rhs=rhs, start=start, stop=stop, skip_group_check=True)


def TR(out, in_, ident):
    return lambda e: e.transpose(out=out, in_=in_, identity=ident)


def ACT(out, in_, func, **kw):
    return lambda e: e.activation(out=out, in_=in_, func=func, **kw)


def TT(out, in0, in1, op):
    return lambda e: e.tensor_tensor(out=out, in0=in0, in1=in1, op=op)


def TS(out, in0, s1, s2, op0, op1=None):
    if op1 is None:
        return lambda e: e.tensor_scalar(out=out, in0=in0, scalar1=s1, scalar2=None, op0=op0)
    return lambda e: e.tensor_scalar(out=out, in0=in0, scalar1=s1, scalar2=s2, op0=op0, op1=op1)


def STT(out, in0, scalar, in1, op0, op1):
    return lambda e: e.scalar_tensor_tensor(out=out, in0=in0, scalar=scalar, in1=in1, op0=op0, op1=op1)


def CP(out, in_):
    return lambda e: e.tensor_copy(out=out, in_=in_)


def MEMSET(ap, v):
    return lambda e: e.memset(ap, v)


def RECIP(out, in_):
    return lambda e: e.reciprocal(out=out, in_=in_)


def ASEL(out, in_, pattern, op, fill, base, cm):
    return lambda e: e.affine_select(out=out, in_=in_, pattern=pattern, compare_op=op, fill=fill,
                                     base=base, channel_multiplier=cm)


DT_SIZE = {F32: 4, BF16: 2, I32: 4}
ARENA_BYTES = 204 * 1024


class Builder:
    def __init__(self, nc, es, dbg=()):
        self.nc = nc
        self.es = es
        self.S = Sched(nc)
        self.dbg = set(dbg)
        self.dbg_outs = {}
        self.arena = nc.alloc_sbuf_tensor("arena", [128, ARENA_BYTES // 4], F32).ap()
        self.top = 0
        self.banks = [es.enter_context(nc.psum_tensor(f"pb{i}", [128, 512], F32)).ap() for i in range(8)]
        self.bank_i = 0
        self.pinned = set()
        self.din = {}

    def inp(self, name, shape, dt=F32):
        ap = self.nc.dram_tensor(name, list(shape), dt, kind="ExternalInput").ap()
        self.din[name] = ap
        return ap

    def alloc(self, name, shape, dt=F32):
        n = int(np.prod(shape[1:]))
        nbytes = (n * DT_SIZE[dt] + 31) // 32 * 32
        off = self.top
        self.top += nbytes
        self.max_top = max(getattr(self, "max_top", 0), self.top)
        assert self.top <= ARENA_BYTES, f"arena overflow at {name}: {self.top}"
        ap = self.arena[0:shape[0], off // 4:(off + nbytes) // 4]
        if dt != F32:
            ap = ap.bitcast(dt)
        ap = ap[:, 0:n]
        if len(shape) == 3:
            ap = ap.rearrange("p (a b) -> p a b", b=shape[2])
        elif len(shape) == 4:
            ap = ap.rearrange("p (a b c) -> p a b c", b=shape[2], c=shape[3])
        return ap

    def ring(self, name, shape, dt, n):
        return [(self.alloc(f"{name}{i}", shape, dt), f"{name}{i}") for i in range(n)]

    def mark(self):
        return self.top

    def release(self, mark):
        scr = self.barrier_scr
        self.S.op("pool", MEMSET(scr, 0.0), reads=[], writes=["PHASE", "barrier_scr"])
        self.top = mark

    def bank(self, pin=False):
        while self.bank_i in self.pinned:
            self.bank_i = (self.bank_i + 1) % 8
        i = self.bank_i
        self.bank_i = (i + 1) % 8
        if pin:
            self.pinned.add(i)
        return self.banks[i], ("PB", i)

    def unpin(self, key):
        self.pinned.discard(key[1])

    def dump(self, name, ap, reads, dt=F32):
        if name not in self.dbg:
            return
        shape = list(ap.shape)
        d = self.nc.dram_tensor("dbg_" + name, shape, dt, kind="ExternalOutput").ap()
        self.dbg_outs[name] = shape
        self.S.dma(d, ap, reads=reads, out=True)


def build_program(dbg=(), stop_after=None):
    nc = bass.Bass("TRN2", target_bir_lowering=False)
    es = ExitStack()
    with es:
        b = Builder(nc, es, dbg)
        S = b.S
        xT = b.inp("xT", [D, S_LEN])
        x_in = b.inp("x", [S_LEN, D])
        memT = b.inp("memT", [D, 256])
        pos = b.inp("pos", [1, S_LEN], I32)
        posc = b.inp("posc", [1, 256], I32)
        wfm_nsa = b.inp("wfm_nsa", [D, 2304])
        wtm_nsa = b.inp("wtm_nsa", [D, 280])
        wfm_mla = b.inp("wfm_mla", [D, 896])
        w1k_d = b.inp("w1k", [128, 32 * 128])
        w1v_d = b.inp("w1v", [128, 32 * 128])
        poskT_d = b.inp("poskT", [128, 64])
        posvT_d = b.inp("posvT", [128, 64])
        w2k_d = b.inp("w2k", [128, 256])
        w2v_d = b.inp("w2v", [128, 64])
        b2v_d = b.inp("b2v", [1, 64])
        cover_d = b.inp("cover", [256, 64])
        cand_d = b.inp("cand", [S_LEN, 64])
        forced_d = b.inp("forced", [S_LEN, 64])
        psc_d = b.inp("psc", [128, 16])
        wuq_a_d = b.inp("wuq_a", [384, 768])
        wuq_b_d = b.inp("wuq_b", [384, 768])
        wuk_d = b.inp("wuk", [256, 512])
        wuv_d = b.inp("wuv", [256, 512])
        wo_d = b.inp("w_o", [D, D])
        mwq_d = b.inp("mem_wq", [D, D])
        mwk_d = b.inp("mem_wk", [D, D])
        mwv_d = b.inp("mem_wv", [D, D])
        mwo_d = b.inp("mem_wo", [D, D])
        wup_d = b.inp("w_up", [D, 2 * DFF])
        wdn_d = b.inp("w_dn", [DFF, D])
        lngb_d = b.inp("lngb", [6, D])
        convw_d = b.inp("convw", [128, NCH * 3])
        convb_d = b.inp("convb", [128, NCH])
        out_d = nc.dram_tensor("out", [S_LEN, D], F32, kind="ExternalOutput").ap()
        wo_s = nc.dram_tensor("wo_s", [D, D], BF16).ap()
        mwq_s = nc.dram_tensor("mwq_s", [D, D], BF16).ap()
        mwo_s = nc.dram_tensor("mwo_s", [D, D], BF16).ap()
        wup_s = nc.dram_tensor("wup_s", [D, 2 * DFF], BF16).ap()
        wdn_s = nc.dram_tensor("wdn_s", [DFF, D], BF16).ap()
        oT_s = nc.dram_tensor("oT_s", [D, S_LEN], BF16).ap()

        ident = b.alloc("ident", [128, 128], BF16)
        caus = b.alloc("caus", [128, 128], BF16)
        wedge = b.alloc("wedge", [128, 128], BF16)
        ones = b.alloc("ones", [128, 128], BF16)
        psc = b.alloc("psc", [128, 16], F32)
        b.barrier_scr = b.alloc("barrier_scr", [128, 8], F32)
        zeros = b.alloc("zeros", [128, 512], BF16)
        S.pool(MEMSET(zeros, 0.0), writes=["zeros"])
        S.pool(MEMSET(ones, 1.0), writes=["ones"])
        S.pool(MEMSET(ident, 1.0), writes=["ident"])
        S.pool(ASEL(ident, ident, [[1, 128]], ALU.is_equal, 0.0, 0, -1), reads=["ident"], writes=["ident"])
        S.pool(ASEL(caus, zeros[:, 0:128], [[1, 128]], ALU.is_ge, NEG, 0, -1), reads=["zeros"], writes=["caus"])
        S.pool(ASEL(wedge, zeros[:, 0:128], [[-1, 128]], ALU.is_gt, NEG, 0, 1), reads=["zeros"], writes=["wedge"])
        S.dma(psc, psc_d, writes=["psc"])
        for (dst, src, rows, key) in ((wo_s, wo_d, D, "wo_s"), (mwq_s, mwq_d, D, "mwq_s"), (mwo_s, mwo_d, D, "mwo_s"),
                                      (wup_s, wup_d, D, "wup_s"), (wdn_s, wdn_d, DFF, "wdn_s")):
            for r0 in range(0, rows, 128):
                S.dma(dst[r0:r0 + 128, :], src[r0:r0 + 128, :], writes=[(key, r0)], q="pool")
        base_mark = b.mark()

        PSC = lambda c: psc[:, c:c + 1]

        def load_xT_block(tb, ring_slot):
            xt, xk = ring_slot
            src = xT.rearrange("(kc p) t -> p kc t", p=128)
            for k0 in range(0, 8, 2):
                S.dma(xt[:, k0:k0 + 2, :], src[:, k0:k0 + 2, tb * 512:(tb + 1) * 512], writes=[(xk, k0)], q="pool")
            return xt, [(xk, k0) for k0 in range(0, 8, 2)]

        def rope_tables(pos_src, n, inv_c, sgn_c, nsg_c, tiles, prow=slice(0, 128)):
            (posi, kpi), (posf, kpf), (m1, km1), (m2, km2), (cos, kc), (sin, ks) = tiles
            S.dma(posi[:, 0:n], pos_src.partition_broadcast(128), writes=[kpi])
            S.dve(CP(posf[:, 0:n], posi[:, 0:n]), reads=[kpi], writes=[kpf])
            C1_, C2_ = 6.28125, 2 * PI - 6.28125
            S.dve(TS(m1[:, 0:n], posf[:, 0:n], PSC(inv_c), None, ALU.mult), reads=[kpf, "psc"], writes=[km1])
            S.dve(TS(m2[:, 0:n], posf[:, 0:n], PSC(inv_c), 0.5 * PI, ALU.mult, ALU.add), reads=[kpf, "psc"], writes=[km2])
            S.dve(TS(sin[:, 0:n], m1[:, 0:n], 1.0 / (2 * PI), None, ALU.mult), reads=[km1], writes=[ks])
            S.dve(CP(posi[:, 0:n], sin[:, 0:n]), reads=[ks], writes=[kpi])
            S.dve(CP(cos[:, 0:n], posi[:, 0:n]), reads=[kpi], writes=[kc])
            S.dve(STT(m1[:, 0:n], cos[:, 0:n], -C1_, m1[:, 0:n], ALU.mult, ALU.add), reads=[kc, km1], writes=[km1])
            S.dve(STT(m1[:, 0:n], cos[:, 0:n], -C2_, m1[:, 0:n], ALU.mult, ALU.add), reads=[kc, km1], writes=[km1])
            S.dve(TS(m1[:, 0:n], m1[:, 0:n], PI, -PI, ALU.min, ALU.max), reads=[km1], writes=[km1])
            S.act(ACT(sin[:, 0:n], m1[:, 0:n], AF.Sin, scale=PSC(sgn_c)), reads=[km1, "psc"], writes=[ks])
            S.dve(TS(cos[:, 0:n], m2[:, 0:n], 1.0 / (2 * PI), None, ALU.mult), reads=[km2], writes=[kc])
            S.dve(CP(posi[:, 0:n], cos[:, 0:n]), reads=[kc], writes=[kpi])
            S.dve(CP(posf[:, 0:n], posi[:, 0:n]), reads=[kpi], writes=[kpf])
            S.dve(STT(m2[:, 0:n], posf[:, 0:n], -C1_, m2[:, 0:n], ALU.mult, ALU.add), reads=[kpf, km2], writes=[km2])
            S.dve(STT(m2[:, 0:n], posf[:, 0:n], -C2_, m2[:, 0:n], ALU.mult, ALU.add), reads=[kpf, km2], writes=[km2])
            S.dve(TS(m2[:, 0:n], m2[:, 0:n], PI, -PI, ALU.min, ALU.max), reads=[km2], writes=[km2])
            S.act(ACT(cos[:, 0:n], m2[:, 0:n], AF.Sin), reads=[km2], writes=[kc])

        def proj_fm(xt, xkeys, w, wkey, col0, ncols=128):
            pb, pk = b.bank()
            for kc in range(8):
                S.pe(MM(pb[0:ncols, :], w[:, kc, col0:col0 + ncols], xt[:, kc, :], kc == 0, kc == 7),
                     reads=[wkey] + xkeys, writes=[pk])
            return pb, pk

        QT = b.alloc("QT", [128, 4, S_LEN], BF16)
        KS = b.alloc("KS", [128, 2, S_LEN], BF16)
        KW = b.alloc("KW", [128, 2, S_LEN], BF16)
        VS = b.alloc("VS", [128, NT, 2, 65], BF16)
        VW = b.alloc("VW", [128, NT, 2, 65], BF16)
        GATES = b.alloc("GATES", [128, NT, 24], F32)
        KCMP = b.alloc("KCMP", [128, 2, 256], BF16)
        CV = b.alloc("CV", [128, 2, 2, 129], BF16)
        nsa_state_mark = b.mark()
        KC = b.alloc("KC", [128, S_LEN], BF16)
        VC = b.alloc("VC", [128, S_LEN], BF16)
        cmp_mark = b.mark()
        WFM = b.alloc("WFM", [128, 8, 2304], BF16)
        WTM = b.alloc("WTM", [128, 8, 280], BF16)
        XTR = b.ring("XT", [128, 8, 512], BF16, 2)
        tabs = [b.ring(nm, [128, 512], dt_, 2 if nm in ("cos", "sin") else 1) for nm, dt_ in
                (("posi", I32), ("posf", F32), ("m1", F32), ("m2", F32), ("cos", F32), ("sin", F32))]
        T1 = b.ring("T1", [128, 512], F32, 2)
        T2 = b.ring("T2", [128, 512], F32, 2)

        S.pool(MEMSET(VS, 1.0), writes=["VS"])
        S.pool(MEMSET(VW, 1.0), writes=["VW"])
        for kc in range(8):
            S.dma(WFM[:, kc, :], wfm_nsa[kc * 128:(kc + 1) * 128, :], writes=["WFM"], q="pool")
        S.dma(WTM, wtm_nsa.rearrange("(kc p) n -> p kc n", p=128), writes=["WTM"], q="pool")

        rope_groups = [(QT[:, hp, :], hp, 4 + hp) for hp in range(4)] + \
                      [(KS[:, g, :], 8 + g, 10 + g) for g in range(2)] + \
                      [(KW[:, g, :], 12 + g, 14 + g) for g in range(2)]
        for tb in range(NB):
            xt, xkeys = load_xT_block(tb, XTR[tb % 2])
            tl = [t[tb % len(t)] for t in tabs]
            rope_tables(pos[:, tb * 512:(tb + 1) * 512], 512, 0, 1, 2, tl)
            cos, kcos = tl[4]
            sin, ksin = tl[5]
            tsl = slice(tb * 512, (tb + 1) * 512)
            for gi, (dst, ga, gb) in enumerate(rope_groups):
                pa, pka = proj_fm(xt, xkeys, WFM, "WFM", ga * 128)
                pb_, pkb = proj_fm(xt, xkeys, WFM, "WFM", gb * 128)
                t1, k1 = T1[gi % 2]
                t2, k2 = T2[gi % 2]
                S.dve(TT(t1, pa, cos, ALU.mult), reads=[pka, kcos], writes=[k1])
                S.dve(TT(t2, pb_, sin, ALU.mult), reads=[pkb, ksin], writes=[k2])
                S.pool(TT(dst[:, tsl], t1, t2, ALU.add), reads=[k1, k2], writes=[("ropeout", gi, tb)])
            for dst, gidx, nm in ((KC, 16, "KC"), (VC, 17, "VC")):
                pa, pka = proj_fm(xt, xkeys, WFM, "WFM", gidx * 128)
                S.act(ACT(dst[:, tsl], pa, AF.Copy), reads=[pka], writes=[(nm, tb)])
            for tt in range(4):
                kt = tb * 4 + tt
                pb_, pk = b.bank()
                for kc in range(8):
                    S.pe(MM(pb_[:, 0:280], xt[:, kc, tt * 128:(tt + 1) * 128], WTM[:, kc, :], kc == 0, kc == 7),
                         reads=["WTM"] + xkeys, writes=[pk])
                S.act(ACT(VS[:, kt, :, 0:64], pb_[:, 0:128].rearrange("p (g d) -> p g d", d=64), AF.Copy),
                      reads=[pk, "VS"], writes=[("VS", kt)])
                S.act(ACT(VW[:, kt, :, 0:64], pb_[:, 128:256].rearrange("p (g d) -> p g d", d=64), AF.Copy),
                      reads=[pk, "VW"], writes=[("VW", kt)])
                S.act(ACT(GATES[:, kt, :], pb_[:, 256:280], AF.Sigmoid), reads=[pk], writes=[("GATES", kt)])
        QTK = [("ropeout", gi, tb) for gi in range(8) for tb in range(NB)]
        b.dump("QT", QT[:, 0, :], QTK, BF16)
        b.dump("KS", KS[:, 1, :], QTK, BF16)
        b.dump("GATES", GATES, [("GATES", kt) for kt in range(NT)])
        b.dump("VS", VS, [("VS", kt) for kt in range(NT)], BF16)
        if stop_after == "1a":
            S.emit(es)
            return nc, b
        b.release(cmp_mark)

        W1 = b.alloc("W1", [128, 32, 128], BF16)
        POST = b.alloc("POST", [128, 32, 2], BF16)
        W2K = b.alloc("W2K", [128, 256], BF16)
        W2V = b.alloc("W2V", [128, 64], BF16)
        B2V = b.alloc("B2V", [128, 64], F32)
        C1 = b.alloc("C1", [128, 2], F32)
        U = b.alloc("U", [128, 256], F32)
        U2 = b.alloc("U2", [128, 256], F32)
        U3 = b.alloc("U3", [128, 256], F32)
        GT = b.alloc("GT", [128, 256], BF16)
        ctab = [b.ring(nm, [128, 512], dt_, 1) for nm, dt_ in
                (("cposi", I32), ("cposf", F32), ("cm1", F32), ("cm2", F32), ("ccos", F32), ("csin", F32))]
        ctl = [t[0] for t in ctab]
        rope_tables(posc, 256, 0, 1, 2, ctl)
        ccos, kccos = ctl[4]
        csin, kcsin = ctl[5]
        S.dma(W2K, w2k_d, writes=["W2K"], q="pool")
        S.dma(W2V, w2v_d, writes=["W2V"], q="pool")
        S.dma(B2V, b2v_d.partition_broadcast(128), writes=["B2V"])
        S.dma(CV[:, 0, 0, 0:64], cover_d[0:128, :], reads=[], writes=[("CVc", 0)], q="pool")
        S.dma(CV[:, 1, 0, 0:64], cover_d[128:256, :], reads=[], writes=[("CVc", 1)], q="pool")
        S.pool(MEMSET(GT, 0.0), writes=["GT"])
        for which in range(2):
            src = KC if which == 0 else VC
            S.dma(W1, (w1k_d if which == 0 else w1v_d).rearrange("p (l h) -> p l h", h=128), writes=["W1"], q="pool")
            S.dma(POST, (poskT_d if which == 0 else posvT_d).rearrange("p (l t) -> p l t", t=2), writes=["POST"], q="pool")
            pc, pkc = b.bank()
            for l in range(32):
                S.pe(MM(pc[:, 0:2], W1[0:64, l, :], POST[0:64, l, :], l == 0, l == 31), reads=["W1", "POST"], writes=[pkc])
            S.dve(TS(C1, pc[:, 0:2], PSC(6 + which), None, ALU.add), reads=[pkc, "psc"], writes=["C1"])
            for g in range(2):
                rows = slice(g * 64, (g + 1) * 64)
                ph, pkh = b.bank()
                for l in range(32):
                    S.pe(MM(ph[:, 0:255], W1[rows, l, :], src[rows, l:l + 16 * 254 + 1:16], l == 0, l == 31),
                         reads=["W1"] + [("KC" if which == 0 else "VC", tb) for tb in range(NB)], writes=[pkh])
                S.act(ACT(U[:, 0:255], ph[:, 0:255], AF.Identity, bias=C1[:, 0:1], scale=1.0), reads=[pkh, "C1"], writes=["U"])
                S.dve(TT(U2[:, 0:255], U[:, 0:255], U[:, 0:255], ALU.mult), reads=["U"], writes=["U2"])
                S.dve(TS(U2[:, 0:255], U2[:, 0:255], 0.044715, 1.0, ALU.mult, ALU.add), reads=["U2"], writes=["U2"])
                S.dve(TT(U2[:, 0:255], U2[:, 0:255], U[:, 0:255], ALU.mult), reads=["U2", "U"], writes=["U2"])
                S.act(ACT(U3[:, 0:255], U2[:, 0:255], AF.Tanh, scale=0.7978845608028654), reads=["U2"], writes=["U3"])
                S.dve(TS(U3[:, 0:255], U3[:, 0:255], 0.5, 0.5, ALU.mult, ALU.add), reads=["U3"], writes=["U3"])
                S.dve(TT(GT[:, 0:255], U3[:, 0:255], U[:, 0:255], ALU.mult), reads=["U3", "U", "GT"], writes=["GT"])
                if which == 0:
                    pa, pka = b.bank()
                    pb_, pkb = b.bank()
                    S.pe(MM(pa[:, 0:256], W2K[:, 0:128], GT, True, True), reads=["W2K", "GT"], writes=[pka])
                    S.pe(MM(pb_[:, 0:256], W2K[:, 128:256], GT, True, True), reads=["W2K", "GT"], writes=[pkb])
                    S.dve(STT(U[:, 0:256], pa[:, 0:256], PSC(8), ccos[:, 0:256], ALU.add, ALU.mult),
                          reads=[pka, kccos, "psc", "U"], writes=["U"])
                    S.dve(STT(U2[:, 0:256], pb_[:, 0:256], PSC(9), csin[:, 0:256], ALU.add, ALU.mult),
                          reads=[pkb, kcsin, "psc", "U2"], writes=["U2"])
                    S.pool(TT(KCMP[:, g, :], U[:, 0:256], U2[:, 0:256], ALU.add), reads=["U", "U2"], writes=[("KCMP", g)])
                else:
                    for nt in range(2):
                        pv, pkv = b.bank()
                        S.pe(MM(pv[:, 0:64], GT[:, nt * 128:(nt + 1) * 128], W2V, True, True), reads=["GT", "W2V"], writes=[pkv])
                        S.dve(TT(CV[:, nt, g, 65:129], pv[:, 0:64], B2V, ALU.add), reads=[pkv, "B2V"], writes=[("CVv", nt, g)])
        for nt in range(2):
            S.pool(CP(CV[:, nt, 1, 0:64], CV[:, nt, 0, 0:64]), reads=[("CVc", nt)], writes=[("CVc2", nt)])
            S.pool(MEMSET(CV[:, nt, :, 64:65], 1.0), writes=[("CVo", nt)])
        CVK = [("CVc", nt) for nt in range(2)] + [("CVc2", nt) for nt in range(2)] + [("CVo", nt) for nt in range(2)] + \
              [("CVv", nt, g) for nt in range(2) for g in range(2)]
        b.dump("KCMP", KCMP, [("KCMP", 0), ("KCMP", 1)], BF16)
        b.dump("CV", CV, CVK, BF16)
        if stop_after == "1ap":
            S.emit(es)
            return nc, b
        b.release(nsa_state_mark)

        ETAB = b.alloc("ETAB", [128, 32, 128], BF16)
        S.pool(MEMSET(ETAB, 1.0), writes=["ETAB"])
        S.pool(ASEL(ETAB, ETAB, [[-2, 32], [-1, 2], [0, 64]], ALU.is_equal, 0.0, 0, 1), reads=["ETAB"], writes=["ETAB"])
        MASKC = b.ring("MASKC", [128, 2, 512], BF16, 2)
        PT = b.ring("PT", [128, 512], BF16, 6)
        ONSA = b.alloc("ONSA", [128, 4, 512], F32)
        ONSAB = b.alloc("ONSAB", [128, 4, 512], BF16)
        OTB = b.ring("OTB", [128, 4, 512], BF16, 2)
        IMP = b.alloc("IMP", [128, 2, 4, 64], F32)
        CANDT = b.ring("CANDT", [128, 4, 64], F32, 2)
        FORCT = b.ring("FORCT", [128, 4, 64], F32, 2)
        SM = b.ring("SM", [128, 16], F32, 4)
        IMPM = b.alloc("IMPM", [128, 64], F32)
        SCR = b.alloc("SCR", [128, 64], F32)
        M8 = b.alloc("M8", [128, 16], F32)
        SEL = b.alloc("SEL", [128, 64], F32)
        NEGB8 = b.alloc("NEGB8", [128, 8, 64], BF16)
        NEGT = b.alloc("NEGT", [128, 2, 512], BF16)
        S.pool(MEMSET(NEGT, 0.0), writes=[("NEGT", 0), ("NEGT", 1)])
        pt_i = [0]
        sm_i = [0]
        KSX = [b.alloc("KSL", [128, 2, S_LEN], BF16), b.alloc("KSH", [128, 2, S_LEN], BF16)]
        KWX = [b.alloc("KWL", [128, 2, S_LEN], BF16), b.alloc("KWH", [128, 2, S_LEN], BF16)]
        KCX = [b.alloc("KCL", [128, 2, 256], BF16), b.alloc("KCH", [128, 2, 256], BF16)]
        for half in range(2):
            rws = slice(half * 64, half * 64 + 64)
            for dst, src, nm in ((KSX[half], KS, "KSX"), (KWX[half], KW, "KWX"), (KCX[half], KCMP, "KCX")):
                if nm == "KSX":
                    S.dve(MEMSET(dst, 0.0), writes=[(nm, half)])
                else:
                    S.pool(MEMSET(dst, 0.0), writes=[(nm, half)])
                if nm == "KSX":
                    S.act(ACT(dst[rws], src[rws], AF.Copy), reads=[(nm, half)], writes=[(nm, half)])
                else:
                    S.dve(CP(dst[rws], src[rws]), reads=[(nm, half)], writes=[(nm, half)])

        ATT = {"staged": []}
        DEPTH = 3
        NPT = 6

        def _emit_pv(item):
            (kt, jl, jh, pt, pkt, accs, nper, Vfn, v_keys, last_kt, is_last, post_fn) = item
            for j in range(jl, jh + 1):
                acc, pka, j0 = accs[j // nper]
                S.pe(MM(acc[:, j - j0, :], pt[:, j * 128:(j + 1) * 128], Vfn(kt), False, last_kt[j] == kt),
                     reads=[pkt] + v_keys, writes=[pka])
            if is_last:
                for (_a, pka, _j) in accs:
                    b.unpin(pka)
                post_fn(accs)

        def att_flush():
            while ATT["staged"]:
                _emit_pv(ATT["staged"].pop(0))

        def attention(QTa, KTa, Vfn, tiles, qb, scale, ncols, q_keys, k_keys, v_keys, post_fn, blockmask=None):
            nper = min(4, 512 // ncols)
            accs = []
            for j0 in range(0, 4, nper):
                pa, pka = b.bank(pin=True)
                S.pe(MM(pa[:, 0:nper * ncols], zeros[:, 0:128], zeros[:, 0:nper * ncols], True, True), reads=["zeros"], writes=[pka])
                accs.append((pa[:, 0:nper * ncols].rearrange("p (j c) -> p j c", c=ncols), pka, j0))
            last_kt = {}
            for (kt, jl, jh, masks) in tiles:
                for j in range(jl, jh + 1):
                    last_kt[j] = kt
            for ti, (kt, jl, jh, masks) in enumerate(tiles):
                c0, c1 = jl * 128, (jh + 1) * 128
                ps, pks = b.bank()
                nmm = 1 + (1 if blockmask is not None else 0) + len(masks)
                done = 1
                S.pe(MM(ps[:, c0:c1], KTa[:, kt * 128:(kt + 1) * 128], QTa[:, qb * 512 + c0:qb * 512 + c1], True, done == nmm),
                     reads=q_keys + k_keys, writes=[pks])
                if blockmask is not None:
                    done += 1
                    negt, nkey = blockmask
                    S.pe(MM(ps[:, c0:c1], ETAB[:, kt, :], negt[:, c0:c1], False, done == nmm),
                         reads=["ETAB", nkey], writes=[pks])
                for (kind, j) in masks:
                    done += 1
                    S.pe(MM(ps[:, j * 128:(j + 1) * 128], ident, caus if kind == "c" else wedge, False, done == nmm),
                         reads=["ident", "caus", "wedge"], writes=[pks])
                pt, pkt = PT[pt_i[0] % NPT]
                pt_i[0] += 1
                S.act(ACT(pt[:, c0:c1], ps[:, c0:c1], AF.Exp, scale=scale), reads=[pks], writes=[pkt])
                ATT["staged"].append((kt, jl, jh, pt, pkt, accs, nper, Vfn, v_keys, last_kt, ti == len(tiles) - 1, post_fn))
                while len(ATT["staged"]) > DEPTH:
                    _emit_pv(ATT["staged"].pop(0))

        def recip_sums(accs, ncols, sumcol):
            sm, smk = SM[sm_i[0] % 4]
            sm_i[0] += 1
            for (acc, pka, j0) in accs:
                nper = acc.shape[1]
                S.dve(TS(sm[:, j0:j0 + nper], acc[:, :, sumcol], 1e-30, None, ALU.max), reads=[pka], writes=[smk])
            S.dve(RECIP(sm[:, 4:8], sm[:, 0:4]), reads=[smk], writes=[smk])
            return sm, smk

        ALLQ = []
        for qb in range(NB):
            mk, mkk = MASKC[qb % 2]
            for nt in range(2):
                S.pool(ASEL(mk[:, nt, :], zeros, [[1, 512]], ALU.is_ge, NEG, qb * 512 - 2048 * nt - 31, -16),
                       reads=["zeros"], writes=[(mkk, nt)])
            cand, candk = CANDT[qb % 2]
            forc, forck = FORCT[qb % 2]
            S.dma(cand, cand_d[qb * 512:(qb + 1) * 512, :].rearrange("(j p) c -> p j c", p=128), writes=[candk])
            S.dma(forc, forced_d[qb * 512:(qb + 1) * 512, :].rearrange("(j p) c -> p j c", p=128), writes=[forck])
            if stop_after == "2a_tab":
                b.dump("ETAB", ETAB, ["ETAB"], BF16)
                b.dump("MASKC", mk, [(mkk, 0), (mkk, 1)], BF16)
                b.dump("CAND", cand, [candk])
                S.emit(es)
                return nc, b
            gview = lambda br, h: GATES[:, qb * 4:(qb + 1) * 4, h * 3 + br]
            gkeys = [("GATES", kt) for kt in range(qb * 4, qb * 4 + 4)]
            def cmp_scores(h):
                g, hp, half = h // 4, h // 2, h % 2
                ets = []
                for nt in range(2):
                    ps, pks = b.bank()
                    S.pe(MM(ps, KCX[half][:, g, nt * 128:(nt + 1) * 128], QT[:, hp, qb * 512:(qb + 1) * 512], True, False),
                         reads=[("KCX", half)], writes=[pks])
                    S.pe(MM(ps, ident, mk[:, nt, :], False, True), reads=["ident", (mkk, nt)], writes=[pks])
                    pt, pkt = PT[pt_i[0] % 6]
                    pt_i[0] += 1
                    S.act(ACT(pt, ps, AF.Exp, scale=0.125), reads=[pks], writes=[pkt])
                    ets.append((pt, pkt))
                return ets

            def cmp_pv(h, ets):
                g = h // 4
                accs = []
                for j0 in (0, 2):
                    pa, pka = b.bank()
                    acc = pa[:, 0:258].rearrange("p (j c) -> p j c", c=129)
                    for j in (j0, j0 + 1):
                        for nt in range(2):
                            S.pe(MM(acc[:, j - j0, :], ets[nt][0][:, j * 128:(j + 1) * 128], CV[:, nt, g, :], nt == 0, nt == 1),
                                 reads=[ets[nt][1]], writes=[pka])
                    accs.append((acc, pka, j0))
                sm, smk = recip_sums(accs, 129, 64)
                S.dve(TT(sm[:, 8:12], sm[:, 4:8], gview(0, h), ALU.mult), reads=[smk] + gkeys, writes=[smk])
                for (acc, pka, j0) in accs:
                    for j in (j0, j0 + 1):
                        S.dve(TS(ONSA[:, j, h * 64:(h + 1) * 64], acc[:, j - j0, 65:129], sm[:, 8 + j:9 + j], None, ALU.mult),
                              reads=[pka, smk], writes=[("ONSA", h)])
                        if qb < 2:
                            pass
                        elif h % 4 == 0:
                            S.dve(TS(IMP[:, g, j, :], acc[:, j - j0, 0:64], sm[:, 4 + j:5 + j], None, ALU.mult),
                                  reads=[pka, smk], writes=[("IMP", g)])
                        else:
                            S.dve(STT(IMP[:, g, j, :], acc[:, j - j0, 0:64], sm[:, 4 + j:5 + j], IMP[:, g, j, :], ALU.mult, ALU.add),
                                  reads=[pka, smk, ("IMP", g)], writes=[("IMP", g)])

            ets_cur = cmp_scores(0)
            for h in range(8):
                ets_next = cmp_scores(h + 1) if h + 1 < 8 else None
                cmp_pv(h, ets_cur)
                ets_cur = ets_next
            if stop_after == "2a_cmp":
                b.dump("IMP", IMP, [("IMP", 0), ("IMP", 1)])
                b.dump("ONSA_c", ONSA, [("ONSA", h) for h in range(8)])
                S.emit(es)
                return nc, b
            if qb == 1:
                b.dump("IMP", IMP, [("IMP", 0), ("IMP", 1)])
                b.dump("ONSA_c", ONSA, [("ONSA", h) for h in range(8)])
            def sel_dve():
                for g in range(2):
                    for j in range(4):
                        negb = NEGB8[:, g * 4 + j, :]
                        S.dve(TT(IMPM, IMP[:, g, j, :], cand[:, j, :], ALU.mult), reads=[("IMP", g), candk], writes=["IMPM"])
                        S.dve(lambda e: e.max(out=M8[:, 0:8], in_=IMPM), reads=["IMPM"], writes=["M8a"])
                        S.dve(lambda e: e.match_replace(out=SCR, in_to_replace=M8[:, 0:8], in_values=IMPM, imm_value=-1.0),
                              reads=["IMPM", "M8a"], writes=["SCR"])
                        S.dve(lambda e: e.max(out=M8[:, 8:16], in_=SCR), reads=["SCR"], writes=["M8b"])
                        S.dve(TS(SEL, IMPM, M8[:, 12:13], None, ALU.is_ge), reads=["IMPM", "M8b"], writes=["SEL"])
                        S.dve(TT(SEL, SEL, cand[:, j, :], ALU.mult), reads=["SEL", candk], writes=["SEL"])
                        S.dve(TT(SEL, SEL, forc[:, j, :], ALU.add), reads=["SEL", forck], writes=["SEL"])
                        S.dve(TS(negb, SEL, -1.0, -NEG, ALU.add, ALU.mult), reads=["SEL"], writes=[("NEGB", g, j)])

            def sel_pe():
                for g in range(2):
                    for j in range(4):
                        negb = NEGB8[:, g * 4 + j, :]
                        pb_, pk = b.bank()
                        ptr = pb_.bitcast(BF16)
                        S.pe(TR(ptr[0:64, 0:128], negb, ident), reads=[("NEGB", g, j), "ident"], writes=[pk])
                        S.act(ACT(NEGT[0:64, g, j * 128:(j + 1) * 128], ptr[0:64, 0:128], AF.Copy), reads=[pk], writes=[("NEGT", g)])

            if qb >= 2:
                sel_dve()
            if stop_after == "2a_sel":
                b.dump("NEGT", NEGT, [("NEGT", 0), ("NEGT", 1)], BF16)
                S.emit(es)
                return nc, b
            if qb == 3:
                b.dump("NEGT", NEGT, [("NEGT", 0), ("NEGT", 1)], BF16)
            for br in (2, 1):
                if br == 1 and qb >= 2:
                    sel_pe()
                for h in range(8):
                    g, hp, half = h // 4, h // 2, h % 2
                    QTa = QT[:, hp, :]
                    if br == 1:
                        KTa = KSX[half][:, g, :]
                        kkeys = [("KSX", half)]
                        Vt = VS
                        tiles = []
                        for kt in range(0, 4 * qb + 4):
                            jl = max(kt - 4 * qb, 0)
                            masks = [("c", kt - 4 * qb)] if kt >= 4 * qb else []
                            tiles.append((kt, jl, 3, masks))
                        bm = (NEGT[:, g, :], ("NEGT", g)) if qb >= 2 else None
                    else:
                        KTa = KWX[half][:, g, :]
                        kkeys = [("KWX", half)]
                        Vt = VW
                        tiles = []
                        for kt in range(max(0, 4 * qb - 4), 4 * qb + 4):
                            jl = max(kt - 4 * qb, 0)
                            jh = min(kt + 4 - 4 * qb, 3)
                            masks = []
                            if kt >= 4 * qb:
                                masks.append(("c", kt - 4 * qb))
                            if 0 <= kt + 4 - 4 * qb <= 3:
                                masks.append(("w", kt + 4 - 4 * qb))
                            tiles.append((kt, jl, jh, masks))
                        bm = None

                    def post_nsa(accs, h=h, br=br, qb=qb):
                        sm, smk = recip_sums(accs, 65, 64)
                        S.dve(TT(sm[:, 8:12], sm[:, 4:8], GATES[:, qb * 4:(qb + 1) * 4, h * 3 + br], ALU.mult),
                              reads=[smk], writes=[smk])
                        acc, pka, _ = accs[0]
                        for j in range(4):
                            dst = ONSA[:, j, h * 64:(h + 1) * 64]
                            S.dve(STT(dst, acc[:, j, 0:64], sm[:, 8 + j:9 + j], dst, ALU.mult, ALU.add),
                                  reads=[pka, smk, ("ONSA", h)], writes=[("ONSA", h)])

                    attention(QTa, KTa, lambda kt, Vt=Vt, g=g: Vt[:, kt, g, :], tiles, qb, 0.125, 65,
                              [], kkeys, [], post_nsa, blockmask=bm)
            att_flush()
            if qb == 3:
                b.dump("ONSA", ONSA, [("ONSA", h) for h in range(8)])
            S.act(ACT(ONSAB, ONSA, AF.Copy), reads=[("ONSA", h) for h in range(8)], writes=["ONSAB"])
            otb, otk = OTB[qb % 2]
            for fc in range(4):
                pb_, pk = b.bank()
                ptr = pb_.bitcast(BF16)
                for j in range(4):
                    S.pe(TR(ptr[:, j * 128:(j + 1) * 128], ONSAB[:, j, fc * 128:(fc + 1) * 128], ident),
                         reads=["ONSAB", "ident"], writes=[pk])
                S.dve(CP(otb[:, fc, :], ptr[:, 0:512]), reads=[pk], writes=[(otk, fc)])
            S.dma(oT_s[0:512, qb * 512:(qb + 1) * 512].rearrange("(fc p) t -> p fc t", p=128), otb,
                  reads=[(otk, fc) for fc in range(4)], writes=[("oT_s", 0, qb)])
            if stop_after == "2a_qb0":
                b.dump("ONSA0", ONSA, [("ONSA", h) for h in range(8)])
                S.emit(es)
                return nc, b
        if stop_after == "2a":
            S.emit(es)
            return nc, b
        b.release(base_mark)

        QM = b.alloc("QM", [128, 8, S_LEN], BF16)
        NMKV = b.alloc("NMKV", [128, 2, S_LEN], BF16)
        KPE = b.alloc("KPE", [128, S_LEN], BF16)
        S.dve(MEMSET(QM[96:128], 0.0), writes=["QMpad"])
        mla_mark = b.mark()
        WFM2 = b.alloc("WFM2", [128, 8, 896], BF16)
        WUQA = b.alloc("WUQA", [128, 3, 768], BF16)
        WUQB = b.alloc("WUQB", [128, 3, 768], BF16)
        XTR = b.ring("XTb", [128, 8, 512], BF16, 2)
        tabs = [b.ring(nm, [128, 512], dt_, 2) for nm, dt_ in
                (("bposi", I32), ("bposf", F32), ("bm1", F32), ("bm2", F32), ("bcos", F32), ("bsin", F32))]
        T1 = b.ring("bT1", [128, 512], F32, 2)
        T2 = b.ring("bT2", [128, 512], F32, 2)
        SQ = b.ring("SQ", [128, 512], BF16, 3)
        RR = b.ring("RR", [128, 512], F32, 2)
        NMQ = b.ring("NMQ", [128, 3, 512], BF16, 2)
        for kc in range(8):
            S.dma(WFM2[:, kc, :], wfm_mla[kc * 128:(kc + 1) * 128, :], writes=["WFM2"], q="pool")
        S.dma(WUQA, wuq_a_d.rearrange("(kc p) n -> p kc n", p=128), writes=["WUQA"], q="pool")
        S.dma(WUQB, wuq_b_d.rearrange("(kc p) n -> p kc n", p=128), writes=["WUQB"], q="pool")
        pe_rows = slice(64, 96)
        for tb in range(NB):
            xt, xkeys = load_xT_block(tb, XTR[tb % 2])
            tl = [t[tb % len(t)] for t in tabs]
            rope_tables(pos[:, tb * 512:(tb + 1) * 512], 512, 3, 4, 5, tl)
            cos, kcos = tl[4]
            sin, ksin = tl[5]
            tsl = slice(tb * 512, (tb + 1) * 512)

            def rmsnorm_group(g0, nchunks, width, gcol, dst_fn, dkey):
                pbs = [proj_fm(xt, xkeys, WFM2, "WFM2", (g0 + c) * 128) for c in range(nchunks)]
                pss, pkss = b.bank()
                for c in range(nchunks):
                    sq, sqk = SQ[c]
                    S.act(ACT(sq, pbs[c][0], AF.Square), reads=[pbs[c][1]], writes=[sqk])
                    S.pe(MM(pss, ones, sq, c == 0, c == nchunks - 1), reads=["ones", sqk], writes=[pkss])
                rr, rrk = RR[0]
                r2, r2k = RR[1]
                S.act(ACT(rr, pss, AF.Sqrt, scale=1.0 / width, bias=1e-6), reads=[pkss], writes=[rrk])
                S.dve(RECIP(r2, rr), reads=[rrk], writes=[r2k])
                for c in range(nchunks):
                    S.dve(STT(dst_fn(c), pbs[c][0], PSC(gcol + c), r2, ALU.mult, ALU.mult),
                          reads=[pbs[c][1], r2k, "psc"], writes=[(dkey, c)])

            nmq, nmqk = NMQ[tb % 2]
            rmsnorm_group(0, 3, 384.0, 10, lambda c: nmq[:, c, :], nmqk)
            rmsnorm_group(3, 2, 256.0, 13, lambda c: NMKV[:, c, tsl], ("NMKV", tb))
            pa, pka = proj_fm(xt, xkeys, WFM2, "WFM2", 5 * 128)
            pb_, pkb = proj_fm(xt, xkeys, WFM2, "WFM2", 6 * 128)
            t1, k1 = T1[0]
            t2, k2 = T2[0]
            S.dve(TT(t1[pe_rows], pa[pe_rows], cos[pe_rows], ALU.mult), reads=[pka, kcos], writes=[k1])
            S.dve(TT(t2[pe_rows], pb_[pe_rows], sin[pe_rows], ALU.mult), reads=[pkb, ksin], writes=[k2])
            S.pool(TT(KPE[pe_rows, tsl], t1[pe_rows], t2[pe_rows], ALU.add), reads=[k1, k2], writes=[("KPE", tb)])
            for h in range(8):
                pa, pka = b.bank()
                pb_, pkb = b.bank()
                for c in range(3):
                    S.pe(MM(pa[0:96, :], WUQA[:, c, h * 96:(h + 1) * 96], nmq[:, c, :], c == 0, c == 2),
                         reads=["WUQA"] + [(nmqk, cc) for cc in range(3)], writes=[pka])
                for c in range(3):
                    S.pe(MM(pb_[0:96, :], WUQB[:, c, h * 96:(h + 1) * 96], nmq[:, c, :], c == 0, c == 2),
                         reads=["WUQB"] + [(nmqk, cc) for cc in range(3)], writes=[pkb])
                S.act(ACT(QM[0:64, h, tsl], pa[0:64, :], AF.Copy), reads=[pka], writes=[("QMn", h, tb)])
                t1, k1 = T1[(h + 1) % 2]
                t2, k2 = T2[(h + 1) % 2]
                S.dve(TT(t1[pe_rows], pa[pe_rows], cos[pe_rows], ALU.mult), reads=[pka, kcos], writes=[k1])
                S.dve(TT(t2[pe_rows], pb_[pe_rows], sin[pe_rows], ALU.mult), reads=[pkb, ksin], writes=[k2])
                S.pool(TT(QM[pe_rows, h, tsl], t1[pe_rows], t2[pe_rows], ALU.add), reads=[k1, k2], writes=[("QMp", h, tb)])
        QMK = [("QMn", h, tb) for h in range(8) for tb in range(NB)] + [("QMp", h, tb) for h in range(8) for tb in range(NB)]
        NMKVK = [(("NMKV", tb), c) for tb in range(NB) for c in range(2)]
        KPEK = [("KPE", tb) for tb in range(NB)]
        b.dump("QM", QM[0:96, 0, :], QMK, BF16)
        b.dump("NMKV", NMKV[:, 0, :], NMKVK, BF16)
        b.dump("KPE", KPE[64:96, :], KPEK, BF16)
        if stop_after == "1b":
            S.emit(es)
            return nc, b
        b.release(mla_mark)

        WUK = b.alloc("WUK", [128, 2, 512], BF16)
        WUV = b.alloc("WUV", [128, 2, 512], BF16)
        KM = b.ring("KM", [128, S_LEN], BF16, 2)
        VM = b.alloc("VM", [128, NT, 2, 65], BF16)
        PT = b.ring("PTb", [128, 512], BF16, 6)
        SM = b.ring("SMb", [128, 16], F32, 4)
        OM = b.ring("OM", [128, 4, 128], BF16, 2)
        OTB2 = b.ring("OTB2", [128, 512], BF16, 2)
        S.dma(WUK, wuk_d.rearrange("(kc p) n -> p kc n", p=128), writes=["WUK"], q="pool")
        S.dma(WUV, wuv_d.rearrange("(kc p) n -> p kc n", p=128), writes=["WUV"], q="pool")
        S.pool(MEMSET(VM, 1.0), writes=["VM"])
        for (km_, kmk_) in KM:
            S.pool(MEMSET(km_[96:128, :], 0.0), writes=[(kmk_, "pad")])
        mla_scale = 96.0 ** -0.5
        for hp in range(4):
            for hh in range(2):
                h = hp * 2 + hh
                km, kmk = KM[hh]
                for kb in range(NB):
                    pa, pka = b.bank()
                    for c in range(2):
                        S.pe(MM(pa[0:64, :], WUK[:, c, h * 64:(h + 1) * 64], NMKV[:, c, kb * 512:(kb + 1) * 512], c == 0, c == 1),
                             reads=["WUK"], writes=[pka])
                    S.dve(CP(km[0:64, kb * 512:(kb + 1) * 512], pa[0:64, :]), reads=[pka], writes=[(kmk, kb)])
                S.pool(CP(km[pe_rows, :], KPE[pe_rows, :]), reads=[], writes=[(kmk, "pe")])
            for kt in range(NT):
                pa, pka = b.bank()
                for c in range(2):
                    S.pe(MM(pa[:, 0:128], NMKV[:, c, kt * 128:(kt + 1) * 128], WUV[:, c, hp * 128:(hp + 1) * 128], c == 0, c == 1),
                         reads=["WUV"], writes=[pka])
                S.dve(CP(VM[:, kt, :, 0:64], pa[:, 0:128].rearrange("p (g d) -> p g d", d=64)),
                      reads=[pka, "VM"], writes=[("VM", kt)])
            if hp == 0:
                b.dump("KM0", KM[0][0][0:96, :], [("KM0", kb) for kb in range(NB)] + [("KM0", "pe")], BF16)
                b.dump("VM", VM, [("VM", kt) for kt in range(NT)], BF16)
            for qb in range(NB):
                om, omk = OM[qb % 2]
                for hh in range(2):
                    h = hp * 2 + hh
                    km, kmk = KM[hh]
                    tiles = []
                    for kt in range(0, 4 * qb + 4):
                        jl = max(kt - 4 * qb, 0)
                        masks = [("c", kt - 4 * qb)] if kt >= 4 * qb else []
                        tiles.append((kt, jl, 3, masks))

                    def post_mla(accs, hh=hh, qb=qb, om=om, omk=omk, hp=hp):
                        sm, smk = recip_sums(accs, 65, 64)
                        acc, pka, _ = accs[0]
                        for j in range(4):
                            S.dve(TS(om[:, j, hh * 64:(hh + 1) * 64], acc[:, j, 0:64], sm[:, 4 + j:5 + j], None, ALU.mult),
                                  reads=[pka, smk], writes=[(omk, hh)])
                        if hh == 1:
                            pb_, pk = b.bank()
                            ptr = pb_.bitcast(BF16)
                            for j in range(4):
                                S.pe(TR(ptr[:, j * 128:(j + 1) * 128], om[:, j, :], ident), reads=[(omk, 0), (omk, 1), "ident"], writes=[pk])
                            otb, otk = OTB2[qb % 2]
                            S.dve(CP(otb, ptr[:, 0:512]), reads=[pk], writes=[otk])
                            S.dma(oT_s[512 + hp * 128:512 + (hp + 1) * 128, qb * 512:(qb + 1) * 512], otb, reads=[otk],
                                  writes=[("oT_s", 1 + hp, qb)])

                    attention(QM[:, h, :], km, lambda kt, hh=hh: VM[:, kt, hh, :], tiles, qb, mla_scale, 65,
                              [], [(kmk, kb) for kb in range(NB)] + [(kmk, "pe"), (kmk, "pad")],
                              [("VM", kt) for kt in range(NT)] + ["VM"], post_mla)
            att_flush()
        if stop_after == "2b":
            S.emit(es)
            return nc, b
        b.release(base_mark)

        LNT = b.alloc("LNT", [128, 6, D], F32)
        CW = b.alloc("CW", [128, NCH, 3], F32)
        CB = b.alloc("CB", [128, NCH], F32)
        HIST = b.alloc("HIST", [128, NCH, 2], F32)
        KMEM = b.alloc("KMEM", [128, 8, 256], BF16)
        VMEM = b.alloc("VMEM", [128, 2, 4, 257], BF16)
        NWS = 6
        WS = b.ring("WS", [128, 8, 512], BF16, NWS)
        OTL = b.ring("OTL", [128, 8, 512], BF16, 1)
        XIN = b.ring("XIN", [128, D], F32, 2)
        Y = b.alloc("Y", [128, D], F32)
        XR = b.alloc("XR", [128, 4, D], F32)
        XB = b.alloc("XB", [128, D], BF16)
        XRT = b.alloc("XRT", [128, 8, 512], BF16)
        QME = b.alloc("QME", [128, 8, 512], BF16)
        OME = b.alloc("OME", [128, 4, D], BF16)
        OMET = b.alloc("OMET", [128, 8, 512], BF16)
        HT = b.alloc("HT", [128, NCH, 512], BF16)
        GC = b.ring("GC", [128, 514], F32, 2)
        GA = b.ring("GA", [128, 512], F32, 2)
        GS = b.ring("GS", [128, 512], F32, 2)
        PT = b.ring("PTc", [128, 512], BF16, 4)
        SM = b.ring("SMc", [128, 16], F32, 4)
        STAT = b.alloc("STAT", [128, 32], F32)
        MEMTB = b.alloc("MEMTB", [128, 8, 256], BF16)

        S.dma(LNT.rearrange("p k d -> p (k d)"), lngb_d.rearrange("(o k) d -> o (k d)", o=1).partition_broadcast(128),
              writes=["LNT"])
        S.dma(CW, convw_d.rearrange("p (c k) -> p c k", k=3), writes=["CW"])
        S.dma(CB, convb_d, writes=["CB"])
        S.pool(MEMSET(HIST, 0.0), writes=["HIST"])
        S.pool(MEMSET(VMEM, 1.0), writes=["VMEM"])
        S.dma(MEMTB, memT.rearrange("(kc p) t -> p kc t", p=128), writes=["MEMTB"], q="pool")
        ws_i = [0]

        def stream_w(src_ap, keys_src):
            ws, wsk = WS[ws_i[0] % NWS]
            ws_i[0] += 1
            kk, nn = src_ap.shape[1], src_ap.shape[2]
            S.dma(ws[:, 0:kk, 0:nn], src_ap, reads=keys_src, writes=[wsk])
            return ws, wsk

        wo_v = wo_s.rearrange("(kc p) n -> p kc n", p=128)
        mwq_v = mwq_s.rearrange("(kc p) n -> p kc n", p=128)
        mwo_v = mwo_s.rearrange("(kc p) n -> p kc n", p=128)
        wup_v = wup_s.rearrange("(kc p) n -> p kc n", p=128)
        wdn_v = wdn_s.rearrange("(c p) n -> p c n", p=128)
        allk = lambda key, rows: [(key, r0) for r0 in range(0, rows, 128)]

        for nb in range(2):
            ws, wsk = WS[ws_i[0] % NWS]
            ws_i[0] += 1
            S.dma(ws, mwk_d.rearrange("(kc p) n -> p kc n", p=128)[:, :, nb * 512:(nb + 1) * 512], writes=[wsk], q="pool")
            for oc in range(4):
                pa, pka = b.bank()
                for kc in range(8):
                    S.pe(MM(pa[:, 0:256], ws[:, kc, oc * 128:(oc + 1) * 128], MEMTB[:, kc, :], kc == 0, kc == 7),
                         reads=[wsk, "MEMTB"], writes=[pka])
                S.act(ACT(KMEM[:, nb * 4 + oc, :], pa[:, 0:256], AF.Copy), reads=[pka], writes=[("KMEM", nb * 4 + oc)])
        for nb in range(2):
            ws, wsk = WS[ws_i[0] % NWS]
            ws_i[0] += 1
            S.dma(ws, mwv_d.rearrange("(kc p) n -> p kc n", p=128)[:, :, nb * 512:(nb + 1) * 512], writes=[wsk], q="pool")
            for kt in range(2):
                pa, pka = b.bank()
                for kc in range(8):
                    S.pe(MM(pa, MEMTB[:, kc, kt * 128:(kt + 1) * 128], ws[:, kc, :], kc == 0, kc == 7),
                         reads=[wsk, "MEMTB"], writes=[pka])
                S.act(ACT(VMEM[:, kt, nb * 2:nb * 2 + 2, 0:256], pa.rearrange("p (h d) -> p h d", d=256), AF.Copy),
                      reads=[pka, "VMEM"], writes=[("VMEM", kt, nb)])
        KMEMK = [("KMEM", i) for i in range(8)]
        VMEMK = [("VMEM", kt, nb) for kt in range(2) for nb in range(2)]

        def layer_norm(src, srcks, ln_idx, dst, dstk):
            S.dve(lambda e: e.bn_stats(out=STAT[:, 0:6], in_=src[:, 0:512]), reads=srcks, writes=["STATa"])
            S.dve(lambda e: e.bn_stats(out=STAT[:, 6:12], in_=src[:, 512:1024]), reads=srcks, writes=["STATb"])
            S.dve(lambda e: e.bn_aggr(out=STAT[:, 12:14], in_=STAT[:, 0:12].rearrange("p (a b) -> p a b", b=6)),
                  reads=["STATa", "STATb"], writes=["STATc"])
            S.act(ACT(STAT[:, 16:17], STAT[:, 13:14], AF.Sqrt, scale=1.0, bias=1e-5), reads=["STATc"], writes=["STATd"])
            S.dve(RECIP(STAT[:, 17:18], STAT[:, 16:17]), reads=["STATd"], writes=["STATe"])
            S.dve(TS(dst, src, STAT[:, 12:13], STAT[:, 17:18], ALU.subtract, ALU.mult), reads=srcks + ["STATc", "STATe"], writes=[dstk])
            S.pool(TT(dst, dst, LNT[:, 2 * ln_idx, :], ALU.mult), reads=[dstk, "LNT"], writes=[dstk])
            S.pool(TT(dst, dst, LNT[:, 2 * ln_idx + 1, :], ALU.add), reads=[dstk, "LNT"], writes=[dstk])

        def to_feature_major(src, srck, tt, tag):
            S.act(ACT(XB, src, AF.Copy), reads=[srck], writes=["XB"])
            for half in range(2):
                pb_, pk = b.bank()
                ptr = pb_.bitcast(BF16)
                for c in range(4):
                    S.pe(TR(ptr[:, c * 128:(c + 1) * 128], XB[:, (half * 4 + c) * 128:(half * 4 + c + 1) * 128], ident),
                         reads=["XB", "ident"], writes=[pk])
                S.dve(CP(XRT[:, half * 4:half * 4 + 4, tt * 128:(tt + 1) * 128],
                         ptr[:, 0:512].rearrange("p (c t) -> p c t", t=128)), reads=[pk], writes=[("XRT", tag, tt, half)])

        def res_mm_ln(tiles, lhsT_fn, lhs_keys, wblk, res_fn, ln_idx):
            for tt in tiles:
                res, resk = res_fn(tt)
                for nbk in range(2):
                    pa, pka = b.bank()
                    ws, wsk = wblk[nbk]
                    for kc in range(8):
                        S.pe(MM(pa, lhsT_fn(kc, tt), ws[:, kc, :], kc == 0, kc == 7), reads=lhs_keys + [wsk], writes=[pka])
                    S.dve(STT(Y[:, nbk * 512:(nbk + 1) * 512], res[:, nbk * 512:(nbk + 1) * 512], ALPHA, pa, ALU.mult, ALU.add),
                          reads=[resk, pka], writes=[("Y", nbk)])
                layer_norm(Y, [("Y", 0), ("Y", 1)], ln_idx, XR[:, tt, :], ("XR", tt))

        XRTK = lambda tag, tiles=range(4): [("XRT", tag, tt, half) for tt in tiles for half in range(2)]

        for tb in range(NB):
            tsl = slice(tb * 512, (tb + 1) * 512)
            GRP = ((0, 1), (2, 3))
            otl, otlk = OTL[0]
            S.dma(otl, oT_s.rearrange("(kc p) t -> p kc t", p=128)[:, :, tsl],
                  reads=[("oT_s", i, tb) for i in range(5)], writes=[otlk])

            def res_x(tt, tb=tb):
                xi, xik = XIN[tt % 2]
                S.dma(xi, x_in[tb * 512 + tt * 128:tb * 512 + (tt + 1) * 128, :], writes=[xik])
                return xi, xik

            wo_blk = [stream_w(wo_v[:, :, nbk * 512:(nbk + 1) * 512], allk("wo_s", D)) for nbk in range(2)]
            wq_blk = [stream_w(mwq_v[:, :, nbk * 512:(nbk + 1) * 512], allk("mwq_s", D)) for nbk in range(2)]
            wmo_blk = [stream_w(mwo_v[:, :, nbk * 512:(nbk + 1) * 512], allk("mwo_s", D)) for nbk in range(2)]

            def stage_q(tiles):
                c0, c1 = tiles[0] * 128, (tiles[-1] + 1) * 128
                for nbk in range(2):
                    ws, wsk = wq_blk[nbk]
                    for oc in range(4):
                        pa, pka = b.bank()
                        for kc in range(8):
                            S.pe(MM(pa[:, 0:c1 - c0], ws[:, kc, oc * 128:(oc + 1) * 128], XRT[:, kc, c0:c1], kc == 0, kc == 7),
                                 reads=[wsk] + XRTK("a", tiles), writes=[pka])
                        S.act(ACT(QME[:, nbk * 4 + oc, c0:c1], pa[:, 0:c1 - c0], AF.Copy), reads=[pka],
                              writes=[("QME", nbk * 4 + oc, tiles[0])])

            def stage_att(tiles):
                c0, c1 = tiles[0] * 128, (tiles[-1] + 1) * 128
                for h in range(4):
                    pts = []
                    for kt in range(2):
                        ps, pks = b.bank()
                        for dc in range(2):
                            S.pe(MM(ps[:, 0:c1 - c0], KMEM[:, 2 * h + dc, kt * 128:(kt + 1) * 128], QME[:, 2 * h + dc, c0:c1], dc == 0, dc == 1),
                                 reads=KMEMK + [("QME", 2 * h + dc, tiles[0])], writes=[pks])
                        pt, pkt = PT[pt_i[0] % 4]
                        pt_i[0] += 1
                        S.act(ACT(pt[:, 0:c1 - c0], ps[:, 0:c1 - c0], AF.Exp, scale=1.0 / 16.0), reads=[pks], writes=[pkt])
                        pts.append((pt, pkt))
                    for ji, j in enumerate(tiles):
                        pa, pka = b.bank()
                        for kt in range(2):
                            S.pe(MM(pa[:, 0:257], pts[kt][0][:, ji * 128:(ji + 1) * 128], VMEM[:, kt, h, :], kt == 0, kt == 1),
                                 reads=[pts[kt][1]] + VMEMK + ["VMEM"], writes=[pka])
                        sm, smk = SM[sm_i[0] % 4]
                        sm_i[0] += 1
                        S.dve(RECIP(sm[:, 0:1], pa[:, 256:257]), reads=[pka], writes=[smk])
                        S.dve(TS(OME[:, j, h * 256:(h + 1) * 256], pa[:, 0:256], sm[:, 0:1], None, ALU.mult),
                              reads=[pka, smk], writes=[("OME", j, h)])

            def stage_omet(tiles):
                for j in tiles:
                    for half in range(2):
                        pb_, pk = b.bank()
                        ptr = pb_.bitcast(BF16)
                        for c in range(4):
                            S.pe(TR(ptr[:, c * 128:(c + 1) * 128], OME[:, j, (half * 4 + c) * 128:(half * 4 + c + 1) * 128], ident),
                                 reads=[("OME", j, hh_) for hh_ in range(4)] + ["ident"], writes=[pk])
                        S.dve(CP(OMET[:, half * 4:half * 4 + 4, j * 128:(j + 1) * 128],
                                 ptr[:, 0:512].rearrange("p (c t) -> p c t", t=128)), reads=[pk], writes=[("OMET", j, half)])

            for gi, tiles in enumerate(GRP):
                res_mm_ln(tiles, lambda kc, tt: otl[:, kc, tt * 128:(tt + 1) * 128], [otlk], wo_blk, res_x, 0)
            if tb == 0:
                b.dump("X1", XR, [("XR", tt) for tt in range(4)])
            for gi, tiles in enumerate(GRP):
                for tt in tiles:
                    to_feature_major(XR[:, tt, :], ("XR", tt), tt, "a")
                stage_q(tiles)
            for gi, tiles in enumerate(GRP):
                stage_att(tiles)
            OMETK = lambda tiles: [("OMET", j, half) for j in tiles for half in range(2)]
            for gi, tiles in enumerate(GRP):
                stage_omet(tiles)
                res_mm_ln(tiles, lambda kc, tt: OMET[:, kc, tt * 128:(tt + 1) * 128], OMETK(tiles), wmo_blk,
                          lambda tt: (XR[:, tt, :], ("XR", tt)), 1)
            for gi, tiles in enumerate(GRP):
                for tt in tiles:
                    to_feature_major(XR[:, tt, :], ("XR", tt), tt, "b")
            if tb == 0:
                b.dump("X2", XR, [("XR", tt) for tt in range(4)])
            for cb in range(6):
                ncol = 512 if cb < 5 else 256
                wg, wgk = stream_w(wup_v[:, :, cb * 512:cb * 512 + ncol], allk("wup_s", D))
                wu, wuk_ = stream_w(wup_v[:, :, DFF + cb * 512:DFF + cb * 512 + ncol], allk("wup_s", D))
                for cc in range(ncol // 128):
                    c = cb * 4 + cc
                    pg, pkg = b.bank()
                    pu, pku = b.bank()
                    for kc in range(8):
                        S.pe(MM(pg, wg[:, kc, cc * 128:(cc + 1) * 128], XRT[:, kc, :], kc == 0, kc == 7),
                             reads=[wgk] + XRTK("b"), writes=[pkg])
                    for kc in range(8):
                        S.pe(MM(pu, wu[:, kc, cc * 128:(cc + 1) * 128], XRT[:, kc, :], kc == 0, kc == 7),
                             reads=[wuk_] + XRTK("b"), writes=[pku])
                    gc, gck = GC[c % 2]
                    ga, gak = GA[c % 2]
                    gs, gsk = GS[c % 2]
                    S.pool(CP(gc[:, 0:2], HIST[:, c, :]), reads=[("HIST", c)], writes=[(gck, "h")])
                    S.act(ACT(gc[:, 2:514], pg, AF.Copy), reads=[pkg], writes=[(gck, "m")])
                    S.pool(CP(HIST[:, c, :], gc[:, 512:514]), reads=[(gck, "m"), (gck, "h")], writes=[("HIST", c)])
                    S.act(ACT(ga, pg, AF.Identity, scale=CW[:, c, 2:3], bias=CB[:, c:c + 1]), reads=[pkg, "CW", "CB"], writes=[gak])
                    S.dve(STT(ga, gc[:, 1:513], CW[:, c, 1:2], ga, ALU.mult, ALU.add), reads=[(gck, "m"), (gck, "h"), gak, "CW"], writes=[gak])
                    S.dve(STT(ga, gc[:, 0:512], CW[:, c, 0:1], ga, ALU.mult, ALU.add), reads=[(gck, "m"), (gck, "h"), gak, "CW"], writes=[gak])
                    S.act(ACT(gs, ga, AF.Silu), reads=[gak], writes=[gsk])
                    S.dve(TT(HT[:, c, :], gs, pu, ALU.mult), reads=[gsk, pku], writes=[("HT", c)])
            HTK = [("HT", c) for c in range(NCH)]
            if tb == 0:
                b.dump("HT", HT, HTK, BF16)
            accb = [b.bank() for _ in range(8)]
            for c0 in range(0, NCH, 8):
                nc_ = min(8, NCH - c0)
                wblk = [stream_w(wdn_v[:, c0:c0 + nc_, nbk * 512:(nbk + 1) * 512], allk("wdn_s", DFF)) for nbk in range(2)]
                for tt in range(4):
                    for nbk in range(2):
                        pa, pka = accb[tt * 2 + nbk]
                        ws, wsk = wblk[nbk]
                        for ci in range(nc_):
                            c = c0 + ci
                            S.pe(MM(pa, HT[:, c, tt * 128:(tt + 1) * 128], ws[:, ci, :], c == 0, c == NCH - 1),
                                 reads=HTK + [wsk], writes=[pka])
            for tt in range(4):
                for nbk in range(2):
                    pa, pka = accb[tt * 2 + nbk]
                    S.dve(STT(Y[:, nbk * 512:(nbk + 1) * 512], XR[:, tt, nbk * 512:(nbk + 1) * 512], ALPHA, pa, ALU.mult, ALU.add),
                          reads=[("XR", tt), pka], writes=[("Y", nbk)])
                layer_norm(Y, [("Y", 0), ("Y", 1)], 2, XR[:, tt, :], ("XR", tt))
                S.dma(out_d[tb * 512 + tt * 128:tb * 512 + (tt + 1) * 128, :], XR[:, tt, :], reads=[("XR", tt)], out=True, q="pool")
        S.emit(es)
    return nc, b


def _rot(cols, half):
    return np.concatenate([cols[half:], cols[:half]])


def prep_shared(inp):
    f = np.float32
    w_in = np.asarray(inp["w_in"])[0]
    C1, C2, C3, C4, C5 = 512, 1280, 1304, 1688, 1944
    groups = []
    for hp in range(4):
        groups.append(np.arange(hp * 128, hp * 128 + 128))
    for hp in range(4):
        groups.append(np.concatenate([_rot(np.arange(h * 64, h * 64 + 64), 32) for h in (2 * hp, 2 * hp + 1)]))
    for base in (C1 + 256, C1 + 512):
        for g in range(2):
            c = np.arange(base + g * 64, base + g * 64 + 64)
            groups.append(np.concatenate([c, c]))
        for g in range(2):
            c = _rot(np.arange(base + g * 64, base + g * 64 + 64), 32)
            groups.append(np.concatenate([c, c]))
    groups.append(np.arange(C1, C1 + 128))
    groups.append(np.arange(C1 + 128, C1 + 256))
    wfm_nsa = np.ascontiguousarray(w_in[:, np.concatenate(groups)])
    tm_cols = np.concatenate([np.arange(C1 + 256 + 128, C1 + 512), np.arange(C1 + 512 + 128, C1 + 768), np.arange(C2, C2 + 24)])
    wtm_nsa = np.ascontiguousarray(w_in[:, tm_cols])
    wfm_mla = np.zeros((D, 896), f)
    wfm_mla[:, 0:384] = w_in[:, C3:C4]
    wfm_mla[:, 384:640] = w_in[:, C4:C5]
    wfm_mla[:, 640 + 64:640 + 96] = w_in[:, C5:C5 + 32]
    wfm_mla[:, 768 + 64:768 + 96] = w_in[:, _rot(np.arange(C5, C5 + 32), 16)]

    def w1_layout(w1):
        a = np.asarray(w1)[0].reshape(32, 64, 128).transpose(1, 0, 2)
        return np.ascontiguousarray(np.concatenate([a, a], 0).reshape(128, 32 * 128))

    def posT_layout(p):
        a = np.asarray(p)[0].T
        a = np.repeat(a[:, :, None], 2, axis=2)
        return np.ascontiguousarray(np.concatenate([a, a], 0).reshape(128, 64))

    w2k = np.asarray(inp["nsa_ck_w2"])[0]
    rc = _rot(np.arange(64), 32)
    w2k_l = np.ascontiguousarray(np.concatenate([w2k, w2k, w2k[:, rc], w2k[:, rc]], 1))
    b2k = np.asarray(inp["nsa_ck_b2"])[0]
    cover = np.zeros((256, 64), f)
    n = np.arange(256)[:, None] * 16
    j = np.arange(64)[None, :] * 64
    cover[:, :] = np.clip(np.minimum(n + 32, j + 64) - np.maximum(n, j), 0, None).astype(f) / 32.0
    t = np.arange(S_LEN)
    cur = (t // 64)[:, None]
    blk = np.arange(64)[None, :]
    forced = ((blk == 0) | (blk == cur) | (blk == cur - 1)) & (blk <= cur)
    cand = (blk >= 1) & (blk <= cur - 2)
    psc = np.zeros((128, 16), f)
    p = np.arange(128)
    d64 = p % 64
    psc[:, 0] = 10000.0 ** (-(2.0 * (d64 % 32)) / 64.0)
    sg = np.where(d64 < 32, -1.0, 1.0)
    psc[:, 1] = sg
    psc[:, 2] = -sg * np.pi
    d32 = p % 32
    psc[:, 3] = 10000.0 ** (-(2.0 * (d32 % 16)) / 32.0)
    sg2 = np.where(d32 < 16, -1.0, 1.0)
    psc[:, 4] = sg2
    psc[:, 5] = -sg2 * np.pi
    psc[:, 6] = np.asarray(inp["nsa_ck_b1"])[0]
    psc[:, 7] = np.asarray(inp["nsa_cv_b1"])[0]
    psc[:, 8] = np.concatenate([b2k, b2k])
    psc[:, 9] = np.concatenate([b2k[rc], b2k[rc]])
    psc[:, 10:13] = np.asarray(inp["mla_q_norm"])[0].reshape(3, 128).T
    psc[:, 13:15] = np.asarray(inp["mla_kv_norm"])[0].reshape(2, 128).T
    psc[:, 15] = -np.pi
    wuq = np.asarray(inp["mla_w_uq"])[0]
    wuq_b = np.zeros_like(wuq)
    for h in range(8):
        pe = np.arange(h * 96 + 64, h * 96 + 96)
        wuq_b[:, pe] = wuq[:, _rot(pe, 16)]
    wukv = np.asarray(inp["mla_w_ukv"])[0]
    wuk = np.ascontiguousarray(np.concatenate([wukv[:, h * 128:h * 128 + 64] for h in range(8)], 1))
    wuv = np.ascontiguousarray(np.concatenate([wukv[:, h * 128 + 64:h * 128 + 128] for h in range(8)], 1))
    lngb = np.stack([np.asarray(inp[k])[0] for k in ("ln1_g", "ln1_b", "ln2_g", "ln2_b", "ln3_g", "ln3_b")]).astype(f)
    convw = np.ascontiguousarray(np.asarray(inp["ffn_conv_w"])[0].T.reshape(NCH, 128, 3).transpose(1, 0, 2).reshape(128, NCH * 3))
    convb = np.ascontiguousarray(np.asarray(inp["ffn_conv_b"])[0].reshape(NCH, 128).T)
    return {
        "wfm_nsa": wfm_nsa, "wtm_nsa": wtm_nsa, "wfm_mla": wfm_mla,
        "w1k": w1_layout(inp["nsa_ck_w1"]), "w1v": w1_layout(inp["nsa_cv_w1"]),
        "poskT": posT_layout(inp["nsa_k_pos"]), "posvT": posT_layout(inp["nsa_v_pos"]),
        "w2k": w2k_l, "w2v": np.ascontiguousarray(np.asarray(inp["nsa_cv_w2"])[0]),
        "b2v": np.ascontiguousarray(np.asarray(inp["nsa_cv_b2"])[0][None, :]),
        "cover": cover, "cand": cand.astype(f), "forced": forced.astype(f), "psc": psc,
        "wuq_a": np.ascontiguousarray(wuq), "wuq_b": wuq_b, "wuk": wuk, "wuv": wuv,
        "w_o": np.ascontiguousarray(np.asarray(inp["w_o"])[0]),
        "mem_wq": np.ascontiguousarray(np.asarray(inp["mem_wq"])[0]),
        "mem_wk": np.ascontiguousarray(np.asarray(inp["mem_wk"])[0]),
        "mem_wv": np.ascontiguousarray(np.asarray(inp["mem_wv"])[0]),
        "mem_wo": np.ascontiguousarray(np.asarray(inp["mem_wo"])[0]),
        "w_up": np.ascontiguousarray(np.asarray(inp["ffn_w_up"])[0]),
        "w_dn": np.ascontiguousarray(np.asarray(inp["ffn_w_down"])[0]),
        "lngb": lngb, "convw": convw, "convb": convb,
    }


def prep_core(inp, bi):
    x = np.asarray(inp["x"])[bi]
    pos = np.asarray(inp["positions"])[bi].astype(np.int32)
    posc = np.zeros((1, 256), np.int32)
    posc[0, :255] = pos[31::16][:255]
    return {
        "xT": np.ascontiguousarray(x.T), "x": np.ascontiguousarray(x),
        "memT": np.ascontiguousarray(np.asarray(inp["mem"])[bi].T),
        "pos": np.ascontiguousarray(pos[None, :]), "posc": posc,
    }


_CACHE = {}


def kernel(**inputs):
    if "nc" not in _CACHE:
        _CACHE["nc"] = build_program()[0]
    nc = _CACHE["nc"]
    shared = prep_shared(inputs)
    in_maps = []
    for bi in range(8):
        m = dict(shared)
        m.update(prep_core(inputs, bi))
        in_maps.append(m)
    res = run_bass_kernel_spmd(nc, in_maps, core_ids=list(range(8)))
    return np.stack([np.asarray(r["out"], dtype=np.float32) for r in res.results], 0)
```

```python
import numpy as np
from contextlib import ExitStack
import concourse.bass as bass
import concourse.mybir as mybir
from concourse.bass_utils import run_bass_kernel_spmd

F32 = mybir.dt.float32
BF16 = mybir.dt.bfloat16
I32 = mybir.dt.int32
AF = mybir.ActivationFunctionType
ALU = mybir.AluOpType

S_LEN = 4096
D = 1024
NT = 32
NB = 8
DFF = 2816
NCH = 22
ALPHA = 2.0 ** 0.25
PI = float(np.pi)
NEG = -30000.0

SEM_LIMIT = 20000
DMA_RING = 8
import os as _os
SAME_ENGINE_SYNC = set(_os.environ.get("KSES", "act,dve,pool").split(","))


class Sched:
    STREAMS = ("pe", "act", "dve", "pool", "sp")

    def __init__(self, nc):
        self.nc = nc
        self.ops = []
        self.last_writer = {}
        self.readers = {}
        self.out_dmas = []

    def op(self, stream, fn, reads=(), writes=(), dma=False, out=False):
        idx = len(self.ops)
        deps = set()
        reads = list(reads) + ["PHASE"]
        for r in reads:
            w = self.last_writer.get(r)
            if w is not None:
                deps.add(w)
        for w_ in writes:
            w = self.last_writer.get(w_)
            if w is not None:
                deps.add(w)
            for rd in self.readers.get(w_, ()):
                deps.add(rd)
        for r in reads:
            self.readers.setdefault(r, []).append(idx)
        for w_ in writes:
            self.last_writer[w_] = idx
            self.readers[w_] = []
        self.ops.append(dict(stream=stream, fn=fn, deps=deps, dma=dma))
        if out:
            self.out_dmas.append(idx)
        return idx

    def pe(self, fn, reads=(), writes=()):
        return self.op("pe", fn, reads, writes)

    def act(self, fn, reads=(), writes=()):
        return self.op("act", fn, reads, writes)

    def dve(self, fn, reads=(), writes=()):
        return self.op("dve", fn, reads, writes)

    def pool(self, fn, reads=(), writes=()):
        return self.op("pool", fn, reads, writes)

    def dma(self, out_ap, in_ap, reads=(), writes=(), q="sp", out=False):
        return self.op(q, lambda e: e.dma_start(out=out_ap, in_=in_ap), reads, writes, dma=True, out=out)

    def emit(self, es):
        nc = self.nc
        ops = self.ops
        ops.append(dict(stream="sp", fn=None, deps=set(self.out_dmas), dma=False))
        n = len(ops)
        dma_count = {s: 0 for s in self.STREAMS}
        dma_hist = {s: [] for s in self.STREAMS}
        for i, o in enumerate(ops):
            if o["dma"]:
                s = o["stream"]
                k = dma_count[s]
                o["dma_k"] = k
                if k >= DMA_RING:
                    o["deps"].add(dma_hist[s][k - DMA_RING])
                dma_hist[s].append(i)
                dma_count[s] += 1
        has_dep = [False] * n
        for i, o in enumerate(ops):
            for d in o["deps"]:
                od = ops[d]
                if od["dma"] or od["stream"] != o["stream"]:
                    has_dep[d] = True
                elif od["stream"] in SAME_ENGINE_SYNC:
                    has_dep[d] = True
        sem_cnt = [0]

        def new_sem(tag):
            sem_cnt[0] += 1
            return es.enter_context(nc.semaphore(f"s_{tag}_{sem_cnt[0]}"))

        cur_sem, cur_cnt, rings = {}, {}, {}
        for i, o in enumerate(ops):
            s = o["stream"]
            if o["dma"]:
                if s not in rings:
                    rings[s] = [new_sem(f"dq{s}") for _ in range(DMA_RING)]
                k = o["dma_k"]
                o["sig"] = (rings[s][k % DMA_RING], 16 * (k // DMA_RING + 1), 16)
            elif has_dep[i]:
                if s not in cur_sem or cur_cnt[s] >= SEM_LIMIT:
                    cur_sem[s] = new_sem(s)
                    cur_cnt[s] = 0
                cur_cnt[s] += 1
                o["sig"] = (cur_sem[s], cur_cnt[s], 1)
            else:
                o["sig"] = None
        waited = {s: {} for s in self.STREAMS}
        per_stream = {s: [] for s in self.STREAMS}
        for i, o in enumerate(ops):
            s = o["stream"]
            need = {}
            for d in o["deps"]:
                od = ops[d]
                if (not od["dma"]) and od["stream"] == s and (s not in SAME_ENGINE_SYNC):
                    continue
                sem, val, _ = od["sig"]
                key = id(sem)
                if waited[s].get(key, 0) >= val:
                    continue
                if key not in need or need[key][1] < val:
                    need[key] = (sem, val)
            for key, (sem, val) in need.items():
                waited[s][key] = val
            o["waits"] = list(need.values())
            per_stream[s].append(o)
        self.n_sems = sem_cnt[0]
        self.stream_sizes = {s: len(v) for s, v in per_stream.items()}
        block = es.enter_context(nc.Block())

        def run(stream_ops):
            def body(eng):
                for o in stream_ops:
                    for sem, val in o["waits"]:
                        eng.wait_ge(sem, val)
                    if o["fn"] is None:
                        continue
                    ins = o["fn"](eng)
                    if o["sig"] is not None:
                        ins.then_inc(o["sig"][0], o["sig"][2])
            return body

        block.tensor(run(per_stream["pe"]))
        block.scalar(run(per_stream["act"]))
        block.vector(run(per_stream["dve"]))
        block.gpsimd(run(per_stream["pool"]))
        block.sync(run(per_stream["sp"]))


def MM(out, lhsT, rhs, start, stop):
    return lambda e: e.matmul(out, lhsT=lhsT, rhs=rhs, start=start, stop=stop, skip_group_check=True)


def TR(out, in_, ident):
    return lambda e: e.transpose(out=out, in_=in_, identity=ident)


def ACT(out, in_, func, **kw):
    return lambda e: e.activation(out=out, in_=in_, func=func, **kw)


def TT(out, in0, in1, op):
    return lambda e: e.tensor_tensor(out=out, in0=in0, in1=in1, op=op)


def TS(out, in0, s1, s2, op0, op1=None):
    if op1 is None:
        return lambda e: e.tensor_scalar(out=out, in0=in0, scalar1=s1, scalar2=None, op0=op0)
    return lambda e: e.tensor_scalar(out=out, in0=in0, scalar1=s1, scalar2=s2, op0=op0, op1=op1)


def STT(out, in0, scalar, in1, op0, op1):
    return lambda e: e.scalar_tensor_tensor(out=out, in0=in0, scalar=scalar, in1=in1, op0=op0, op1=op1)


def CP(out, in_):
    return lambda e: e.tensor_copy(out=out, in_=in_)


def MEMSET(ap, v):
    return lambda e: e.memset(ap, v)


def RECIP(out, in_):
    return lambda e: e.reciprocal(out=out, in_=in_)


def ASEL(out, in_, pattern, op, fill, base, cm):
    return lambda e: e.affine_select(out=out, in_=in_, pattern=pattern, compare_op=op, fill=fill,
                                     base=base, channel_multiplier=cm)


DT_SIZE = {F32: 4, BF16: 2, I32: 4}
ARENA_BYTES = 204 * 1024


class Builder:
    def __init__(self, nc, es, dbg=()):
        self.nc = nc
        self.es = es
        self.S = Sched(nc)
        self.dbg = set(dbg)
        self.dbg_outs = {}
        self.arena = nc.alloc_sbuf_tensor("arena", [128, ARENA_BYTES // 4], F32).ap()
        self.top = 0
        self.banks = [es.enter_context(nc.psum_tensor(f"pb{i}", [128, 512], F32)).ap() for i in range(8)]
        self.bank_i = 0
        self.pinned = set()
        self.din = {}

    def inp(self, name, shape, dt=F32):
        ap = self.nc.dram_tensor(name, list(shape), dt, kind="ExternalInput").ap()
        self.din[name] = ap
        return ap

    def alloc(self, name, shape, dt=F32):
        n = int(np.prod(shape[1:]))
        nbytes = (n * DT_SIZE[dt] + 31) // 32 * 32
        off = self.top
        self.top += nbytes
        self.max_top = max(getattr(self, "max_top", 0), self.top)
        assert self.top <= ARENA_BYTES, f"arena overflow at {name}: {self.top}"
        ap = self.arena[0:shape[0], off // 4:(off + nbytes) // 4]
        if dt != F32:
            ap = ap.bitcast(dt)
        ap = ap[:, 0:n]
        if len(shape) == 3:
            ap = ap.rearrange("p (a b) -> p a b", b=shape[2])
        elif len(shape) == 4:
            ap = ap.rearrange("p (a b c) -> p a b c", b=shape[2], c=shape[3])
        return ap

    def ring(self, name, shape, dt, n):
        return [(self.alloc(f"{name}{i}", shape, dt), f"{name}{i}") for i in range(n)]

    def mark(self):
        return self.top

    def release(self, mark):
        scr = self.barrier_scr
        self.S.op("pool", MEMSET(scr, 0.0), reads=[], writes=["PHASE", "barrier_scr"])
        self.top = mark

    def bank(self, pin=False):
        while self.bank_i in self.pinned:
            self.bank_i = (self.bank_i + 1) % 8
        i = self.bank_i
        self.bank_i = (i + 1) % 8
        if pin:
            self.pinned.add(i)
        return self.banks[i], ("PB", i)

    def unpin(self, key):
        self.pinned.discard(key[1])

    def dump(self, name, ap, reads, dt=F32):
        if name not in self.dbg:
            return
        shape = list(ap.shape)
        d = self.nc.dram_tensor("dbg_" + name, shape, dt, kind="ExternalOutput").ap()
        self.dbg_outs[name] = shape
        self.S.dma(d, ap, reads=reads, out=True)


def build_program(dbg=(), stop_after=None):
    nc = bass.Bass("TRN2", target_bir_lowering=False)
    es = ExitStack()
    with es:
        b = Builder(nc, es, dbg)
        S = b.S
        xT = b.inp("xT", [D, S_LEN])
        x_in = b.inp("x", [S_LEN, D])
        memT = b.inp("memT", [D, 256])
        pos = b.inp("pos", [1, S_LEN], I32)
        posc = b.inp("posc", [1, 256], I32)
        wfm_nsa = b.inp("wfm_nsa", [D, 2304])
        wtm_nsa = b.inp("wtm_nsa", [D, 280])
        wfm_mla = b.inp("wfm_mla", [D, 896])
        w1k_d = b.inp("w1k", [128, 32 * 128])
        w1v_d = b.inp("w1v", [128, 32 * 128])
        poskT_d = b.inp("poskT", [128, 64])
        posvT_d = b.inp("posvT", [128, 64])
        w2k_d = b.inp("w2k", [128, 256])
        w2v_d = b.inp("w2v", [128, 64])
        b2v_d = b.inp("b2v", [1, 64])
        cover_d = b.inp("cover", [256, 64])
        cand_d = b.inp("cand", [S_LEN, 64])
        forced_d = b.inp("forced", [S_LEN, 64])
        psc_d = b.inp("psc", [128, 16])
        wuq_a_d = b.inp("wuq_a", [384, 768])
        wuq_b_d = b.inp("wuq_b", [384, 768])
        wuk_d = b.inp("wuk", [256, 512])
        wuv_d = b.inp("wuv", [256, 512])
        wo_d = b.inp("w_o", [D, D])
        mwq_d = b.inp("mem_wq", [D, D])
        mwk_d = b.inp("mem_wk", [D, D])
        mwv_d = b.inp("mem_wv", [D, D])
        mwo_d = b.inp("mem_wo", [D, D])
        wup_d = b.inp("w_up", [D, 2 * DFF])
        wdn_d = b.inp("w_dn", [DFF, D])
        lngb_d = b.inp("lngb", [6, D])
        convw_d = b.inp("convw", [128, NCH * 3])
        convb_d = b.inp("convb", [128, NCH])
        out_d = nc.dram_tensor("out", [S_LEN, D], F32, kind="ExternalOutput").ap()
        wo_s = nc.dram_tensor("wo_s", [D, D], BF16).ap()
        mwq_s = nc.dram_tensor("mwq_s", [D, D], BF16).ap()
        mwo_s = nc.dram_tensor("mwo_s", [D, D], BF16).ap()
        wup_s = nc.dram_tensor("wup_s", [D, 2 * DFF], BF16).ap()
        wdn_s = nc.dram_tensor("wdn_s", [DFF, D], BF16).ap()
        oT_s = nc.dram_tensor("oT_s", [D, S_LEN], BF16).ap()

        ident = b.alloc("ident", [128, 128], BF16)
        caus = b.alloc("caus", [128, 128], BF16)
        wedge = b.alloc("wedge", [128, 128], BF16)
        ones = b.alloc("ones", [128, 128], BF16)
        psc = b.alloc("psc", [128, 16], F32)
        b.barrier_scr = b.alloc("barrier_scr", [128, 8], F32)
        zeros = b.alloc("zeros", [128, 512], BF16)
        S.pool(MEMSET(zeros, 0.0), writes=["zeros"])
        S.pool(MEMSET(ones, 1.0), writes=["ones"])
        S.pool(MEMSET(ident, 1.0), writes=["ident"])
        S.pool(ASEL(ident, ident, [[1, 128]], ALU.is_equal, 0.0, 0, -1), reads=["ident"], writes=["ident"])
        S.pool(ASEL(caus, zeros[:, 0:128], [[1, 128]], ALU.is_ge, NEG, 0, -1), reads=["zeros"], writes=["caus"])
        S.pool(ASEL(wedge, zeros[:, 0:128], [[-1, 128]], ALU.is_gt, NEG, 0, 1), reads=["zeros"], writes=["wedge"])
        S.dma(psc, psc_d, writes=["psc"])
        for (dst, src, rows, key) in ((wo_s, wo_d, D, "wo_s"), (mwq_s, mwq_d, D, "mwq_s"), (mwo_s, mwo_d, D, "mwo_s"),
                                      (wup_s, wup_d, D, "wup_s"), (wdn_s, wdn_d, DFF, "wdn_s")):
            for r0 in range(0, rows, 128):
                S.dma(dst[r0:r0 + 128, :], src[r0:r0 + 128, :], writes=[(key, r0)], q="pool")
        base_mark = b.mark()

        PSC = lambda c: psc[:, c:c + 1]

        def load_xT_block(tb, ring_slot):
            xt, xk = ring_slot
            src = xT.rearrange("(kc p) t -> p kc t", p=128)
            for k0 in range(0, 8, 2):
                S.dma(xt[:, k0:k0 + 2, :], src[:, k0:k0 + 2, tb * 512:(tb + 1) * 512], writes=[(xk, k0)], q="pool")
            return xt, [(xk, k0) for k0 in range(0, 8, 2)]

        def rope_tables(pos_src, n, inv_c, sgn_c, nsg_c, tiles, prow=slice(0, 128)):
            (posi, kpi), (posf, kpf), (m1, km1), (m2, km2), (cos, kc), (sin, ks) = tiles
            S.dma(posi[:, 0:n], pos_src.partition_broadcast(128), writes=[kpi])
            S.dve(CP(posf[:, 0:n], posi[:, 0:n]), reads=[kpi], writes=[kpf])
            C1_, C2_ = 6.28125, 2 * PI - 6.28125
            S.dve(TS(m1[:, 0:n], posf[:, 0:n], PSC(inv_c), None, ALU.mult), reads=[kpf, "psc"], writes=[km1])
            S.dve(TS(m2[:, 0:n], posf[:, 0:n], PSC(inv_c), 0.5 * PI, ALU.mult, ALU.add), reads=[kpf, "psc"], writes=[km2])
            S.dve(TS(sin[:, 0:n], m1[:, 0:n], 1.0 / (2 * PI), None, ALU.mult), reads=[km1], writes=[ks])
            S.dve(CP(posi[:, 0:n], sin[:, 0:n]), reads=[ks], writes=[kpi])
            S.dve(CP(cos[:, 0:n], posi[:, 0:n]), reads=[kpi], writes=[kc])
            S.dve(STT(m1[:, 0:n], cos[:, 0:n], -C1_, m1[:, 0:n], ALU.mult, ALU.add), reads=[kc, km1], writes=[km1])
            S.dve(STT(m1[:, 0:n], cos[:, 0:n], -C2_, m1[:, 0:n], ALU.mult, ALU.add), reads=[kc, km1], writes=[km1])
            S.dve(TS(m1[:, 0:n], m1[:, 0:n], PI, -PI, ALU.min, ALU.max), reads=[km1], writes=[km1])
            S.act(ACT(sin[:, 0:n], m1[:, 0:n], AF.Sin, scale=PSC(sgn_c)), reads=[km1, "psc"], writes=[ks])
            S.dve(TS(cos[:, 0:n], m2[:, 0:n], 1.0 / (2 * PI), None, ALU.mult), reads=[km2], writes=[kc])
            S.dve(CP(posi[:, 0:n], cos[:, 0:n]), reads=[kc], writes=[kpi])
            S.dve(CP(posf[:, 0:n], posi[:, 0:n]), reads=[kpi], writes=[kpf])
            S.dve(STT(m2[:, 0:n], posf[:, 0:n], -C1_, m2[:, 0:n], ALU.mult, ALU.add), reads=[kpf, km2], writes=[km2])
            S.dve(STT(m2[:, 0:n], posf[:, 0:n], -C2_, m2[:, 0:n], ALU.mult, ALU.add), reads=[kpf, km2], writes=[km2])
            S.dve(TS(m2[:, 0:n], m2[:, 0:n], PI, -PI, ALU.min, ALU.max), reads=[km2], writes=[km2])
            S.act(ACT(cos[:, 0:n], m2[:, 0:n], AF.Sin), reads=[km2], writes=[kc])

        def proj_fm(xt, xkeys, w, wkey, col0, ncols=128):
            pb, pk = b.bank()
            for kc in range(8):
                S.pe(MM(pb[0:ncols, :], w[:, kc, col0:col0 + ncols], xt[:, kc, :], kc == 0, kc == 7),
                     reads=[wkey] + xkeys, writes=[pk])
            return pb, pk

        QT = b.alloc("QT", [128, 4, S_LEN], BF16)
        KS = b.alloc("KS", [128, 2, S_LEN], BF16)
        KW = b.alloc("KW", [128, 2, S_LEN], BF16)
        VS = b.alloc("VS", [128, NT, 2, 65], BF16)
        VW = b.alloc("VW", [128, NT, 2, 65], BF16)
        GATES = b.alloc("GATES", [128, NT, 24], F32)
        KCMP = b.alloc("KCMP", [128, 2, 256], BF16)
        CV = b.alloc("CV", [128, 2, 2, 129], BF16)
        nsa_state_mark = b.mark()
        KC = b.alloc("KC", [128, S_LEN], BF16)
        VC = b.alloc("VC", [128, S_LEN], BF16)
        cmp_mark = b.mark()
        WFM = b.alloc("WFM", [128, 8, 2304], BF16)
        WTM = b.alloc("WTM", [128, 8, 280], BF16)
        XTR = b.ring("XT", [128, 8, 512], BF16, 2)
        tabs = [b.ring(nm, [128, 512], dt_, 2 if nm in ("cos", "sin") else 1) for nm, dt_ in
                (("posi", I32), ("posf", F32), ("m1", F32), ("m2", F32), ("cos", F32), ("sin", F32))]
        T1 = b.ring("T1", [128, 512], F32, 2)
        T2 = b.ring("T2", [128, 512], F32, 2)

        S.pool(MEMSET(VS, 1.0), writes=["VS"])
        S.pool(MEMSET(VW, 1.0), writes=["VW"])
        for kc in range(8):
            S.dma(WFM[:, kc, :], wfm_nsa[kc * 128:(kc + 1) * 128, :], writes=["WFM"], q="pool")
        S.dma(WTM, wtm_nsa.rearrange("(kc p) n -> p kc n", p=128), writes=["WTM"], q="pool")

        rope_groups = [(QT[:, hp, :], hp, 4 + hp) for hp in range(4)] + \
                      [(KS[:, g, :], 8 + g, 10 + g) for g in range(2)] + \
                      [(KW[:, g, :], 12 + g, 14 + g) for g in range(2)]
        for tb in range(NB):
            xt, xkeys = load_xT_block(tb, XTR[tb % 2])
            tl = [t[tb % len(t)] for t in tabs]
            rope_tables(pos[:, tb * 512:(tb + 1) * 512], 512, 0, 1, 2, tl)
            cos, kcos = tl[4]
            sin, ksin = tl[5]
            tsl = slice(tb * 512, (tb + 1) * 512)
            for gi, (dst, ga, gb) in enumerate(rope_groups):
                pa, pka = proj_fm(xt, xkeys, WFM, "WFM", ga * 128)
                pb_, pkb = proj_fm(xt, xkeys, WFM, "WFM", gb * 128)
                t1, k1 = T1[gi % 2]
                t2, k2 = T2[gi % 2]
                S.dve(TT(t1, pa, cos, ALU.mult), reads=[pka, kcos], writes=[k1])
                S.dve(TT(t2, pb_, sin, ALU.mult), reads=[pkb, ksin], writes=[k2])
                S.pool(TT(dst[:, tsl], t1, t2, ALU.add), reads=[k1, k2], writes=[("ropeout", gi, tb)])
            for dst, gidx, nm in ((KC, 16, "KC"), (VC, 17, "VC")):
                pa, pka = proj_fm(xt, xkeys, WFM, "WFM", gidx * 128)
                S.act(ACT(dst[:, tsl], pa, AF.Copy), reads=[pka], writes=[(nm, tb)])
            for tt in range(4):
                kt = tb * 4 + tt
                pb_, pk = b.bank()
                for kc in range(8):
                    S.pe(MM(pb_[:, 0:280], xt[:, kc, tt * 128:(tt + 1) * 128], WTM[:, kc, :], kc == 0, kc == 7),
                         reads=["WTM"] + xkeys, writes=[pk])
                S.act(ACT(VS[:, kt, :, 0:64], pb_[:, 0:128].rearrange("p (g d) -> p g d", d=64), AF.Copy),
                      reads=[pk, "VS"], writes=[("VS", kt)])
                S.act(ACT(VW[:, kt, :, 0:64], pb_[:, 128:256].rearrange("p (g d) -> p g d", d=64), AF.Copy),
                      reads=[pk, "VW"], writes=[("VW", kt)])
                S.act(ACT(GATES[:, kt, :], pb_[:, 256:280], AF.Sigmoid), reads=[pk], writes=[("GATES", kt)])
        QTK = [("ropeout", gi, tb) for gi in range(8) for tb in range(NB)]
        b.dump("QT", QT[:, 0, :], QTK, BF16)
        b.dump("KS", KS[:, 1, :], QTK, BF16)
        b.dump("GATES", GATES, [("GATES", kt) for kt in range(NT)])
        b.dump("VS", VS, [("VS", kt) for kt in range(NT)], BF16)
        if stop_after == "1a":
            S.emit(es)
            return nc, b
        b.release(cmp_mark)

        W1 = b.alloc("W1", [128, 32, 128], BF16)
        POST = b.alloc("POST", [128, 32, 2], BF16)
        W2K = b.alloc("W2K", [128, 256], BF16)
        W2V = b.alloc("W2V", [128, 64], BF16)
        B2V = b.alloc("B2V", [128, 64], F32)
        C1 = b.alloc("C1", [128, 2], F32)
        U = b.alloc("U", [128, 256], F32)
        U2 = b.alloc("U2", [128, 256], F32)
        U3 = b.alloc("U3", [128, 256], F32)
        GT = b.alloc("GT", [128, 256], BF16)
        ctab = [b.ring(nm, [128, 512], dt_, 1) for nm, dt_ in
                (("cposi", I32), ("cposf", F32), ("cm1", F32), ("cm2", F32), ("ccos", F32), ("csin", F32))]
        ctl = [t[0] for t in ctab]
        rope_tables(posc, 256, 0, 1, 2, ctl)
        ccos, kccos = ctl[4]
        csin, kcsin = ctl[5]
        S.dma(W2K, w2k_d, writes=["W2K"], q="pool")
        S.dma(W2V, w2v_d, writes=["W2V"], q="pool")
        S.dma(B2V, b2v_d.partition_broadcast(128), writes=["B2V"])
        S.dma(CV[:, 0, 0, 0:64], cover_d[0:128, :], reads=[], writes=[("CVc", 0)], q="pool")
        S.dma(CV[:, 1, 0, 0:64], cover_d[128:256, :], reads=[], writes=[("CVc", 1)], q="pool")
        S.pool(MEMSET(GT, 0.0), writes=["GT"])
        for which in range(2):
            src = KC if which == 0 else VC
            S.dma(W1, (w1k_d if which == 0 else w1v_d).rearrange("p (l h) -> p l h", h=128), writes=["W1"], q="pool")
            S.dma(POST, (poskT_d if which == 0 else posvT_d).rearrange("p (l t) -> p l t", t=2), writes=["POST"], q="pool")
            pc, pkc = b.bank()
            for l in range(32):
                S.pe(MM(pc[:, 0:2], W1[0:64, l, :], POST[0:64, l, :], l == 0, l == 31), reads=["W1", "POST"], writes=[pkc])
            S.dve(TS(C1, pc[:, 0:2], PSC(6 + which), None, ALU.add), reads=[pkc, "psc"], writes=["C1"])
            for g in range(2):
                rows = slice(g * 64, (g + 1) * 64)
                ph, pkh = b.bank()
                for l in range(32):
                    S.pe(MM(ph[:, 0:255], W1[rows, l, :], src[rows, l:l + 16 * 254 + 1:16], l == 0, l == 31),
                         reads=["W1"] + [("KC" if which == 0 else "VC", tb) for tb in range(NB)], writes=[pkh])
                S.act(ACT(U[:, 0:255], ph[:, 0:255], AF.Identity, bias=C1[:, 0:1], scale=1.0), reads=[pkh, "C1"], writes=["U"])
                S.dve(TT(U2[:, 0:255], U[:, 0:255], U[:, 0:255], ALU.mult), reads=["U"], writes=["U2"])
                S.dve(TS(U2[:, 0:255], U2[:, 0:255], 0.044715, 1.0, ALU.mult, ALU.add), reads=["U2"], writes=["U2"])
                S.dve(TT(U2[:, 0:255], U2[:, 0:255], U[:, 0:255], ALU.mult), reads=["U2", "U"], writes=["U2"])
                S.act(ACT(U3[:, 0:255], U2[:, 0:255], AF.Tanh, scale=0.7978845608028654), reads=["U2"], writes=["U3"])
                S.dve(TS(U3[:, 0:255], U3[:, 0:255], 0.5, 0.5, ALU.mult, ALU.add), reads=["U3"], writes=["U3"])
                S.dve(TT(GT[:, 0:255], U3[:, 0:255], U[:, 0:255], ALU.mult), reads=["U3", "U", "GT"], writes=["GT"])
                if which == 0:
                    pa, pka = b.bank()
                    pb_, pkb = b.bank()
                    S.pe(MM(pa[:, 0:256], W2K[:, 0:128], GT, True, True), reads=["W2K", "GT"], writes=[pka])
                    S.pe(MM(pb_[:, 0:256], W2K[:, 128:256], GT, True, True), reads=["W2K", "GT"], writes=[pkb])
                    S.dve(STT(U[:, 0:256], pa[:, 0:256], PSC(8), ccos[:, 0:256], ALU.add, ALU.mult),
                          reads=[pka, kccos, "psc", "U"], writes=["U"])
                    S.dve(STT(U2[:, 0:256], pb_[:, 0:256], PSC(9), csin[:, 0:256], ALU.add, ALU.mult),
                          reads=[pkb, kcsin, "psc", "U2"], writes=["U2"])
                    S.pool(TT(KCMP[:, g, :], U[:, 0:256], U2[:, 0:256], ALU.add), reads=["U", "U2"], writes=[("KCMP", g)])
                else:
                    for nt in range(2):
                        pv, pkv = b.bank()
                        S.pe(MM(pv[:, 0:64], GT[:, nt * 128:(nt + 1) * 128], W2V, True, True), reads=["GT", "W2V"], writes=[pkv])
                        S.dve(TT(CV[:, nt, g, 65:129], pv[:, 0:64], B2V, ALU.add), reads=[pkv, "B2V"], writes=[("CVv", nt, g)])
        for nt in range(2):
            S.pool(CP(CV[:, nt, 1, 0:64], CV[:, nt, 0, 0:64]), reads=[("CVc", nt)], writes=[("CVc2", nt)])
            S.pool(MEMSET(CV[:, nt, :, 64:65], 1.0), writes=[("CVo", nt)])
        CVK = [("CVc", nt) for nt in range(2)] + [("CVc2", nt) for nt in range(2)] + [("CVo", nt) for nt in range(2)] + \
              [("CVv", nt, g) for nt in range(2) for g in range(2)]
        b.dump("KCMP", KCMP, [("KCMP", 0), ("KCMP", 1)], BF16)
        b.dump("CV", CV, CVK, BF16)
        if stop_after == "1ap":
            S.emit(es)
            return nc, b
        b.release(nsa_state_mark)

        ETAB = b.alloc("ETAB", [128, 32, 128], BF16)
        S.pool(MEMSET(ETAB, 1.0), writes=["ETAB"])
        S.pool(ASEL(ETAB, ETAB, [[-2, 32], [-1, 2], [0, 64]], ALU.is_equal, 0.0, 0, 1), reads=["ETAB"], writes=["ETAB"])
        MASKC = b.ring("MASKC", [128, 2, 512], BF16, 2)
        PT = b.ring("PT", [128, 512], BF16, 6)
        ONSA = b.alloc("ONSA", [128, 4, 512], F32)
        ONSAB = b.alloc("ONSAB", [128, 4, 512], BF16)
        OTB = b.ring("OTB", [128, 4, 512], BF16, 2)
        IMP = b.alloc("IMP", [128, 2, 4, 64], F32)
        CANDT = b.ring("CANDT", [128, 4, 64], F32, 2)
        FORCT = b.ring("FORCT", [128, 4, 64], F32, 2)
        SM = b.ring("SM", [128, 16], F32, 4)
        IMPM = b.alloc("IMPM", [128, 64], F32)
        SCR = b.alloc("SCR", [128, 64], F32)
        M8 = b.alloc("M8", [128, 16], F32)
        SEL = b.alloc("SEL", [128, 64], F32)
        NEGB8 = b.alloc("NEGB8", [128, 8, 64], BF16)
        NEGT = b.alloc("NEGT", [128, 2, 512], BF16)
        S.pool(MEMSET(NEGT, 0.0), writes=[("NEGT", 0), ("NEGT", 1)])
        pt_i = [0]
        sm_i = [0]
        KSX = [b.alloc("KSL", [128, 2, S_LEN], BF16), b.alloc("KSH", [128, 2, S_LEN], BF16)]
        KWX = [b.alloc("KWL", [128, 2, S_LEN], BF16), b.alloc("KWH", [128, 2, S_LEN], BF16)]
        KCX = [b.alloc("KCL", [128, 2, 256], BF16), b.alloc("KCH", [128, 2, 256], BF16)]
        for half in range(2):
            rws = slice(half * 64, half * 64 + 64)
            for dst, src, nm in ((KSX[half], KS, "KSX"), (KWX[half], KW, "KWX"), (KCX[half], KCMP, "KCX")):
                if nm == "KSX":
                    S.dve(MEMSET(dst, 0.0), writes=[(nm, half)])
                else:
                    S.pool(MEMSET(dst, 0.0), writes=[(nm, half)])
                if nm == "KSX":
                    S.act(ACT(dst[rws], src[rws], AF.Copy), reads=[(nm, half)], writes=[(nm, half)])
                else:
                    S.dve(CP(dst[rws], src[rws]), reads=[(nm, half)], writes=[(nm, half)])

        ATT = {"staged": []}
        DEPTH = 3
        NPT = 6

        def _emit_pv(item):
            (kt, jl, jh, pt, pkt, accs, nper, Vfn, v_keys, last_kt, is_last, post_fn) = item
            for j in range(jl, jh + 1):
                acc, pka, j0 = accs[j // nper]
                S.pe(MM(acc[:, j - j0, :], pt[:, j * 128:(j + 1) * 128], Vfn(kt), False, last_kt[j] == kt),
                     reads=[pkt] + v_keys, writes=[pka])
            if is_last:
                for (_a, pka, _j) in accs:
                    b.unpin(pka)
                post_fn(accs)

        def att_flush():
            while ATT["staged"]:
                _emit_pv(ATT["staged"].pop(0))

        def attention(QTa, KTa, Vfn, tiles, qb, scale, ncols, q_keys, k_keys, v_keys, post_fn, blockmask=None):
            nper = min(4, 512 // ncols)
            accs = []
            for j0 in range(0, 4, nper):
                pa, pka = b.bank(pin=True)
                S.pe(MM(pa[:, 0:nper * ncols], zeros[:, 0:128], zeros[:, 0:nper * ncols], True, True), reads=["zeros"], writes=[pka])
                accs.append((pa[:, 0:nper * ncols].rearrange("p (j c) -> p j c", c=ncols), pka, j0))
            last_kt = {}
            for (kt, jl, jh, masks) in tiles:
                for j in range(jl, jh + 1):
                    last_kt[j] = kt
            for ti, (kt, jl, jh, masks) in enumerate(tiles):
                c0, c1 = jl * 128, (jh + 1) * 128
                ps, pks = b.bank()
                nmm = 1 + (1 if blockmask is not None else 0) + len(masks)
                done = 1
                S.pe(MM(ps[:, c0:c1], KTa[:, kt * 128:(kt + 1) * 128], QTa[:, qb * 512 + c0:qb * 512 + c1], True, done == nmm),
                     reads=q_keys + k_keys, writes=[pks])
                if blockmask is not None:
                    done += 1
                    negt, nkey = blockmask
                    S.pe(MM(ps[:, c0:c1], ETAB[:, kt, :], negt[:, c0:c1], False, done == nmm),
                         reads=["ETAB", nkey], writes=[pks])
                for (kind, j) in masks:
                    done += 1
                    S.pe(MM(ps[:, j * 128:(j + 1) * 128], ident, caus if kind == "c" else wedge, False, done == nmm),
                         reads=["ident", "caus", "wedge"], writes=[pks])
                pt, pkt = PT[pt_i[0] % NPT]
                pt_i[0] += 1
                S.act(ACT(pt[:, c0:c1], ps[:, c0:c1], AF.Exp, scale=scale), reads=[pks], writes=[pkt])
                ATT["staged"].append((kt, jl, jh, pt, pkt, accs, nper, Vfn, v_keys, last_kt, ti == len(tiles) - 1, post_fn))
                while len(ATT["staged"]) > DEPTH:
                    _emit_pv(ATT["staged"].pop(0))

        def recip_sums(accs, ncols, sumcol):
            sm, smk = SM[sm_i[0] % 4]
            sm_i[0] += 1
            for (acc, pka, j0) in accs:
                nper = acc.shape[1]
                S.dve(TS(sm[:, j0:j0 + nper], acc[:, :, sumcol], 1e-30, None, ALU.max), reads=[pka], writes=[smk])
            S.dve(RECIP(sm[:, 4:8], sm[:, 0:4]), reads=[smk], writes=[smk])
            return sm, smk

        ALLQ = []
        for qb in range(NB):
            mk, mkk = MASKC[qb % 2]
            for nt in range(2):
                S.pool(ASEL(mk[:, nt, :], zeros, [[1, 512]], ALU.is_ge, NEG, qb * 512 - 2048 * nt - 31, -16),
                       reads=["zeros"], writes=[(mkk, nt)])
            cand, candk = CANDT[qb % 2]
            forc, forck = FORCT[qb % 2]
            S.dma(cand, cand_d[qb * 512:(qb + 1) * 512, :].rearrange("(j p) c -> p j c", p=128), writes=[candk])
            S.dma(forc, forced_d[qb * 512:(qb + 1) * 512, :].rearrange("(j p) c -> p j c", p=128), writes=[forck])
            if stop_after == "2a_tab":
                b.dump("ETAB", ETAB, ["ETAB"], BF16)
                b.dump("MASKC", mk, [(mkk, 0), (mkk, 1)], BF16)
                b.dump("CAND", cand, [candk])
                S.emit(es)
                return nc, b
            gview = lambda br, h: GATES[:, qb * 4:(qb + 1) * 4, h * 3 + br]
            gkeys = [("GATES", kt) for kt in range(qb * 4, qb * 4 + 4)]
            def cmp_scores(h):
                g, hp, half = h // 4, h // 2, h % 2
                ets = []
                for nt in (range(2) if qb >= 4 else range(1)):
                    ps, pks = b.bank()
                    S.pe(MM(ps, KCX[half][:, g, nt * 128:(nt + 1) * 128], QT[:, hp, qb * 512:(qb + 1) * 512], True, False),
                         reads=[("KCX", half)], writes=[pks])
                    S.pe(MM(ps, ident, mk[:, nt, :], False, True), reads=["ident", (mkk, nt)], writes=[pks])
                    pt, pkt = PT[pt_i[0] % 6]
                    pt_i[0] += 1
                    S.act(ACT(pt, ps, AF.Exp, scale=0.125), reads=[pks], writes=[pkt])
                    ets.append((pt, pkt, nt))
                return ets

            def cmp_pv(h, ets):
                g = h // 4
                accs = []
                for j0 in (0, 2):
                    pa, pka = b.bank()
                    acc = pa[:, 0:258].rearrange("p (j c) -> p j c", c=129)
                    for j in (j0, j0 + 1):
                        for ei, (pt_, pkt_, nt) in enumerate(ets):
                            S.pe(MM(acc[:, j - j0, :], pt_[:, j * 128:(j + 1) * 128], CV[:, nt, g, :], ei == 0, ei == len(ets) - 1),
                                 reads=[pkt_], writes=[pka])
                    accs.append((acc, pka, j0))
                sm, smk = recip_sums(accs, 129, 64)
                S.dve(TT(sm[:, 8:12], sm[:, 4:8], gview(0, h), ALU.mult), reads=[smk] + gkeys, writes=[smk])
                for (acc, pka, j0) in accs:
                    for j in (j0, j0 + 1):
                        S.dve(TS(ONSA[:, j, h * 64:(h + 1) * 64], acc[:, j - j0, 65:129], sm[:, 8 + j:9 + j], None, ALU.mult),
                              reads=[pka, smk], writes=[("ONSA", h)])
                        if qb < 2:
                            pass
                        elif h % 4 == 0:
                            S.dve(TS(IMP[:, g, j, :], acc[:, j - j0, 0:64], sm[:, 4 + j:5 + j], None, ALU.mult),
                                  reads=[pka, smk], writes=[("IMP", g)])
                        else:
                            S.dve(STT(IMP[:, g, j, :], acc[:, j - j0, 0:64], sm[:, 4 + j:5 + j], IMP[:, g, j, :], ALU.mult, ALU.add),
                                  reads=[pka, smk, ("IMP", g)], writes=[("IMP", g)])

            ets_cur = cmp_scores(0)
            for h in range(8):
                ets_next = cmp_scores(h + 1) if h + 1 < 8 else None
                cmp_pv(h, ets_cur)
                ets_cur = ets_next
            if stop_after == "2a_cmp":
                b.dump("IMP", IMP, [("IMP", 0), ("IMP", 1)])
                b.dump("ONSA_c", ONSA, [("ONSA", h) for h in range(8)])
                S.emit(es)
                return nc, b
            if qb == 1:
                b.dump("IMP", IMP, [("IMP", 0), ("IMP", 1)])
                b.dump("ONSA_c", ONSA, [("ONSA", h) for h in range(8)])
            def sel_dve():
                for g in range(2):
                    for j in range(4):
                        negb = NEGB8[:, g * 4 + j, :]
                        S.dve(TT(IMPM, IMP[:, g, j, :], cand[:, j, :], ALU.mult), reads=[("IMP", g), candk], writes=["IMPM"])
                        S.dve(lambda e: e.max(out=M8[:, 0:8], in_=IMPM), reads=["IMPM"], writes=["M8a"])
                        S.dve(lambda e: e.match_replace(out=SCR, in_to_replace=M8[:, 0:8], in_values=IMPM, imm_value=-1.0),
                              reads=["IMPM", "M8a"], writes=["SCR"])
                        S.dve(lambda e: e.max(out=M8[:, 8:16], in_=SCR), reads=["SCR"], writes=["M8b"])
                        S.dve(TS(SEL, IMPM, M8[:, 12:13], None, ALU.is_ge), reads=["IMPM", "M8b"], writes=["SEL"])
                        S.dve(TT(SEL, SEL, cand[:, j, :], ALU.mult), reads=["SEL", candk], writes=["SEL"])
                        S.dve(TT(SEL, SEL, forc[:, j, :], ALU.add), reads=["SEL", forck], writes=["SEL"])
                        S.dve(TS(negb, SEL, -1.0, -NEG, ALU.add, ALU.mult), reads=["SEL"], writes=[("NEGB", g, j)])

            def sel_pe():
                for g in range(2):
                    for j in range(4):
                        negb = NEGB8[:, g * 4 + j, :]
                        pb_, pk = b.bank()
                        ptr = pb_.bitcast(BF16)
                        S.pe(TR(ptr[0:64, 0:128], negb, ident), reads=[("NEGB", g, j), "ident"], writes=[pk])
                        S.act(ACT(NEGT[0:64, g, j * 128:(j + 1) * 128], ptr[0:64, 0:128], AF.Copy), reads=[pk], writes=[("NEGT", g)])

            if qb >= 2:
                sel_dve()
            if stop_after == "2a_sel":
                b.dump("NEGT", NEGT, [("NEGT", 0), ("NEGT", 1)], BF16)
                S.emit(es)
                return nc, b
            if qb == 3:
                b.dump("NEGT", NEGT, [("NEGT", 0), ("NEGT", 1)], BF16)
            for br in (2, 1):
                if br == 1 and qb >= 2:
                    sel_pe()
                for h in range(8):
                    g, hp, half = h // 4, h // 2, h % 2
                    QTa = QT[:, hp, :]
                    if br == 1:
                        KTa = KSX[half][:, g, :]
                        kkeys = [("KSX", half)]
                        Vt = VS
                        tiles = []
                        for kt in range(0, 4 * qb + 4):
                            jl = max(kt - 4 * qb, 0)
                            masks = [("c", kt - 4 * qb)] if kt >= 4 * qb else []
                            tiles.append((kt, jl, 3, masks))
                        bm = (NEGT[:, g, :], ("NEGT", g)) if qb >= 2 else None
                    else:
                        KTa = KWX[half][:, g, :]
                        kkeys = [("KWX", half)]
                        Vt = VW
                        tiles = []
                        for kt in range(max(0, 4 * qb - 4), 4 * qb + 4):
                            jl = max(kt - 4 * qb, 0)
                            jh = min(kt + 4 - 4 * qb, 3)
                            masks = []
                            if kt >= 4 * qb:
                                masks.append(("c", kt - 4 * qb))
                            if 0 <= kt + 4 - 4 * qb <= 3:
                                masks.append(("w", kt + 4 - 4 * qb))
                            tiles.append((kt, jl, jh, masks))
                        bm = None

                    def post_nsa(accs, h=h, br=br, qb=qb):
                        sm, smk = recip_sums(accs, 65, 64)
                        S.dve(TT(sm[:, 8:12], sm[:, 4:8], GATES[:, qb * 4:(qb + 1) * 4, h * 3 + br], ALU.mult),
                              reads=[smk], writes=[smk])
                        acc, pka, _ = accs[0]
                        for j in range(4):
                            dst = ONSA[:, j, h * 64:(h + 1) * 64]
                            S.dve(STT(dst, acc[:, j, 0:64], sm[:, 8 + j:9 + j], dst, ALU.mult, ALU.add),
                                  reads=[pka, smk, ("ONSA", h)], writes=[("ONSA", h)])

                    attention(QTa, KTa, lambda kt, Vt=Vt, g=g: Vt[:, kt, g, :], tiles, qb, 0.125, 65,
                              [], kkeys, [], post_nsa, blockmask=bm)
            att_flush()
            if qb == 3:
                b.dump("ONSA", ONSA, [("ONSA", h) for h in range(8)])
            S.act(ACT(ONSAB, ONSA, AF.Copy), reads=[("ONSA", h) for h in range(8)], writes=["ONSAB"])
            otb, otk = OTB[qb % 2]
            for fc in range(4):
                pb_, pk = b.bank()
                ptr = pb_.bitcast(BF16)
                for j in range(4):
                    S.pe(TR(ptr[:, j * 128:(j + 1) * 128], ONSAB[:, j, fc * 128:(fc + 1) * 128], ident),
                         reads=["ONSAB", "ident"], writes=[pk])
                S.dve(CP(otb[:, fc, :], ptr[:, 0:512]), reads=[pk], writes=[(otk, fc)])
            S.dma(oT_s[0:512, qb * 512:(qb + 1) * 512].rearrange("(fc p) t -> p fc t", p=128), otb,
                  reads=[(otk, fc) for fc in range(4)], writes=[("oT_s", 0, qb)])
            if stop_after == "2a_qb0":
                b.dump("ONSA0", ONSA, [("ONSA", h) for h in range(8)])
                S.emit(es)
                return nc, b
        if stop_after == "2a":
            S.emit(es)
            return nc, b
        b.release(base_mark)

        QM = b.alloc("QM", [128, 8, S_LEN], BF16)
        NMKV = b.alloc("NMKV", [128, 2, S_LEN], BF16)
        KPE = b.alloc("KPE", [128, S_LEN], BF16)
        S.dve(MEMSET(QM[96:128], 0.0), writes=["QMpad"])
        mla_mark = b.mark()
        WFM2 = b.alloc("WFM2", [128, 8, 896], BF16)
        WUQA = b.alloc("WUQA", [128, 3, 768], BF16)
        WUQB = b.alloc("WUQB", [128, 3, 768], BF16)
        XTR = b.ring("XTb", [128, 8, 512], BF16, 2)
        tabs = [b.ring(nm, [128, 512], dt_, 2) for nm, dt_ in
                (("bposi", I32), ("bposf", F32), ("bm1", F32), ("bm2", F32), ("bcos", F32), ("bsin", F32))]
        T1 = b.ring("bT1", [128, 512], F32, 2)
        T2 = b.ring("bT2", [128, 512], F32, 2)
        SQ = b.ring("SQ", [128, 512], BF16, 3)
        RR = b.ring("RR", [128, 512], F32, 2)
        NMQ = b.ring("NMQ", [128, 3, 512], BF16, 2)
        for kc in range(8):
            S.dma(WFM2[:, kc, :], wfm_mla[kc * 128:(kc + 1) * 128, :], writes=["WFM2"], q="pool")
        S.dma(WUQA, wuq_a_d.rearrange("(kc p) n -> p kc n", p=128), writes=["WUQA"], q="pool")
        S.dma(WUQB, wuq_b_d.rearrange("(kc p) n -> p kc n", p=128), writes=["WUQB"], q="pool")
        pe_rows = slice(64, 96)
        for tb in range(NB):
            xt, xkeys = load_xT_block(tb, XTR[tb % 2])
            tl = [t[tb % len(t)] for t in tabs]
            rope_tables(pos[:, tb * 512:(tb + 1) * 512], 512, 3, 4, 5, tl)
            cos, kcos = tl[4]
            sin, ksin = tl[5]
            tsl = slice(tb * 512, (tb + 1) * 512)

            def rmsnorm_group(g0, nchunks, width, gcol, dst_fn, dkey):
                pbs = [proj_fm(xt, xkeys, WFM2, "WFM2", (g0 + c) * 128) for c in range(nchunks)]
                pss, pkss = b.bank()
                for c in range(nchunks):
                    sq, sqk = SQ[c]
                    S.act(ACT(sq, pbs[c][0], AF.Square), reads=[pbs[c][1]], writes=[sqk])
                    S.pe(MM(pss, ones, sq, c == 0, c == nchunks - 1), reads=["ones", sqk], writes=[pkss])
                rr, rrk = RR[0]
                r2, r2k = RR[1]
                S.act(ACT(rr, pss, AF.Sqrt, scale=1.0 / width, bias=1e-6), reads=[pkss], writes=[rrk])
                S.dve(RECIP(r2, rr), reads=[rrk], writes=[r2k])
                for c in range(nchunks):
                    S.dve(STT(dst_fn(c), pbs[c][0], PSC(gcol + c), r2, ALU.mult, ALU.mult),
                          reads=[pbs[c][1], r2k, "psc"], writes=[(dkey, c)])

            nmq, nmqk = NMQ[tb % 2]
            rmsnorm_group(0, 3, 384.0, 10, lambda c: nmq[:, c, :], nmqk)
            rmsnorm_group(3, 2, 256.0, 13, lambda c: NMKV[:, c, tsl], ("NMKV", tb))
            pa, pka = proj_fm(xt, xkeys, WFM2, "WFM2", 5 * 128)
            pb_, pkb = proj_fm(xt, xkeys, WFM2, "WFM2", 6 * 128)
            t1, k1 = T1[0]
            t2, k2 = T2[0]
            S.dve(TT(t1[pe_rows], pa[pe_rows], cos[pe_rows], ALU.mult), reads=[pka, kcos], writes=[k1])
            S.dve(TT(t2[pe_rows], pb_[pe_rows], sin[pe_rows], ALU.mult), reads=[pkb, ksin], writes=[k2])
            S.pool(TT(KPE[pe_rows, tsl], t1[pe_rows], t2[pe_rows], ALU.add), reads=[k1, k2], writes=[("KPE", tb)])
            for h in range(8):
                pa, pka = b.bank()
                pb_, pkb = b.bank()
                for c in range(3):
                    S.pe(MM(pa[0:96, :], WUQA[:, c, h * 96:(h + 1) * 96], nmq[:, c, :], c == 0, c == 2),
                         reads=["WUQA"] + [(nmqk, cc) for cc in range(3)], writes=[pka])
                for c in range(3):
                    S.pe(MM(pb_[0:96, :], WUQB[:, c, h * 96:(h + 1) * 96], nmq[:, c, :], c == 0, c == 2),
                         reads=["WUQB"] + [(nmqk, cc) for cc in range(3)], writes=[pkb])
                S.act(ACT(QM[0:64, h, tsl], pa[0:64, :], AF.Copy), reads=[pka], writes=[("QMn", h, tb)])
                t1, k1 = T1[(h + 1) % 2]
                t2, k2 = T2[(h + 1) % 2]
                S.dve(TT(t1[pe_rows], pa[pe_rows], cos[pe_rows], ALU.mult), reads=[pka, kcos], writes=[k1])
                S.dve(TT(t2[pe_rows], pb_[pe_rows], sin[pe_rows], ALU.mult), reads=[pkb, ksin], writes=[k2])
                S.pool(TT(QM[pe_rows, h, tsl], t1[pe_rows], t2[pe_rows], ALU.add), reads=[k1, k2], writes=[("QMp", h, tb)])
        QMK = [("QMn", h, tb) for h in range(8) for tb in range(NB)] + [("QMp", h, tb) for h in range(8) for tb in range(NB)]
        NMKVK = [(("NMKV", tb), c) for tb in range(NB) for c in range(2)]
        KPEK = [("KPE", tb) for tb in range(NB)]
        b.dump("QM", QM[0:96, 0, :], QMK, BF16)
        b.dump("NMKV", NMKV[:, 0, :], NMKVK, BF16)
        b.dump("KPE", KPE[64:96, :], KPEK, BF16)
        if stop_after == "1b":
            S.emit(es)
            return nc, b
        b.release(mla_mark)

        WUK = b.alloc("WUK", [128, 2, 512], BF16)
        WUV = b.alloc("WUV", [128, 2, 512], BF16)
        KM = b.ring("KM", [128, S_LEN], BF16, 2)
        VM = b.alloc("VM", [128, NT, 2, 65], BF16)
        PT = b.ring("PTb", [128, 512], BF16, 6)
        SM = b.ring("SMb", [128, 16], F32, 4)
        OM = b.ring("OM", [128, 4, 128], BF16, 2)
        OTB2 = b.ring("OTB2", [128, 512], BF16, 2)
        S.dma(WUK, wuk_d.rearrange("(kc p) n -> p kc n", p=128), writes=["WUK"], q="pool")
        S.dma(WUV, wuv_d.rearrange("(kc p) n -> p kc n", p=128), writes=["WUV"], q="pool")
        S.pool(MEMSET(VM, 1.0), writes=["VM"])
        for (km_, kmk_) in KM:
            S.pool(MEMSET(km_[96:128, :], 0.0), writes=[(kmk_, "pad")])
        mla_scale = 96.0 ** -0.5
        for hp in range(4):
            for hh in range(2):
                h = hp * 2 + hh
                km, kmk = KM[hh]
                for kb in range(NB):
                    pa, pka = b.bank()
                    for c in range(2):
                        S.pe(MM(pa[0:64, :], WUK[:, c, h * 64:(h + 1) * 64], NMKV[:, c, kb * 512:(kb + 1) * 512], c == 0, c == 1),
                             reads=["WUK"], writes=[pka])
                    S.dve(CP(km[0:64, kb * 512:(kb + 1) * 512], pa[0:64, :]), reads=[pka], writes=[(kmk, kb)])
                S.pool(CP(km[pe_rows, :], KPE[pe_rows, :]), reads=[], writes=[(kmk, "pe")])
            for kt in range(NT):
                pa, pka = b.bank()
                for c in range(2):
                    S.pe(MM(pa[:, 0:128], NMKV[:, c, kt * 128:(kt + 1) * 128], WUV[:, c, hp * 128:(hp + 1) * 128], c == 0, c == 1),
                         reads=["WUV"], writes=[pka])
                S.dve(CP(VM[:, kt, :, 0:64], pa[:, 0:128].rearrange("p (g d) -> p g d", d=64)),
                      reads=[pka, "VM"], writes=[("VM", kt)])
            if hp == 0:
                b.dump("KM0", KM[0][0][0:96, :], [("KM0", kb) for kb in range(NB)] + [("KM0", "pe")], BF16)
                b.dump("VM", VM, [("VM", kt) for kt in range(NT)], BF16)
            for qb in range(NB):
                om, omk = OM[qb % 2]
                for hh in range(2):
                    h = hp * 2 + hh
                    km, kmk = KM[hh]
                    tiles = []
                    for kt in range(0, 4 * qb + 4):
                        jl = max(kt - 4 * qb, 0)
                        masks = [("c", kt - 4 * qb)] if kt >= 4 * qb else []
                        tiles.append((kt, jl, 3, masks))

                    def post_mla(accs, hh=hh, qb=qb, om=om, omk=omk, hp=hp):
                        sm, smk = recip_sums(accs, 65, 64)
                        acc, pka, _ = accs[0]
                        for j in range(4):
                            S.dve(TS(om[:, j, hh * 64:(hh + 1) * 64], acc[:, j, 0:64], sm[:, 4 + j:5 + j], None, ALU.mult),
                                  reads=[pka, smk], writes=[(omk, hh)])
                        if hh == 1:
                            pb_, pk = b.bank()
                            ptr = pb_.bitcast(BF16)
                            for j in range(4):
                                S.pe(TR(ptr[:, j * 128:(j + 1) * 128], om[:, j, :], ident), reads=[(omk, 0), (omk, 1), "ident"], writes=[pk])
                            otb, otk = OTB2[qb % 2]
                            S.dve(CP(otb, ptr[:, 0:512]), reads=[pk], writes=[otk])
                            S.dma(oT_s[512 + hp * 128:512 + (hp + 1) * 128, qb * 512:(qb + 1) * 512], otb, reads=[otk],
                                  writes=[("oT_s", 1 + hp, qb)])

                    attention(QM[:, h, :], km, lambda kt, hh=hh: VM[:, kt, hh, :], tiles, qb, mla_scale, 65,
                              [], [(kmk, kb) for kb in range(NB)] + [(kmk, "pe"), (kmk, "pad")],
                              [("VM", kt) for kt in range(NT)] + ["VM"], post_mla)
            att_flush()
        if stop_after == "2b":
            S.emit(es)
            return nc, b
        b.release(base_mark)

        LNT = b.alloc("LNT", [128, 6, D], F32)
        CW = b.alloc("CW", [128, NCH, 3], F32)
        CB = b.alloc("CB", [128, NCH], F32)
        HIST = b.alloc("HIST", [128, NCH, 2], F32)
        KMEM = b.alloc("KMEM", [128, 8, 256], BF16)
        VMEM = b.alloc("VMEM", [128, 2, 4, 257], BF16)
        NWS = 6
        WS = b.ring("WS", [128, 8, 512], BF16, NWS)
        OTL = b.ring("OTL", [128, 8, 512], BF16, 1)
        XIN = b.ring("XIN", [128, D], F32, 2)
        Y = b.alloc("Y", [128, D], F32)
        XR = b.alloc("XR", [128, 4, D], F32)
        XB = b.alloc("XB", [128, D], BF16)
        XRT = b.alloc("XRT", [128, 8, 512], BF16)
        QME = b.alloc("QME", [128, 8, 512], BF16)
        OME = b.alloc("OME", [128, 4, D], BF16)
        OMET = b.alloc("OMET", [128, 8, 512], BF16)
        HT = b.alloc("HT", [128, NCH, 512], BF16)
        GC = b.ring("GC", [128, 514], F32, 2)
        GA = b.ring("GA", [128, 512], F32, 2)
        GS = b.ring("GS", [128, 512], F32, 2)
        PT = b.ring("PTc", [128, 512], BF16, 4)
        SM = b.ring("SMc", [128, 16], F32, 4)
        STAT = b.alloc("STAT", [128, 32], F32)
        MEMTB = b.alloc("MEMTB", [128, 8, 256], BF16)

        S.dma(LNT.rearrange("p k d -> p (k d)"), lngb_d.rearrange("(o k) d -> o (k d)", o=1).partition_broadcast(128),
              writes=["LNT"])
        S.dma(CW, convw_d.rearrange("p (c k) -> p c k", k=3), writes=["CW"])
        S.dma(CB, convb_d, writes=["CB"])
        S.pool(MEMSET(HIST, 0.0), writes=["HIST"])
        S.pool(MEMSET(VMEM, 1.0), writes=["VMEM"])
        S.dma(MEMTB, memT.rearrange("(kc p) t -> p kc t", p=128), writes=["MEMTB"], q="pool")
        ws_i = [0]

        def stream_w(src_ap, keys_src):
            ws, wsk = WS[ws_i[0] % NWS]
            ws_i[0] += 1
            kk, nn = src_ap.shape[1], src_ap.shape[2]
            S.dma(ws[:, 0:kk, 0:nn], src_ap, reads=keys_src, writes=[wsk])
            return ws, wsk

        wo_v = wo_s.rearrange("(kc p) n -> p kc n", p=128)
        mwq_v = mwq_s.rearrange("(kc p) n -> p kc n", p=128)
        mwo_v = mwo_s.rearrange("(kc p) n -> p kc n", p=128)
        wup_v = wup_s.rearrange("(kc p) n -> p kc n", p=128)
        wdn_v = wdn_s.rearrange("(c p) n -> p c n", p=128)
        allk = lambda key, rows: [(key, r0) for r0 in range(0, rows, 128)]

        for nb in range(2):
            ws, wsk = WS[ws_i[0] % NWS]
            ws_i[0] += 1
            S.dma(ws, mwk_d.rearrange("(kc p) n -> p kc n", p=128)[:, :, nb * 512:(nb + 1) * 512], writes=[wsk], q="pool")
            for oc in range(4):
                pa, pka = b.bank()
                for kc in range(8):
                    S.pe(MM(pa[:, 0:256], ws[:, kc, oc * 128:(oc + 1) * 128], MEMTB[:, kc, :], kc == 0, kc == 7),
                         reads=[wsk, "MEMTB"], writes=[pka])
                S.act(ACT(KMEM[:, nb * 4 + oc, :], pa[:, 0:256], AF.Copy), reads=[pka], writes=[("KMEM", nb * 4 + oc)])
        for nb in range(2):
            ws, wsk = WS[ws_i[0] % NWS]
            ws_i[0] += 1
            S.dma(ws, mwv_d.rearrange("(kc p) n -> p kc n", p=128)[:, :, nb * 512:(nb + 1) * 512], writes=[wsk], q="pool")
            for kt in range(2):
                pa, pka = b.bank()
                for kc in range(8):
                    S.pe(MM(pa, MEMTB[:, kc, kt * 128:(kt + 1) * 128], ws[:, kc, :], kc == 0, kc == 7),
                         reads=[wsk, "MEMTB"], writes=[pka])
                S.act(ACT(VMEM[:, kt, nb * 2:nb * 2 + 2, 0:256], pa.rearrange("p (h d) -> p h d", d=256), AF.Copy),
                      reads=[pka, "VMEM"], writes=[("VMEM", kt, nb)])
        KMEMK = [("KMEM", i) for i in range(8)]
        VMEMK = [("VMEM", kt, nb) for kt in range(2) for nb in range(2)]

        def layer_norm(src, srcks, ln_idx, dst, dstk):
            S.dve(lambda e: e.bn_stats(out=STAT[:, 0:6], in_=src[:, 0:512]), reads=srcks, writes=["STATa"])
            S.dve(lambda e: e.bn_stats(out=STAT[:, 6:12], in_=src[:, 512:1024]), reads=srcks, writes=["STATb"])
            S.dve(lambda e: e.bn_aggr(out=STAT[:, 12:14], in_=STAT[:, 0:12].rearrange("p (a b) -> p a b", b=6)),
                  reads=["STATa", "STATb"], writes=["STATc"])
            S.act(ACT(STAT[:, 16:17], STAT[:, 13:14], AF.Sqrt, scale=1.0, bias=1e-5), reads=["STATc"], writes=["STATd"])
            S.dve(RECIP(STAT[:, 17:18], STAT[:, 16:17]), reads=["STATd"], writes=["STATe"])
            S.dve(TS(dst, src, STAT[:, 12:13], STAT[:, 17:18], ALU.subtract, ALU.mult), reads=srcks + ["STATc", "STATe"], writes=[dstk])
            S.pool(TT(dst, dst, LNT[:, 2 * ln_idx, :], ALU.mult), reads=[dstk, "LNT"], writes=[dstk])
            S.pool(TT(dst, dst, LNT[:, 2 * ln_idx + 1, :], ALU.add), reads=[dstk, "LNT"], writes=[dstk])

        def to_feature_major(src, srck, tt, tag):
            S.act(ACT(XB, src, AF.Copy), reads=[srck], writes=["XB"])
            for half in range(2):
                pb_, pk = b.bank()
                ptr = pb_.bitcast(BF16)
                for c in range(4):
                    S.pe(TR(ptr[:, c * 128:(c + 1) * 128], XB[:, (half * 4 + c) * 128:(half * 4 + c + 1) * 128], ident),
                         reads=["XB", "ident"], writes=[pk])
                S.dve(CP(XRT[:, half * 4:half * 4 + 4, tt * 128:(tt + 1) * 128],
                         ptr[:, 0:512].rearrange("p (c t) -> p c t", t=128)), reads=[pk], writes=[("XRT", tag, tt, half)])

        def res_mm_ln(tiles, lhsT_fn, lhs_keys, wblk, res_fn, ln_idx):
            for tt in tiles:
                res, resk = res_fn(tt)
                for nbk in range(2):
                    pa, pka = b.bank()
                    ws, wsk = wblk[nbk]
                    for kc in range(8):
                        S.pe(MM(pa, lhsT_fn(kc, tt), ws[:, kc, :], kc == 0, kc == 7), reads=lhs_keys + [wsk], writes=[pka])
                    S.dve(STT(Y[:, nbk * 512:(nbk + 1) * 512], res[:, nbk * 512:(nbk + 1) * 512], ALPHA, pa, ALU.mult, ALU.add),
                          reads=[resk, pka], writes=[("Y", nbk)])
                layer_norm(Y, [("Y", 0), ("Y", 1)], ln_idx, XR[:, tt, :], ("XR", tt))

        XRTK = lambda tag, tiles=range(4): [("XRT", tag, tt, half) for tt in tiles for half in range(2)]

        for tb in range(NB):
            tsl = slice(tb * 512, (tb + 1) * 512)
            GRP = ((0, 1), (2, 3))
            otl, otlk = OTL[0]
            S.dma(otl, oT_s.rearrange("(kc p) t -> p kc t", p=128)[:, :, tsl],
                  reads=[("oT_s", i, tb) for i in range(5)], writes=[otlk])

            def res_x(tt, tb=tb):
                xi, xik = XIN[tt % 2]
                S.dma(xi, x_in[tb * 512 + tt * 128:tb * 512 + (tt + 1) * 128, :], writes=[xik])
                return xi, xik

            wo_blk = [stream_w(wo_v[:, :, nbk * 512:(nbk + 1) * 512], allk("wo_s", D)) for nbk in range(2)]
            wq_blk = [stream_w(mwq_v[:, :, nbk * 512:(nbk + 1) * 512], allk("mwq_s", D)) for nbk in range(2)]
            wmo_blk = [stream_w(mwo_v[:, :, nbk * 512:(nbk + 1) * 512], allk("mwo_s", D)) for nbk in range(2)]

            def stage_q(tiles):
                c0, c1 = tiles[0] * 128, (tiles[-1] + 1) * 128
                for nbk in range(2):
                    ws, wsk = wq_blk[nbk]
                    for oc in range(4):
                        pa, pka = b.bank()
                        for kc in range(8):
                            S.pe(MM(pa[:, 0:c1 - c0], ws[:, kc, oc * 128:(oc + 1) * 128], XRT[:, kc, c0:c1], kc == 0, kc == 7),
                                 reads=[wsk] + XRTK("a", tiles), writes=[pka])
                        S.act(ACT(QME[:, nbk * 4 + oc, c0:c1], pa[:, 0:c1 - c0], AF.Copy), reads=[pka],
                              writes=[("QME", nbk * 4 + oc, tiles[0])])

            def stage_att(tiles):
                c0, c1 = tiles[0] * 128, (tiles[-1] + 1) * 128
                for h in range(4):
                    pts = []
                    for kt in range(2):
                        ps, pks = b.bank()
                        for dc in range(2):
                            S.pe(MM(ps[:, 0:c1 - c0], KMEM[:, 2 * h + dc, kt * 128:(kt + 1) * 128], QME[:, 2 * h + dc, c0:c1], dc == 0, dc == 1),
                                 reads=KMEMK + [("QME", 2 * h + dc, tiles[0])], writes=[pks])
                        pt, pkt = PT[pt_i[0] % 4]
                        pt_i[0] += 1
                        S.act(ACT(pt[:, 0:c1 - c0], ps[:, 0:c1 - c0], AF.Exp, scale=1.0 / 16.0), reads=[pks], writes=[pkt])
                        pts.append((pt, pkt))
                    for ji, j in enumerate(tiles):
                        pa, pka = b.bank()
                        for kt in range(2):
                            S.pe(MM(pa[:, 0:257], pts[kt][0][:, ji * 128:(ji + 1) * 128], VMEM[:, kt, h, :], kt == 0, kt == 1),
                                 reads=[pts[kt][1]] + VMEMK + ["VMEM"], writes=[pka])
                        sm, smk = SM[sm_i[0] % 4]
                        sm_i[0] += 1
                        S.dve(RECIP(sm[:, 0:1], pa[:, 256:257]), reads=[pka], writes=[smk])
                        S.dve(TS(OME[:, j, h * 256:(h + 1) * 256], pa[:, 0:256], sm[:, 0:1], None, ALU.mult),
                              reads=[pka, smk], writes=[("OME", j, h)])

            def stage_omet(tiles):
                for j in tiles:
                    for half in range(2):
                        pb_, pk = b.bank()
                        ptr = pb_.bitcast(BF16)
                        for c in range(4):
                            S.pe(TR(ptr[:, c * 128:(c + 1) * 128], OME[:, j, (half * 4 + c) * 128:(half * 4 + c + 1) * 128], ident),
                                 reads=[("OME", j, hh_) for hh_ in range(4)] + ["ident"], writes=[pk])
                        S.dve(CP(OMET[:, half * 4:half * 4 + 4, j * 128:(j + 1) * 128],
                                 ptr[:, 0:512].rearrange("p (c t) -> p c t", t=128)), reads=[pk], writes=[("OMET", j, half)])

            for gi, tiles in enumerate(GRP):
                res_mm_ln(tiles, lambda kc, tt: otl[:, kc, tt * 128:(tt + 1) * 128], [otlk], wo_blk, res_x, 0)
            if tb == 0:
                b.dump("X1", XR, [("XR", tt) for tt in range(4)])
            for gi, tiles in enumerate(GRP):
                for tt in tiles:
                    to_feature_major(XR[:, tt, :], ("XR", tt), tt, "a")
                stage_q(tiles)
            for gi, tiles in enumerate(GRP):
                stage_att(tiles)
            OMETK = lambda tiles: [("OMET", j, half) for j in tiles for half in range(2)]
            for gi, tiles in enumerate(GRP):
                stage_omet(tiles)
                res_mm_ln(tiles, lambda kc, tt: OMET[:, kc, tt * 128:(tt + 1) * 128], OMETK(tiles), wmo_blk,
                          lambda tt: (XR[:, tt, :], ("XR", tt)), 1)
            for gi, tiles in enumerate(GRP):
                for tt in tiles:
                    to_feature_major(XR[:, tt, :], ("XR", tt), tt, "b")
            if tb == 0:
                b.dump("X2", XR, [("XR", tt) for tt in range(4)])
            for cb in range(6):
                ncol = 512 if cb < 5 else 256
                wg, wgk = stream_w(wup_v[:, :, cb * 512:cb * 512 + ncol], allk("wup_s", D))
                wu, wuk_ = stream_w(wup_v[:, :, DFF + cb * 512:DFF + cb * 512 + ncol], allk("wup_s", D))
                for cc in range(ncol // 128):
                    c = cb * 4 + cc
                    pg, pkg = b.bank()
                    pu, pku = b.bank()
                    for kc in range(8):
                        S.pe(MM(pg, wg[:, kc, cc * 128:(cc + 1) * 128], XRT[:, kc, :], kc == 0, kc == 7),
                             reads=[wgk] + XRTK("b"), writes=[pkg])
                    for kc in range(8):
                        S.pe(MM(pu, wu[:, kc, cc * 128:(cc + 1) * 128], XRT[:, kc, :], kc == 0, kc == 7),
                             reads=[wuk_] + XRTK("b"), writes=[pku])
                    gc, gck = GC[c % 2]
                    ga, gak = GA[c % 2]
                    gs, gsk = GS[c % 2]
                    S.pool(CP(gc[:, 0:2], HIST[:, c, :]), reads=[("HIST", c)], writes=[(gck, "h")])
                    S.act(ACT(gc[:, 2:514], pg, AF.Copy), reads=[pkg], writes=[(gck, "m")])
                    S.pool(CP(HIST[:, c, :], gc[:, 512:514]), reads=[(gck, "m"), (gck, "h")], writes=[("HIST", c)])
                    S.act(ACT(ga, pg, AF.Identity, scale=CW[:, c, 2:3], bias=CB[:, c:c + 1]), reads=[pkg, "CW", "CB"], writes=[gak])
                    S.dve(STT(ga, gc[:, 1:513], CW[:, c, 1:2], ga, ALU.mult, ALU.add), reads=[(gck, "m"), (gck, "h"), gak, "CW"], writes=[gak])
                    S.dve(STT(ga, gc[:, 0:512], CW[:, c, 0:1], ga, ALU.mult, ALU.add), reads=[(gck, "m"), (gck, "h"), gak, "CW"], writes=[gak])
                    S.act(ACT(gs, ga, AF.Silu), reads=[gak], writes=[gsk])
                    S.dve(TT(HT[:, c, :], gs, pu, ALU.mult), reads=[gsk, pku], writes=[("HT", c)])
            HTK = [("HT", c) for c in range(NCH)]
            if tb == 0:
                b.dump("HT", HT, HTK, BF16)
            accb = [b.bank() for _ in range(8)]
            for c0 in range(0, NCH, 8):
                nc_ = min(8, NCH - c0)
                wblk = [stream_w(wdn_v[:, c0:c0 + nc_, nbk * 512:(nbk + 1) * 512], allk("wdn_s", DFF)) for nbk in range(2)]
                for tt in range(4):
                    for nbk in range(2):
                        pa, pka = accb[tt * 2 + nbk]
                        ws, wsk = wblk[nbk]
                        for ci in range(nc_):
                            c = c0 + ci
                            S.pe(MM(pa, HT[:, c, tt * 128:(tt + 1) * 128], ws[:, ci, :], c == 0, c == NCH - 1),
                                 reads=HTK + [wsk], writes=[pka])
            for tt in range(4):
                for nbk in range(2):
                    pa, pka = accb[tt * 2 + nbk]
                    S.dve(STT(Y[:, nbk * 512:(nbk + 1) * 512], XR[:, tt, nbk * 512:(nbk + 1) * 512], ALPHA, pa, ALU.mult, ALU.add),
                          reads=[("XR", tt), pka], writes=[("Y", nbk)])
                layer_norm(Y, [("Y", 0), ("Y", 1)], 2, XR[:, tt, :], ("XR", tt))
                S.dma(out_d[tb * 512 + tt * 128:tb * 512 + (tt + 1) * 128, :], XR[:, tt, :], reads=[("XR", tt)], out=True, q="pool")
        S.emit(es)
    return nc, b


def _rot(cols, half):
    return np.concatenate([cols[half:], cols[:half]])


def prep_shared(inp):
    f = np.float32
    w_in = np.asarray(inp["w_in"])[0]
    C1, C2, C3, C4, C5 = 512, 1280, 1304, 1688, 1944
    groups = []
    for hp in range(4):
        groups.append(np.arange(hp * 128, hp * 128 + 128))
    for hp in range(4):
        groups.append(np.concatenate([_rot(np.arange(h * 64, h * 64 + 64), 32) for h in (2 * hp, 2 * hp + 1)]))
    for base in (C1 + 256, C1 + 512):
        for g in range(2):
            c = np.arange(base + g * 64, base + g * 64 + 64)
            groups.append(np.concatenate([c, c]))
        for g in range(2):
            c = _rot(np.arange(base + g * 64, base + g * 64 + 64), 32)
            groups.append(np.concatenate([c, c]))
    groups.append(np.arange(C1, C1 + 128))
    groups.append(np.arange(C1 + 128, C1 + 256))
    wfm_nsa = np.ascontiguousarray(w_in[:, np.concatenate(groups)])
    tm_cols = np.concatenate([np.arange(C1 + 256 + 128, C1 + 512), np.arange(C1 + 512 + 128, C1 + 768), np.arange(C2, C2 + 24)])
    wtm_nsa = np.ascontiguousarray(w_in[:, tm_cols])
    wfm_mla = np.zeros((D, 896), f)
    wfm_mla[:, 0:384] = w_in[:, C3:C4]
    wfm_mla[:, 384:640] = w_in[:, C4:C5]
    wfm_mla[:, 640 + 64:640 + 96] = w_in[:, C5:C5 + 32]
    wfm_mla[:, 768 + 64:768 + 96] = w_in[:, _rot(np.arange(C5, C5 + 32), 16)]

    def w1_layout(w1):
        a = np.asarray(w1)[0].reshape(32, 64, 128).transpose(1, 0, 2)
        return np.ascontiguousarray(np.concatenate([a, a], 0).reshape(128, 32 * 128))

    def posT_layout(p):
        a = np.asarray(p)[0].T
        a = np.repeat(a[:, :, None], 2, axis=2)
        return np.ascontiguousarray(np.concatenate([a, a], 0).reshape(128, 64))

    w2k = np.asarray(inp["nsa_ck_w2"])[0]
    rc = _rot(np.arange(64), 32)
    w2k_l = np.ascontiguousarray(np.concatenate([w2k, w2k, w2k[:, rc], w2k[:, rc]], 1))
    b2k = np.asarray(inp["nsa_ck_b2"])[0]
    cover = np.zeros((256, 64), f)
    n = np.arange(256)[:, None] * 16
    j = np.arange(64)[None, :] * 64
    cover[:, :] = np.clip(np.minimum(n + 32, j + 64) - np.maximum(n, j), 0, None).astype(f) / 32.0
    t = np.arange(S_LEN)
    cur = (t // 64)[:, None]
    blk = np.arange(64)[None, :]
    forced = ((blk == 0) | (blk == cur) | (blk == cur - 1)) & (blk <= cur)
    cand = (blk >= 1) & (blk <= cur - 2)
    psc = np.zeros((128, 16), f)
    p = np.arange(128)
    d64 = p % 64
    psc[:, 0] = 10000.0 ** (-(2.0 * (d64 % 32)) / 64.0)
    sg = np.where(d64 < 32, -1.0, 1.0)
    psc[:, 1] = sg
    psc[:, 2] = -sg * np.pi
    d32 = p % 32
    psc[:, 3] = 10000.0 ** (-(2.0 * (d32 % 16)) / 32.0)
    sg2 = np.where(d32 < 16, -1.0, 1.0)
    psc[:, 4] = sg2
    psc[:, 5] = -sg2 * np.pi
    psc[:, 6] = np.asarray(inp["nsa_ck_b1"])[0]
    psc[:, 7] = np.asarray(inp["nsa_cv_b1"])[0]
    psc[:, 8] = np.concatenate([b2k, b2k])
    psc[:, 9] = np.concatenate([b2k[rc], b2k[rc]])
    psc[:, 10:13] = np.asarray(inp["mla_q_norm"])[0].reshape(3, 128).T
    psc[:, 13:15] = np.asarray(inp["mla_kv_norm"])[0].reshape(2, 128).T
    psc[:, 15] = -np.pi
    wuq = np.asarray(inp["mla_w_uq"])[0]
    wuq_b = np.zeros_like(wuq)
    for h in range(8):
        pe = np.arange(h * 96 + 64, h * 96 + 96)
        wuq_b[:, pe] = wuq[:, _rot(pe, 16)]
    wukv = np.asarray(inp["mla_w_ukv"])[0]
    wuk = np.ascontiguousarray(np.concatenate([wukv[:, h * 128:h * 128 + 64] for h in range(8)], 1))
    wuv = np.ascontiguousarray(np.concatenate([wukv[:, h * 128 + 64:h * 128 + 128] for h in range(8)], 1))
    lngb = np.stack([np.asarray(inp[k])[0] for k in ("ln1_g", "ln1_b", "ln2_g", "ln2_b", "ln3_g", "ln3_b")]).astype(f)
    convw = np.ascontiguousarray(np.asarray(inp["ffn_conv_w"])[0].T.reshape(NCH, 128, 3).transpose(1, 0, 2).reshape(128, NCH * 3))
    convb = np.ascontiguousarray(np.asarray(inp["ffn_conv_b"])[0].reshape(NCH, 128).T)
    return {
        "wfm_nsa": wfm_nsa, "wtm_nsa": wtm_nsa, "wfm_mla": wfm_mla,
        "w1k": w1_layout(inp["nsa_ck_w1"]), "w1v": w1_layout(inp["nsa_cv_w1"]),
        "poskT": posT_layout(inp["nsa_k_pos"]), "posvT": posT_layout(inp["nsa_v_pos"]),
        "w2k": w2k_l, "w2v": np.ascontiguousarray(np.asarray(inp["nsa_cv_w2"])[0]),
        "b2v": np.ascontiguousarray(np.asarray(inp["nsa_cv_b2"])[0][None, :]),
        "cover": cover, "cand": cand.astype(f), "forced": forced.astype(f), "psc": psc,
        "wuq_a": np.ascontiguousarray(wuq), "wuq_b": wuq_b, "wuk": wuk, "wuv": wuv,
        "w_o": np.ascontiguousarray(np.asarray(inp["w_o"])[0]),
        "mem_wq": np.ascontiguousarray(np.asarray(inp["mem_wq"])[0]),
        "mem_wk": np.ascontiguousarray(np.asarray(inp["mem_wk"])[0]),
        "mem_wv": np.ascontiguousarray(np.asarray(inp["mem_wv"])[0]),
        "mem_wo": np.ascontiguousarray(np.asarray(inp["mem_wo"])[0]),
        "w_up": np.ascontiguousarray(np.asarray(inp["ffn_w_up"])[0]),
        "w_dn": np.ascontiguousarray(np.asarray(inp["ffn_w_down"])[0]),
        "lngb": lngb, "convw": convw, "convb": convb,
    }


def prep_core(inp, bi):
    x = np.asarray(inp["x"])[bi]
    pos = np.asarray(inp["positions"])[bi].astype(np.int32)
    posc = np.zeros((1, 256), np.int32)
    posc[0, :255] = pos[31::16][:255]
    return {
        "xT": np.ascontiguousarray(x.T), "x": np.ascontiguousarray(x),
        "memT": np.ascontiguousarray(np.asarray(inp["mem"])[bi].T),
        "pos": np.ascontiguousarray(pos[None, :]), "posc": posc,
    }


_CACHE = {}


def kernel(**inputs):
    if "nc" not in _CACHE:
        _CACHE["nc"] = build_program()[0]
    nc = _CACHE["nc"]
    shared = prep_shared(inputs)
    in_maps = []
    for bi in range(8):
        m = dict(shared)
        m.update(prep_core(inputs, bi))
        in_maps.append(m)
    res = run_bass_kernel_spmd(nc, in_maps, core_ids=list(range(8)))
    return np.stack([np.asarray(r["out"], dtype=np.float32) for r in res.results], 0)
```
